# Optimizing a Trainium2 kernel written in Bass

```python
import jax, jax.numpy as jnp
from jax import lax
import numpy as np

D_MODEL = 1024
BATCH = 8
SEQ = 2048
DEPTH = 2
DEC_BATCH = 128
DEC_SEQ = 8
PAST_LEN = 16384
PAGE_SIZE = 128

N_META = 16
EPS = 1e-6
D_FF = 2816
N_BRANCH = 4
BRANCH_WIDTH = 256

DN_HEADS = 4
DN_DK = 64
DN_DV = 64
DN_QK = DN_HEADS * DN_DK
DN_VW = DN_HEADS * DN_DV
DN_CONV = 4
DN_CHUNK = 64

S5_WIDTH = 256
S5_GROUP = 16
S5_GROUPS = S5_WIDTH // S5_GROUP
S5_STATE = 64

LRU_WIDTH = 256
LRU_BLOCKS = 4
LRU_BLOCK = LRU_WIDTH // LRU_BLOCKS
LRU_CONV = 4
LRU_C = 8.0

CV_WIDTH = 256
CV_KERNEL = 31

IN_SIZES = (DN_QK, DN_QK, DN_VW, DN_VW, DN_HEADS, DN_HEADS, S5_WIDTH, LRU_WIDTH, LRU_WIDTH, CV_WIDTH, CV_WIDTH, N_BRANCH * D_MODEL)
N_IN = 2 * DN_QK + 2 * DN_VW + 2 * DN_HEADS + S5_WIDTH + 2 * LRU_WIDTH + 2 * CV_WIDTH + N_BRANCH * D_MODEL

kernel_name = 'hybrid_gated_parallel_decoder_step'


def rmsnorm(x, g):
    xf = x.astype(jnp.float32)
    y = xf * lax.rsqrt(jnp.mean(xf * xf, axis=-1, keepdims=True) + EPS)
    return (y * g.astype(jnp.float32)).astype(x.dtype)


def layernorm(x, g, b):
    xf = x.astype(jnp.float32)
    xc = xf - jnp.mean(xf, axis=-1, keepdims=True)
    y = xc * lax.rsqrt(jnp.mean(xc * xc, axis=-1, keepdims=True) + EPS)
    return (y * g.astype(jnp.float32) + b.astype(jnp.float32)).astype(x.dtype)


def l2norm(x):
    return x * lax.rsqrt(jnp.sum(x * x, axis=-1, keepdims=True) + EPS)


def swiglu(x, w_gu, w_down):
    gate, up = jnp.split(x @ w_gu, 2, axis=-1)
    return (jax.nn.silu(gate) * up) @ w_down


def causal_dwconv(buf, x, w):
    k_width, ch = w.shape
    xf = jnp.concatenate([buf.astype(x.dtype), x], axis=1)
    y = lax.conv_general_dilated(xf, w[:, None, :].astype(x.dtype), window_strides=(1,), padding='VALID',
                                 dimension_numbers=('NWC', 'WIO', 'NWC'), feature_group_count=ch)
    return y, xf[:, xf.shape[1] - (k_width - 1):]


def linear_scan(a, b):
    def combine(e1, e2):
        a1, b1 = e1
        a2, b2 = e2
        return a1 * a2, a2 * b1 + b2
    return lax.associative_scan(combine, (a, b), axis=1)[1]


def to_chunks(a, n, c):
    bsz, _, h = a.shape[:3]
    a = a.reshape(bsz, n, c, h, *a.shape[3:])
    return jnp.moveaxis(jnp.moveaxis(a, 1, 0), 3, 2)


def gated_delta_rule(q, k, v, g, beta, s0):
    bsz, t, h, dk = q.shape
    dv = v.shape[-1]
    c = min(DN_CHUNK, t)
    pad = (-t) % c
    n = (t + pad) // c

    def lpad(a):
        return jnp.pad(a, [(0, 0), (pad, 0)] + [(0, 0)] * (a.ndim - 2))

    qc = to_chunks(lpad(q * dk ** -0.5), n, c)
    kc = to_chunks(lpad(k), n, c)
    vc = to_chunks(lpad(v), n, c)
    bc = to_chunks(lpad(beta), n, c)
    gc = jnp.cumsum(to_chunks(lpad(g), n, c), axis=-1)
    causal = jnp.tril(jnp.ones((c, c), bool))
    strict = jnp.tril(jnp.ones((c, c), bool), -1)
    diff = gc[..., :, None] - gc[..., None, :]
    decay = jnp.where(causal, jnp.exp(jnp.where(causal, diff, 0.0)), 0.0)
    kb = kc * bc[..., None]
    lower = jnp.where(strict, jnp.einsum('nbhid,nbhjd->nbhij', kb, kc) * decay, 0.0)
    rhs = jnp.concatenate([vc * bc[..., None], kb * jnp.exp(gc)[..., None]], axis=-1)
    sol = lax.linalg.triangular_solve(lower + jnp.eye(c, dtype=q.dtype), rhs, left_side=True, lower=True)
    u_c, w_c = sol[..., :dv], sol[..., dv:]
    intra = jnp.where(causal, jnp.einsum('nbhid,nbhjd->nbhij', qc, kc) * decay, 0.0)

    def step(s, inp):
        qi, ki, ui, wi, ai, gi = inp
        v_new = ui - jnp.einsum('bhck,bhkv->bhcv', wi, s)
        o = (jnp.einsum('bhck,bhkv->bhcv', qi * jnp.exp(gi)[..., None], s)
             + jnp.einsum('bhij,bhjv->bhiv', ai, v_new))
        g_last = gi[..., -1]
        k_dec = ki * jnp.exp(g_last[..., None] - gi)[..., None]
        s = s * jnp.exp(g_last)[..., None, None] + jnp.einsum('bhck,bhcv->bhkv', k_dec, v_new)
        return s, o

    s_fin, o = lax.scan(step, s0, (qc, kc, u_c, w_c, intra, gc))
    o = jnp.moveaxis(jnp.moveaxis(o, 2, 3), 0, 1).reshape(bsz, n * c, h, dv)[:, pad:]
    return o, s_fin


def delta_branch(q_in, k_in, v_in, z, b_in, a_in, conv_buf, s0, conv_w, a_log, dt_bias, norm_g):
    bsz, t, _ = q_in.shape
    f32 = jnp.float32
    qkv, new_buf = causal_dwconv(conv_buf, jnp.concatenate([q_in, k_in, v_in], axis=-1), conv_w)
    qkv = jax.nn.silu(qkv.astype(f32))
    q, k, v = jnp.split(qkv, [DN_QK, 2 * DN_QK], axis=-1)
    q = l2norm(q.reshape(bsz, t, DN_HEADS, DN_DK))
    k = l2norm(k.reshape(bsz, t, DN_HEADS, DN_DK))
    v = v.reshape(bsz, t, DN_HEADS, DN_DV)
    beta = jax.nn.sigmoid(b_in.astype(f32))
    g = -jnp.exp(a_log.astype(f32)) * jax.nn.softplus(a_in.astype(f32) + dt_bias.astype(f32))
    o, s_new = gated_delta_rule(q, k, v, g, beta, s0.astype(f32))
    o = o * lax.rsqrt(jnp.mean(o * o, axis=-1, keepdims=True) + EPS) * norm_g.astype(f32)
    o = o * jax.nn.silu(z.astype(f32).reshape(bsz, t, DN_HEADS, DN_DV))
    return o.reshape(bsz, t, DN_VW).astype(q_in.dtype), new_buf, s_new.astype(s0.dtype)


def s5_branch(u, h0_re, h0_im, lam_re, lam_im, log_step, b_re, b_im, c_re, c_im, d_skip, w_glu, b_glu):
    bsz, t, _ = u.shape
    f32 = jnp.float32
    uf = u.astype(f32).reshape(bsz, t, S5_GROUPS, S5_GROUP)
    lam_re = lam_re.astype(f32)
    lam_im = lam_im.astype(f32)
    dt = jnp.exp(log_step.astype(f32))[:, None]
    mag = jnp.exp(lam_re * dt)
    lb_re = mag * jnp.cos(lam_im * dt)
    lb_im = mag * jnp.sin(lam_im * dt)
    den = lam_re * lam_re + lam_im * lam_im
    cf_re = ((lb_re - 1.0) * lam_re + lb_im * lam_im) / den
    cf_im = (lb_im * lam_re - (lb_re - 1.0) * lam_im) / den
    bu_re = jnp.einsum('btgc,gpc->btgp', uf, b_re.astype(f32))
    bu_im = jnp.einsum('btgc,gpc->btgp', uf, b_im.astype(f32))
    x_re = cf_re * bu_re - cf_im * bu_im
    x_im = cf_re * bu_im + cf_im * bu_re
    h0r = h0_re.astype(f32)
    h0i = h0_im.astype(f32)
    x_re = x_re.at[:, 0].add(lb_re * h0r - lb_im * h0i)
    x_im = x_im.at[:, 0].add(lb_re * h0i + lb_im * h0r)
    a_re = jnp.broadcast_to(lb_re, x_re.shape)
    a_im = jnp.broadcast_to(lb_im, x_im.shape)

    def combine(e1, e2):
        a1r, a1i, b1r, b1i = e1
        a2r, a2i, b2r, b2i = e2
        return (a1r * a2r - a1i * a2i, a1r * a2i + a1i * a2r,
                a2r * b1r - a2i * b1i + b2r, a2r * b1i + a2i * b1r + b2i)

    _, _, h_re, h_im = lax.associative_scan(combine, (a_re, a_im, x_re, x_im), axis=1)
    y = (jnp.einsum('btgp,gcp->btgc', h_re, c_re.astype(f32)) - jnp.einsum('btgp,gcp->btgc', h_im, c_im.astype(f32))
         + d_skip.astype(f32).reshape(S5_GROUPS, S5_GROUP) * uf)
    y = jax.nn.gelu(y.reshape(bsz, t, S5_WIDTH))
    ga, gb = jnp.split(y @ w_glu.astype(f32) + b_glu.astype(f32), 2, axis=-1)
    out = ga * jax.nn.sigmoid(gb)
    return out.astype(u.dtype), h_re[:, -1].astype(h0_re.dtype), h_im[:, -1].astype(h0_im.dtype)


def lru_branch(x_in, gate_in, conv_buf, h0, conv_w, conv_b, w_a, b_a, w_x, b_x, lam):
    bsz, t, _ = x_in.shape
    f32 = jnp.float32
    xc, new_buf = causal_dwconv(conv_buf, x_in, conv_w)
    xf = xc.astype(f32) + conv_b.astype(f32)
    xb = xf.reshape(bsz, t, LRU_BLOCKS, LRU_BLOCK)
    r = jax.nn.sigmoid(jnp.einsum('btnc,ncd->btnd', xb, w_a.astype(f32)).reshape(bsz, t, LRU_WIDTH) + b_a.astype(f32))
    i = jax.nn.sigmoid(jnp.einsum('btnc,ncd->btnd', xb, w_x.astype(f32)).reshape(bsz, t, LRU_WIDTH) + b_x.astype(f32))
    log_a = -LRU_C * r * jax.nn.softplus(-lam.astype(f32))
    a = jnp.exp(log_a)
    b = jnp.sqrt(-jnp.expm1(2.0 * log_a)) * (i * xf)
    b = b.at[:, 0].add(a[:, 0] * h0.astype(f32))
    h = linear_scan(a, b)
    out = h * jax.nn.gelu(gate_in.astype(f32))
    return out.astype(x_in.dtype), new_buf, h[:, -1].astype(h0.dtype)


def conv_branch(val, gate, conv_buf, conv_w, conv_b, ln_g, ln_b):
    glu = val * jax.nn.sigmoid(gate)
    y, new_buf = causal_dwconv(conv_buf, glu, conv_w)
    y = layernorm(y + conv_b.astype(y.dtype), ln_g, ln_b)
    return jax.nn.silu(y), new_buf


def decoder_layer(x, st, p):
    s_dn, s_dnc, s_re, s_im, s_lru, s_lruc, s_cv = st
    x = x + 0.5 * swiglu(rmsnorm(x, p['ffn1_norm']), p['ffn1_w_gu'], p['ffn1_w_down'])
    u = rmsnorm(x, p['mix_norm'])
    z = u @ p['w_in']
    dq, dk, dv, dz, db, da, su, lx, lg, cval, cgate, zg = jnp.split(z, np.cumsum(IN_SIZES)[:-1].tolist(), axis=-1)
    oa, n_dnc, n_dn = delta_branch(dq, dk, dv, dz, db, da, s_dnc, s_dn, p['dn_conv_w'], p['dn_a_log'],
                                   p['dn_dt_bias'], p['dn_norm'])
    ob, n_re, n_im = s5_branch(su, s_re, s_im, p['s5_lam_re'], p['s5_lam_im'], p['s5_log_step'], p['s5_b_re'],
                               p['s5_b_im'], p['s5_c_re'], p['s5_c_im'], p['s5_d'], p['s5_w_glu'], p['s5_b_glu'])
    oc, n_lruc, n_lru = lru_branch(lx, lg, s_lruc, s_lru, p['lru_conv_w'], p['lru_conv_b'], p['lru_w_a'],
                                   p['lru_b_a'], p['lru_w_x'], p['lru_b_x'], p['lru_lam'])
    od, n_cv = conv_branch(cval, cgate, s_cv, p['cv_conv_w'], p['cv_conv_b'], p['cv_ln_g'], p['cv_ln_b'])
    branches = jnp.stack([oa, ob, oc, od], axis=-2)
    proj = jnp.einsum('btnc,ncd->btnd', branches, p['w_branch'])
    gates = jax.nn.sigmoid(zg.reshape(*zg.shape[:-1], N_BRANCH, D_MODEL))
    x = x + jnp.sum(gates * proj, axis=-2) @ p['w_out']
    x = x + 0.5 * swiglu(rmsnorm(x, p['ffn2_norm']), p['ffn2_w_gu'], p['ffn2_w_down'])
    return x, (n_dn, n_dnc, n_re, n_im, n_lru, n_lruc, n_cv)


def empty_state(batch, dtype):
    return (jnp.zeros((batch, DN_HEADS, DN_DK, DN_DV), dtype),
            jnp.zeros((batch, DN_CONV - 1, 2 * DN_QK + DN_VW), dtype),
            jnp.zeros((batch, S5_GROUPS, S5_STATE), dtype),
            jnp.zeros((batch, S5_GROUPS, S5_STATE), dtype),
            jnp.zeros((batch, LRU_WIDTH), dtype),
            jnp.zeros((batch, LRU_CONV - 1, LRU_WIDTH), dtype),
            jnp.zeros((batch, CV_KERNEL - 1, CV_WIDTH), dtype))


def setup_inputs(seed: int = 0) -> dict:
    key = jax.random.key(seed)
    ks = iter(jax.random.split(key, 64))

    def nrm(shape, scale):
        return scale * jax.random.normal(next(ks), shape, jnp.float32)

    def unif(shape, lo, hi):
        return jax.random.uniform(next(ks), shape, jnp.float32, lo, hi)

    L = DEPTH
    cw = 2 * DN_QK + DN_VW
    dt = jnp.exp(unif((L, DN_HEADS), float(np.log(1e-3)), float(np.log(1e-1))))
    a0 = unif((L, LRU_WIDTH), 0.9, 0.999) ** (1.0 / LRU_C)
    lam_im = jnp.broadcast_to(jnp.pi * jnp.arange(S5_STATE, dtype=jnp.float32), (L, S5_GROUPS, S5_STATE))
    return {
        'x_prompt': nrm((BATCH, SEQ, D_MODEL), 1.0),
        'x_sample': nrm((DEC_BATCH, DEC_SEQ, D_MODEL), 1.0),
        'state_delta': nrm((L, DEC_BATCH, DN_HEADS, DN_DK, DN_DV), 0.1),
        'state_delta_conv': nrm((L, DEC_BATCH, DN_CONV - 1, cw), 1.0),
        'state_s5_re': nrm((L, DEC_BATCH, S5_GROUPS, S5_STATE), 0.1),
        'state_s5_im': nrm((L, DEC_BATCH, S5_GROUPS, S5_STATE), 0.1),
        'state_lru': nrm((L, DEC_BATCH, LRU_WIDTH), 0.5),
        'state_lru_conv': nrm((L, DEC_BATCH, LRU_CONV - 1, LRU_WIDTH), 1.0),
        'state_conv': nrm((L, DEC_BATCH, CV_KERNEL - 1, CV_WIDTH), 0.5),
        'meta_tokens': nrm((N_META, D_MODEL), 1.0),
        'ffn1_norm': 1.0 + nrm((L, D_MODEL), 0.01),
        'ffn1_w_gu': nrm((L, D_MODEL, 2 * D_FF), D_MODEL ** -0.5),
        'ffn1_w_down': nrm((L, D_FF, D_MODEL), D_FF ** -0.5),
        'mix_norm': 1.0 + nrm((L, D_MODEL), 0.01),
        'w_in': nrm((L, D_MODEL, N_IN), D_MODEL ** -0.5),
        'dn_conv_w': nrm((L, DN_CONV, cw), 0.5),
        'dn_a_log': jnp.log(unif((L, DN_HEADS), 1.0, 16.0)),
        'dn_dt_bias': dt + jnp.log(-jnp.expm1(-dt)),
        'dn_norm': 1.0 + nrm((L, DN_DV), 0.01),
        's5_lam_re': -0.5 + nrm((L, S5_GROUPS, S5_STATE), 0.01),
        's5_lam_im': lam_im + nrm((L, S5_GROUPS, S5_STATE), 0.01),
        's5_log_step': unif((L, S5_GROUPS), float(np.log(1e-3)), float(np.log(1e-1))),
        's5_b_re': nrm((L, S5_GROUPS, S5_STATE, S5_GROUP), (2.0 * S5_GROUP) ** -0.5),
        's5_b_im': nrm((L, S5_GROUPS, S5_STATE, S5_GROUP), (2.0 * S5_GROUP) ** -0.5),
        's5_c_re': nrm((L, S5_GROUPS, S5_GROUP, S5_STATE), 0.5),
        's5_c_im': nrm((L, S5_GROUPS, S5_GROUP, S5_STATE), 0.5),
        's5_d': nrm((L, S5_WIDTH), 1.0),
        's5_w_glu': nrm((L, S5_WIDTH, 2 * S5_WIDTH), S5_WIDTH ** -0.5),
        's5_b_glu': nrm((L, 2 * S5_WIDTH), 0.01),
        'lru_conv_w': nrm((L, LRU_CONV, LRU_WIDTH), 0.5),
        'lru_conv_b': nrm((L, LRU_WIDTH), 0.01),
        'lru_w_a': nrm((L, LRU_BLOCKS, LRU_BLOCK, LRU_BLOCK), LRU_BLOCK ** -0.5),
        'lru_b_a': nrm((L, LRU_WIDTH), 0.01),
        'lru_w_x': nrm((L, LRU_BLOCKS, LRU_BLOCK, LRU_BLOCK), LRU_BLOCK ** -0.5),
        'lru_b_x': nrm((L, LRU_WIDTH), 0.01),
        'lru_lam': jnp.log(a0) - jnp.log1p(-a0),
        'cv_conv_w': nrm((L, CV_KERNEL, CV_WIDTH), CV_KERNEL ** -0.5),
        'cv_conv_b': nrm((L, CV_WIDTH), 0.01),
        'cv_ln_g': 1.0 + nrm((L, CV_WIDTH), 0.01),
        'cv_ln_b': nrm((L, CV_WIDTH), 0.01),
        'w_branch': nrm((L, N_BRANCH, BRANCH_WIDTH, D_MODEL), BRANCH_WIDTH ** -0.5),
        'w_out': nrm((L, D_MODEL, D_MODEL), D_MODEL ** -0.5),
        'ffn2_norm': 1.0 + nrm((L, D_MODEL), 0.01),
        'ffn2_w_gu': nrm((L, D_MODEL, 2 * D_FF), D_MODEL ** -0.5),
        'ffn2_w_down': nrm((L, D_FF, D_MODEL), D_FF ** -0.5),
        'final_norm': 1.0 + nrm((D_MODEL,), 0.01),
    }


def stack_layers(states, i):
    return jnp.stack([st[i] for st in states], axis=0)


def reference(x_prompt, x_sample, state_delta, state_delta_conv, state_s5_re, state_s5_im, state_lru,
              state_lru_conv, state_conv, meta_tokens, ffn1_norm, ffn1_w_gu, ffn1_w_down, mix_norm, w_in,
              dn_conv_w, dn_a_log, dn_dt_bias, dn_norm, s5_lam_re, s5_lam_im, s5_log_step, s5_b_re, s5_b_im,
              s5_c_re, s5_c_im, s5_d, s5_w_glu, s5_b_glu, lru_conv_w, lru_conv_b, lru_w_a, lru_b_a, lru_w_x,
              lru_b_x, lru_lam, cv_conv_w, cv_conv_b, cv_ln_g, cv_ln_b, w_branch, w_out, ffn2_norm, ffn2_w_gu,
              ffn2_w_down, final_norm):
    bp = x_prompt.shape[0]
    meta = jnp.broadcast_to(meta_tokens.astype(x_prompt.dtype)[None], (bp, N_META, D_MODEL))
    xp = jnp.concatenate([meta, x_prompt], axis=1)
    xs = x_sample
    prompt_new = []
    sample_new = []
    for l in range(DEPTH):
        p = {'ffn1_norm': ffn1_norm[l], 'ffn1_w_gu': ffn1_w_gu[l], 'ffn1_w_down': ffn1_w_down[l],
             'mix_norm': mix_norm[l], 'w_in': w_in[l], 'dn_conv_w': dn_conv_w[l], 'dn_a_log': dn_a_log[l],
             'dn_dt_bias': dn_dt_bias[l], 'dn_norm': dn_norm[l], 's5_lam_re': s5_lam_re[l],
             's5_lam_im': s5_lam_im[l], 's5_log_step': s5_log_step[l], 's5_b_re': s5_b_re[l],
             's5_b_im': s5_b_im[l], 's5_c_re': s5_c_re[l], 's5_c_im': s5_c_im[l], 's5_d': s5_d[l],
             's5_w_glu': s5_w_glu[l], 's5_b_glu': s5_b_glu[l], 'lru_conv_w': lru_conv_w[l],
             'lru_conv_b': lru_conv_b[l], 'lru_w_a': lru_w_a[l], 'lru_b_a': lru_b_a[l], 'lru_w_x': lru_w_x[l],
             'lru_b_x': lru_b_x[l], 'lru_lam': lru_lam[l], 'cv_conv_w': cv_conv_w[l], 'cv_conv_b': cv_conv_b[l],
             'cv_ln_g': cv_ln_g[l], 'cv_ln_b': cv_ln_b[l], 'w_branch': w_branch[l], 'w_out': w_out[l],
             'ffn2_norm': ffn2_norm[l], 'ffn2_w_gu': ffn2_w_gu[l], 'ffn2_w_down': ffn2_w_down[l]}
        xp, st_p = decoder_layer(xp, empty_state(bp, xp.dtype), p)
        xs, st_s = decoder_layer(xs, (state_delta[l], state_delta_conv[l], state_s5_re[l], state_s5_im[l],
                                      state_lru[l], state_lru_conv[l], state_conv[l]), p)
        prompt_new.append(st_p)
        sample_new.append(st_s)
    y_prompt = rmsnorm(xp[:, N_META:], final_norm)
    y_sample = rmsnorm(xs, final_norm)
    p_delta = stack_layers(prompt_new, 0)
    p_delta_conv = stack_layers(prompt_new, 1)
    p_s5_re = stack_layers(prompt_new, 2)
    p_s5_im = stack_layers(prompt_new, 3)
    p_lru = stack_layers(prompt_new, 4)
    p_lru_conv = stack_layers(prompt_new, 5)
    p_conv = stack_layers(prompt_new, 6)
    s_delta = stack_layers(sample_new, 0)
    s_delta_conv = stack_layers(sample_new, 1)
    s_s5_re = stack_layers(sample_new, 2)
    s_s5_im = stack_layers(sample_new, 3)
    s_lru = stack_layers(sample_new, 4)
    s_lru_conv = stack_layers(sample_new, 5)
    s_conv = stack_layers(sample_new, 6)
    return (y_prompt, y_sample, p_delta, p_delta_conv, p_s5_re, p_s5_im, p_lru, p_lru_conv, p_conv,
            s_delta, s_delta_conv, s_s5_re, s_s5_im, s_lru, s_lru_conv, s_conv)
```

```python
import numpy as np
from contextlib import ExitStack
import concourse.bass as bass
import concourse.mybir as mybir
from concourse.bass_utils import run_bass_kernel_spmd

F32 = mybir.dt.float32
BF16 = mybir.dt.bfloat16
I32 = mybir.dt.int32
AF = mybir.ActivationFunctionType
ALU = mybir.AluOpType

D = 1024
KC = 8
FF = 2816
FJ = 22
NL = 2
EPS = 1e-6
TA = 1040
TB = 1152
TSM = 1152
N_IN = 6408
OFF = dict(dq=0, dk=256, dv=512, dz=768, db=1024, da=1028, su=1032, lx=1288, lg=1544, cval=1800, cgate=2056, zg=2312)

WNAMES = ['meta_tokens', 'ffn1_norm', 'ffn1_w_gu', 'ffn1_w_down', 'mix_norm', 'w_in', 'dn_conv_w', 'dn_a_log', 'dn_dt_bias',
          'dn_norm', 's5_lam_re', 's5_lam_im', 's5_log_step', 's5_b_re', 's5_b_im', 's5_c_re', 's5_c_im', 's5_d', 's5_w_glu',
          's5_b_glu', 'lru_conv_w', 'lru_conv_b', 'lru_w_a', 'lru_b_a', 'lru_w_x', 'lru_b_x', 'lru_lam', 'cv_conv_w',
          'cv_conv_b', 'cv_ln_g', 'cv_ln_b', 'w_branch', 'w_out', 'ffn2_norm', 'ffn2_w_gu', 'ffn2_w_down', 'final_norm']
SNAMES = ['state_delta', 'state_delta_conv', 'state_s5_re', 'state_s5_im', 'state_lru', 'state_lru_conv', 'state_conv']
ONAMES = ['y_prompt', 'y_sample', 'p_delta', 'p_delta_conv', 'p_s5_re', 'p_s5_im', 'p_lru', 'p_lru_conv', 'p_conv',
          's_delta', 's_delta_conv', 's_s5_re', 's_s5_im', 's_lru', 's_lru_conv', 's_conv']


class Trk:
    def __init__(self, name, obj, sem):
        self.name, self.obj, self.sem = name, obj, sem
        self.cnt = 0
        self.seen = {}


class Dep:
    __slots__ = ("w", "r")

    def __init__(self):
        self.w = None
        self.r = {}


def _wait(eng, reads, writes):
    need = {}

    def add(t, v):
        if t is eng and (eng.name == "pe" or v > eng.cnt):
            return
        if need.get(t, 0) < v:
            need[t] = v
    for d in reads:
        if d.w is not None:
            add(*d.w)
    for d in writes:
        if d.w is not None:
            add(*d.w)
        for t, v in d.r.items():
            add(t, v)
    for t, v in need.items():
        if eng.seen.get(t, 0) < v:
            eng.obj.wait_ge(t.sem, v)
            eng.seen[t] = v


def op(eng, fn, reads=(), writes=(), inc=True):
    _wait(eng, reads, writes)
    ins = fn()
    if inc:
        ins.then_inc(eng.sem, 1)
        eng.cnt += 1
        val = eng.cnt
    else:
        val = eng.cnt + 1
    for d in reads:
        if d.r.get(eng, 0) < val:
            d.r[eng] = val
    for d in writes:
        d.w = (eng, val)
        d.r = {}
    return ins


def dma(eng, dsem, out, in_, reads=(), writes=(), **kw):
    _wait(eng, reads, writes)
    ins = eng.obj.dma_start(out=out, in_=in_, **kw)
    ins.then_inc(dsem.sem, 16)
    dsem.cnt += 16
    val = dsem.cnt
    for d in reads:
        if d.r.get(dsem, 0) < val:
            d.r[dsem] = val
    for d in writes:
        d.w = (dsem, val)
        d.r = {}
    return ins


def split_tiles(n, mx=512):
    k = -(-n // mx)
    base = -(-n // k)
    out = []
    s = 0
    while s < n:
        w = min(base, n - s)
        out.append((s, w))
        s += w
    return out


class K:
    pass


def build(debug=None):
    debug = debug or {}
    nc = bass.Bass("TRN2", target_bir_lowering=False)
    g = K()
    g.nc = nc
    g.dbg_outs = {}
    shapes_in = dict(
        x_prompt=[2048, D], x_sample=[128, D],
        state_delta=[NL, 16, 4, 64, 64], state_delta_conv=[NL, 16, 3, 768], state_s5_re=[NL, 16, 1024],
        state_s5_im=[NL, 16, 1024], state_lru=[NL, 16, 256], state_lru_conv=[NL, 16, 3, 256], state_conv=[NL, 16, 30, 256],
        meta_tokens=[16, D], ffn1_norm=[NL, D], ffn1_w_gu=[NL, D, 2 * FF], ffn1_w_down=[NL, FF, D], mix_norm=[NL, D],
        w_in=[NL, D, N_IN], dn_conv_w=[NL, 4, 768], dn_a_log=[NL, 4], dn_dt_bias=[NL, 4], dn_norm=[NL, 64],
        s5_lam_re=[NL, 1024], s5_lam_im=[NL, 1024], s5_log_step=[NL, 16], s5_b_re=[NL, 1024, 16], s5_b_im=[NL, 1024, 16],
        s5_c_re=[NL, 256, 64], s5_c_im=[NL, 256, 64], s5_d=[NL, 256], s5_w_glu=[NL, 256, 512], s5_b_glu=[NL, 512],
        lru_conv_w=[NL, 4, 256], lru_conv_b=[NL, 256], lru_w_a=[NL, 256, 64], lru_b_a=[NL, 256], lru_w_x=[NL, 256, 64],
        lru_b_x=[NL, 256], lru_lam=[NL, 256], cv_conv_w=[NL, 31, 256], cv_conv_b=[NL, 256], cv_ln_g=[NL, 256],
        cv_ln_b=[NL, 256], w_branch=[NL, 4, 256, D], w_out=[NL, D, D], ffn2_norm=[NL, D], ffn2_w_gu=[NL, D, 2 * FF],
        ffn2_w_down=[NL, FF, D], final_norm=[1, D])
    shapes_out = dict(
        y_prompt=[2048, D], y_sample=[128, D], p_delta=[NL, 4, 64, 64], p_delta_conv=[NL, 3, 768], p_s5_re=[NL, 1024],
        p_s5_im=[NL, 1024], p_lru=[NL, 256], p_lru_conv=[NL, 3, 256], p_conv=[NL, 30, 256],
        s_delta=[NL, 16, 4, 64, 64], s_delta_conv=[NL, 16, 3, 768], s_s5_re=[NL, 16, 1024], s_s5_im=[NL, 16, 1024],
        s_lru=[NL, 16, 256], s_lru_conv=[NL, 16, 3, 256], s_conv=[NL, 16, 30, 256])
    g.I = {k: nc.dram_tensor(k, v, F32, kind="ExternalInput").ap() for k, v in shapes_in.items()}
    g.O = {k: nc.dram_tensor("o_" + k, v, F32, kind="ExternalOutput").ap() for k, v in shapes_out.items()}
    g.shapes_in = shapes_in
    g.shapes_out = shapes_out

    with ExitStack() as es:
        g.es = es
        _emit(g, debug)
    return nc, g


def _emit(g, debug):
    nc, es = g.nc, g.es
    I, O = g.I, g.O
    cnt = [0]

    def sb(shape, dt=F32, name=None):
        cnt[0] += 1
        return es.enter_context(nc.sbuf_tensor(name or f"t{cnt[0]}", shape, dt))

    def sem(name):
        return es.enter_context(nc.semaphore(name))

    PE = Trk("pe", nc.tensor, sem("s_pe"))
    ACT = Trk("act", nc.scalar, sem("s_act"))
    DVE = Trk("dve", nc.vector, sem("s_dve"))
    POOL = Trk("pool", nc.gpsimd, sem("s_pool"))
    SP = Trk("sp", nc.sync, sem("s_sp"))
    g.PE, g.ACT, g.DVE, g.POOL, g.SP = PE, ACT, DVE, POOL, SP
    engines = [PE, ACT, DVE, POOL, SP]
    dsems = []

    def dsem(name):
        t = Trk(name, None, sem(name))
        dsems.append(t)
        return t

    ld_sems = [dsem(f"ld{i}") for i in range(8)]
    st_sems = [dsem(f"st{i}") for i in range(4)]
    rr = dict(ld=0, st=0)

    def ldsem():
        rr['ld'] += 1
        return ld_sems[rr['ld'] % len(ld_sems)]

    def stsem():
        rr['st'] += 1
        return st_sems[rr['st'] % len(st_sems)]

    def load(out, in_, writes, reads=(), **kw):
        return dma(SP, ldsem(), out, in_, reads=reads, writes=writes, **kw)

    def store(out, in_, reads, **kw):
        return dma(SP, stsem(), out, in_, reads=reads, **kw)

    def barrier():
        for e in engines:
            for t in engines + dsems:
                if t is e:
                    continue
                if t.cnt > 0 and e.seen.get(t, 0) < t.cnt:
                    e.obj.wait_ge(t.sem, t.cnt)
                    e.seen[t] = t.cnt

    banks = [es.enter_context(nc.psum_tensor(f"bank{i}", [128, 512], F32)) for i in range(8)]
    bank_dep = [Dep() for _ in range(8)]
    bk = [0]

    def pbank():
        i = bk[0] % 8
        bk[0] += 1
        return banks[i], bank_dep[i]

    def dump(name, ap, shape, dep_list):
        if name not in debug:
            return
        t = nc.dram_tensor("dbg_" + name, shape, ap.dtype, kind="ExternalOutput").ap()
        g.dbg_outs[name] = t
        store(t, ap, reads=dep_list)

    it_i = sb([128, 128], I32)
    itf = sb([128, 128])
    ident = sb([128, 128])
    ones_bf = sb([128, 128], BF16)
    d_const = Dep()
    op(POOL, lambda: nc.gpsimd.iota(it_i[:], pattern=[[1, 128]], base=0, channel_multiplier=-1), writes=[d_const])
    op(DVE, lambda: nc.vector.tensor_copy(out=itf[:], in_=it_i[:]), reads=[d_const], writes=[d_const])
    op(DVE, lambda: nc.vector.tensor_single_scalar(out=ident[:], in_=itf[:], scalar=0.0, op=ALU.is_equal), reads=[d_const], writes=[d_const])
    op(DVE, lambda: nc.vector.memset(ones_bf[:], 1.0), writes=[d_const])
    g.ident, g.itf, g.d_const = ident, itf, d_const
    ones_f = sb([128, 128], F32)
    op(DVE, lambda: nc.vector.memset(ones_f[:], 1.0), writes=[d_const])

    xT = sb([128, KC, TSM], F32, "xT")
    uT = sb([128, KC, TSM], BF16, "uT")
    brT = sb([128, 8, TSM], BF16, "brT")
    ARENA_F32 = 19712
    arena = sb([128, ARENA_F32], F32, "arena")
    MAXT = 3
    x_dep = [[Dep() for _ in range(MAXT)] for _ in range(KC)]
    u_dep = [[Dep() for _ in range(MAXT)] for _ in range(KC)]
    br_dep = [[Dep() for _ in range(MAXT)] for _ in range(8)]

    gains = sb([128, KC, 8], F32, "gains")
    d_gain = Dep()
    GIDX = dict(ffn1=0, mix=2, ffn2=4)

    stg = [sb([128, D], F32, f"stg{i}") for i in range(2)]
    stg_dep = [Dep() for _ in range(2)]
    sg = [0]

    def staging():
        i = sg[0] % 2
        sg[0] += 1
        return stg[i], stg_dep[i]

    NSLOT = 6
    SLOT_ELEMS = 2048
    wslots = [sb([128, SLOT_ELEMS], BF16, f"wslot{i}") for i in range(NSLOT)]
    wslot_dep = [Dep() for _ in range(NSLOT)]
    wslot_sem = [dsem(f"ws{i}") for i in range(NSLOT)]
    ws = [0]

    def wslot():
        i = ws[0] % NSLOT
        ws[0] += 1
        return wslots[i], wslot_dep[i], wslot_sem[i]

    def wload(slot_ap, src_ap, sdep, ssem):
        return dma(POOL, ssem, slot_ap, src_ap, writes=[sdep])

    NTMP = 4
    tmps = [sb([128, 512], F32, f"tmp{i}") for i in range(NTMP)]
    tmp_dep = [Dep() for _ in range(NTMP)]
    tp = [0]

    def tmp():
        i = tp[0] % NTMP
        tp[0] += 1
        return tmps[i], tmp_dep[i]

    sqs = [sb([128, KC, 512], BF16, f"sq{i}") for i in range(1)] * 2
    sq_dep = [Dep()] * 2
    sqi = [0]

    def load_T(dst_fn, src, R, C, wdeps):
        st, sd = staging()
        if isinstance(src, list):
            r0 = 0
            for (ap, nr) in src:
                load(st[r0:r0 + nr, 0:C], ap, writes=[sd])
                r0 += nr
            assert r0 == R
        else:
            load(st[0:R, 0:C], src, writes=[sd])
        nchunk = -(-C // 128)
        per_bank = max(1, 512 // R)
        c = 0
        while c < nchunk:
            bank, bd = pbank()
            grp = list(range(c, min(nchunk, c + per_bank)))
            for i, cc in enumerate(grp):
                cw = min(128, C - cc * 128)
                op(PE, lambda cc=cc, cw=cw, i=i: nc.tensor.transpose(bank[0:cw, i * R:(i + 1) * R], st[0:R, cc * 128:cc * 128 + cw], ident[0:R, 0:R]),
                   reads=[sd, d_const], writes=[bd], inc=(i == len(grp) - 1))
            for i, cc in enumerate(grp):
                cw = min(128, C - cc * 128)
                op(ACT, lambda cc=cc, cw=cw, i=i: nc.scalar.copy(out=dst_fn(cc), in_=bank[0:cw, i * R:(i + 1) * R]), reads=[bd], writes=wdeps)
            c += per_bank

    def store_T(dst, src_fn, R, C, rdeps):
        st, sd = staging()
        nchunk = -(-C // 128)
        c = 0
        while c < nchunk:
            bank, bd = pbank()
            grp = list(range(c, min(nchunk, c + 4)))
            for i, cc in enumerate(grp):
                cw = min(128, C - cc * 128)
                src = src_fn(cc)
                if len(src.shape) > 2:
                    tt_, ttd = tmp()
                    o_ = tt_[0:cw, 0:R].rearrange("p (a b) -> p a b", a=src.shape[1])
                    op(DVE, lambda o_=o_, src=src: nc.vector.tensor_copy(out=o_, in_=src), reads=list(rdeps), writes=[ttd])
                    op(PE, lambda cw=cw, i=i, tt_=tt_: nc.tensor.transpose(bank[0:R, i * 128:i * 128 + cw], tt_[0:cw, 0:R], ident[0:cw, 0:cw]),
                       reads=[ttd, d_const], writes=[bd], inc=(i == len(grp) - 1))
                    continue
                op(PE, lambda cc=cc, cw=cw, i=i: nc.tensor.transpose(bank[0:R, i * 128:i * 128 + cw], src_fn(cc), ident[0:cw, 0:cw]),
                   reads=list(rdeps) + [d_const], writes=[bd], inc=(i == len(grp) - 1))
            w = min(C - c * 128, 512)
            op(ACT, lambda c=c, w=w: nc.scalar.copy(out=st[0:R, c * 128:c * 128 + w], in_=bank[0:R, 0:w]), reads=[bd], writes=[sd])
            c += 4
        store(dst, st[0:R, 0:C], reads=[sd])

    def rmsnorm_to(tiles, gidx, out_fn, out_deps_fn, final=False):
        n = len(tiles)
        SQ_OFF = 10000
        sqv = [arena[:, SQ_OFF + i * 2048:SQ_OFF + (i + 1) * 2048].bitcast(BF16).rearrange("p (c t) -> p c t", c=KC) for i in range(n)]
        sqd = [Dep() for _ in range(n)]
        bl, rl = [], []
        for ti, (s, w) in enumerate(tiles):
            op(ACT, lambda ti=ti, s=s, w=w: nc.scalar.activation(out=sqv[ti][:, :, 0:w], in_=xT[:, :, s:s + w], func=AF.Square),
               reads=[x_dep[c][ti] for c in range(KC)], writes=[sqd[ti]])
        for ti, (s, w) in enumerate(tiles):
            bank, bd = pbank()
            bl.append((bank, bd))
            for c in range(KC):
                op(PE, lambda c=c, ti=ti, w=w, bank=bank: nc.tensor.matmul(bank[:, 0:w], lhsT=ones_bf[:], rhs=sqv[ti][:, c, 0:w], start=(c == 0), stop=(c == KC - 1)),
                   reads=[sqd[ti], d_const], writes=[bd], inc=(c == KC - 1))
        for ti, (s, w) in enumerate(tiles):
            bank, bd = bl[ti]
            rs, rsd = tmp()
            rl.append((rs, rsd))
            op(ACT, lambda rs=rs, bank=bank, w=w: nc.scalar.activation(out=rs[:, 0:w], in_=bank[:, 0:w], func=AF.Ln, scale=1.0 / D, bias=eps_t[:, 0:1]), reads=[bd, d_const], writes=[rsd])
        for ti, (s, w) in enumerate(tiles):
            rs, rsd = rl[ti]
            op(ACT, lambda rs=rs, w=w: nc.scalar.activation(out=rs[:, 0:w], in_=rs[:, 0:w], func=AF.Exp, scale=-0.5), reads=[rsd], writes=[rsd])
        for ti, (s, w) in enumerate(tiles):
            rs, rsd = rl[ti]
            for c in range(KC):
                op(DVE, lambda c=c, rs=rs, s=s, w=w: nc.vector.scalar_tensor_tensor(out=out_fn(c, s, w), in0=xT[:, c, s:s + w], scalar=gains[:, c, gidx:gidx + 1],
                                                                                    in1=rs[:, 0:w], op0=ALU.mult, op1=ALU.mult),
                   reads=[x_dep[c][ti], rsd, d_gain], writes=out_deps_fn(c, ti))

    load_T(lambda c: gains[:, c, 0:7],
           [(I[nm][l:l + 1, :], 1) for (nm, l) in [("ffn1_norm", 0), ("ffn1_norm", 1), ("mix_norm", 0), ("mix_norm", 1),
                                                    ("ffn2_norm", 0), ("ffn2_norm", 1), ("final_norm", 0)]], 7, D, [d_gain])
    eps_t = sb([128, 1], F32, "eps")
    op(DVE, lambda: nc.vector.memset(eps_t[:], EPS), writes=[d_const])
    g.eps_t = eps_t

    def ffn(l, which, tiles):
        wgu = I[f"{which}_w_gu"][l]
        wdn = I[f"{which}_w_down"][l]
        gidx = GIDX[which] + l
        rmsnorm_to(tiles, gidx, lambda c, s, w: uT[:, c, s:s + w], lambda c, ti: [u_dep[c][ti]])
        hT = arena[:, 0:11 * TSM // 2].bitcast(BF16).rearrange("p (j t) -> p j t", j=11)
        h_dep = [[Dep() for _ in range(MAXT)] for _ in range(11)]
        wgu_v = wgu.rearrange("(k p) n -> p k n", p=128)
        wdn_v = wdn.rearrange("(j p) n -> p j n", p=128)
        for half in range(2):
            for jj in range(11):
                j = half * 11 + jj
                slot, sdep, ssem = wslot()
                sv = slot[:].rearrange("p (a k n) -> p a k n", a=2, k=KC)
                wload(sv[:, 0], wgu_v[:, :, j * 128:(j + 1) * 128], sdep, ssem)
                wload(sv[:, 1], wgu_v[:, :, FF + j * 128:FF + (j + 1) * 128], sdep, ssem)
                for ti, (s, w) in enumerate(tiles):
                    bg, bgd = pbank()
                    bu, bud = pbank()
                    for a, (bank, bd) in enumerate(((bg, bgd), (bu, bud))):
                        for k in range(KC):
                            op(PE, lambda a=a, k=k, bank=bank: nc.tensor.matmul(bank[:, 0:w], lhsT=sv[:, a, k, :], rhs=uT[:, k, s:s + w],
                                                                                   start=(k == 0), stop=(k == KC - 1)),
                               reads=[sdep, u_dep[k][ti]], writes=[bd], inc=(k == KC - 1))
                    t, td = tmp()
                    op(ACT, lambda: nc.scalar.activation(out=t[:, 0:w], in_=bg[:, 0:w], func=AF.Silu), reads=[bgd], writes=[td])
                    op(DVE, lambda: nc.vector.tensor_tensor(out=hT[:, jj, s:s + w], in0=t[:, 0:w], in1=bu[:, 0:w], op=ALU.mult),
                       reads=[td, bud], writes=[h_dep[jj][ti]])
            for m in range(KC):
                slot, sdep, ssem = wslot()
                sv = slot[:, 0:11 * 128].rearrange("p (j n) -> p j n", j=11)
                wload(sv, wdn_v[:, half * 11:(half + 1) * 11, m * 128:(m + 1) * 128], sdep, ssem)
                for ti, (s, w) in enumerate(tiles):
                    bank, bd = pbank()
                    for jj in range(11):
                        op(PE, lambda jj=jj: nc.tensor.matmul(bank[:, 0:w], lhsT=sv[:, jj, :], rhs=hT[:, jj, s:s + w], start=(jj == 0), stop=(jj == 10)),
                           reads=[sdep, h_dep[jj][ti]], writes=[bd], inc=(jj == 10))
                    op(DVE, lambda: nc.vector.scalar_tensor_tensor(out=xT[:, m, s:s + w], in0=bank[:, 0:w], scalar=0.5, in1=xT[:, m, s:s + w],
                                                                   op0=ALU.mult, op1=ALU.add),
                       reads=[bd, x_dep[m][ti]], writes=[x_dep[m][ti]])


    w_in_v = [I["w_in"][l].rearrange("(k p) n -> p k n", p=128) for l in range(NL)]

    class Arena:
        def __init__(self):
            self.off = 0

        def take(self, n):
            a = arena[:, self.off:self.off + n]
            self.off += n
            assert self.off <= ARENA_F32, self.off
            return a

    def proj_in(l, off, ncols, tiles, evac):
        slot, sdep, ssem = wslot()
        sv = slot[:, 0:KC * ncols].rearrange("p (k n) -> p k n", k=KC)
        wload(sv, w_in_v[l][:, :, off:off + ncols], sdep, ssem)
        for ti, (s, w) in enumerate(tiles):
            bank, bd = pbank()
            for k in range(KC):
                op(PE, lambda k=k: nc.tensor.matmul(bank[0:ncols, 0:w], lhsT=sv[:, k, :], rhs=uT[:, k, s:s + w], start=(k == 0), stop=(k == KC - 1)),
                   reads=[sdep, u_dep[k][ti]], writes=[bd], inc=(k == KC - 1))
            evac(bank, bd, ti, s, w)

    def pieces_of(name):
        if name == "A":
            return [("p", 0, 1, TA)]
        return [("p", 0, 1, 1024), ("s", 1024, 16, 8)]

    def scatter(pieces, s, w, fn):
        for pi, (kind, col0, nb, T) in enumerate(pieces):
            lo, hi = max(s, col0), min(s + w, col0 + nb * T)
            if lo >= hi:
                continue
            if kind == "s":
                assert lo == col0 and hi == col0 + nb * T
            fn(pi, pieces[pi], lo - col0, hi - lo, lo - s)

    def xf_view(xf, kind, nb, T, Kt, c, p_off, n):
        if kind == "p":
            return xf[:, c, 0, Kt + p_off:Kt + p_off + n]
        return xf[:, c, :, Kt:Kt + T]

    def src_view(ap2d, kind, nb, T):
        if kind == "p":
            return ap2d
        return ap2d.rearrange("p (b t) -> p b t", b=nb)

    cvtail = [sb([128, 2, 30], F32, f"cvtail{l}") for l in range(NL)]
    lrutail = [sb([128, 2, 3], F32, f"lrutail{l}") for l in range(NL)]
    lruh = [sb([128, 2], F32, f"lruh{l}") for l in range(NL)]
    st_dep = Dep()

    def load_small(dst_fn, srcs, R, C, deps):
        load_T(dst_fn, srcs, R, C, deps)

    def conv_taps(xf_p, acc_p, wv, bv, c, Kw, T, reads, writes):
        sl = lambda j: xf_p[..., j:j + T]
        op(DVE, lambda: nc.vector.tensor_scalar(out=acc_p, in0=sl(0), scalar1=wv[:, c, 0:1], scalar2=bv[:, c:c + 1], op0=ALU.mult, op1=ALU.add),
           reads=reads, writes=writes)
        for j in range(1, Kw):
            op(DVE, lambda j=j: nc.vector.scalar_tensor_tensor(out=acc_p, in0=sl(j), scalar=wv[:, c, j:j + 1], in1=acc_p, op0=ALU.mult, op1=ALU.add),
               reads=list(reads) + list(writes), writes=writes)

    def branch_conv(l, name, TS, tiles):
        pieces = pieces_of(name)
        A = Arena()
        dW = Dep()
        wv = A.take(2 * 32).rearrange("p (c j) -> p c j", c=2)
        prm = A.take(8).rearrange("p (c j) -> p c j", c=2)
        load_T(lambda c: wv[:, c, 0:31], I["cv_conv_w"][l], 31, 256, [dW])
        load_T(lambda c: prm[:, c, 0:3], [(I["cv_conv_b"][l:l + 1, :], 1), (I["cv_ln_g"][l:l + 1, :], 1), (I["cv_ln_b"][l:l + 1, :], 1)], 3, 256, [dW])
        xfs, accs, dxf = [], [], []
        for (kind, col0, nb, T) in pieces:
            xfs.append(A.take(2 * nb * (30 + T)).rearrange("p (c b t) -> p c b t", c=2, b=nb))
            dxf.append([Dep(), Dep()])
        acc = A.take(2 * TS).rearrange("p (c t) -> p c t", c=2)
        dacc = [[Dep() for _ in tiles] for _ in range(2)]
        for pi, (kind, col0, nb, T) in enumerate(pieces):
            for c in range(2):
                if kind == "p" and name == "A":
                    op(DVE, lambda c=c, pi=pi: nc.vector.memset(xfs[pi][:, c, :, 0:30], 0.0), writes=[dxf[pi][c]])
                elif kind == "p":
                    op(DVE, lambda c=c, pi=pi: nc.vector.tensor_copy(out=xfs[pi][:, c, 0, 0:30], in_=cvtail[l][:, c, :]), reads=[st_dep], writes=[dxf[pi][c]])
            if kind == "s":
                for b0 in range(0, 16, 4):
                    load_T(lambda c, b0=b0, pi=pi: xfs[pi][:, c, b0:b0 + 4, 0:30], I["state_conv"][l, b0:b0 + 4].rearrange("b j c -> (b j) c"), 120, 256,
                           [dxf[pi][0], dxf[pi][1]])
        sgt = A.take(TS)
        dsg = [Dep() for _ in tiles]
        for c in range(2):

            def ev_gate(bank, bd, ti, s, w):
                op(ACT, lambda: nc.scalar.activation(out=sgt[:, s:s + w], in_=bank[:, 0:w], func=AF.Sigmoid), reads=[bd], writes=[dsg[ti]])

            def ev_val(bank, bd, ti, s, w, c=c):
                def f(pi, piece, p_off, n, b_off):
                    kind, col0, nb, T = piece
                    op(DVE, lambda: nc.vector.tensor_tensor(out=xf_view(xfs[pi], kind, nb, T, 30, c, p_off, n),
                                                            in0=src_view(bank[:, b_off:b_off + n], kind, nb, T),
                                                            in1=src_view(sgt[:, s + b_off:s + b_off + n], kind, nb, T), op=ALU.mult),
                       reads=[bd, dsg[ti]], writes=[dxf[pi][c]])
                scatter(pieces, s, w, f)
            proj_in(l, OFF["cgate"] + c * 128, 128, tiles, ev_gate)
            proj_in(l, OFF["cval"] + c * 128, 128, tiles, ev_val)
        for pi, (kind, col0, nb, T) in enumerate(pieces):
            for c in range(2):
                xf_p = xfs[pi][:, c, 0, :] if kind == "p" else xfs[pi][:, c, :, :]
                acc_p = acc[:, c, col0:col0 + nb * T] if kind == "p" else acc[:, c, col0:col0 + nb * T].rearrange("p (b t) -> p b t", b=nb)
                conv_taps(xf_p, acc_p, wv, prm[:, :, 0], c, 31, T, [dW, dxf[pi][c]], [dacc[c][ti] for ti in range(len(tiles))])
        nt = len(tiles)
        sqs_ = [A.take(2 * 512).rearrange("p (c t) -> p c t", c=2) for _ in range(nt)]
        dsq = [Dep() for _ in range(nt)]
        means = [A.take(512) for _ in range(nt)]
        rstds = [A.take(512) for _ in range(nt)]
        dmean = [Dep() for _ in range(nt)]
        drstd = [Dep() for _ in range(nt)]
        bks = []
        for ti, (s, w) in enumerate(tiles):
            op(ACT, lambda ti=ti, s=s, w=w: nc.scalar.activation(out=sqs_[ti][:, :, 0:w], in_=acc[:, :, s:s + w], func=AF.Square), reads=[dacc[0][ti], dacc[1][ti]], writes=[dsq[ti]])
        for ti, (s, w) in enumerate(tiles):
            b1, b1d = pbank()
            b2, b2d = pbank()
            bks.append((b1, b1d, b2, b2d))
            for c in range(2):
                op(PE, lambda c=c, b1=b1, s=s, w=w: nc.tensor.matmul(b1[:, 0:w], lhsT=ones_f[:], rhs=acc[:, c, s:s + w], start=(c == 0), stop=(c == 1)),
                   reads=[dacc[c][ti], d_const], writes=[b1d], inc=(c == 1))
            for c in range(2):
                op(PE, lambda c=c, b2=b2, ti=ti, w=w: nc.tensor.matmul(b2[:, 0:w], lhsT=ones_f[:], rhs=sqs_[ti][:, c, 0:w], start=(c == 0), stop=(c == 1)),
                   reads=[dsq[ti], d_const], writes=[b2d], inc=(c == 1))
        for ti, (s, w) in enumerate(tiles):
            b1, b1d, b2, b2d = bks[ti]
            op(ACT, lambda ti=ti, b1=b1, w=w: nc.scalar.mul(out=means[ti][:, 0:w], in_=b1[:, 0:w], mul=1.0 / 256), reads=[b1d], writes=[dmean[ti]])
        for ti, (s, w) in enumerate(tiles):
            op(DVE, lambda ti=ti, w=w: nc.vector.tensor_tensor(out=rstds[ti][:, 0:w], in0=means[ti][:, 0:w], in1=means[ti][:, 0:w], op=ALU.mult), reads=[dmean[ti]], writes=[drstd[ti]])
        for ti, (s, w) in enumerate(tiles):
            b1, b1d, b2, b2d = bks[ti]
            op(DVE, lambda ti=ti, b2=b2, w=w: nc.vector.scalar_tensor_tensor(out=rstds[ti][:, 0:w], in0=b2[:, 0:w], scalar=1.0 / 256, in1=rstds[ti][:, 0:w], op0=ALU.mult, op1=ALU.subtract),
               reads=[b2d, drstd[ti]], writes=[drstd[ti]])
        for ti, (s, w) in enumerate(tiles):
            op(ACT, lambda ti=ti, w=w: nc.scalar.activation(out=rstds[ti][:, 0:w], in_=rstds[ti][:, 0:w], func=AF.Ln, bias=eps_t[:, 0:1]), reads=[drstd[ti], d_const], writes=[drstd[ti]])
        for ti, (s, w) in enumerate(tiles):
            op(ACT, lambda ti=ti, w=w: nc.scalar.activation(out=rstds[ti][:, 0:w], in_=rstds[ti][:, 0:w], func=AF.Exp, scale=-0.5), reads=[drstd[ti]], writes=[drstd[ti]])
        for ti, (s, w) in enumerate(tiles):
            for c in range(2):
                op(DVE, lambda c=c, ti=ti, s=s, w=w: nc.vector.tensor_tensor(out=acc[:, c, s:s + w], in0=acc[:, c, s:s + w], in1=means[ti][:, 0:w], op=ALU.subtract),
                   reads=[dacc[c][ti], dmean[ti]], writes=[dacc[c][ti]])
                op(DVE, lambda c=c, ti=ti, s=s, w=w: nc.vector.tensor_tensor(out=acc[:, c, s:s + w], in0=acc[:, c, s:s + w], in1=rstds[ti][:, 0:w], op=ALU.mult),
                   reads=[dacc[c][ti], drstd[ti]], writes=[dacc[c][ti]])
                op(ACT, lambda c=c, s=s, w=w: nc.scalar.activation(out=brT[:, 6 + c, s:s + w], in_=acc[:, c, s:s + w], func=AF.Silu, scale=prm[:, c, 1:2], bias=prm[:, c, 2:3]),
                   reads=[dacc[c][ti], dW], writes=[br_dep[6 + c][ti]])
        for pi, (kind, col0, nb, T) in enumerate(pieces):
            if kind == "p" and name == "A":
                for c in range(2):
                    op(DVE, lambda c=c, pi=pi: nc.vector.tensor_copy(out=cvtail[l][:, c, :], in_=xfs[pi][:, c, 0, T:T + 30]), reads=[dxf[pi][c]], writes=[st_dep])
            elif kind == "p":
                store_T(O["p_conv"][l], lambda c, pi=pi, T=T: xfs[pi][:, c, 0, T:T + 30], 30, 256, dxf[pi])
            else:
                for b0 in range(0, 16, 4):
                    store_T(O["s_conv"][l, b0:b0 + 4].rearrange("b j c -> (b j) c"), lambda c, pi=pi, b0=b0, T=T: xfs[pi][:, c, b0:b0 + 4, T:T + 30], 120, 256, dxf[pi])

    def branch_lru(l, name, TS, tiles):
        pieces = pieces_of(name)
        A = Arena()
        dW = Dep()
        wv = A.take(8).rearrange("p (c j) -> p c j", c=2)
        prm = A.take(12).rearrange("p (c j) -> p c j", c=2)
        load_T(lambda c: wv[:, c, 0:4], I["lru_conv_w"][l], 4, 256, [dW])
        load_T(lambda c: prm[:, c, 0:4], [(I[k][l:l + 1, :], 1) for k in ("lru_conv_b", "lru_b_a", "lru_b_x", "lru_lam")], 4, 256, [dW])
        op(ACT, lambda: nc.scalar.activation(out=prm[:, :, 4:5], in_=prm[:, :, 3:4], func=AF.Exp, scale=-1.0), reads=[dW], writes=[dW])
        op(ACT, lambda: nc.scalar.activation(out=prm[:, :, 4:5], in_=prm[:, :, 4:5], func=AF.Ln, bias=1.0), reads=[dW], writes=[dW])
        op(ACT, lambda: nc.scalar.mul(out=prm[:, :, 4:5], in_=prm[:, :, 4:5], mul=-8.0), reads=[dW], writes=[dW])
        wg = A.take(2 * 2 * 128).rearrange("p (a c n) -> p a c n", a=2, c=2)
        op(DVE, lambda: nc.vector.memset(wg, 0.0), writes=[dW])
        for a, nm in enumerate(("lru_w_a", "lru_w_x")):
            for c in range(2):
                for i in range(2):
                    load(wg[i * 64:(i + 1) * 64, a, c, i * 64:(i + 1) * 64], I[nm][l, (2 * c + i) * 64:(2 * c + i + 1) * 64, :], writes=[dW])
        xfs, dxf = [], []
        for (kind, col0, nb, T) in pieces:
            xfs.append(A.take(2 * nb * (3 + T)).rearrange("p (c b t) -> p c b t", c=2, b=nb))
            dxf.append([Dep(), Dep()])
        mk = lambda: A.take(2 * TS).rearrange("p (c t) -> p c t", c=2)
        xc, ra, ib, hh, gl = mk(), mk(), mk(), mk(), mk()
        dxc, dra, dib, dhh, dgl = [[Dep(), Dep()] for _ in range(5)]
        h0s = A.take(2 * 16).rearrange("p (c b) -> p c b", c=2)
        dh0 = Dep()
        for pi, (kind, col0, nb, T) in enumerate(pieces):
            for c in range(2):
                if kind == "p" and name == "A":
                    op(DVE, lambda c=c, pi=pi: nc.vector.memset(xfs[pi][:, c, :, 0:3], 0.0), writes=[dxf[pi][c]])
                elif kind == "p":
                    op(DVE, lambda c=c, pi=pi: nc.vector.tensor_copy(out=xfs[pi][:, c, 0, 0:3], in_=lrutail[l][:, c, :]), reads=[st_dep], writes=[dxf[pi][c]])
            if kind == "s":
                load_T(lambda c, pi=pi: xfs[pi][:, c, :, 0:3], I["state_lru_conv"][l].rearrange("b j c -> (b j) c"), 48, 256, [dxf[pi][0], dxf[pi][1]])
                load_T(lambda c: h0s[:, c, :], I["state_lru"][l], 16, 256, [dh0])
        for c in range(2):
            def ev_x(bank, bd, ti, s, w, c=c):
                def f(pi, piece, p_off, n, b_off):
                    kind, col0, nb, T = piece
                    op(ACT, lambda: nc.scalar.copy(out=xf_view(xfs[pi], kind, nb, T, 3, c, p_off, n), in_=src_view(bank[:, b_off:b_off + n], kind, nb, T)),
                       reads=[bd], writes=[dxf[pi][c]])
                scatter(pieces, s, w, f)

            def ev_g(bank, bd, ti, s, w, c=c):
                op(ACT, lambda: nc.scalar.activation(out=gl[:, c, s:s + w], in_=bank[:, 0:w], func=AF.Gelu), reads=[bd], writes=[dgl[c]])
            proj_in(l, OFF["lx"] + c * 128, 128, tiles, ev_x)
            proj_in(l, OFF["lg"] + c * 128, 128, tiles, ev_g)
        for pi, (kind, col0, nb, T) in enumerate(pieces):
            for c in range(2):
                xf_p = xfs[pi][:, c, 0, :] if kind == "p" else xfs[pi][:, c, :, :]
                v = lambda t: (t[:, c, col0:col0 + nb * T] if kind == "p" else t[:, c, col0:col0 + nb * T].rearrange("p (b t) -> p b t", b=nb))
                conv_taps(xf_p, v(xc), wv, prm[:, :, 0], c, 4, T, [dW, dxf[pi][c]], [dxc[c]])
        for c in range(2):
            for ti, (s, w) in enumerate(tiles):
                for a, (dst, dd) in enumerate(((ra, dra), (ib, dib))):
                    bank, bd = pbank()
                    op(PE, lambda a=a: nc.tensor.matmul(bank[:, 0:w], lhsT=wg[:, a, c, :], rhs=xc[:, c, s:s + w], start=True, stop=True), reads=[dW, dxc[c]], writes=[bd])
                    op(ACT, lambda a=a, dst=dst, bank=bank: nc.scalar.activation(out=dst[:, c, s:s + w], in_=bank[:, 0:w], func=AF.Sigmoid, bias=prm[:, c, 1 + a:2 + a]),
                       reads=[bd, dW], writes=[dd[c]])
            op(ACT, lambda: nc.scalar.activation(out=ra[:, c, 0:TS], in_=ra[:, c, 0:TS], func=AF.Exp, scale=prm[:, c, 4:5]), reads=[dra[c], dW], writes=[dra[c]])
            op(DVE, lambda: nc.vector.tensor_tensor(out=ib[:, c, 0:TS], in0=ib[:, c, 0:TS], in1=xc[:, c, 0:TS], op=ALU.mult), reads=[dib[c], dxc[c]], writes=[dib[c]])
            op(DVE, lambda: nc.vector.tensor_tensor(out=hh[:, c, 0:TS], in0=ra[:, c, 0:TS], in1=ra[:, c, 0:TS], op=ALU.mult), reads=[dra[c]], writes=[dhh[c]])
            op(ACT, lambda: nc.scalar.activation(out=hh[:, c, 0:TS], in_=hh[:, c, 0:TS], func=AF.Sqrt, scale=-1.0, bias=1.0), reads=[dhh[c]], writes=[dhh[c]])
            op(DVE, lambda: nc.vector.tensor_tensor(out=ib[:, c, 0:TS], in0=ib[:, c, 0:TS], in1=hh[:, c, 0:TS], op=ALU.mult), reads=[dib[c], dhh[c]], writes=[dib[c]])
            for pi, (kind, col0, nb, T) in enumerate(pieces):
                a3 = ra[:, c, col0:col0 + nb * T].rearrange("p (b t) -> p b t", b=nb)
                b3 = ib[:, c, col0:col0 + nb * T].rearrange("p (b t) -> p b t", b=nb)
                if not (kind == "p" and name == "A"):
                    h0v = lruh[l][:, c:c + 1] if kind == "p" else h0s[:, c, :]
                    hd = st_dep if kind == "p" else dh0
                    t0, t0d = tmp()
                    op(DVE, lambda: nc.vector.tensor_tensor(out=t0[:, 0:nb], in0=a3[:, :, 0], in1=h0v, op=ALU.mult), reads=[dra[c], hd], writes=[t0d])
                    op(DVE, lambda: nc.vector.tensor_tensor(out=b3[:, :, 0], in0=b3[:, :, 0], in1=t0[:, 0:nb], op=ALU.add), reads=[dib[c], t0d], writes=[dib[c]])
                if nb > 1:
                    op(DVE, lambda: nc.vector.memset(a3[:, :, 0:1], 0.0), reads=[dra[c]], writes=[dra[c]])
                op(DVE, lambda: nc.vector.tensor_tensor_scan(out=hh[:, c, col0:col0 + nb * T], data0=ra[:, c, col0:col0 + nb * T], data1=ib[:, c, col0:col0 + nb * T],
                                                             initial=0.0, op0=ALU.mult, op1=ALU.add), reads=[dra[c], dib[c], dhh[c]], writes=[dhh[c]])
            for ti, (s, w) in enumerate(tiles):
                op(DVE, lambda: nc.vector.tensor_tensor(out=brT[:, 4 + c, s:s + w], in0=hh[:, c, s:s + w], in1=gl[:, c, s:s + w], op=ALU.mult),
                   reads=[dhh[c], dgl[c]], writes=[br_dep[4 + c][ti]])
        for pi, (kind, col0, nb, T) in enumerate(pieces):
            if kind == "p" and name == "A":
                for c in range(2):
                    op(DVE, lambda c=c, pi=pi: nc.vector.tensor_copy(out=lrutail[l][:, c, :], in_=xfs[pi][:, c, 0, T:T + 3]), reads=[dxf[pi][c]], writes=[st_dep])
                    op(DVE, lambda c=c: nc.vector.tensor_copy(out=lruh[l][:, c:c + 1], in_=hh[:, c, col0 + T - 1:col0 + T]), reads=[dhh[c]], writes=[st_dep])
            elif kind == "p":
                store_T(O["p_lru_conv"][l], lambda c, pi=pi, T=T: xfs[pi][:, c, 0, T:T + 3], 3, 256, dxf[pi])
                store_T(O["p_lru"][l:l + 1, :], lambda c, col0=col0, T=T: hh[:, c, col0 + T - 1:col0 + T], 1, 256, dhh)
            else:
                store_T(O["s_lru_conv"][l].rearrange("b j c -> (b j) c"), lambda c, pi=pi, T=T: xfs[pi][:, c, :, T:T + 3], 48, 256, dxf[pi])
                store_T(O["s_lru"][l], lambda c, col0=col0, nb=nb, T=T: hh[:, c, col0:col0 + nb * T].rearrange("p (b t) -> p b t", b=nb)[:, :, T - 1], 16, 256, dhh)


    s5h = [sb([128, 8, 2, 1], F32, f"s5h{l}") for l in range(NL)]

    def branch_s5(l, name, TS, tiles):
        import math
        pieces = pieces_of(name)
        A = Arena()
        dW = Dep()
        LT = 130 if name == "A" else 128
        dv = lambda f, reads=(), writes=(): op(DVE, f, reads=list(reads) + [dW], writes=list(writes) + [dW])
        av = lambda f, reads=(), writes=(): op(ACT, f, reads=list(reads) + [dW], writes=list(writes) + [dW])
        lam = A.take(16).rearrange("p (m r) -> p m r", m=8)
        load_T(lambda m: lam[:, m, 0:2], [(I["s5_lam_re"][l:l + 1, :], 1), (I["s5_lam_im"][l:l + 1, :], 1)], 2, 1024, [dW])
        row = A.take(16)
        load(row[0:1, 0:16], I["s5_log_step"][l:l + 1, :], writes=[dW])
        bank, bd = pbank()
        op(PE, lambda: nc.tensor.matmul(bank[:, 0:16], lhsT=ones_f[0:1, :], rhs=row[0:1, 0:16], start=True, stop=True), reads=[dW, d_const], writes=[bd])
        sm = A.take(8 * 16).rearrange("p (m j) -> p m j", m=8)
        V = lambda j: sm[:, :, j]
        DT, TH, MAG, CC, SS, LBR, LBI, CFR, CFI, DEN, T1, T2, T3, NCFI = range(14)
        LR, LI = lam[:, :, 0], lam[:, :, 1]
        for h in range(2):
            av(lambda h=h: nc.scalar.activation(out=sm[h * 64:(h + 1) * 64, :, DT], in_=bank[h * 64:(h + 1) * 64, 0:16].rearrange("p (m x) -> p m x", x=2)[:, :, h], func=AF.Exp), reads=[bd])
        tt = lambda o, a, b, f: dv(lambda: nc.vector.tensor_tensor(out=o, in0=a, in1=b, op=f))
        tt(V(TH), LI, V(DT), ALU.mult)
        tt(V(T1), LR, V(DT), ALU.mult)
        av(lambda: nc.scalar.activation(out=V(MAG), in_=V(T1), func=AF.Exp))
        hp = A.take(1)
        dv(lambda: nc.vector.memset(hp, math.pi / 2))
        av(lambda: nc.scalar.activation(out=V(SS), in_=V(TH), func=AF.Sin, scale=1.0 / 16))
        av(lambda: nc.scalar.activation(out=V(CC), in_=V(TH), func=AF.Sin, scale=1.0 / 16, bias=hp[:, 0:1]))
        for _ in range(4):
            tt(V(T1), V(CC), V(CC), ALU.mult)
            tt(V(T2), V(SS), V(SS), ALU.mult)
            tt(V(T3), V(CC), V(SS), ALU.mult)
            tt(V(CC), V(T1), V(T2), ALU.subtract)
            tt(V(SS), V(T3), V(T3), ALU.add)
        tt(V(LBR), V(MAG), V(CC), ALU.mult)
        tt(V(LBI), V(MAG), V(SS), ALU.mult)
        tt(V(T1), LR, LR, ALU.mult)
        tt(V(T2), LI, LI, ALU.mult)
        tt(V(DEN), V(T1), V(T2), ALU.add)
        dv(lambda: nc.vector.reciprocal(out=V(DEN), in_=V(DEN)))
        dv(lambda: nc.vector.tensor_scalar_add(out=V(T3), in0=V(LBR), scalar1=-1.0))
        tt(V(T1), V(T3), LR, ALU.mult)
        tt(V(T2), V(LBI), LI, ALU.mult)
        tt(V(T1), V(T1), V(T2), ALU.add)
        tt(V(CFR), V(T1), V(DEN), ALU.mult)
        tt(V(T1), V(LBI), LR, ALU.mult)
        tt(V(T2), V(T3), LI, ALU.mult)
        tt(V(T1), V(T1), V(T2), ALU.subtract)
        tt(V(CFI), V(T1), V(DEN), ALU.mult)
        dv(lambda: nc.vector.tensor_scalar_mul(out=V(NCFI), in0=V(CFI), scalar1=-1.0))
        Bre = A.take(128).rearrange("p (m c) -> p m c", m=8)
        Bim = A.take(128).rearrange("p (m c) -> p m c", m=8)
        Bp = A.take(128).rearrange("p (m c) -> p m c", m=8)
        tb = A.take(16)
        for m in range(8):
            load(Bre[:, m, :], I["s5_b_re"][l, m * 128:(m + 1) * 128, :], writes=[dW])
            load(Bim[:, m, :], I["s5_b_im"][l, m * 128:(m + 1) * 128, :], writes=[dW])
        BT = A.take(8 * 2 * 128).rearrange("p (m r n) -> p m r n", m=8, r=2)
        CT = A.take(8 * 2 * 128).rearrange("p (m r n) -> p m r n", m=8, r=2)
        E = A.take(8 * 128).rearrange("p (m n) -> p m n", m=8)
        for ri in range(2):
            dv(lambda: nc.vector.memset(E, 0.0))
            for m in range(8):
                X1, X2, sc2 = (Bre, Bim, NCFI) if ri == 0 else (Bim, Bre, CFI)
                dv(lambda m=m, X2=X2, sc2=sc2: nc.vector.tensor_scalar_mul(out=tb, in0=X2[:, m, :], scalar1=sm[:, m, sc2:sc2 + 1]))
                dv(lambda m=m, X1=X1: nc.vector.scalar_tensor_tensor(out=Bp[:, m, :], in0=X1[:, m, :], scalar=sm[:, m, CFR:CFR + 1], in1=tb, op0=ALU.mult, op1=ALU.add))
                for h in range(2):
                    o0 = (m % 4) * 32 + h * 16
                    dv(lambda m=m, h=h, o0=o0: nc.vector.tensor_copy(out=E[h * 64:(h + 1) * 64, m, o0:o0 + 16], in_=Bp[h * 64:(h + 1) * 64, m, :]))
            for m in range(8):
                bk, bkd = pbank()
                op(PE, lambda m=m: nc.tensor.transpose(bk[:, 0:128], E[:, m, :], ident[:]), reads=[dW, d_const], writes=[bkd])
                av(lambda m=m, bk=bk: nc.scalar.copy(out=BT[:, m, ri, :], in_=bk[:, 0:128]), reads=[bkd])
        dv(lambda: nc.vector.memset(CT, 0.0))
        for ri, nm in enumerate(("s5_c_re", "s5_c_im")):
            for rb in range(2):
                st, sd = staging()
                load(st[:, 0:64], I[nm][l, rb * 128:(rb + 1) * 128, :], writes=[sd])
                load(st[:, 64:128], I[nm][l, rb * 128:(rb + 1) * 128, :], writes=[sd])
                bk, bkd = pbank()
                op(PE, lambda: nc.tensor.transpose(bk[:, 0:128], st[:, 0:128], ident[:]), reads=[sd, d_const], writes=[bkd])
                for m in range(rb * 4, rb * 4 + 4):
                    for h in range(2):
                        o0 = (m % 4) * 32 + h * 16
                        av(lambda m=m, h=h, o0=o0, bk=bk: nc.scalar.mul(out=CT[h * 64:(h + 1) * 64, m, ri, o0:o0 + 16], in_=bk[h * 64:(h + 1) * 64, o0:o0 + 16],
                                                                        mul=(1.0 if ri == 0 else -1.0)), reads=[bkd])
        cosT = A.take(8 * LT).rearrange("p (m t) -> p m t", m=8)
        sinT = A.take(8 * LT).rearrange("p (m t) -> p m t", m=8)
        rho = A.take(8 * LT).rearrange("p (m t) -> p m t", m=8)
        rhos = A.take(8 * 128).rearrange("p (m t) -> p m t", m=8)
        ta = A.take(8 * 65).rearrange("p (m t) -> p m t", m=8)
        tb2 = A.take(8 * 65).rearrange("p (m t) -> p m t", m=8)
        dv(lambda: nc.vector.tensor_copy(out=cosT[:, :, 0:1], in_=sm[:, :, CC:CC + 1]))
        dv(lambda: nc.vector.tensor_copy(out=sinT[:, :, 0:1], in_=sm[:, :, SS:SS + 1]))
        n = 1
        while n < LT:
            k = min(n, LT - n)
            cn = cosT[:, :, n - 1:n].to_broadcast([128, 8, k])
            sn = sinT[:, :, n - 1:n].to_broadcast([128, 8, k])
            tt(ta[:, :, 0:k], cosT[:, :, 0:k], cn, ALU.mult)
            tt(tb2[:, :, 0:k], sinT[:, :, 0:k], sn, ALU.mult)
            tt(cosT[:, :, n:n + k], ta[:, :, 0:k], tb2[:, :, 0:k], ALU.subtract)
            tt(ta[:, :, 0:k], sinT[:, :, 0:k], cn, ALU.mult)
            tt(tb2[:, :, 0:k], cosT[:, :, 0:k], sn, ALU.mult)
            tt(sinT[:, :, n:n + k], ta[:, :, 0:k], tb2[:, :, 0:k], ALU.add)
            n += k
        dv(lambda: nc.vector.tensor_copy(out=rho, in_=sm[:, :, MAG:MAG + 1].to_broadcast([128, 8, LT])))
        dv(lambda: nc.vector.tensor_copy(out=rhos, in_=sm[:, :, MAG:MAG + 1].to_broadcast([128, 8, 128])))
        dv(lambda: nc.vector.memset(rhos.rearrange("p m (b t) -> p m b t", t=8)[:, :, :, 0:1], 0.0))
        su = A.take(2 * TS).rearrange("p (c t) -> p c t", c=2)
        dsu = Dep()
        for c in range(2):
            proj_in(l, OFF["su"] + c * 128, 128, tiles,
                    lambda bank, bd, ti, s, w, c=c: op(ACT, lambda: nc.scalar.copy(out=su[:, c, s:s + w], in_=bank[:, 0:w]), reads=[bd], writes=[dsu]))
        dsk = A.take(2)
        load_T(lambda c: dsk[:, c:c + 1], I["s5_d"][l:l + 1, :], 1, 256, [dW])
        Hs = A.take(8 * 2 * LT).rearrange("p (m r t) -> p m r t", m=8, r=2)
        dH = Dep()
        xt = [A.take(8 * LT).rearrange("p (m t) -> p m t", m=8) for _ in range(4)]
        dx = Dep()
        hS = A.take(8 * 2 * 16).rearrange("p (m r b) -> p m r b", m=8, r=2)
        dHP = Dep()
        dv(lambda: nc.vector.memset(rho[:, :, 0:1], 0.0))
        for (kind, col0, nb, T) in pieces:
            if kind == "s":
                for ri, nm in enumerate(("state_s5_re", "state_s5_im")):
                    load_T(lambda m, ri=ri: hS[:, m, ri, 0:16], I[nm][l], 16, 1024, [dW])
                hprev = hS
                subs = [(col0, nb, T)]
            else:
                hprev = s5h[l]
                if name == "A":
                    dv(lambda: nc.vector.memset(s5h[l][:], 0.0), writes=[st_dep])
                subs = [(col0 + i * LT, 1, LT) for i in range(T // LT)]
            for (c0, nbb, Tl) in subs:
                nn = nbb * Tl
                assert nn == LT
                gsz = max(1, 512 // nn)
                groups = [list(range(i, min(8, i + gsz))) for i in range(0, 8, gsz)]
                v4 = lambda ap: ap.rearrange("p m (b t) -> p m b t", b=nbb)
                rd = [dx, st_dep, dW, dHP]
                f = lambda o, a, b, g_, extra=(): op(DVE, lambda: nc.vector.tensor_tensor(out=o, in0=a, in1=b, op=g_), reads=rd + list(extra), writes=[dx])
                for grp in groups:
                    g0, gl = grp[0], len(grp)
                    br_, brd = pbank()
                    bi_, bid = pbank()
                    for (bk_, bkd_, ri) in ((br_, brd, 0), (bi_, bid, 1)):
                        for j, m in enumerate(grp):
                            op(PE, lambda m=m, j=j, bk_=bk_, ri=ri: nc.tensor.matmul(bk_[:, j * nn:(j + 1) * nn], lhsT=BT[:, m, ri, :], rhs=su[:, m // 4, c0:c0 + nn], start=True, stop=True),
                               reads=[dW, dsu], writes=[bkd_], inc=(j == gl - 1))
                    pr = br_[:, 0:gl * nn].rearrange("p (m b t) -> p m b t", m=gl, b=nbb)
                    pi_ = bi_[:, 0:gl * nn].rearrange("p (m b t) -> p m b t", m=gl, b=nbb)
                    cv = cosT[:, g0:g0 + gl, 0:Tl].unsqueeze(2).to_broadcast([128, gl, nbb, Tl])
                    sv_ = sinT[:, g0:g0 + gl, 0:Tl].unsqueeze(2).to_broadcast([128, gl, nbb, Tl])
                    Xg = [v4(x[:, g0:g0 + gl, :]) for x in xt]
                    f(Xg[0], pr, cv, ALU.mult, [brd])
                    f(Xg[2], pi_, sv_, ALU.mult, [bid])
                    f(Xg[1], pi_, cv, ALU.mult, [bid])
                    f(Xg[3], pr, sv_, ALU.mult, [brd])
                f(xt[0], xt[0], xt[2], ALU.add)
                f(xt[1], xt[1], xt[3], ALU.subtract)
                X4 = [v4(x) for x in xt]
                for ri in range(2):
                    tsc = xt[2][:, :, 0:nbb]
                    f(tsc, hprev[:, :, ri, 0:nbb], sm[:, :, MAG:MAG + 1].to_broadcast([128, 8, nbb]), ALU.mult)
                    f(X4[ri][:, :, :, 0], X4[ri][:, :, :, 0], tsc, ALU.add)
                rv = rho if nbb == 1 else rhos
                for ri in range(2):
                    op(DVE, lambda ri=ri: nc.vector.tensor_tensor_scan(out=xt[ri].rearrange("p m t -> p (m t)"), data0=rv.rearrange("p m t -> p (m t)"),
                                                                       data1=xt[ri].rearrange("p m t -> p (m t)"), initial=0.0, op0=ALU.mult, op1=ALU.add),
                       reads=[dx, dW], writes=[dx])
                cvA = cosT[:, :, 0:Tl].unsqueeze(2).to_broadcast([128, 8, nbb, Tl])
                svA = sinT[:, :, 0:Tl].unsqueeze(2).to_broadcast([128, 8, nbb, Tl])
                H0, H1 = v4(Hs[:, :, 0, 0:nn]), v4(Hs[:, :, 1, 0:nn])
                fh = lambda o, a, b, g_, wr: op(DVE, lambda: nc.vector.tensor_tensor(out=o, in0=a, in1=b, op=g_), reads=[dx, dH, dW], writes=[wr])
                fh(X4[2], X4[0], cvA, ALU.mult, dx)
                fh(X4[3], X4[1], svA, ALU.mult, dx)
                fh(H0, X4[2], X4[3], ALU.subtract, dH)
                fh(X4[2], X4[0], svA, ALU.mult, dx)
                fh(X4[3], X4[1], cvA, ALU.mult, dx)
                fh(H1, X4[2], X4[3], ALU.add, dH)
                for ri in range(2):
                    op(DVE, lambda ri=ri: nc.vector.tensor_copy(out=hprev[:, :, ri, 0:nbb], in_=v4(Hs[:, :, ri, 0:nn])[:, :, :, Tl - 1]), reads=[dH, dx], writes=[st_dep, dHP])
                for cc in range(2):
                    by, byd = pbank()
                    i = 0
                    for m in range(4 * cc, 4 * cc + 4):
                        for ri in range(2):
                            op(PE, lambda m=m, ri=ri, i=i: nc.tensor.matmul(by[:, 0:nn], lhsT=CT[:, m, ri, :], rhs=Hs[:, m, ri, 0:nn], start=(i == 0), stop=(i == 7)),
                               reads=[dH, dW], writes=[byd], inc=(i == 7))
                            i += 1
                    t0, t0d = tmp()
                    op(DVE, lambda: nc.vector.scalar_tensor_tensor(out=t0[:, 0:nn], in0=su[:, cc, c0:c0 + nn], scalar=dsk[:, cc:cc + 1], in1=by[:, 0:nn], op0=ALU.mult, op1=ALU.add),
                       reads=[dsu, byd, dW], writes=[t0d])
                    op(ACT, lambda: nc.scalar.activation(out=su[:, cc, c0:c0 + nn], in_=t0[:, 0:nn], func=AF.Gelu), reads=[t0d], writes=[dsu])
            if kind == "p" and name == "B":
                for ri, nm in enumerate(("p_s5_re", "p_s5_im")):
                    store_T(O[nm][l:l + 1, :], lambda m, ri=ri: s5h[l][:, m, ri, 0:1], 1, 1024, [st_dep])
            elif kind == "s":
                for ri, nm in enumerate(("s_s5_re", "s_s5_im")):
                    store_T(O[nm][l], lambda m, ri=ri: hS[:, m, ri, 0:16], 16, 1024, [dW, dHP])
        wgl = E.rearrange("p m n -> p (m n)").rearrange("p (k n) -> p k n", k=2)
        load(wgl, I["s5_w_glu"][l].rearrange("(k p) n -> p k n", p=128), writes=[dW])
        bg = A.take(4)
        load_T(lambda c: bg[:, c:c + 1], I["s5_b_glu"][l:l + 1, :], 1, 512, [dW])
        for ti, (s, w) in enumerate(tiles):
            for oc in range(2):
                ba, bad = pbank()
                bb, bbd = pbank()
                for (bk, bkd, o) in ((ba, bad, oc), (bb, bbd, 2 + oc)):
                    for kc in range(2):
                        op(PE, lambda bk=bk, o=o, kc=kc: nc.tensor.matmul(bk[:, 0:w], lhsT=wgl[:, kc, o * 128:(o + 1) * 128], rhs=su[:, kc, s:s + w], start=(kc == 0), stop=(kc == 1)),
                           reads=[dsu, dW], writes=[bkd], inc=(kc == 1))
                t1_, t1d = tmp()
                t2_, t2d = tmp()
                op(ACT, lambda: nc.scalar.activation(out=t1_[:, 0:w], in_=ba[:, 0:w], func=AF.Identity, bias=bg[:, oc:oc + 1]), reads=[bad, dW], writes=[t1d])
                op(ACT, lambda: nc.scalar.activation(out=t2_[:, 0:w], in_=bb[:, 0:w], func=AF.Sigmoid, bias=bg[:, 2 + oc:3 + oc]), reads=[bbd, dW], writes=[t2d])
                op(DVE, lambda: nc.vector.tensor_tensor(out=brT[:, 2 + oc, s:s + w], in0=t1_[:, 0:w], in1=t2_[:, 0:w], op=ALU.mult), reads=[t1d, t2d], writes=[br_dep[2 + oc][ti]])


    dnS = [sb([64, 4, 1, 64], F32, f"dnS{l}") for l in range(NL)]
    dntail = [sb([128, 6, 1, 3], F32, f"dntail{l}") for l in range(NL)]
    dn_wsem = dsem("dnw")

    def branch_dn(l, name, TS, tiles):
        pieces = pieces_of(name)
        A = Arena()
        dW = Dep()
        dv = lambda f, reads=(), writes=(): op(DVE, f, reads=list(reads) + [dW], writes=list(writes) + [dW])
        av = lambda f, reads=(), writes=(): op(ACT, f, reads=list(reads) + [dW], writes=list(writes) + [dW])
        wv = A.take(24).rearrange("p (c j) -> p c j", c=6)
        zb6 = A.take(6)
        load_T(lambda c: wv[:, c, 0:4], I["dn_conv_w"][l], 4, 768, [dW])
        dv(lambda: nc.vector.memset(zb6, 0.0))
        row = A.take(72)
        load(row[0:1, 0:4], I["dn_a_log"][l:l + 1, :], writes=[dW])
        load(row[0:1, 4:8], I["dn_dt_bias"][l:l + 1, :], writes=[dW])
        load(row[0:1, 8:72], I["dn_norm"][l:l + 1, :], writes=[dW])
        prmB = A.take(72)
        bk, bkd = pbank()
        op(PE, lambda: nc.tensor.matmul(bk[:, 0:72], lhsT=ones_f[0:1, :], rhs=row[0:1, 0:72], start=True, stop=True), reads=[dW, d_const], writes=[bkd])
        av(lambda: nc.scalar.copy(out=prmB, in_=bk[:, 0:72]), reads=[bkd])
        negA = A.take(4)
        av(lambda: nc.scalar.activation(out=negA, in_=prmB[:, 0:4], func=AF.Exp))
        dv(lambda: nc.vector.tensor_scalar_mul(out=negA, in0=negA, scalar1=-1.0))
        ngB = prmB[:, 8:72]
        triU, trilS, blk1, blkS, triUs, trilSs, blkones = [A.take(128) for _ in range(7)]
        dv(lambda: nc.vector.tensor_single_scalar(out=triU, in_=itf[:], scalar=0.0, op=ALU.is_ge), reads=[d_const])
        dv(lambda: nc.vector.tensor_single_scalar(out=trilS, in_=itf[:], scalar=0.0, op=ALU.is_lt), reads=[d_const])
        dv(lambda: nc.vector.memset(blk1, 1.0))
        dv(lambda: nc.vector.memset(blkones, 0.0))
        dv(lambda: nc.vector.memset(blkones[0:64, 0:64], 1.0))
        dv(lambda: nc.vector.memset(blkones[64:128, 64:128], 1.0))
        has_s = any(k == "s" for (k, _, _, _) in pieces)
        Ecol = A.take(16)
        if has_s:
            Ei = A.take(128)
            Ef = A.take(128)
            Ef2 = A.take(128)
            op(POOL, lambda: nc.gpsimd.iota(Ei[0:16, :].bitcast(I32), pattern=[[1, 128]], base=0, channel_multiplier=-8), writes=[dW])
            dv(lambda: nc.vector.tensor_copy(out=Ef[0:16, :], in_=Ei[0:16, :].bitcast(I32)))
            dv(lambda: nc.vector.tensor_single_scalar(out=Ef2[0:16, :], in_=Ef[0:16, :], scalar=0.0, op=ALU.is_ge))
            dv(lambda: nc.vector.tensor_single_scalar(out=Ef[0:16, :], in_=Ef[0:16, :], scalar=7.0, op=ALU.is_le))
            dv(lambda: nc.vector.tensor_tensor(out=Ef[0:16, :], in0=Ef[0:16, :], in1=Ef2[0:16, :], op=ALU.mult))
            bk, bkd = pbank()
            op(PE, lambda: nc.tensor.matmul(bk[:, 0:128], lhsT=Ef[0:16, :], rhs=Ef[0:16, :], start=True, stop=True), reads=[dW], writes=[bkd])
            av(lambda: nc.scalar.copy(out=blkS, in_=bk[:, 0:128]), reads=[bkd])
            bk2, bk2d = pbank()
            op(PE, lambda: nc.tensor.transpose(bk2[:, 0:16], Ef[0:16, :], ident[0:16, 0:16]), reads=[dW, d_const], writes=[bk2d])
            av(lambda: nc.scalar.copy(out=Ecol, in_=bk2[:, 0:16]), reads=[bk2d])
            dv(lambda: nc.vector.tensor_tensor(out=triUs, in0=triU, in1=blkS, op=ALU.mult))
            dv(lambda: nc.vector.tensor_tensor(out=trilSs, in0=trilS, in1=blkS, op=ALU.mult))
        wz = A.take(KC * 264 // 2).bitcast(BF16).rearrange("p (k n) -> p k n", k=KC)
        dma(POOL, dn_wsem, wz, w_in_v[l][:, :, OFF["dz"]:OFF["dz"] + 264], writes=[dW])
        qkv = A.take(6 * TS).rearrange("p (c t) -> p c t", c=6)
        dq = [Dep() for _ in range(6)]
        xf1 = [A.take(nb * (3 + T)).rearrange("p (b t) -> p b t", b=nb) for (kind, col0, nb, T) in pieces]
        dxf = [Dep() for _ in pieces]
        tailS = A.take(6 * 16 * 3).rearrange("p (c b j) -> p c b j", c=6, b=16)
        tailN = A.take(6 * 16 * 3).rearrange("p (c b j) -> p c b j", c=6, b=16)
        dtl = Dep()
        if has_s:
            load_T(lambda c: tailS[:, c, :, :], I["state_delta_conv"][l].rearrange("b j c -> (b j) c"), 48, 768, [dtl])
        for c6 in range(6):
            for pi, (kind, col0, nb, T) in enumerate(pieces):
                if kind == "p" and name == "A":
                    dv(lambda pi=pi: nc.vector.memset(xf1[pi][:, :, 0:3], 0.0), writes=[dxf[pi]])
                elif kind == "p":
                    dv(lambda pi=pi, c6=c6: nc.vector.tensor_copy(out=xf1[pi][:, :, 0:3], in_=dntail[l][:, c6, :, :]), reads=[st_dep], writes=[dxf[pi]])
                else:
                    dv(lambda pi=pi, c6=c6: nc.vector.tensor_copy(out=xf1[pi][:, :, 0:3], in_=tailS[:, c6, :, :]), reads=[dtl], writes=[dxf[pi]])

            def ev(bank, bd, ti, s, w):
                def f(pi, piece, p_off, n, b_off):
                    kind, col0, nb, T = piece
                    dst = xf1[pi][:, 0, 3 + p_off:3 + p_off + n] if kind == "p" else xf1[pi][:, :, 3:3 + T]
                    op(ACT, lambda: nc.scalar.copy(out=dst, in_=src_view(bank[:, b_off:b_off + n], kind, nb, T)), reads=[bd], writes=[dxf[pi]])
                scatter(pieces, s, w, f)
            proj_in(l, OFF["dq"] + c6 * 128, 128, tiles, ev)
            for pi, (kind, col0, nb, T) in enumerate(pieces):
                xf_p = xf1[pi][:, 0, :] if kind == "p" else xf1[pi][:, :, :]
                o_p = qkv[:, c6, col0:col0 + nb * T] if kind == "p" else qkv[:, c6, col0:col0 + nb * T].rearrange("p (b t) -> p b t", b=nb)
                conv_taps(xf_p, o_p, wv, zb6, c6, 4, T, [dW, dxf[pi]], [dq[c6]])
                if kind == "p" and name == "A":
                    dv(lambda pi=pi, c6=c6, T=T: nc.vector.tensor_copy(out=dntail[l][:, c6, :, :], in_=xf1[pi][:, :, T:T + 3]), reads=[dxf[pi]], writes=[st_dep])
                else:
                    dv(lambda pi=pi, c6=c6, T=T, nb=nb, kind=kind: nc.vector.tensor_copy(out=(tailN[:, c6, 0:1, :] if kind == "p" else tailS[:, c6, :, :]), in_=xf1[pi][:, :, T:T + 3]),
                       reads=[dxf[pi], dtl], writes=[dtl])
            op(ACT, lambda c6=c6: nc.scalar.activation(out=qkv[:, c6, 0:TS], in_=qkv[:, c6, 0:TS], func=AF.Silu), reads=[dq[c6]], writes=[dq[c6]])
        if name == "B":
            store_T(O["p_delta_conv"][l], lambda c: tailN[:, c, 0, :], 3, 768, [dtl])
            store_T(O["s_delta_conv"][l].rearrange("b j c -> (b j) c"), lambda c: tailS[:, c, :, :], 48, 768, [dtl])
        for c4 in range(4):
            tl = [tmp() for _ in tiles]
            bl = []
            for ti, (s, w) in enumerate(tiles):
                t0, t0d = tl[ti]
                op(ACT, lambda t0=t0, s=s, w=w: nc.scalar.activation(out=t0[:, 0:w], in_=qkv[:, c4, s:s + w], func=AF.Square), reads=[dq[c4]], writes=[t0d])
            for ti, (s, w) in enumerate(tiles):
                t0, t0d = tl[ti]
                bk, bkd = pbank()
                bl.append((bk, bkd))
                op(PE, lambda t0=t0, bk=bk, w=w: nc.tensor.matmul(bk[:, 0:w], lhsT=blkones, rhs=t0[:, 0:w], start=True, stop=True), reads=[t0d, dW], writes=[bkd])
            for ti, (s, w) in enumerate(tiles):
                t0, t0d = tl[ti]
                bk, bkd = bl[ti]
                op(ACT, lambda t0=t0, bk=bk, w=w: nc.scalar.activation(out=t0[:, 0:w], in_=bk[:, 0:w], func=AF.Ln, bias=eps_t[:, 0:1]), reads=[bkd, d_const], writes=[t0d])
            for ti, (s, w) in enumerate(tiles):
                t0, t0d = tl[ti]
                op(ACT, lambda t0=t0, w=w: nc.scalar.activation(out=t0[:, 0:w], in_=t0[:, 0:w], func=AF.Exp, scale=-0.5), reads=[t0d], writes=[t0d])
            for ti, (s, w) in enumerate(tiles):
                t0, t0d = tl[ti]
                op(DVE, lambda t0=t0, s=s, w=w: nc.vector.scalar_tensor_tensor(out=qkv[:, c4, s:s + w], in0=qkv[:, c4, s:s + w], scalar=(0.125 if c4 < 2 else 1.0), in1=t0[:, 0:w],
                                                                               op0=ALU.mult, op1=ALU.mult), reads=[dq[c4], t0d], writes=[dq[c4]])
        dqa = dq
        HB = 4
        mkh = lambda n: [A.take(n) for _ in range(HB)]
        Xb, Nb, NTb, ATb, gBb, wTb, tTb = mkh(128), mkh(128), mkh(128), mkh(128), mkh(128), mkh(128), mkh(128)
        kdb, vnb, ob, o1b = mkh(64), mkh(64), mkh(64), mkh(64)
        qsb = mkh(128)
        dh = [Dep() for _ in range(HB)]
        zsil2 = [A.take(256) for _ in range(2)]
        oat2 = [A.take(256) for _ in range(2)]
        sv42 = [A.take(64).rearrange("p (j h) -> p j h", h=4) for _ in range(2)]
        ktv2 = [A.take(512) for _ in range(2)]
        dC2 = [Dep(), Dep()]
        Ssm = A.take(16 * 64).rearrange("p (b v) -> p b v", b=16)
        kdm = A.take(64)
        dS = Dep()
        CHUNK = debug.get("chunk", 128)

        def prologue(kind, c0, C, par):
            zsil, sv4, ktv, dC = zsil2[par], sv42[par], ktv2[par], dC2[par]
            mU, mL, mB = (triUs, trilSs, blkS) if kind == "s" else (triU, trilS, blk1)
            cd = lambda f, reads=(), writes=(): op(DVE, f, reads=list(reads) + [dC, dW], writes=list(writes) + [dC])
            ca = lambda f, reads=(), writes=(): op(ACT, f, reads=list(reads) + [dC, dW], writes=list(writes) + [dC])
            bz, bzd = pbank()
            for k in range(KC):
                op(PE, lambda k=k: nc.tensor.matmul(bz[0:C, 0:264], lhsT=uT[:, k, c0:c0 + C], rhs=wz[:, k, :], start=(k == 0), stop=(k == KC - 1)),
                   reads=[dW] + [u_dep[k][ti] for ti in range(len(tiles))], writes=[bzd], inc=(k == KC - 1))
            ca(lambda: nc.scalar.activation(out=zsil[0:C, :], in_=bz[0:C, 0:256], func=AF.Silu), reads=[bzd])
            ca(lambda: nc.scalar.activation(out=sv4[0:C, 0, :], in_=bz[0:C, 256:260], func=AF.Sigmoid), reads=[bzd])
            cd(lambda: nc.vector.tensor_tensor(out=sv4[0:C, 8, :], in0=bz[0:C, 260:264], in1=prmB[0:C, 4:8], op=ALU.add), reads=[bzd])
            yield
            cd(lambda: nc.vector.tensor_scalar_mul(out=sv4[0:C, 1, :], in0=sv4[0:C, 0, :], scalar1=-1.0))
            ca(lambda: nc.scalar.activation(out=sv4[0:C, 8, :], in_=sv4[0:C, 8, :], func=AF.Exp))
            ca(lambda: nc.scalar.activation(out=sv4[0:C, 8, :], in_=sv4[0:C, 8, :], func=AF.Ln, bias=1.0))
            yield
            cd(lambda: nc.vector.tensor_tensor(out=sv4[0:C, 2, :], in0=sv4[0:C, 8, :], in1=negA[0:C, :], op=ALU.mult))
            yield
            bg_, bgd = pbank()
            op(PE, lambda: nc.tensor.matmul(bg_[0:C, 0:4], lhsT=mU[0:C, 0:C], rhs=sv4[0:C, 2, :], start=True, stop=True), reads=[dC, dW], writes=[bgd])
            op(PE, lambda: nc.tensor.matmul(bg_[0:C, 4:8], lhsT=mB[0:C, 0:C], rhs=sv4[0:C, 2, :], start=True, stop=True), reads=[dC, dW], writes=[bgd])
            ca(lambda: nc.scalar.copy(out=sv4[0:C, 3:5, :], in_=bg_[0:C, 0:8].rearrange("p (j h) -> p j h", h=4)), reads=[bgd])
            bkt, bktd = pbank()
            for j, c6 in enumerate((2, 3, 4, 5)):
                op(PE, lambda j=j, c6=c6: nc.tensor.transpose(bkt[0:C, j * 128:(j + 1) * 128], qkv[:, c6, c0:c0 + C], ident[:]), reads=[dqa[c6], d_const], writes=[bktd], inc=(j == 3))
            ca(lambda: nc.scalar.copy(out=ktv[0:C, :], in_=bkt[0:C, :]), reads=[bktd])
            yield
            ca(lambda: nc.scalar.activation(out=sv4[0:C, 5, :], in_=sv4[0:C, 3, :], func=AF.Exp))
            cd(lambda: nc.vector.tensor_tensor(out=sv4[0:C, 7, :], in0=sv4[0:C, 4, :], in1=sv4[0:C, 3, :], op=ALU.subtract))
            yield
            cd(lambda: nc.vector.tensor_tensor(out=sv4[0:C, 6, :], in0=sv4[0:C, 0, :], in1=sv4[0:C, 5, :], op=ALU.mult))
            ca(lambda: nc.scalar.activation(out=sv4[0:C, 7, :], in_=sv4[0:C, 7, :], func=AF.Exp))

        def head(kind, c0, C, par, h, nbb, Tq):
            zsil, sv4, ktv, dC, oat = zsil2[par], sv42[par], ktv2[par], dC2[par], oat2[par]
            mU, mL, mB = (triUs, trilSs, blkS) if kind == "s" else (triU, trilS, blk1)
            levels = max(1, (Tq - 1).bit_length())
            hc, hp = h // 2, (h % 2) * 64
            hd = dh[h]
            hdv = lambda f, reads=(), writes=(): op(DVE, f, reads=list(reads) + [hd, dC, dW], writes=list(writes) + [hd])
            hav = lambda f, reads=(), writes=(): op(ACT, f, reads=list(reads) + [hd, dC, dW], writes=list(writes) + [hd])
            hpe = lambda f, reads=(), writes=(), inc=True: op(PE, f, reads=list(reads) + [hd, dC, dW], writes=list(writes), inc=inc)
            X, N, NT, AT, gB, wT, tT, kd, vn, oo, o1 = Xb[h], Nb[h], NTb[h], ATb[h], gBb[h], wTb[h], tTb[h], kdb[h], vnb[h], ob[h], o1b[h]
            qT = qkv[hp:hp + 64, 0 + hc, c0:c0 + C]
            kT = qkv[hp:hp + 64, 2 + hc, c0:c0 + C]
            ktok = ktv[0:C, hc * 128 + hp:hc * 128 + hp + 64]
            vtok = ktv[0:C, (2 + hc) * 128 + hp:(2 + hc) * 128 + hp + 64]
            hdv(lambda: nc.vector.tensor_copy(out=gB[0:C, :], in_=sv4[0:C, 2, h:h + 1].to_broadcast([C, 128])))
            hdv(lambda: nc.vector.tensor_scalar_mul(out=X[0:C, 0:64], in0=ktok, scalar1=sv4[0:C, 6, h:h + 1]))
            hdv(lambda: nc.vector.tensor_scalar_mul(out=X[0:C, 64:128], in0=vtok, scalar1=sv4[0:C, 0, h:h + 1]))
            hdv(lambda: nc.vector.tensor_scalar_mul(out=kd[0:C, :], in0=ktok, scalar1=sv4[0:C, 7, h:h + 1]))
            yield
            b1, b1d = pbank()
            hpe(lambda: nc.tensor.matmul(b1[0:C, 0:C], lhsT=gB[0:C, 0:C], rhs=mU[0:C, 0:C], start=True, stop=True), writes=[b1d])
            hpe(lambda: nc.tensor.matmul(b1[0:64, 128:128 + C], lhsT=gB[0:C, 0:64], rhs=mB[0:C, 0:C], start=True, stop=True), writes=[b1d])
            hdv(lambda: nc.vector.tensor_scalar(out=N[0:C, 0:C], in0=b1[0:C, 0:C], scalar1=sv4[0:C, 3, h:h + 1], scalar2=0.0, op0=ALU.subtract, op1=ALU.max), reads=[b1d])
            hdv(lambda: nc.vector.tensor_scalar(out=AT[0:C, 0:C], in0=b1[0:C, 0:C], scalar1=sv4[0:C, 3, h:h + 1], scalar2=0.0, op0=ALU.subtract, op1=ALU.min), reads=[b1d])
            hav(lambda: nc.scalar.activation(out=tT[0:64, 0:C], in_=b1[0:64, 128:128 + C], func=AF.Exp), reads=[b1d])
            bq, bqd = pbank()
            hpe(lambda: nc.tensor.matmul(bq[0:64, 0:C], lhsT=ident[:, hp:hp + 64], rhs=qkv[:, hc, c0:c0 + C], start=True, stop=True), reads=[dqa[hc], d_const], writes=[bqd])
            qs = qsb[h]
            hav(lambda: nc.scalar.copy(out=qs[0:64, 0:C], in_=bq[0:64, 0:C]), reads=[bqd])
            yield
            hav(lambda: nc.scalar.activation(out=N[0:C, 0:C], in_=N[0:C, 0:C], func=AF.Exp, scale=-1.0))
            hav(lambda: nc.scalar.activation(out=AT[0:C, 0:C], in_=AT[0:C, 0:C], func=AF.Exp))
            yield
            hdv(lambda: nc.vector.tensor_tensor(out=N[0:C, 0:C], in0=N[0:C, 0:C], in1=mL[0:C, 0:C], op=ALU.mult))
            hdv(lambda: nc.vector.tensor_tensor(out=AT[0:C, 0:C], in0=AT[0:C, 0:C], in1=mU[0:C, 0:C], op=ALU.mult))
            b2, b2d = pbank()
            op(PE, lambda: nc.tensor.matmul(b2[0:C, 0:C], lhsT=kT, rhs=kT, start=True, stop=True), reads=[dqa[2 + hc]], writes=[b2d])
            op(PE, lambda: nc.tensor.matmul(b2[0:C, 128:128 + C], lhsT=kT, rhs=qT, start=True, stop=True), reads=[dqa[2 + hc], dqa[hc]], writes=[b2d])
            hdv(lambda: nc.vector.scalar_tensor_tensor(out=N[0:C, 0:C], in0=b2[0:C, 0:C], scalar=sv4[0:C, 1, h:h + 1], in1=N[0:C, 0:C], op0=ALU.mult, op1=ALU.mult), reads=[b2d])
            hdv(lambda: nc.vector.tensor_tensor(out=AT[0:C, 0:C], in0=b2[0:C, 128:128 + C], in1=AT[0:C, 0:C], op=ALU.mult), reads=[b2d])
            yield
            b3, b3d = pbank()
            hpe(lambda: nc.tensor.transpose(b3[0:C, 0:C], N[0:C, 0:C], ident[0:C, 0:C]), reads=[d_const], writes=[b3d])
            hav(lambda: nc.scalar.copy(out=NT[0:C, 0:C], in_=b3[0:C, 0:C]), reads=[b3d])
            yield
            for lev in range(levels):
                b4, b4d = pbank()
                hpe(lambda: nc.tensor.matmul(b4[0:C, 0:128], lhsT=NT[0:C, 0:C], rhs=X[0:C, :], start=True, stop=True), writes=[b4d])
                if lev < levels - 1:
                    hpe(lambda: nc.tensor.matmul(b4[0:C, 128:128 + C], lhsT=NT[0:C, 0:C], rhs=N[0:C, 0:C], start=True, stop=True), writes=[b4d])
                    hpe(lambda: nc.tensor.matmul(b4[0:C, 256:256 + C], lhsT=N[0:C, 0:C], rhs=NT[0:C, 0:C], start=True, stop=True), writes=[b4d])
                hdv(lambda: nc.vector.tensor_tensor(out=X[0:C, :], in0=X[0:C, :], in1=b4[0:C, 0:128], op=ALU.add), reads=[b4d])
                if lev < levels - 1:
                    hav(lambda: nc.scalar.copy(out=N[0:C, 0:C], in_=b4[0:C, 128:128 + C]), reads=[b4d])
                    hav(lambda: nc.scalar.copy(out=NT[0:C, 0:C], in_=b4[0:C, 256:256 + C]), reads=[b4d])
                yield
            b5, b5d = pbank()
            hpe(lambda: nc.tensor.transpose(b5[:, 0:C], X[0:C, :], ident[0:C, 0:C]), reads=[d_const], writes=[b5d])
            hav(lambda: nc.scalar.copy(out=wT[0:64, 0:C], in_=b5[0:64, 0:C]), reads=[b5d])
            if kind == "s":
                Sv = Ssm
                sdp = dS
                for b in range(16):
                    load(Ssm[0:64, b, :], I["state_delta"][l, b, h], writes=[dS])
            else:
                Sv = dnS[l][:, h, :, :]
                sdp = st_dep
            yield
            b6, b6d = pbank()
            for b in range(nbb):
                hpe(lambda b=b: nc.tensor.matmul(b6[0:64, b * Tq:(b + 1) * Tq], lhsT=Sv[0:64, b, :], rhs=wT[0:64, b * Tq:(b + 1) * Tq], start=True, stop=True), reads=[sdp], writes=[b6d], inc=(b == nbb - 1))
            for b in range(nbb):
                hpe(lambda b=b: nc.tensor.matmul(b6[0:64, 128 + b * Tq:128 + (b + 1) * Tq], lhsT=Sv[0:64, b, :], rhs=qs[0:64, b * Tq:(b + 1) * Tq], start=True, stop=True),
                    reads=[sdp, dqa[hc]], writes=[b6d], inc=(b == nbb - 1))
            hav(lambda: nc.scalar.copy(out=wT[0:64, 0:C], in_=b6[0:64, 0:C]), reads=[b6d])
            hav(lambda: nc.scalar.copy(out=gB[0:64, 0:C], in_=b6[0:64, 128:128 + C]), reads=[b6d])
            yield
            b7, b7d = pbank()
            hpe(lambda: nc.tensor.transpose(b7[0:C, 0:64], wT[0:64, 0:C], ident[0:64, 0:64]), reads=[d_const], writes=[b7d])
            hpe(lambda: nc.tensor.transpose(b7[0:C, 64:128], gB[0:64, 0:C], ident[0:64, 0:64]), reads=[d_const], writes=[b7d])
            hdv(lambda: nc.vector.tensor_tensor(out=vn[0:C, :], in0=X[0:C, 64:128], in1=b7[0:C, 0:64], op=ALU.subtract), reads=[b7d])
            hdv(lambda: nc.vector.tensor_scalar_mul(out=o1[0:C, :], in0=b7[0:C, 64:128], scalar1=sv4[0:C, 5, h:h + 1]), reads=[b7d])
            yield
            b8, b8d = pbank()
            hpe(lambda: nc.tensor.matmul(b8[0:C, 0:64], lhsT=AT[0:C, 0:C], rhs=vn[0:C, :], start=True, stop=True), writes=[b8d])
            hdv(lambda: nc.vector.tensor_tensor(out=oo[0:C, :], in0=o1[0:C, :], in1=b8[0:C, 0:64], op=ALU.add), reads=[b8d])
            if nbb == 1:
                b9, b9d = pbank()
                hpe(lambda: nc.tensor.matmul(b9[0:64, 0:64], lhsT=kd[0:C, :], rhs=vn[0:C, :], start=True, stop=True), writes=[b9d])
                op(DVE, lambda: nc.vector.scalar_tensor_tensor(out=Sv[0:64, 0, :], in0=Sv[0:64, 0, :], scalar=tT[0:64, 0:1], in1=b9[0:64, 0:64], op0=ALU.mult, op1=ALU.add),
                   reads=[b9d, hd, sdp], writes=[sdp])
            else:
                for half in range(2):
                    b9, b9d = pbank()
                    for bb in range(8):
                        b = half * 8 + bb
                        hdv(lambda b=b: nc.vector.tensor_scalar_mul(out=kdm[0:C, :], in0=kd[0:C, :], scalar1=Ecol[0:C, b:b + 1]))
                        hpe(lambda bb=bb, b9=b9: nc.tensor.matmul(b9[0:64, bb * 64:(bb + 1) * 64], lhsT=kdm[0:C, :], rhs=vn[0:C, :], start=True, stop=True), writes=[b9d])
                        op(DVE, lambda b=b, bb=bb, b9=b9: nc.vector.scalar_tensor_tensor(out=Sv[0:64, b, :], in0=Sv[0:64, b, :], scalar=tT[0:64, b * 8:b * 8 + 1], in1=b9[0:64, bb * 64:(bb + 1) * 64],
                                                                                        op0=ALU.mult, op1=ALU.add), reads=[b9d, hd, sdp], writes=[sdp])
                for b in range(16):
                    store(O["s_delta"][l, b, h], Ssm[0:64, b, :], reads=[dS])
            yield
            hav(lambda: nc.scalar.activation(out=vn[0:C, :], in_=oo[0:C, :], func=AF.Square))
            yield
            hdv(lambda: nc.vector.reduce_sum(out=kd[0:C, 0:1], in_=vn[0:C, :], axis=mybir.AxisListType.X))
            yield
            hav(lambda: nc.scalar.activation(out=kd[0:C, 0:1], in_=kd[0:C, 0:1], func=AF.Ln, scale=1.0 / 64, bias=eps_t[0:C, 0:1]), reads=[d_const])
            hav(lambda: nc.scalar.activation(out=kd[0:C, 0:1], in_=kd[0:C, 0:1], func=AF.Exp, scale=-0.5))
            yield
            hdv(lambda: nc.vector.scalar_tensor_tensor(out=oo[0:C, :], in0=oo[0:C, :], scalar=kd[0:C, 0:1], in1=ngB[0:C, :], op0=ALU.mult, op1=ALU.mult))
            op(DVE, lambda: nc.vector.tensor_tensor(out=oat[0:C, h * 64:(h + 1) * 64], in0=oo[0:C, :], in1=zsil[0:C, h * 64:(h + 1) * 64], op=ALU.mult), reads=[hd, dC], writes=[dC])

        def epilogue(c0, C, par):
            oat, dC = oat2[par], dC2[par]
            bo, bod = pbank()
            for c in range(2):
                op(PE, lambda c=c: nc.tensor.transpose(bo[:, c * 128:c * 128 + C], oat[0:C, c * 128:(c + 1) * 128], ident[0:C, 0:C]), reads=[dC, d_const], writes=[bod], inc=(c == 1))
            for c in range(2):
                op(ACT, lambda c=c: nc.scalar.copy(out=brT[:, c, c0:c0 + C], in_=bo[:, c * 128:c * 128 + C]), reads=[bod], writes=[br_dep[c][ti] for ti in range(len(tiles))])

        def drain(gens):
            gens = list(gens)
            while gens:
                for g_ in list(gens):
                    try:
                        next(g_)
                    except StopIteration:
                        gens.remove(g_)

        chunks = []
        for (kind, col0, nb, T) in pieces:
            if kind == "p":
                Cs = ([16] if name == "A" else []) + [CHUNK] * (1024 // CHUNK)
                c0 = col0
                for C in Cs:
                    chunks.append(("p", c0, C, 1, C))
                    c0 += C
            else:
                chunks.append(("s", col0, 128, nb, T))
        if name == "A":
            dv(lambda: nc.vector.memset(dnS[l][:], 0.0), writes=[st_dep])
        drain([prologue(chunks[0][0], chunks[0][1], chunks[0][2], 0)])
        for ci, (kind, c0, C, nbb, Tq) in enumerate(chunks):
            par = ci % 2
            nxt = [prologue(chunks[ci + 1][0], chunks[ci + 1][1], chunks[ci + 1][2], 1 - par)] if ci + 1 < len(chunks) else []
            hs = [head(kind, c0, C, par, h, nbb, Tq) for h in range(4)]
            if kind == "p":
                drain(hs + nxt)
            else:
                for hg in hs:
                    drain([hg])
                drain(nxt)
            epilogue(c0, C, par)
            if kind == "p" and name == "B" and (ci + 1 == len(chunks) or chunks[ci + 1][0] != "p"):
                for h in range(4):
                    store(O["p_delta"][l, h], dnS[l][:, h, 0, :], reads=[st_dep])

    def merge(l, name, TS, tiles):
        A = Arena()
        mg = A.take(KC * TSM // 2).bitcast(BF16).rearrange("p (c t) -> p c t", c=KC)
        dmg = [[Dep() for _ in tiles] for _ in range(KC)]
        wb_v2 = I["w_branch"][l].rearrange("n (k p) d -> p n k d", p=128)
        wo_v = I["w_out"][l].rearrange("(k p) d -> p k d", p=128)
        accs = [[A.take(512) for _ in tiles] for _ in range(2)]
        dac = [[Dep() for _ in tiles] for _ in range(2)]
        sgs = [A.take(512) for _ in range(3)]
        dsg = [Dep() for _ in range(3)]
        si = 0
        for mp in range(KC // 2):
            bslot, bdep, bsem = wslot()
            bv = bslot[:, 0:2048].rearrange("p (n k d) -> p n k d", n=4, k=2)
            for n in range(4):
                wload(bv[:, n], wb_v2[:, n, :, mp * 256:(mp + 1) * 256], bdep, bsem)
            for n in range(4):
                slot, sdep, ssem = wslot()
                sv = slot[:, 0:KC * 256].rearrange("p (k d) -> p k d", k=KC)
                c0 = OFF["zg"] + n * D + mp * 256
                wload(sv, w_in_v[l][:, :, c0:c0 + 256], sdep, ssem)
                for mm in range(2):
                    m = 2 * mp + mm
                    for ti, (s, w) in enumerate(tiles):
                        acc, da = accs[mm][ti], dac[mm][ti]
                        bz, bzd = pbank()
                        bp, bpd = pbank()
                        for k in range(KC):
                            op(PE, lambda k=k: nc.tensor.matmul(bz[:, 0:w], lhsT=sv[:, k, mm * 128:(mm + 1) * 128], rhs=uT[:, k, s:s + w], start=(k == 0), stop=(k == KC - 1)),
                               reads=[sdep, u_dep[k][ti]], writes=[bzd], inc=(k == KC - 1))
                        for k in range(2):
                            op(PE, lambda k=k: nc.tensor.matmul(bp[:, 0:w], lhsT=bv[:, n, k, mm * 128:(mm + 1) * 128], rhs=brT[:, 2 * n + k, s:s + w], start=(k == 0), stop=(k == 1)),
                               reads=[bdep, br_dep[2 * n + k][ti]], writes=[bpd], inc=(k == 1))
                        sgt, sgd = sgs[si % 3], dsg[si % 3]
                        si += 1
                        op(ACT, lambda: nc.scalar.activation(out=sgt[:, 0:w], in_=bz[:, 0:w], func=AF.Sigmoid), reads=[bzd], writes=[sgd])
                        if n == 0:
                            op(DVE, lambda: nc.vector.tensor_tensor(out=acc[:, 0:w], in0=sgt[:, 0:w], in1=bp[:, 0:w], op=ALU.mult), reads=[sgd, bpd], writes=[da])
                        else:
                            op(DVE, lambda: nc.vector.tensor_tensor(out=sgt[:, 0:w], in0=sgt[:, 0:w], in1=bp[:, 0:w], op=ALU.mult), reads=[sgd, bpd], writes=[sgd])
                            if n < 3:
                                op(DVE, lambda: nc.vector.tensor_tensor(out=acc[:, 0:w], in0=acc[:, 0:w], in1=sgt[:, 0:w], op=ALU.add), reads=[sgd, da], writes=[da])
                            else:
                                op(DVE, lambda: nc.vector.tensor_tensor(out=mg[:, m, s:s + w], in0=acc[:, 0:w], in1=sgt[:, 0:w], op=ALU.add), reads=[sgd, da], writes=[dmg[m][ti]])
        for m in range(KC):
            slot, sdep, ssem = wslot()
            sv = slot[:, 0:KC * 128].rearrange("p (k n) -> p k n", k=KC)
            wload(sv, wo_v[:, :, m * 128:(m + 1) * 128], sdep, ssem)
            for ti, (s, w) in enumerate(tiles):
                bank, bd = pbank()
                for k in range(KC):
                    op(PE, lambda k=k: nc.tensor.matmul(bank[:, 0:w], lhsT=sv[:, k, :], rhs=mg[:, k, s:s + w], start=(k == 0), stop=(k == KC - 1)),
                       reads=[sdep, dmg[k][ti]], writes=[bd], inc=(k == KC - 1))
                op(DVE, lambda: nc.vector.tensor_tensor(out=xT[:, m, s:s + w], in0=xT[:, m, s:s + w], in1=bank[:, 0:w], op=ALU.add),
                   reads=[bd, x_dep[m][ti]], writes=[x_dep[m][ti]])

    def zero_branch(i, tiles):
        for c in range(2):
            for ti, (s, w) in enumerate(tiles):
                op(DVE, lambda: nc.vector.memset(brT[:, 2 * i + c, s:s + w], 0.0), writes=[br_dep[2 * i + c][ti]])

    def mixer(g, l, name, TS, tiles):
        rmsnorm_to(tiles, GIDX["mix"] + l, lambda c, s, w: uT[:, c, s:s + w], lambda c, ti: [u_dep[c][ti]])
        barrier()
        skip = debug.get("skip", "")
        for i, (ch, fn) in enumerate((("a", branch_dn), ("b", branch_s5), ("c", branch_lru), ("d", branch_conv))):
            if fn is None or ch in skip:
                zero_branch(i, tiles)
            else:
                fn(l, name, TS, tiles)
                barrier()
            if f"{name}{l}_o{ch}" in debug:
                dump(f"{name}{l}_o{ch}", brT[:, 2 * i:2 * i + 2, 0:TS], [128, 2, TS], [br_dep[2 * i + c][ti] for c in range(2) for ti in range(len(tiles))])
        merge(l, name, TS, tiles)
        if f"{name}{l}_x2" in debug:
            dump(f"{name}{l}_x2", xT[:, :, 0:TS], [128, KC, TS], [x_dep[c][ti] for c in range(KC) for ti in range(len(tiles))])

    def load_tokens(src_rows, R, col0):
        load_T(lambda c: xT[:, c, col0:col0 + R], src_rows, R, D, [x_dep[c][ti] for c in range(KC) for ti in range(MAXT)])

    def run_st(name, TS, blocks, yblocks):
        tiles = split_tiles(TS)
        for (src, R, col0) in blocks:
            load_tokens(src, R, col0)
        if f"{name}0_x0" in debug:
            dump(f"{name}0_x0", xT[:, :, 0:TS], [128, KC, TS], [x_dep[c][ti] for c in range(KC) for ti in range(len(tiles))])
        for l in range(debug.get("nl", NL)):
            ffn(l, "ffn1", tiles)
            if f"{name}{l}_x1" in debug:
                dump(f"{name}{l}_x1", xT[:, :, 0:TS], [128, KC, TS], [x_dep[c][ti] for c in range(KC) for ti in range(len(tiles))])
            barrier()
            mixer(g, l, name, TS, tiles)
            barrier()
            ffn(l, "ffn2", tiles)
            barrier()
        if debug.get("nofinal"):
            return
        yT = arena[:, 0:KC * TSM].rearrange("p (c t) -> p c t", c=KC)
        y_dep = [[Dep() for _ in range(MAXT)] for _ in range(KC)]
        rmsnorm_to(tiles, 6, lambda c, s, w: yT[:, c, s:s + w], lambda c, ti: [y_dep[c][ti]])
        ally = [y_dep[c][ti] for c in range(KC) for ti in range(len(tiles))]
        for (dst, R, col0) in yblocks:
            store_T(dst, lambda c: yT[:, c, col0:col0 + R], R, D, ally)
        barrier()

    xp = I["x_prompt"]
    blocksA = [(I["meta_tokens"], 16, 0)] + [(xp[i * 128:(i + 1) * 128, :], 128, 16 + i * 128) for i in range(8)]
    yA = [(O["y_prompt"][i * 128:(i + 1) * 128, :], 128, 16 + i * 128) for i in range(8)]
    blocksB = [(xp[1024 + i * 128:1024 + (i + 1) * 128, :], 128, i * 128) for i in range(8)] + [(I["x_sample"], 128, 1024)]
    yB = [(O["y_prompt"][1024 + i * 128:1024 + (i + 1) * 128, :], 128, i * 128) for i in range(8)] + [(O["y_sample"], 128, 1024)]
    sts = debug.get("sts", "AB")
    if "A" in sts:
        run_st("A", TA, blocksA, yA)
    if "B" in sts:
        run_st("B", TB, blocksB, yB)

    for t in st_sems:
        if t.cnt:
            nc.sync.wait_ge(t.sem, t.cnt)


_CACHE = {}


def make_in_maps(inputs, cores=range(8)):
    maps = []
    for c in cores:
        m = {}
        m["x_prompt"] = np.ascontiguousarray(inputs["x_prompt"][c])
        m["x_sample"] = np.ascontiguousarray(inputs["x_sample"][16 * c:16 * c + 16]).reshape(128, D)
        for k in SNAMES:
            m[k] = np.ascontiguousarray(inputs[k][:, 16 * c:16 * c + 16])
        for k in WNAMES:
            m[k] = np.ascontiguousarray(inputs[k])
        maps.append(m)
    return maps


def kernel(**inputs):
    inputs = {k: np.asarray(v, dtype=np.float32) for k, v in inputs.items()}
    if "nc" not in _CACHE:
        _CACHE["nc"] = build()
    nc, g = _CACHE["nc"]
    maps = make_in_maps(inputs)
    for m in maps:
        for k, shp in g.shapes_in.items():
            m[k] = m[k].reshape(shp)
    res = run_bass_kernel_spmd(nc, maps, core_ids=list(range(8)))
    outs = []
    for nm in ONAMES:
        per = [r["o_" + nm] for r in res.results]
        if nm == "y_prompt":
            outs.append(np.stack(per, 0))
        elif nm == "y_sample":
            outs.append(np.concatenate([p.reshape(16, 8, D) for p in per], 0))
        elif nm.startswith("p_"):
            full = np.stack(per, 1)
            outs.append(full)
        else:
            full = np.concatenate(per, 1)
            outs.append(full)
    ref_shapes = dict(p_s5_re=(NL, 8, 16, 64), p_s5_im=(NL, 8, 16, 64), s_s5_re=(NL, 128, 16, 64), s_s5_im=(NL, 128, 16, 64))
    outs = [o.reshape(ref_shapes[nm]) if nm in ref_shapes else o for nm, o in zip(ONAMES, outs)]
    return tuple(np.ascontiguousarray(o, dtype=np.float32) for o in outs)
```

```python
import numpy as np
from contextlib import ExitStack
import concourse.bass as bass
import concourse.mybir as mybir
from concourse.bass_utils import run_bass_kernel_spmd

F32 = mybir.dt.float32
BF16 = mybir.dt.bfloat16
I32 = mybir.dt.int32
AF = mybir.ActivationFunctionType
ALU = mybir.AluOpType

D = 1024
KC = 8
FF = 2816
FJ = 22
NL = 2
EPS = 1e-6
TA = 1040
TB = 1152
TSM = 1152
N_IN = 6408
OFF = dict(dq=0, dk=256, dv=512, dz=768, db=1024, da=1028, su=1032, lx=1288, lg=1544, cval=1800, cgate=2056, zg=2312)

WNAMES = ['meta_tokens', 'ffn1_norm', 'ffn1_w_gu', 'ffn1_w_down', 'mix_norm', 'w_in', 'dn_conv_w', 'dn_a_log', 'dn_dt_bias',
          'dn_norm', 's5_lam_re', 's5_lam_im', 's5_log_step', 's5_b_re', 's5_b_im', 's5_c_re', 's5_c_im', 's5_d', 's5_w_glu',
          's5_b_glu', 'lru_conv_w', 'lru_conv_b', 'lru_w_a', 'lru_b_a', 'lru_w_x', 'lru_b_x', 'lru_lam', 'cv_conv_w',
          'cv_conv_b', 'cv_ln_g', 'cv_ln_b', 'w_branch', 'w_out', 'ffn2_norm', 'ffn2_w_gu', 'ffn2_w_down', 'final_norm']
SNAMES = ['state_delta', 'state_delta_conv', 'state_s5_re', 'state_s5_im', 'state_lru', 'state_lru_conv', 'state_conv']
ONAMES = ['y_prompt', 'y_sample', 'p_delta', 'p_delta_conv', 'p_s5_re', 'p_s5_im', 'p_lru', 'p_lru_conv', 'p_conv',
          's_delta', 's_delta_conv', 's_s5_re', 's_s5_im', 's_lru', 's_lru_conv', 's_conv']


class Trk:
    def __init__(self, name, obj, sem):
        self.name, self.obj, self.sem = name, obj, sem
        self.cnt = 0
        self.seen = {}


class Dep:
    __slots__ = ("w", "r")

    def __init__(self):
        self.w = None
        self.r = {}


def _wait(eng, reads, writes):
    need = {}

    def add(t, v):
        if t is eng and (eng.name == "pe" or v > eng.cnt):
            return
        if need.get(t, 0) < v:
            need[t] = v
    for d in reads:
        if d.w is not None:
            add(*d.w)
    for d in writes:
        if d.w is not None:
            add(*d.w)
        for t, v in d.r.items():
            add(t, v)
    for t, v in need.items():
        if eng.seen.get(t, 0) < v:
            eng.obj.wait_ge(t.sem, v)
            eng.seen[t] = v


def op(eng, fn, reads=(), writes=(), inc=True):
    _wait(eng, reads, writes)
    ins = fn()
    if inc:
        ins.then_inc(eng.sem, 1)
        eng.cnt += 1
        val = eng.cnt
    else:
        val = eng.cnt + 1
    for d in reads:
        if d.r.get(eng, 0) < val:
            d.r[eng] = val
    for d in writes:
        d.w = (eng, val)
        d.r = {}
    return ins


def dma(eng, dsem, out, in_, reads=(), writes=(), **kw):
    _wait(eng, reads, writes)
    ins = eng.obj.dma_start(out=out, in_=in_, **kw)
    ins.then_inc(dsem.sem, 16)
    dsem.cnt += 16
    val = dsem.cnt
    for d in reads:
        if d.r.get(dsem, 0) < val:
            d.r[dsem] = val
    for d in writes:
        d.w = (dsem, val)
        d.r = {}
    return ins


def split_tiles(n, mx=512):
    k = -(-n // mx)
    base = -(-n // k)
    out = []
    s = 0
    while s < n:
        w = min(base, n - s)
        out.append((s, w))
        s += w
    return out


class K:
    pass


def build(debug=None):
    debug = debug or {}
    nc = bass.Bass("TRN2", target_bir_lowering=False)
    g = K()
    g.nc = nc
    g.dbg_outs = {}
    shapes_in = dict(
        x_prompt=[2048, D], x_sample=[128, D],
        state_delta=[NL, 16, 4, 64, 64], state_delta_conv=[NL, 16, 3, 768], state_s5_re=[NL, 16, 1024],
        state_s5_im=[NL, 16, 1024], state_lru=[NL, 16, 256], state_lru_conv=[NL, 16, 3, 256], state_conv=[NL, 16, 30, 256],
        meta_tokens=[16, D], ffn1_norm=[NL, D], ffn1_w_gu=[NL, D, 2 * FF], ffn1_w_down=[NL, FF, D], mix_norm=[NL, D],
        w_in=[NL, D, N_IN], dn_conv_w=[NL, 4, 768], dn_a_log=[NL, 4], dn_dt_bias=[NL, 4], dn_norm=[NL, 64],
        s5_lam_re=[NL, 1024], s5_lam_im=[NL, 1024], s5_log_step=[NL, 16], s5_b_re=[NL, 1024, 16], s5_b_im=[NL, 1024, 16],
        s5_c_re=[NL, 256, 64], s5_c_im=[NL, 256, 64], s5_d=[NL, 256], s5_w_glu=[NL, 256, 512], s5_b_glu=[NL, 512],
        lru_conv_w=[NL, 4, 256], lru_conv_b=[NL, 256], lru_w_a=[NL, 256, 64], lru_b_a=[NL, 256], lru_w_x=[NL, 256, 64],
        lru_b_x=[NL, 256], lru_lam=[NL, 256], cv_conv_w=[NL, 31, 256], cv_conv_b=[NL, 256], cv_ln_g=[NL, 256],
        cv_ln_b=[NL, 256], w_branch=[NL, 4, 256, D], w_out=[NL, D, D], ffn2_norm=[NL, D], ffn2_w_gu=[NL, D, 2 * FF],
        ffn2_w_down=[NL, FF, D], final_norm=[1, D])
    shapes_out = dict(
        y_prompt=[2048, D], y_sample=[128, D], p_delta=[NL, 4, 64, 64], p_delta_conv=[NL, 3, 768], p_s5_re=[NL, 1024],
        p_s5_im=[NL, 1024], p_lru=[NL, 256], p_lru_conv=[NL, 3, 256], p_conv=[NL, 30, 256],
        s_delta=[NL, 16, 4, 64, 64], s_delta_conv=[NL, 16, 3, 768], s_s5_re=[NL, 16, 1024], s_s5_im=[NL, 16, 1024],
        s_lru=[NL, 16, 256], s_lru_conv=[NL, 16, 3, 256], s_conv=[NL, 16, 30, 256])
    g.I = {k: nc.dram_tensor(k, v, F32, kind="ExternalInput").ap() for k, v in shapes_in.items()}
    g.O = {k: nc.dram_tensor("o_" + k, v, F32, kind="ExternalOutput").ap() for k, v in shapes_out.items()}
    g.shapes_in = shapes_in
    g.shapes_out = shapes_out

    with ExitStack() as es:
        g.es = es
        _emit(g, debug)
    return nc, g


def _emit(g, debug):
    nc, es = g.nc, g.es
    I, O = g.I, g.O
    cnt = [0]

    def sb(shape, dt=F32, name=None):
        cnt[0] += 1
        return es.enter_context(nc.sbuf_tensor(name or f"t{cnt[0]}", shape, dt))

    def sem(name):
        return es.enter_context(nc.semaphore(name))

    PE = Trk("pe", nc.tensor, sem("s_pe"))
    ACT = Trk("act", nc.scalar, sem("s_act"))
    DVE = Trk("dve", nc.vector, sem("s_dve"))
    POOL = Trk("pool", nc.gpsimd, sem("s_pool"))
    SP = Trk("sp", nc.sync, sem("s_sp"))
    g.PE, g.ACT, g.DVE, g.POOL, g.SP = PE, ACT, DVE, POOL, SP
    engines = [PE, ACT, DVE, POOL, SP]
    dsems = []

    def dsem(name):
        t = Trk(name, None, sem(name))
        dsems.append(t)
        return t

    ld_sems = [dsem(f"ld{i}") for i in range(8)]
    st_sems = [dsem(f"st{i}") for i in range(4)]
    rr = dict(ld=0, st=0)

    def ldsem():
        rr['ld'] += 1
        return ld_sems[rr['ld'] % len(ld_sems)]

    def stsem():
        rr['st'] += 1
        return st_sems[rr['st'] % len(st_sems)]

    def load(out, in_, writes, reads=(), **kw):
        return dma(SP, ldsem(), out, in_, reads=reads, writes=writes, **kw)

    def store(out, in_, reads, **kw):
        return dma(SP, stsem(), out, in_, reads=reads, **kw)

    def barrier():
        for e in engines:
            for t in engines + dsems:
                if t is e:
                    continue
                if t.cnt > 0 and e.seen.get(t, 0) < t.cnt:
                    e.obj.wait_ge(t.sem, t.cnt)
                    e.seen[t] = t.cnt

    banks = [es.enter_context(nc.psum_tensor(f"bank{i}", [128, 512], F32)) for i in range(8)]
    bank_dep = [Dep() for _ in range(8)]
    bk = [0]

    def pbank():
        i = bk[0] % 8
        bk[0] += 1
        return banks[i], bank_dep[i]

    def dump(name, ap, shape, dep_list):
        if name not in debug:
            return
        t = nc.dram_tensor("dbg_" + name, shape, ap.dtype, kind="ExternalOutput").ap()
        g.dbg_outs[name] = t
        store(t, ap, reads=dep_list)

    it_i = sb([128, 128], I32)
    itf = sb([128, 128])
    ident = sb([128, 128])
    ones_bf = sb([128, 128], BF16)
    d_const = Dep()
    op(POOL, lambda: nc.gpsimd.iota(it_i[:], pattern=[[1, 128]], base=0, channel_multiplier=-1), writes=[d_const])
    op(DVE, lambda: nc.vector.tensor_copy(out=itf[:], in_=it_i[:]), reads=[d_const], writes=[d_const])
    op(DVE, lambda: nc.vector.tensor_single_scalar(out=ident[:], in_=itf[:], scalar=0.0, op=ALU.is_equal), reads=[d_const], writes=[d_const])
    op(DVE, lambda: nc.vector.memset(ones_bf[:], 1.0), writes=[d_const])
    g.ident, g.itf, g.d_const = ident, itf, d_const
    ones_f = sb([128, 128], F32)
    op(DVE, lambda: nc.vector.memset(ones_f[:], 1.0), writes=[d_const])

    xT = sb([128, KC, TSM], F32, "xT")
    uT = sb([128, KC, TSM], BF16, "uT")
    brT = sb([128, 8, TSM], BF16, "brT")
    ARENA_F32 = 19712
    arena = sb([128, ARENA_F32], F32, "arena")
    MAXT = 3
    x_dep = [[Dep() for _ in range(MAXT)] for _ in range(KC)]
    u_dep = [[Dep() for _ in range(MAXT)] for _ in range(KC)]
    br_dep = [[Dep() for _ in range(MAXT)] for _ in range(8)]

    gains = sb([128, KC, 8], F32, "gains")
    d_gain = Dep()
    GIDX = dict(ffn1=0, mix=2, ffn2=4)

    stg = [sb([128, D], F32, f"stg{i}") for i in range(2)]
    stg_dep = [Dep() for _ in range(2)]
    sg = [0]

    def staging():
        i = sg[0] % 2
        sg[0] += 1
        return stg[i], stg_dep[i]

    NSLOT = 6
    SLOT_ELEMS = 2048
    wslots = [sb([128, SLOT_ELEMS], BF16, f"wslot{i}") for i in range(NSLOT)]
    wslot_dep = [Dep() for _ in range(NSLOT)]
    wslot_sem = [dsem(f"ws{i}") for i in range(NSLOT)]
    ws = [0]

    def wslot():
        i = ws[0] % NSLOT
        ws[0] += 1
        return wslots[i], wslot_dep[i], wslot_sem[i]

    def wload(slot_ap, src_ap, sdep, ssem):
        return dma(POOL, ssem, slot_ap, src_ap, writes=[sdep])

    NTMP = 4
    tmps = [sb([128, 512], F32, f"tmp{i}") for i in range(NTMP)]
    tmp_dep = [Dep() for _ in range(NTMP)]
    tp = [0]

    def tmp():
        i = tp[0] % NTMP
        tp[0] += 1
        return tmps[i], tmp_dep[i]

    sqs = [sb([128, KC, 512], BF16, f"sq{i}") for i in range(1)] * 2
    sq_dep = [Dep()] * 2
    sqi = [0]

    def load_T(dst_fn, src, R, C, wdeps):
        st, sd = staging()
        if isinstance(src, list):
            r0 = 0
            for (ap, nr) in src:
                load(st[r0:r0 + nr, 0:C], ap, writes=[sd])
                r0 += nr
            assert r0 == R
        else:
            load(st[0:R, 0:C], src, writes=[sd])
        nchunk = -(-C // 128)
        per_bank = max(1, 512 // R)
        c = 0
        while c < nchunk:
            bank, bd = pbank()
            grp = list(range(c, min(nchunk, c + per_bank)))
            for i, cc in enumerate(grp):
                cw = min(128, C - cc * 128)
                op(PE, lambda cc=cc, cw=cw, i=i: nc.tensor.transpose(bank[0:cw, i * R:(i + 1) * R], st[0:R, cc * 128:cc * 128 + cw], ident[0:R, 0:R]),
                   reads=[sd, d_const], writes=[bd], inc=(i == len(grp) - 1))
            for i, cc in enumerate(grp):
                cw = min(128, C - cc * 128)
                op(ACT, lambda cc=cc, cw=cw, i=i: nc.scalar.copy(out=dst_fn(cc), in_=bank[0:cw, i * R:(i + 1) * R]), reads=[bd], writes=wdeps)
            c += per_bank

    def store_T(dst, src_fn, R, C, rdeps):
        st, sd = staging()
        nchunk = -(-C // 128)
        c = 0
        while c < nchunk:
            bank, bd = pbank()
            grp = list(range(c, min(nchunk, c + 4)))
            for i, cc in enumerate(grp):
                cw = min(128, C - cc * 128)
                src = src_fn(cc)
                if len(src.shape) > 2:
                    tt_, ttd = tmp()
                    o_ = tt_[0:cw, 0:R].rearrange("p (a b) -> p a b", a=src.shape[1])
                    op(DVE, lambda o_=o_, src=src: nc.vector.tensor_copy(out=o_, in_=src), reads=list(rdeps), writes=[ttd])
                    op(PE, lambda cw=cw, i=i, tt_=tt_: nc.tensor.transpose(bank[0:R, i * 128:i * 128 + cw], tt_[0:cw, 0:R], ident[0:cw, 0:cw]),
                       reads=[ttd, d_const], writes=[bd], inc=(i == len(grp) - 1))
                    continue
                op(PE, lambda cc=cc, cw=cw, i=i: nc.tensor.transpose(bank[0:R, i * 128:i * 128 + cw], src_fn(cc), ident[0:cw, 0:cw]),
                   reads=list(rdeps) + [d_const], writes=[bd], inc=(i == len(grp) - 1))
            w = min(C - c * 128, 512)
            op(ACT, lambda c=c, w=w: nc.scalar.copy(out=st[0:R, c * 128:c * 128 + w], in_=bank[0:R, 0:w]), reads=[bd], writes=[sd])
            c += 4
        store(dst, st[0:R, 0:C], reads=[sd])

    def rmsnorm_to(tiles, gidx, out_fn, out_deps_fn, final=False):
        n = len(tiles)
        SQ_OFF = 10000
        sqv = [arena[:, SQ_OFF + i * 2048:SQ_OFF + (i + 1) * 2048].bitcast(BF16).rearrange("p (c t) -> p c t", c=KC) for i in range(n)]
        sqd = [Dep() for _ in range(n)]
        bl, rl = [], []
        for ti, (s, w) in enumerate(tiles):
            op(ACT, lambda ti=ti, s=s, w=w: nc.scalar.activation(out=sqv[ti][:, :, 0:w], in_=xT[:, :, s:s + w], func=AF.Square),
               reads=[x_dep[c][ti] for c in range(KC)], writes=[sqd[ti]])
        for ti, (s, w) in enumerate(tiles):
            bank, bd = pbank()
            bl.append((bank, bd))
            for c in range(KC):
                op(PE, lambda c=c, ti=ti, w=w, bank=bank: nc.tensor.matmul(bank[:, 0:w], lhsT=ones_bf[:], rhs=sqv[ti][:, c, 0:w], start=(c == 0), stop=(c == KC - 1)),
                   reads=[sqd[ti], d_const], writes=[bd], inc=(c == KC - 1))
        for ti, (s, w) in enumerate(tiles):
            bank, bd = bl[ti]
            rs, rsd = tmp()
            rl.append((rs, rsd))
            op(ACT, lambda rs=rs, bank=bank, w=w: nc.scalar.activation(out=rs[:, 0:w], in_=bank[:, 0:w], func=AF.Ln, scale=1.0 / D, bias=eps_t[:, 0:1]), reads=[bd, d_const], writes=[rsd])
        for ti, (s, w) in enumerate(tiles):
            rs, rsd = rl[ti]
            op(ACT, lambda rs=rs, w=w: nc.scalar.activation(out=rs[:, 0:w], in_=rs[:, 0:w], func=AF.Exp, scale=-0.5), reads=[rsd], writes=[rsd])
        for ti, (s, w) in enumerate(tiles):
            rs, rsd = rl[ti]
            for c in range(KC):
                op(DVE, lambda c=c, rs=rs, s=s, w=w: nc.vector.scalar_tensor_tensor(out=out_fn(c, s, w), in0=xT[:, c, s:s + w], scalar=gains[:, c, gidx:gidx + 1],
                                                                                    in1=rs[:, 0:w], op0=ALU.mult, op1=ALU.mult),
                   reads=[x_dep[c][ti], rsd, d_gain], writes=out_deps_fn(c, ti))

    load_T(lambda c: gains[:, c, 0:7],
           [(I[nm][l:l + 1, :], 1) for (nm, l) in [("ffn1_norm", 0), ("ffn1_norm", 1), ("mix_norm", 0), ("mix_norm", 1),
                                                    ("ffn2_norm", 0), ("ffn2_norm", 1), ("final_norm", 0)]], 7, D, [d_gain])
    eps_t = sb([128, 1], F32, "eps")
    op(DVE, lambda: nc.vector.memset(eps_t[:], EPS), writes=[d_const])
    g.eps_t = eps_t

    def ffn(l, which, tiles):
        wgu = I[f"{which}_w_gu"][l]
        wdn = I[f"{which}_w_down"][l]
        gidx = GIDX[which] + l
        rmsnorm_to(tiles, gidx, lambda c, s, w: uT[:, c, s:s + w], lambda c, ti: [u_dep[c][ti]])
        hT = arena[:, 0:11 * TSM // 2].bitcast(BF16).rearrange("p (j t) -> p j t", j=11)
        h_dep = [[Dep() for _ in range(MAXT)] for _ in range(11)]
        wgu_v = wgu.rearrange("(k p) n -> p k n", p=128)
        wdn_v = wdn.rearrange("(j p) n -> p j n", p=128)
        for half in range(2):
            for jj in range(11):
                j = half * 11 + jj
                slot, sdep, ssem = wslot()
                sv = slot[:].rearrange("p (a k n) -> p a k n", a=2, k=KC)
                wload(sv[:, 0], wgu_v[:, :, j * 128:(j + 1) * 128], sdep, ssem)
                wload(sv[:, 1], wgu_v[:, :, FF + j * 128:FF + (j + 1) * 128], sdep, ssem)
                for ti, (s, w) in enumerate(tiles):
                    bg, bgd = pbank()
                    bu, bud = pbank()
                    for a, (bank, bd) in enumerate(((bg, bgd), (bu, bud))):
                        for k in range(KC):
                            op(PE, lambda a=a, k=k, bank=bank: nc.tensor.matmul(bank[:, 0:w], lhsT=sv[:, a, k, :], rhs=uT[:, k, s:s + w],
                                                                                   start=(k == 0), stop=(k == KC - 1)),
                               reads=[sdep, u_dep[k][ti]], writes=[bd], inc=(k == KC - 1))
                    t, td = tmp()
                    op(ACT, lambda: nc.scalar.activation(out=t[:, 0:w], in_=bg[:, 0:w], func=AF.Silu), reads=[bgd], writes=[td])
                    op(DVE, lambda: nc.vector.tensor_tensor(out=hT[:, jj, s:s + w], in0=t[:, 0:w], in1=bu[:, 0:w], op=ALU.mult),
                       reads=[td, bud], writes=[h_dep[jj][ti]])
            for m in range(KC):
                slot, sdep, ssem = wslot()
                sv = slot[:, 0:11 * 128].rearrange("p (j n) -> p j n", j=11)
                wload(sv, wdn_v[:, half * 11:(half + 1) * 11, m * 128:(m + 1) * 128], sdep, ssem)
                for ti, (s, w) in enumerate(tiles):
                    bank, bd = pbank()
                    for jj in range(11):
                        op(PE, lambda jj=jj: nc.tensor.matmul(bank[:, 0:w], lhsT=sv[:, jj, :], rhs=hT[:, jj, s:s + w], start=(jj == 0), stop=(jj == 10)),
                           reads=[sdep, h_dep[jj][ti]], writes=[bd], inc=(jj == 10))
                    op(DVE, lambda: nc.vector.scalar_tensor_tensor(out=xT[:, m, s:s + w], in0=bank[:, 0:w], scalar=0.5, in1=xT[:, m, s:s + w],
                                                                   op0=ALU.mult, op1=ALU.add),
                       reads=[bd, x_dep[m][ti]], writes=[x_dep[m][ti]])


    w_in_v = [I["w_in"][l].rearrange("(k p) n -> p k n", p=128) for l in range(NL)]

    class Arena:
        def __init__(self):
            self.off = 0

        def take(self, n):
            a = arena[:, self.off:self.off + n]
            self.off += n
            assert self.off <= ARENA_F32, self.off
            return a

    def proj_in(l, off, ncols, tiles, evac):
        slot, sdep, ssem = wslot()
        sv = slot[:, 0:KC * ncols].rearrange("p (k n) -> p k n", k=KC)
        wload(sv, w_in_v[l][:, :, off:off + ncols], sdep, ssem)
        for ti, (s, w) in enumerate(tiles):
            bank, bd = pbank()
            for k in range(KC):
                op(PE, lambda k=k: nc.tensor.matmul(bank[0:ncols, 0:w], lhsT=sv[:, k, :], rhs=uT[:, k, s:s + w], start=(k == 0), stop=(k == KC - 1)),
                   reads=[sdep, u_dep[k][ti]], writes=[bd], inc=(k == KC - 1))
            evac(bank, bd, ti, s, w)

    def pieces_of(name):
        if name == "A":
            return [("p", 0, 1, TA)]
        return [("p", 0, 1, 1024), ("s", 1024, 16, 8)]

    def scatter(pieces, s, w, fn):
        for pi, (kind, col0, nb, T) in enumerate(pieces):
            lo, hi = max(s, col0), min(s + w, col0 + nb * T)
            if lo >= hi:
                continue
            if kind == "s":
                assert lo == col0 and hi == col0 + nb * T
            fn(pi, pieces[pi], lo - col0, hi - lo, lo - s)

    def xf_view(xf, kind, nb, T, Kt, c, p_off, n):
        if kind == "p":
            return xf[:, c, 0, Kt + p_off:Kt + p_off + n]
        return xf[:, c, :, Kt:Kt + T]

    def src_view(ap2d, kind, nb, T):
        if kind == "p":
            return ap2d
        return ap2d.rearrange("p (b t) -> p b t", b=nb)

    cvtail = [sb([128, 2, 30], F32, f"cvtail{l}") for l in range(NL)]
    lrutail = [sb([128, 2, 3], F32, f"lrutail{l}") for l in range(NL)]
    lruh = [sb([128, 2], F32, f"lruh{l}") for l in range(NL)]
    st_dep = Dep()

    def load_small(dst_fn, srcs, R, C, deps):
        load_T(dst_fn, srcs, R, C, deps)

    def conv_taps(xf_p, acc_p, wv, bv, c, Kw, T, reads, writes):
        sl = lambda j: xf_p[..., j:j + T]
        op(DVE, lambda: nc.vector.tensor_scalar(out=acc_p, in0=sl(0), scalar1=wv[:, c, 0:1], scalar2=bv[:, c:c + 1], op0=ALU.mult, op1=ALU.add),
           reads=reads, writes=writes)
        for j in range(1, Kw):
            op(DVE, lambda j=j: nc.vector.scalar_tensor_tensor(out=acc_p, in0=sl(j), scalar=wv[:, c, j:j + 1], in1=acc_p, op0=ALU.mult, op1=ALU.add),
               reads=list(reads) + list(writes), writes=writes)

    def branch_conv(l, name, TS, tiles):
        pieces = pieces_of(name)
        A = Arena()
        dW = Dep()
        wv = A.take(2 * 32).rearrange("p (c j) -> p c j", c=2)
        prm = A.take(8).rearrange("p (c j) -> p c j", c=2)
        load_T(lambda c: wv[:, c, 0:31], I["cv_conv_w"][l], 31, 256, [dW])
        load_T(lambda c: prm[:, c, 0:3], [(I["cv_conv_b"][l:l + 1, :], 1), (I["cv_ln_g"][l:l + 1, :], 1), (I["cv_ln_b"][l:l + 1, :], 1)], 3, 256, [dW])
        xfs, accs, dxf = [], [], []
        for (kind, col0, nb, T) in pieces:
            xfs.append(A.take(2 * nb * (30 + T)).rearrange("p (c b t) -> p c b t", c=2, b=nb))
            dxf.append([Dep(), Dep()])
        acc = A.take(2 * TS).rearrange("p (c t) -> p c t", c=2)
        dacc = [[Dep() for _ in tiles] for _ in range(2)]
        for pi, (kind, col0, nb, T) in enumerate(pieces):
            for c in range(2):
                if kind == "p" and name == "A":
                    op(DVE, lambda c=c, pi=pi: nc.vector.memset(xfs[pi][:, c, :, 0:30], 0.0), writes=[dxf[pi][c]])
                elif kind == "p":
                    op(DVE, lambda c=c, pi=pi: nc.vector.tensor_copy(out=xfs[pi][:, c, 0, 0:30], in_=cvtail[l][:, c, :]), reads=[st_dep], writes=[dxf[pi][c]])
            if kind == "s":
                for b0 in range(0, 16, 4):
                    load_T(lambda c, b0=b0, pi=pi: xfs[pi][:, c, b0:b0 + 4, 0:30], I["state_conv"][l, b0:b0 + 4].rearrange("b j c -> (b j) c"), 120, 256,
                           [dxf[pi][0], dxf[pi][1]])
        sgt = A.take(TS)
        dsg = [Dep() for _ in tiles]
        for c in range(2):

            def ev_gate(bank, bd, ti, s, w):
                op(ACT, lambda: nc.scalar.activation(out=sgt[:, s:s + w], in_=bank[:, 0:w], func=AF.Sigmoid), reads=[bd], writes=[dsg[ti]])

            def ev_val(bank, bd, ti, s, w, c=c):
                def f(pi, piece, p_off, n, b_off):
                    kind, col0, nb, T = piece
                    op(DVE, lambda: nc.vector.tensor_tensor(out=xf_view(xfs[pi], kind, nb, T, 30, c, p_off, n),
                                                            in0=src_view(bank[:, b_off:b_off + n], kind, nb, T),
                                                            in1=src_view(sgt[:, s + b_off:s + b_off + n], kind, nb, T), op=ALU.mult),
                       reads=[bd, dsg[ti]], writes=[dxf[pi][c]])
                scatter(pieces, s, w, f)
            proj_in(l, OFF["cgate"] + c * 128, 128, tiles, ev_gate)
            proj_in(l, OFF["cval"] + c * 128, 128, tiles, ev_val)
        for pi, (kind, col0, nb, T) in enumerate(pieces):
            for c in range(2):
                xf_p = xfs[pi][:, c, 0, :] if kind == "p" else xfs[pi][:, c, :, :]
                acc_p = acc[:, c, col0:col0 + nb * T] if kind == "p" else acc[:, c, col0:col0 + nb * T].rearrange("p (b t) -> p b t", b=nb)
                conv_taps(xf_p, acc_p, wv, prm[:, :, 0], c, 31, T, [dW, dxf[pi][c]], [dacc[c][ti] for ti in range(len(tiles))])
        nt = len(tiles)
        sqs_ = [A.take(2 * 512).rearrange("p (c t) -> p c t", c=2) for _ in range(nt)]
        dsq = [Dep() for _ in range(nt)]
        means = [A.take(512) for _ in range(nt)]
        rstds = [A.take(512) for _ in range(nt)]
        dmean = [Dep() for _ in range(nt)]
        drstd = [Dep() for _ in range(nt)]
        bks = []
        for ti, (s, w) in enumerate(tiles):
            op(ACT, lambda ti=ti, s=s, w=w: nc.scalar.activation(out=sqs_[ti][:, :, 0:w], in_=acc[:, :, s:s + w], func=AF.Square), reads=[dacc[0][ti], dacc[1][ti]], writes=[dsq[ti]])
        for ti, (s, w) in enumerate(tiles):
            b1, b1d = pbank()
            b2, b2d = pbank()
            bks.append((b1, b1d, b2, b2d))
            for c in range(2):
                op(PE, lambda c=c, b1=b1, s=s, w=w: nc.tensor.matmul(b1[:, 0:w], lhsT=ones_f[:], rhs=acc[:, c, s:s + w], start=(c == 0), stop=(c == 1)),
                   reads=[dacc[c][ti], d_const], writes=[b1d], inc=(c == 1))
            for c in range(2):
                op(PE, lambda c=c, b2=b2, ti=ti, w=w: nc.tensor.matmul(b2[:, 0:w], lhsT=ones_f[:], rhs=sqs_[ti][:, c, 0:w], start=(c == 0), stop=(c == 1)),
                   reads=[dsq[ti], d_const], writes=[b2d], inc=(c == 1))
        for ti, (s, w) in enumerate(tiles):
            b1, b1d, b2, b2d = bks[ti]
            op(ACT, lambda ti=ti, b1=b1, w=w: nc.scalar.mul(out=means[ti][:, 0:w], in_=b1[:, 0:w], mul=1.0 / 256), reads=[b1d], writes=[dmean[ti]])
        for ti, (s, w) in enumerate(tiles):
            op(DVE, lambda ti=ti, w=w: nc.vector.tensor_tensor(out=rstds[ti][:, 0:w], in0=means[ti][:, 0:w], in1=means[ti][:, 0:w], op=ALU.mult), reads=[dmean[ti]], writes=[drstd[ti]])
        for ti, (s, w) in enumerate(tiles):
            b1, b1d, b2, b2d = bks[ti]
            op(DVE, lambda ti=ti, b2=b2, w=w: nc.vector.scalar_tensor_tensor(out=rstds[ti][:, 0:w], in0=b2[:, 0:w], scalar=1.0 / 256, in1=rstds[ti][:, 0:w], op0=ALU.mult, op1=ALU.subtract),
               reads=[b2d, drstd[ti]], writes=[drstd[ti]])
        for ti, (s, w) in enumerate(tiles):
            op(ACT, lambda ti=ti, w=w: nc.scalar.activation(out=rstds[ti][:, 0:w], in_=rstds[ti][:, 0:w], func=AF.Ln, bias=eps_t[:, 0:1]), reads=[drstd[ti], d_const], writes=[drstd[ti]])
        for ti, (s, w) in enumerate(tiles):
            op(ACT, lambda ti=ti, w=w: nc.scalar.activation(out=rstds[ti][:, 0:w], in_=rstds[ti][:, 0:w], func=AF.Exp, scale=-0.5), reads=[drstd[ti]], writes=[drstd[ti]])
        for ti, (s, w) in enumerate(tiles):
            for c in range(2):
                op(DVE, lambda c=c, ti=ti, s=s, w=w: nc.vector.tensor_tensor(out=acc[:, c, s:s + w], in0=acc[:, c, s:s + w], in1=means[ti][:, 0:w], op=ALU.subtract),
                   reads=[dacc[c][ti], dmean[ti]], writes=[dacc[c][ti]])
                op(DVE, lambda c=c, ti=ti, s=s, w=w: nc.vector.tensor_tensor(out=acc[:, c, s:s + w], in0=acc[:, c, s:s + w], in1=rstds[ti][:, 0:w], op=ALU.mult),
                   reads=[dacc[c][ti], drstd[ti]], writes=[dacc[c][ti]])
                op(ACT, lambda c=c, s=s, w=w: nc.scalar.activation(out=brT[:, 6 + c, s:s + w], in_=acc[:, c, s:s + w], func=AF.Silu, scale=prm[:, c, 1:2], bias=prm[:, c, 2:3]),
                   reads=[dacc[c][ti], dW], writes=[br_dep[6 + c][ti]])
        for pi, (kind, col0, nb, T) in enumerate(pieces):
            if kind == "p" and name == "A":
                for c in range(2):
                    op(DVE, lambda c=c, pi=pi: nc.vector.tensor_copy(out=cvtail[l][:, c, :], in_=xfs[pi][:, c, 0, T:T + 30]), reads=[dxf[pi][c]], writes=[st_dep])
            elif kind == "p":
                store_T(O["p_conv"][l], lambda c, pi=pi, T=T: xfs[pi][:, c, 0, T:T + 30], 30, 256, dxf[pi])
            else:
                for b0 in range(0, 16, 4):
                    store_T(O["s_conv"][l, b0:b0 + 4].rearrange("b j c -> (b j) c"), lambda c, pi=pi, b0=b0, T=T: xfs[pi][:, c, b0:b0 + 4, T:T + 30], 120, 256, dxf[pi])

    def branch_lru(l, name, TS, tiles):
        pieces = pieces_of(name)
        A = Arena()
        dW = Dep()
        wv = A.take(8).rearrange("p (c j) -> p c j", c=2)
        prm = A.take(12).rearrange("p (c j) -> p c j", c=2)
        load_T(lambda c: wv[:, c, 0:4], I["lru_conv_w"][l], 4, 256, [dW])
        load_T(lambda c: prm[:, c, 0:4], [(I[k][l:l + 1, :], 1) for k in ("lru_conv_b", "lru_b_a", "lru_b_x", "lru_lam")], 4, 256, [dW])
        op(ACT, lambda: nc.scalar.activation(out=prm[:, :, 4:5], in_=prm[:, :, 3:4], func=AF.Exp, scale=-1.0), reads=[dW], writes=[dW])
        op(ACT, lambda: nc.scalar.activation(out=prm[:, :, 4:5], in_=prm[:, :, 4:5], func=AF.Ln, bias=1.0), reads=[dW], writes=[dW])
        op(ACT, lambda: nc.scalar.mul(out=prm[:, :, 4:5], in_=prm[:, :, 4:5], mul=-8.0), reads=[dW], writes=[dW])
        wg = A.take(2 * 2 * 128).rearrange("p (a c n) -> p a c n", a=2, c=2)
        op(DVE, lambda: nc.vector.memset(wg, 0.0), writes=[dW])
        for a, nm in enumerate(("lru_w_a", "lru_w_x")):
            for c in range(2):
                for i in range(2):
                    load(wg[i * 64:(i + 1) * 64, a, c, i * 64:(i + 1) * 64], I[nm][l, (2 * c + i) * 64:(2 * c + i + 1) * 64, :], writes=[dW])
        xfs, dxf = [], []
        for (kind, col0, nb, T) in pieces:
            xfs.append(A.take(2 * nb * (3 + T)).rearrange("p (c b t) -> p c b t", c=2, b=nb))
            dxf.append([Dep(), Dep()])
        mk = lambda: A.take(2 * TS).rearrange("p (c t) -> p c t", c=2)
        xc, ra, ib, hh, gl = mk(), mk(), mk(), mk(), mk()
        dxc, dra, dib, dhh, dgl = [[Dep(), Dep()] for _ in range(5)]
        h0s = A.take(2 * 16).rearrange("p (c b) -> p c b", c=2)
        dh0 = Dep()
        for pi, (kind, col0, nb, T) in enumerate(pieces):
            for c in range(2):
                if kind == "p" and name == "A":
                    op(DVE, lambda c=c, pi=pi: nc.vector.memset(xfs[pi][:, c, :, 0:3], 0.0), writes=[dxf[pi][c]])
                elif kind == "p":
                    op(DVE, lambda c=c, pi=pi: nc.vector.tensor_copy(out=xfs[pi][:, c, 0, 0:3], in_=lrutail[l][:, c, :]), reads=[st_dep], writes=[dxf[pi][c]])
            if kind == "s":
                load_T(lambda c, pi=pi: xfs[pi][:, c, :, 0:3], I["state_lru_conv"][l].rearrange("b j c -> (b j) c"), 48, 256, [dxf[pi][0], dxf[pi][1]])
                load_T(lambda c: h0s[:, c, :], I["state_lru"][l], 16, 256, [dh0])
        for c in range(2):
            def ev_x(bank, bd, ti, s, w, c=c):
                def f(pi, piece, p_off, n, b_off):
                    kind, col0, nb, T = piece
                    op(ACT, lambda: nc.scalar.copy(out=xf_view(xfs[pi], kind, nb, T, 3, c, p_off, n), in_=src_view(bank[:, b_off:b_off + n], kind, nb, T)),
                       reads=[bd], writes=[dxf[pi][c]])
                scatter(pieces, s, w, f)

            def ev_g(bank, bd, ti, s, w, c=c):
                op(ACT, lambda: nc.scalar.activation(out=gl[:, c, s:s + w], in_=bank[:, 0:w], func=AF.Gelu), reads=[bd], writes=[dgl[c]])
            proj_in(l, OFF["lx"] + c * 128, 128, tiles, ev_x)
            proj_in(l, OFF["lg"] + c * 128, 128, tiles, ev_g)
        for pi, (kind, col0, nb, T) in enumerate(pieces):
            for c in range(2):
                xf_p = xfs[pi][:, c, 0, :] if kind == "p" else xfs[pi][:, c, :, :]
                v = lambda t: (t[:, c, col0:col0 + nb * T] if kind == "p" else t[:, c, col0:col0 + nb * T].rearrange("p (b t) -> p b t", b=nb))
                conv_taps(xf_p, v(xc), wv, prm[:, :, 0], c, 4, T, [dW, dxf[pi][c]], [dxc[c]])
        for c in range(2):
            gbk = []
            for ti, (s, w) in enumerate(tiles):
                for a, (dst, dd) in enumerate(((ra, dra), (ib, dib))):
                    bank, bd = pbank()
                    gbk.append((bank, bd, a, dst, dd, s, w))
                    op(PE, lambda a=a, bank=bank, s=s, w=w: nc.tensor.matmul(bank[:, 0:w], lhsT=wg[:, a, c, :], rhs=xc[:, c, s:s + w], start=True, stop=True), reads=[dW, dxc[c]], writes=[bd])
            for (bank, bd, a, dst, dd, s, w) in gbk:
                op(ACT, lambda a=a, dst=dst, bank=bank, s=s, w=w: nc.scalar.activation(out=dst[:, c, s:s + w], in_=bank[:, 0:w], func=AF.Sigmoid, bias=prm[:, c, 1 + a:2 + a]),
                   reads=[bd, dW], writes=[dd[c]])
            op(ACT, lambda: nc.scalar.activation(out=ra[:, c, 0:TS], in_=ra[:, c, 0:TS], func=AF.Exp, scale=prm[:, c, 4:5]), reads=[dra[c], dW], writes=[dra[c]])
            op(DVE, lambda: nc.vector.tensor_tensor(out=ib[:, c, 0:TS], in0=ib[:, c, 0:TS], in1=xc[:, c, 0:TS], op=ALU.mult), reads=[dib[c], dxc[c]], writes=[dib[c]])
            op(DVE, lambda: nc.vector.tensor_tensor(out=hh[:, c, 0:TS], in0=ra[:, c, 0:TS], in1=ra[:, c, 0:TS], op=ALU.mult), reads=[dra[c]], writes=[dhh[c]])
            op(ACT, lambda: nc.scalar.activation(out=hh[:, c, 0:TS], in_=hh[:, c, 0:TS], func=AF.Sqrt, scale=-1.0, bias=1.0), reads=[dhh[c]], writes=[dhh[c]])
            op(DVE, lambda: nc.vector.tensor_tensor(out=ib[:, c, 0:TS], in0=ib[:, c, 0:TS], in1=hh[:, c, 0:TS], op=ALU.mult), reads=[dib[c], dhh[c]], writes=[dib[c]])
            for pi, (kind, col0, nb, T) in enumerate(pieces):
                a3 = ra[:, c, col0:col0 + nb * T].rearrange("p (b t) -> p b t", b=nb)
                b3 = ib[:, c, col0:col0 + nb * T].rearrange("p (b t) -> p b t", b=nb)
                if not (kind == "p" and name == "A"):
                    h0v = lruh[l][:, c:c + 1] if kind == "p" else h0s[:, c, :]
                    hd = st_dep if kind == "p" else dh0
                    t0, t0d = tmp()
                    op(DVE, lambda: nc.vector.tensor_tensor(out=t0[:, 0:nb], in0=a3[:, :, 0], in1=h0v, op=ALU.mult), reads=[dra[c], hd], writes=[t0d])
                    op(DVE, lambda: nc.vector.tensor_tensor(out=b3[:, :, 0], in0=b3[:, :, 0], in1=t0[:, 0:nb], op=ALU.add), reads=[dib[c], t0d], writes=[dib[c]])
                if nb > 1:
                    op(DVE, lambda: nc.vector.memset(a3[:, :, 0:1], 0.0), reads=[dra[c]], writes=[dra[c]])
                op(DVE, lambda: nc.vector.tensor_tensor_scan(out=hh[:, c, col0:col0 + nb * T], data0=ra[:, c, col0:col0 + nb * T], data1=ib[:, c, col0:col0 + nb * T],
                                                             initial=0.0, op0=ALU.mult, op1=ALU.add), reads=[dra[c], dib[c], dhh[c]], writes=[dhh[c]])
            for ti, (s, w) in enumerate(tiles):
                op(DVE, lambda: nc.vector.tensor_tensor(out=brT[:, 4 + c, s:s + w], in0=hh[:, c, s:s + w], in1=gl[:, c, s:s + w], op=ALU.mult),
                   reads=[dhh[c], dgl[c]], writes=[br_dep[4 + c][ti]])
        for pi, (kind, col0, nb, T) in enumerate(pieces):
            if kind == "p" and name == "A":
                for c in range(2):
                    op(DVE, lambda c=c, pi=pi: nc.vector.tensor_copy(out=lrutail[l][:, c, :], in_=xfs[pi][:, c, 0, T:T + 3]), reads=[dxf[pi][c]], writes=[st_dep])
                    op(DVE, lambda c=c: nc.vector.tensor_copy(out=lruh[l][:, c:c + 1], in_=hh[:, c, col0 + T - 1:col0 + T]), reads=[dhh[c]], writes=[st_dep])
            elif kind == "p":
                store_T(O["p_lru_conv"][l], lambda c, pi=pi, T=T: xfs[pi][:, c, 0, T:T + 3], 3, 256, dxf[pi])
                store_T(O["p_lru"][l:l + 1, :], lambda c, col0=col0, T=T: hh[:, c, col0 + T - 1:col0 + T], 1, 256, dhh)
            else:
                store_T(O["s_lru_conv"][l].rearrange("b j c -> (b j) c"), lambda c, pi=pi, T=T: xfs[pi][:, c, :, T:T + 3], 48, 256, dxf[pi])
                store_T(O["s_lru"][l], lambda c, col0=col0, nb=nb, T=T: hh[:, c, col0:col0 + nb * T].rearrange("p (b t) -> p b t", b=nb)[:, :, T - 1], 16, 256, dhh)


    s5h = [sb([128, 8, 2, 1], F32, f"s5h{l}") for l in range(NL)]

    def branch_s5(l, name, TS, tiles):
        import math
        pieces = pieces_of(name)
        A = Arena()
        dW = Dep()
        LT = 130 if name == "A" else 128
        dv = lambda f, reads=(), writes=(): op(DVE, f, reads=list(reads) + [dW], writes=list(writes) + [dW])
        av = lambda f, reads=(), writes=(): op(ACT, f, reads=list(reads) + [dW], writes=list(writes) + [dW])
        lam = A.take(16).rearrange("p (m r) -> p m r", m=8)
        load_T(lambda m: lam[:, m, 0:2], [(I["s5_lam_re"][l:l + 1, :], 1), (I["s5_lam_im"][l:l + 1, :], 1)], 2, 1024, [dW])
        row = A.take(16)
        load(row[0:1, 0:16], I["s5_log_step"][l:l + 1, :], writes=[dW])
        bank, bd = pbank()
        op(PE, lambda: nc.tensor.matmul(bank[:, 0:16], lhsT=ones_f[0:1, :], rhs=row[0:1, 0:16], start=True, stop=True), reads=[dW, d_const], writes=[bd])
        sm = A.take(8 * 16).rearrange("p (m j) -> p m j", m=8)
        V = lambda j: sm[:, :, j]
        DT, TH, MAG, CC, SS, LBR, LBI, CFR, CFI, DEN, T1, T2, T3, NCFI = range(14)
        LR, LI = lam[:, :, 0], lam[:, :, 1]
        for h in range(2):
            av(lambda h=h: nc.scalar.activation(out=sm[h * 64:(h + 1) * 64, :, DT], in_=bank[h * 64:(h + 1) * 64, 0:16].rearrange("p (m x) -> p m x", x=2)[:, :, h], func=AF.Exp), reads=[bd])
        tt = lambda o, a, b, f: dv(lambda: nc.vector.tensor_tensor(out=o, in0=a, in1=b, op=f))
        tt(V(TH), LI, V(DT), ALU.mult)
        tt(V(T1), LR, V(DT), ALU.mult)
        av(lambda: nc.scalar.activation(out=V(MAG), in_=V(T1), func=AF.Exp))
        hp = A.take(1)
        dv(lambda: nc.vector.memset(hp, math.pi / 2))
        av(lambda: nc.scalar.activation(out=V(SS), in_=V(TH), func=AF.Sin, scale=1.0 / 16))
        av(lambda: nc.scalar.activation(out=V(CC), in_=V(TH), func=AF.Sin, scale=1.0 / 16, bias=hp[:, 0:1]))
        for _ in range(4):
            tt(V(T1), V(CC), V(CC), ALU.mult)
            tt(V(T2), V(SS), V(SS), ALU.mult)
            tt(V(T3), V(CC), V(SS), ALU.mult)
            tt(V(CC), V(T1), V(T2), ALU.subtract)
            tt(V(SS), V(T3), V(T3), ALU.add)
        tt(V(LBR), V(MAG), V(CC), ALU.mult)
        tt(V(LBI), V(MAG), V(SS), ALU.mult)
        tt(V(T1), LR, LR, ALU.mult)
        tt(V(T2), LI, LI, ALU.mult)
        tt(V(DEN), V(T1), V(T2), ALU.add)
        dv(lambda: nc.vector.reciprocal(out=V(DEN), in_=V(DEN)))
        dv(lambda: nc.vector.tensor_scalar_add(out=V(T3), in0=V(LBR), scalar1=-1.0))
        tt(V(T1), V(T3), LR, ALU.mult)
        tt(V(T2), V(LBI), LI, ALU.mult)
        tt(V(T1), V(T1), V(T2), ALU.add)
        tt(V(CFR), V(T1), V(DEN), ALU.mult)
        tt(V(T1), V(LBI), LR, ALU.mult)
        tt(V(T2), V(T3), LI, ALU.mult)
        tt(V(T1), V(T1), V(T2), ALU.subtract)
        tt(V(CFI), V(T1), V(DEN), ALU.mult)
        dv(lambda: nc.vector.tensor_scalar_mul(out=V(NCFI), in0=V(CFI), scalar1=-1.0))
        Bre = A.take(128).rearrange("p (m c) -> p m c", m=8)
        Bim = A.take(128).rearrange("p (m c) -> p m c", m=8)
        Bp = A.take(128).rearrange("p (m c) -> p m c", m=8)
        tb = A.take(16)
        for m in range(8):
            load(Bre[:, m, :], I["s5_b_re"][l, m * 128:(m + 1) * 128, :], writes=[dW])
            load(Bim[:, m, :], I["s5_b_im"][l, m * 128:(m + 1) * 128, :], writes=[dW])
        BT = A.take(8 * 2 * 128).rearrange("p (m r n) -> p m r n", m=8, r=2)
        CT = A.take(8 * 2 * 128).rearrange("p (m r n) -> p m r n", m=8, r=2)
        E = A.take(8 * 128).rearrange("p (m n) -> p m n", m=8)
        for ri in range(2):
            dv(lambda: nc.vector.memset(E, 0.0))
            for m in range(8):
                X1, X2, sc2 = (Bre, Bim, NCFI) if ri == 0 else (Bim, Bre, CFI)
                dv(lambda m=m, X2=X2, sc2=sc2: nc.vector.tensor_scalar_mul(out=tb, in0=X2[:, m, :], scalar1=sm[:, m, sc2:sc2 + 1]))
                dv(lambda m=m, X1=X1: nc.vector.scalar_tensor_tensor(out=Bp[:, m, :], in0=X1[:, m, :], scalar=sm[:, m, CFR:CFR + 1], in1=tb, op0=ALU.mult, op1=ALU.add))
                for h in range(2):
                    o0 = (m % 4) * 32 + h * 16
                    dv(lambda m=m, h=h, o0=o0: nc.vector.tensor_copy(out=E[h * 64:(h + 1) * 64, m, o0:o0 + 16], in_=Bp[h * 64:(h + 1) * 64, m, :]))
            for m in range(8):
                bk, bkd = pbank()
                op(PE, lambda m=m: nc.tensor.transpose(bk[:, 0:128], E[:, m, :], ident[:]), reads=[dW, d_const], writes=[bkd])
                av(lambda m=m, bk=bk: nc.scalar.copy(out=BT[:, m, ri, :], in_=bk[:, 0:128]), reads=[bkd])
        dv(lambda: nc.vector.memset(CT, 0.0))
        for ri, nm in enumerate(("s5_c_re", "s5_c_im")):
            for rb in range(2):
                st, sd = staging()
                load(st[:, 0:64], I[nm][l, rb * 128:(rb + 1) * 128, :], writes=[sd])
                load(st[:, 64:128], I[nm][l, rb * 128:(rb + 1) * 128, :], writes=[sd])
                bk, bkd = pbank()
                op(PE, lambda: nc.tensor.transpose(bk[:, 0:128], st[:, 0:128], ident[:]), reads=[sd, d_const], writes=[bkd])
                for m in range(rb * 4, rb * 4 + 4):
                    for h in range(2):
                        o0 = (m % 4) * 32 + h * 16
                        av(lambda m=m, h=h, o0=o0, bk=bk: nc.scalar.mul(out=CT[h * 64:(h + 1) * 64, m, ri, o0:o0 + 16], in_=bk[h * 64:(h + 1) * 64, o0:o0 + 16],
                                                                        mul=(1.0 if ri == 0 else -1.0)), reads=[bkd])
        cosT = A.take(8 * LT).rearrange("p (m t) -> p m t", m=8)
        sinT = A.take(8 * LT).rearrange("p (m t) -> p m t", m=8)
        rho = A.take(8 * LT).rearrange("p (m t) -> p m t", m=8)
        rhos = A.take(8 * 128).rearrange("p (m t) -> p m t", m=8)
        ta = A.take(8 * 65).rearrange("p (m t) -> p m t", m=8)
        tb2 = A.take(8 * 65).rearrange("p (m t) -> p m t", m=8)
        dv(lambda: nc.vector.tensor_copy(out=cosT[:, :, 0:1], in_=sm[:, :, CC:CC + 1]))
        dv(lambda: nc.vector.tensor_copy(out=sinT[:, :, 0:1], in_=sm[:, :, SS:SS + 1]))
        n = 1
        while n < LT:
            k = min(n, LT - n)
            cn = cosT[:, :, n - 1:n].to_broadcast([128, 8, k])
            sn = sinT[:, :, n - 1:n].to_broadcast([128, 8, k])
            tt(ta[:, :, 0:k], cosT[:, :, 0:k], cn, ALU.mult)
            tt(tb2[:, :, 0:k], sinT[:, :, 0:k], sn, ALU.mult)
            tt(cosT[:, :, n:n + k], ta[:, :, 0:k], tb2[:, :, 0:k], ALU.subtract)
            tt(ta[:, :, 0:k], sinT[:, :, 0:k], cn, ALU.mult)
            tt(tb2[:, :, 0:k], cosT[:, :, 0:k], sn, ALU.mult)
            tt(sinT[:, :, n:n + k], ta[:, :, 0:k], tb2[:, :, 0:k], ALU.add)
            n += k
        dv(lambda: nc.vector.tensor_copy(out=rho, in_=sm[:, :, MAG:MAG + 1].to_broadcast([128, 8, LT])))
        dv(lambda: nc.vector.tensor_copy(out=rhos, in_=sm[:, :, MAG:MAG + 1].to_broadcast([128, 8, 128])))
        dv(lambda: nc.vector.memset(rhos.rearrange("p m (b t) -> p m b t", t=8)[:, :, :, 0:1], 0.0))
        su = A.take(2 * TS).rearrange("p (c t) -> p c t", c=2)
        dsu = Dep()
        for c in range(2):
            proj_in(l, OFF["su"] + c * 128, 128, tiles,
                    lambda bank, bd, ti, s, w, c=c: op(ACT, lambda: nc.scalar.copy(out=su[:, c, s:s + w], in_=bank[:, 0:w]), reads=[bd], writes=[dsu]))
        dsk = A.take(2)
        load_T(lambda c: dsk[:, c:c + 1], I["s5_d"][l:l + 1, :], 1, 256, [dW])
        Hs = A.take(8 * 2 * LT).rearrange("p (m r t) -> p m r t", m=8, r=2)
        dH = Dep()
        xt = [A.take(8 * LT).rearrange("p (m t) -> p m t", m=8) for _ in range(4)]
        dx = Dep()
        hS = A.take(8 * 2 * 16).rearrange("p (m r b) -> p m r b", m=8, r=2)
        dHP = Dep()
        dv(lambda: nc.vector.memset(rho[:, :, 0:1], 0.0))
        for (kind, col0, nb, T) in pieces:
            if kind == "s":
                for ri, nm in enumerate(("state_s5_re", "state_s5_im")):
                    load_T(lambda m, ri=ri: hS[:, m, ri, 0:16], I[nm][l], 16, 1024, [dW])
                hprev = hS
                subs = [(col0, nb, T)]
            else:
                hprev = s5h[l]
                if name == "A":
                    dv(lambda: nc.vector.memset(s5h[l][:], 0.0), writes=[st_dep])
                subs = [(col0 + i * LT, 1, LT) for i in range(T // LT)]
            for (c0, nbb, Tl) in subs:
                nn = nbb * Tl
                assert nn == LT
                gsz = max(1, 512 // nn)
                groups = [list(range(i, min(8, i + gsz))) for i in range(0, 8, gsz)]
                v4 = lambda ap: ap.rearrange("p m (b t) -> p m b t", b=nbb)
                rd = [dx, st_dep, dW, dHP]
                f = lambda o, a, b, g_, extra=(): op(DVE, lambda: nc.vector.tensor_tensor(out=o, in0=a, in1=b, op=g_), reads=rd + list(extra), writes=[dx])
                for grp in groups:
                    g0, gl = grp[0], len(grp)
                    br_, brd = pbank()
                    bi_, bid = pbank()
                    for (bk_, bkd_, ri) in ((br_, brd, 0), (bi_, bid, 1)):
                        for j, m in enumerate(grp):
                            op(PE, lambda m=m, j=j, bk_=bk_, ri=ri: nc.tensor.matmul(bk_[:, j * nn:(j + 1) * nn], lhsT=BT[:, m, ri, :], rhs=su[:, m // 4, c0:c0 + nn], start=True, stop=True),
                               reads=[dW, dsu], writes=[bkd_], inc=(j == gl - 1))
                    pr = br_[:, 0:gl * nn].rearrange("p (m b t) -> p m b t", m=gl, b=nbb)
                    pi_ = bi_[:, 0:gl * nn].rearrange("p (m b t) -> p m b t", m=gl, b=nbb)
                    cv = cosT[:, g0:g0 + gl, 0:Tl].unsqueeze(2).to_broadcast([128, gl, nbb, Tl])
                    sv_ = sinT[:, g0:g0 + gl, 0:Tl].unsqueeze(2).to_broadcast([128, gl, nbb, Tl])
                    Xg = [v4(x[:, g0:g0 + gl, :]) for x in xt]
                    f(Xg[0], pr, cv, ALU.mult, [brd])
                    f(Xg[2], pi_, sv_, ALU.mult, [bid])
                    f(Xg[1], pi_, cv, ALU.mult, [bid])
                    f(Xg[3], pr, sv_, ALU.mult, [brd])
                f(xt[0], xt[0], xt[2], ALU.add)
                f(xt[1], xt[1], xt[3], ALU.subtract)
                X4 = [v4(x) for x in xt]
                for ri in range(2):
                    tsc = xt[2][:, :, 0:nbb]
                    f(tsc, hprev[:, :, ri, 0:nbb], sm[:, :, MAG:MAG + 1].to_broadcast([128, 8, nbb]), ALU.mult)
                    f(X4[ri][:, :, :, 0], X4[ri][:, :, :, 0], tsc, ALU.add)
                rv = rho if nbb == 1 else rhos
                for ri in range(2):
                    op(DVE, lambda ri=ri: nc.vector.tensor_tensor_scan(out=xt[ri].rearrange("p m t -> p (m t)"), data0=rv.rearrange("p m t -> p (m t)"),
                                                                       data1=xt[ri].rearrange("p m t -> p (m t)"), initial=0.0, op0=ALU.mult, op1=ALU.add),
                       reads=[dx, dW], writes=[dx])
                cvA = cosT[:, :, 0:Tl].unsqueeze(2).to_broadcast([128, 8, nbb, Tl])
                svA = sinT[:, :, 0:Tl].unsqueeze(2).to_broadcast([128, 8, nbb, Tl])
                H0, H1 = v4(Hs[:, :, 0, 0:nn]), v4(Hs[:, :, 1, 0:nn])
                fh = lambda o, a, b, g_, wr: op(DVE, lambda: nc.vector.tensor_tensor(out=o, in0=a, in1=b, op=g_), reads=[dx, dH, dW], writes=[wr])
                fh(X4[2], X4[0], cvA, ALU.mult, dx)
                fh(X4[3], X4[1], svA, ALU.mult, dx)
                fh(H0, X4[2], X4[3], ALU.subtract, dH)
                fh(X4[2], X4[0], svA, ALU.mult, dx)
                fh(X4[3], X4[1], cvA, ALU.mult, dx)
                fh(H1, X4[2], X4[3], ALU.add, dH)
                for ri in range(2):
                    op(DVE, lambda ri=ri: nc.vector.tensor_copy(out=hprev[:, :, ri, 0:nbb], in_=v4(Hs[:, :, ri, 0:nn])[:, :, :, Tl - 1]), reads=[dH, dx], writes=[st_dep, dHP])
                for cc in range(2):
                    by, byd = pbank()
                    i = 0
                    for m in range(4 * cc, 4 * cc + 4):
                        for ri in range(2):
                            op(PE, lambda m=m, ri=ri, i=i: nc.tensor.matmul(by[:, 0:nn], lhsT=CT[:, m, ri, :], rhs=Hs[:, m, ri, 0:nn], start=(i == 0), stop=(i == 7)),
                               reads=[dH, dW], writes=[byd], inc=(i == 7))
                            i += 1
                    t0, t0d = tmp()
                    op(DVE, lambda: nc.vector.scalar_tensor_tensor(out=t0[:, 0:nn], in0=su[:, cc, c0:c0 + nn], scalar=dsk[:, cc:cc + 1], in1=by[:, 0:nn], op0=ALU.mult, op1=ALU.add),
                       reads=[dsu, byd, dW], writes=[t0d])
                    op(ACT, lambda: nc.scalar.activation(out=su[:, cc, c0:c0 + nn], in_=t0[:, 0:nn], func=AF.Gelu), reads=[t0d], writes=[dsu])
            if kind == "p" and name == "B":
                for ri, nm in enumerate(("p_s5_re", "p_s5_im")):
                    store_T(O[nm][l:l + 1, :], lambda m, ri=ri: s5h[l][:, m, ri, 0:1], 1, 1024, [st_dep])
            elif kind == "s":
                for ri, nm in enumerate(("s_s5_re", "s_s5_im")):
                    store_T(O[nm][l], lambda m, ri=ri: hS[:, m, ri, 0:16], 16, 1024, [dW, dHP])
        wgl = E.rearrange("p m n -> p (m n)").rearrange("p (k n) -> p k n", k=2)
        load(wgl, I["s5_w_glu"][l].rearrange("(k p) n -> p k n", p=128), writes=[dW])
        bg = A.take(4)
        load_T(lambda c: bg[:, c:c + 1], I["s5_b_glu"][l:l + 1, :], 1, 512, [dW])
        for ti, (s, w) in enumerate(tiles):
            for oc in range(2):
                ba, bad = pbank()
                bb, bbd = pbank()
                for (bk, bkd, o) in ((ba, bad, oc), (bb, bbd, 2 + oc)):
                    for kc in range(2):
                        op(PE, lambda bk=bk, o=o, kc=kc: nc.tensor.matmul(bk[:, 0:w], lhsT=wgl[:, kc, o * 128:(o + 1) * 128], rhs=su[:, kc, s:s + w], start=(kc == 0), stop=(kc == 1)),
                           reads=[dsu, dW], writes=[bkd], inc=(kc == 1))
                t1_, t1d = tmp()
                t2_, t2d = tmp()
                op(ACT, lambda: nc.scalar.activation(out=t1_[:, 0:w], in_=ba[:, 0:w], func=AF.Identity, bias=bg[:, oc:oc + 1]), reads=[bad, dW], writes=[t1d])
                op(ACT, lambda: nc.scalar.activation(out=t2_[:, 0:w], in_=bb[:, 0:w], func=AF.Sigmoid, bias=bg[:, 2 + oc:3 + oc]), reads=[bbd, dW], writes=[t2d])
                op(DVE, lambda: nc.vector.tensor_tensor(out=brT[:, 2 + oc, s:s + w], in0=t1_[:, 0:w], in1=t2_[:, 0:w], op=ALU.mult), reads=[t1d, t2d], writes=[br_dep[2 + oc][ti]])


    dnS = [sb([64, 4, 1, 64], F32, f"dnS{l}") for l in range(NL)]
    dntail = [sb([128, 6, 1, 3], F32, f"dntail{l}") for l in range(NL)]
    dn_wsem = dsem("dnw")

    def branch_dn(l, name, TS, tiles):
        pieces = pieces_of(name)
        A = Arena()
        dW = Dep()
        dv = lambda f, reads=(), writes=(): op(DVE, f, reads=list(reads) + [dW], writes=list(writes) + [dW])
        av = lambda f, reads=(), writes=(): op(ACT, f, reads=list(reads) + [dW], writes=list(writes) + [dW])
        wv = A.take(24).rearrange("p (c j) -> p c j", c=6)
        zb6 = A.take(6)
        load_T(lambda c: wv[:, c, 0:4], I["dn_conv_w"][l], 4, 768, [dW])
        dv(lambda: nc.vector.memset(zb6, 0.0))
        row = A.take(72)
        load(row[0:1, 0:4], I["dn_a_log"][l:l + 1, :], writes=[dW])
        load(row[0:1, 4:8], I["dn_dt_bias"][l:l + 1, :], writes=[dW])
        load(row[0:1, 8:72], I["dn_norm"][l:l + 1, :], writes=[dW])
        prmB = A.take(72)
        bk, bkd = pbank()
        op(PE, lambda: nc.tensor.matmul(bk[:, 0:72], lhsT=ones_f[0:1, :], rhs=row[0:1, 0:72], start=True, stop=True), reads=[dW, d_const], writes=[bkd])
        av(lambda: nc.scalar.copy(out=prmB, in_=bk[:, 0:72]), reads=[bkd])
        negA = A.take(4)
        av(lambda: nc.scalar.activation(out=negA, in_=prmB[:, 0:4], func=AF.Exp))
        dv(lambda: nc.vector.tensor_scalar_mul(out=negA, in0=negA, scalar1=-1.0))
        ngB = prmB[:, 8:72]
        triU, trilS, blk1, blkS, triUs, trilSs, blkones = [A.take(128) for _ in range(7)]
        dv(lambda: nc.vector.tensor_single_scalar(out=triU, in_=itf[:], scalar=0.0, op=ALU.is_ge), reads=[d_const])
        dv(lambda: nc.vector.tensor_single_scalar(out=trilS, in_=itf[:], scalar=0.0, op=ALU.is_lt), reads=[d_const])
        dv(lambda: nc.vector.memset(blk1, 1.0))
        dv(lambda: nc.vector.memset(blkones, 0.0))
        dv(lambda: nc.vector.memset(blkones[0:64, 0:64], 1.0))
        dv(lambda: nc.vector.memset(blkones[64:128, 64:128], 1.0))
        has_s = any(k == "s" for (k, _, _, _) in pieces)
        Ecol = A.take(16)
        if has_s:
            Ei = A.take(128)
            Ef = A.take(128)
            Ef2 = A.take(128)
            op(POOL, lambda: nc.gpsimd.iota(Ei[0:16, :].bitcast(I32), pattern=[[1, 128]], base=0, channel_multiplier=-8), writes=[dW])
            dv(lambda: nc.vector.tensor_copy(out=Ef[0:16, :], in_=Ei[0:16, :].bitcast(I32)))
            dv(lambda: nc.vector.tensor_single_scalar(out=Ef2[0:16, :], in_=Ef[0:16, :], scalar=0.0, op=ALU.is_ge))
            dv(lambda: nc.vector.tensor_single_scalar(out=Ef[0:16, :], in_=Ef[0:16, :], scalar=7.0, op=ALU.is_le))
            dv(lambda: nc.vector.tensor_tensor(out=Ef[0:16, :], in0=Ef[0:16, :], in1=Ef2[0:16, :], op=ALU.mult))
            bk, bkd = pbank()
            op(PE, lambda: nc.tensor.matmul(bk[:, 0:128], lhsT=Ef[0:16, :], rhs=Ef[0:16, :], start=True, stop=True), reads=[dW], writes=[bkd])
            av(lambda: nc.scalar.copy(out=blkS, in_=bk[:, 0:128]), reads=[bkd])
            bk2, bk2d = pbank()
            op(PE, lambda: nc.tensor.transpose(bk2[:, 0:16], Ef[0:16, :], ident[0:16, 0:16]), reads=[dW, d_const], writes=[bk2d])
            av(lambda: nc.scalar.copy(out=Ecol, in_=bk2[:, 0:16]), reads=[bk2d])
            dv(lambda: nc.vector.tensor_tensor(out=triUs, in0=triU, in1=blkS, op=ALU.mult))
            dv(lambda: nc.vector.tensor_tensor(out=trilSs, in0=trilS, in1=blkS, op=ALU.mult))
        wz = A.take(KC * 264 // 2).bitcast(BF16).rearrange("p (k n) -> p k n", k=KC)
        dma(POOL, dn_wsem, wz, w_in_v[l][:, :, OFF["dz"]:OFF["dz"] + 264], writes=[dW])
        qkv = A.take(6 * TS).rearrange("p (c t) -> p c t", c=6)
        dq = [Dep() for _ in range(6)]
        xf1 = [A.take(nb * (3 + T)).rearrange("p (b t) -> p b t", b=nb) for (kind, col0, nb, T) in pieces]
        dxf = [Dep() for _ in pieces]
        tailS = A.take(6 * 16 * 3).rearrange("p (c b j) -> p c b j", c=6, b=16)
        tailN = A.take(6 * 16 * 3).rearrange("p (c b j) -> p c b j", c=6, b=16)
        dtl = Dep()
        if has_s:
            load_T(lambda c: tailS[:, c, :, :], I["state_delta_conv"][l].rearrange("b j c -> (b j) c"), 48, 768, [dtl])
        for c6 in range(6):
            for pi, (kind, col0, nb, T) in enumerate(pieces):
                if kind == "p" and name == "A":
                    dv(lambda pi=pi: nc.vector.memset(xf1[pi][:, :, 0:3], 0.0), writes=[dxf[pi]])
                elif kind == "p":
                    dv(lambda pi=pi, c6=c6: nc.vector.tensor_copy(out=xf1[pi][:, :, 0:3], in_=dntail[l][:, c6, :, :]), reads=[st_dep], writes=[dxf[pi]])
                else:
                    dv(lambda pi=pi, c6=c6: nc.vector.tensor_copy(out=xf1[pi][:, :, 0:3], in_=tailS[:, c6, :, :]), reads=[dtl], writes=[dxf[pi]])

            def ev(bank, bd, ti, s, w):
                def f(pi, piece, p_off, n, b_off):
                    kind, col0, nb, T = piece
                    dst = xf1[pi][:, 0, 3 + p_off:3 + p_off + n] if kind == "p" else xf1[pi][:, :, 3:3 + T]
                    op(ACT, lambda: nc.scalar.copy(out=dst, in_=src_view(bank[:, b_off:b_off + n], kind, nb, T)), reads=[bd], writes=[dxf[pi]])
                scatter(pieces, s, w, f)
            proj_in(l, OFF["dq"] + c6 * 128, 128, tiles, ev)
            for pi, (kind, col0, nb, T) in enumerate(pieces):
                xf_p = xf1[pi][:, 0, :] if kind == "p" else xf1[pi][:, :, :]
                o_p = qkv[:, c6, col0:col0 + nb * T] if kind == "p" else qkv[:, c6, col0:col0 + nb * T].rearrange("p (b t) -> p b t", b=nb)
                conv_taps(xf_p, o_p, wv, zb6, c6, 4, T, [dW, dxf[pi]], [dq[c6]])
                if kind == "p" and name == "A":
                    dv(lambda pi=pi, c6=c6, T=T: nc.vector.tensor_copy(out=dntail[l][:, c6, :, :], in_=xf1[pi][:, :, T:T + 3]), reads=[dxf[pi]], writes=[st_dep])
                else:
                    dv(lambda pi=pi, c6=c6, T=T, nb=nb, kind=kind: nc.vector.tensor_copy(out=(tailN[:, c6, 0:1, :] if kind == "p" else tailS[:, c6, :, :]), in_=xf1[pi][:, :, T:T + 3]),
                       reads=[dxf[pi], dtl], writes=[dtl])
            op(ACT, lambda c6=c6: nc.scalar.activation(out=qkv[:, c6, 0:TS], in_=qkv[:, c6, 0:TS], func=AF.Silu), reads=[dq[c6]], writes=[dq[c6]])
        if name == "B":
            store_T(O["p_delta_conv"][l], lambda c: tailN[:, c, 0, :], 3, 768, [dtl])
            store_T(O["s_delta_conv"][l].rearrange("b j c -> (b j) c"), lambda c: tailS[:, c, :, :], 48, 768, [dtl])
        for c4 in range(4):
            tl = [tmp() for _ in tiles]
            bl = []
            for ti, (s, w) in enumerate(tiles):
                t0, t0d = tl[ti]
                op(ACT, lambda t0=t0, s=s, w=w: nc.scalar.activation(out=t0[:, 0:w], in_=qkv[:, c4, s:s + w], func=AF.Square), reads=[dq[c4]], writes=[t0d])
            for ti, (s, w) in enumerate(tiles):
                t0, t0d = tl[ti]
                bk, bkd = pbank()
                bl.append((bk, bkd))
                op(PE, lambda t0=t0, bk=bk, w=w: nc.tensor.matmul(bk[:, 0:w], lhsT=blkones, rhs=t0[:, 0:w], start=True, stop=True), reads=[t0d, dW], writes=[bkd])
            for ti, (s, w) in enumerate(tiles):
                t0, t0d = tl[ti]
                bk, bkd = bl[ti]
                op(ACT, lambda t0=t0, bk=bk, w=w: nc.scalar.activation(out=t0[:, 0:w], in_=bk[:, 0:w], func=AF.Ln, bias=eps_t[:, 0:1]), reads=[bkd, d_const], writes=[t0d])
            for ti, (s, w) in enumerate(tiles):
                t0, t0d = tl[ti]
                op(ACT, lambda t0=t0, w=w: nc.scalar.activation(out=t0[:, 0:w], in_=t0[:, 0:w], func=AF.Exp, scale=-0.5), reads=[t0d], writes=[t0d])
            for ti, (s, w) in enumerate(tiles):
                t0, t0d = tl[ti]
                op(DVE, lambda t0=t0, s=s, w=w: nc.vector.scalar_tensor_tensor(out=qkv[:, c4, s:s + w], in0=qkv[:, c4, s:s + w], scalar=(0.125 if c4 < 2 else 1.0), in1=t0[:, 0:w],
                                                                               op0=ALU.mult, op1=ALU.mult), reads=[dq[c4], t0d], writes=[dq[c4]])
        dqa = dq
        HB = 4
        mkh = lambda n: [A.take(n) for _ in range(HB)]
        Xb, Nb, NTb, ATb, gBb, wTb, tTb = mkh(128), mkh(128), mkh(128), mkh(128), mkh(128), mkh(128), mkh(128)
        kdb, vnb, ob, o1b = mkh(64), mkh(64), mkh(64), mkh(64)
        qsb = mkh(128)
        dh = [Dep() for _ in range(HB)]
        zsil2 = [A.take(256) for _ in range(2)]
        oat2 = [A.take(256) for _ in range(2)]
        sv42 = [A.take(64).rearrange("p (j h) -> p j h", h=4) for _ in range(2)]
        ktv2 = [A.take(512) for _ in range(2)]
        dC2 = [Dep(), Dep()]
        Ssm = A.take(16 * 64).rearrange("p (b v) -> p b v", b=16)
        kdm = A.take(64)
        dS = Dep()
        CHUNK = debug.get("chunk", 128)

        def prologue(kind, c0, C, par):
            zsil, sv4, ktv, dC = zsil2[par], sv42[par], ktv2[par], dC2[par]
            mU, mL, mB = (triUs, trilSs, blkS) if kind == "s" else (triU, trilS, blk1)
            cd = lambda f, reads=(), writes=(): op(DVE, f, reads=list(reads) + [dC, dW], writes=list(writes) + [dC])
            ca = lambda f, reads=(), writes=(): op(ACT, f, reads=list(reads) + [dC, dW], writes=list(writes) + [dC])
            bz, bzd = pbank()
            for k in range(KC):
                op(PE, lambda k=k: nc.tensor.matmul(bz[0:C, 0:264], lhsT=uT[:, k, c0:c0 + C], rhs=wz[:, k, :], start=(k == 0), stop=(k == KC - 1)),
                   reads=[dW] + [u_dep[k][ti] for ti in range(len(tiles))], writes=[bzd], inc=(k == KC - 1))
            ca(lambda: nc.scalar.activation(out=zsil[0:C, :], in_=bz[0:C, 0:256], func=AF.Silu), reads=[bzd])
            ca(lambda: nc.scalar.activation(out=sv4[0:C, 0, :], in_=bz[0:C, 256:260], func=AF.Sigmoid), reads=[bzd])
            cd(lambda: nc.vector.tensor_tensor(out=sv4[0:C, 8, :], in0=bz[0:C, 260:264], in1=prmB[0:C, 4:8], op=ALU.add), reads=[bzd])
            yield
            cd(lambda: nc.vector.tensor_scalar_mul(out=sv4[0:C, 1, :], in0=sv4[0:C, 0, :], scalar1=-1.0))
            ca(lambda: nc.scalar.activation(out=sv4[0:C, 8, :], in_=sv4[0:C, 8, :], func=AF.Exp))
            ca(lambda: nc.scalar.activation(out=sv4[0:C, 8, :], in_=sv4[0:C, 8, :], func=AF.Ln, bias=1.0))
            yield
            cd(lambda: nc.vector.tensor_tensor(out=sv4[0:C, 2, :], in0=sv4[0:C, 8, :], in1=negA[0:C, :], op=ALU.mult))
            yield
            bg_, bgd = pbank()
            op(PE, lambda: nc.tensor.matmul(bg_[0:C, 0:4], lhsT=mU[0:C, 0:C], rhs=sv4[0:C, 2, :], start=True, stop=True), reads=[dC, dW], writes=[bgd])
            op(PE, lambda: nc.tensor.matmul(bg_[0:C, 4:8], lhsT=mB[0:C, 0:C], rhs=sv4[0:C, 2, :], start=True, stop=True), reads=[dC, dW], writes=[bgd])
            ca(lambda: nc.scalar.copy(out=sv4[0:C, 3:5, :], in_=bg_[0:C, 0:8].rearrange("p (j h) -> p j h", h=4)), reads=[bgd])
            bkt, bktd = pbank()
            for j, c6 in enumerate((2, 3, 4, 5)):
                op(PE, lambda j=j, c6=c6: nc.tensor.transpose(bkt[0:C, j * 128:(j + 1) * 128], qkv[:, c6, c0:c0 + C], ident[:]), reads=[dqa[c6], d_const], writes=[bktd], inc=(j == 3))
            ca(lambda: nc.scalar.copy(out=ktv[0:C, :], in_=bkt[0:C, :]), reads=[bktd])
            yield
            ca(lambda: nc.scalar.activation(out=sv4[0:C, 5, :], in_=sv4[0:C, 3, :], func=AF.Exp))
            cd(lambda: nc.vector.tensor_tensor(out=sv4[0:C, 7, :], in0=sv4[0:C, 4, :], in1=sv4[0:C, 3, :], op=ALU.subtract))
            yield
            cd(lambda: nc.vector.tensor_tensor(out=sv4[0:C, 6, :], in0=sv4[0:C, 0, :], in1=sv4[0:C, 5, :], op=ALU.mult))
            ca(lambda: nc.scalar.activation(out=sv4[0:C, 7, :], in_=sv4[0:C, 7, :], func=AF.Exp))

        def head(kind, c0, C, par, h, nbb, Tq):
            zsil, sv4, ktv, dC, oat = zsil2[par], sv42[par], ktv2[par], dC2[par], oat2[par]
            mU, mL, mB = (triUs, trilSs, blkS) if kind == "s" else (triU, trilS, blk1)
            levels = max(1, (Tq - 1).bit_length())
            hc, hp = h // 2, (h % 2) * 64
            hd = dh[h]
            hdv = lambda f, reads=(), writes=(): op(DVE, f, reads=list(reads) + [hd, dC, dW], writes=list(writes) + [hd])
            hav = lambda f, reads=(), writes=(): op(ACT, f, reads=list(reads) + [hd, dC, dW], writes=list(writes) + [hd])
            hpe = lambda f, reads=(), writes=(), inc=True: op(PE, f, reads=list(reads) + [hd, dC, dW], writes=list(writes), inc=inc)
            X, N, NT, AT, gB, wT, tT, kd, vn, oo, o1 = Xb[h], Nb[h], NTb[h], ATb[h], gBb[h], wTb[h], tTb[h], kdb[h], vnb[h], ob[h], o1b[h]
            qT = qkv[hp:hp + 64, 0 + hc, c0:c0 + C]
            kT = qkv[hp:hp + 64, 2 + hc, c0:c0 + C]
            ktok = ktv[0:C, hc * 128 + hp:hc * 128 + hp + 64]
            vtok = ktv[0:C, (2 + hc) * 128 + hp:(2 + hc) * 128 + hp + 64]
            hdv(lambda: nc.vector.tensor_copy(out=gB[0:C, :], in_=sv4[0:C, 2, h:h + 1].to_broadcast([C, 128])))
            hdv(lambda: nc.vector.tensor_scalar_mul(out=X[0:C, 0:64], in0=ktok, scalar1=sv4[0:C, 6, h:h + 1]))
            hdv(lambda: nc.vector.tensor_scalar_mul(out=X[0:C, 64:128], in0=vtok, scalar1=sv4[0:C, 0, h:h + 1]))
            hdv(lambda: nc.vector.tensor_scalar_mul(out=kd[0:C, :], in0=ktok, scalar1=sv4[0:C, 7, h:h + 1]))
            yield
            b1, b1d = pbank()
            hpe(lambda: nc.tensor.matmul(b1[0:C, 0:C], lhsT=gB[0:C, 0:C], rhs=mU[0:C, 0:C], start=True, stop=True), writes=[b1d])
            hpe(lambda: nc.tensor.matmul(b1[0:64, 128:128 + C], lhsT=gB[0:C, 0:64], rhs=mB[0:C, 0:C], start=True, stop=True), writes=[b1d])
            hdv(lambda: nc.vector.tensor_scalar(out=N[0:C, 0:C], in0=b1[0:C, 0:C], scalar1=sv4[0:C, 3, h:h + 1], scalar2=0.0, op0=ALU.subtract, op1=ALU.max), reads=[b1d])
            hdv(lambda: nc.vector.tensor_scalar(out=AT[0:C, 0:C], in0=b1[0:C, 0:C], scalar1=sv4[0:C, 3, h:h + 1], scalar2=0.0, op0=ALU.subtract, op1=ALU.min), reads=[b1d])
            hav(lambda: nc.scalar.activation(out=tT[0:64, 0:C], in_=b1[0:64, 128:128 + C], func=AF.Exp), reads=[b1d])
            bq, bqd = pbank()
            hpe(lambda: nc.tensor.matmul(bq[0:64, 0:C], lhsT=ident[:, hp:hp + 64], rhs=qkv[:, hc, c0:c0 + C], start=True, stop=True), reads=[dqa[hc], d_const], writes=[bqd])
            qs = qsb[h]
            hav(lambda: nc.scalar.copy(out=qs[0:64, 0:C], in_=bq[0:64, 0:C]), reads=[bqd])
            yield
            hav(lambda: nc.scalar.activation(out=N[0:C, 0:C], in_=N[0:C, 0:C], func=AF.Exp, scale=-1.0))
            hav(lambda: nc.scalar.activation(out=AT[0:C, 0:C], in_=AT[0:C, 0:C], func=AF.Exp))
            yield
            hdv(lambda: nc.vector.tensor_tensor(out=N[0:C, 0:C], in0=N[0:C, 0:C], in1=mL[0:C, 0:C], op=ALU.mult))
            hdv(lambda: nc.vector.tensor_tensor(out=AT[0:C, 0:C], in0=AT[0:C, 0:C], in1=mU[0:C, 0:C], op=ALU.mult))
            b2, b2d = pbank()
            op(PE, lambda: nc.tensor.matmul(b2[0:C, 0:C], lhsT=kT, rhs=kT, start=True, stop=True), reads=[dqa[2 + hc]], writes=[b2d])
            op(PE, lambda: nc.tensor.matmul(b2[0:C, 128:128 + C], lhsT=kT, rhs=qT, start=True, stop=True), reads=[dqa[2 + hc], dqa[hc]], writes=[b2d])
            hdv(lambda: nc.vector.scalar_tensor_tensor(out=N[0:C, 0:C], in0=b2[0:C, 0:C], scalar=sv4[0:C, 1, h:h + 1], in1=N[0:C, 0:C], op0=ALU.mult, op1=ALU.mult), reads=[b2d])
            hdv(lambda: nc.vector.tensor_tensor(out=AT[0:C, 0:C], in0=b2[0:C, 128:128 + C], in1=AT[0:C, 0:C], op=ALU.mult), reads=[b2d])
            yield
            b3, b3d = pbank()
            hpe(lambda: nc.tensor.transpose(b3[0:C, 0:C], N[0:C, 0:C], ident[0:C, 0:C]), reads=[d_const], writes=[b3d])
            hav(lambda: nc.scalar.copy(out=NT[0:C, 0:C], in_=b3[0:C, 0:C]), reads=[b3d])
            yield
            for lev in range(levels):
                b4, b4d = pbank()
                hpe(lambda: nc.tensor.matmul(b4[0:C, 0:128], lhsT=NT[0:C, 0:C], rhs=X[0:C, :], start=True, stop=True), writes=[b4d])
                if lev < levels - 1:
                    hpe(lambda: nc.tensor.matmul(b4[0:C, 128:128 + C], lhsT=NT[0:C, 0:C], rhs=N[0:C, 0:C], start=True, stop=True), writes=[b4d])
                    hpe(lambda: nc.tensor.matmul(b4[0:C, 256:256 + C], lhsT=N[0:C, 0:C], rhs=NT[0:C, 0:C], start=True, stop=True), writes=[b4d])
                hdv(lambda: nc.vector.tensor_tensor(out=X[0:C, :], in0=X[0:C, :], in1=b4[0:C, 0:128], op=ALU.add), reads=[b4d])
                if lev < levels - 1:
                    hav(lambda: nc.scalar.copy(out=N[0:C, 0:C], in_=b4[0:C, 128:128 + C]), reads=[b4d])
                    hav(lambda: nc.scalar.copy(out=NT[0:C, 0:C], in_=b4[0:C, 256:256 + C]), reads=[b4d])
                yield
            b5, b5d = pbank()
            hpe(lambda: nc.tensor.transpose(b5[:, 0:C], X[0:C, :], ident[0:C, 0:C]), reads=[d_const], writes=[b5d])
            hav(lambda: nc.scalar.copy(out=wT[0:64, 0:C], in_=b5[0:64, 0:C]), reads=[b5d])
            if kind == "s":
                Sv = Ssm
                sdp = dS
                for b in range(16):
                    load(Ssm[0:64, b, :], I["state_delta"][l, b, h], writes=[dS])
            else:
                Sv = dnS[l][:, h, :, :]
                sdp = st_dep
            yield
            b6, b6d = pbank()
            for b in range(nbb):
                hpe(lambda b=b: nc.tensor.matmul(b6[0:64, b * Tq:(b + 1) * Tq], lhsT=Sv[0:64, b, :], rhs=wT[0:64, b * Tq:(b + 1) * Tq], start=True, stop=True), reads=[sdp], writes=[b6d], inc=(b == nbb - 1))
            for b in range(nbb):
                hpe(lambda b=b: nc.tensor.matmul(b6[0:64, 128 + b * Tq:128 + (b + 1) * Tq], lhsT=Sv[0:64, b, :], rhs=qs[0:64, b * Tq:(b + 1) * Tq], start=True, stop=True),
                    reads=[sdp, dqa[hc]], writes=[b6d], inc=(b == nbb - 1))
            hav(lambda: nc.scalar.copy(out=wT[0:64, 0:C], in_=b6[0:64, 0:C]), reads=[b6d])
            hav(lambda: nc.scalar.copy(out=gB[0:64, 0:C], in_=b6[0:64, 128:128 + C]), reads=[b6d])
            yield
            b7, b7d = pbank()
            hpe(lambda: nc.tensor.transpose(b7[0:C, 0:64], wT[0:64, 0:C], ident[0:64, 0:64]), reads=[d_const], writes=[b7d])
            hpe(lambda: nc.tensor.transpose(b7[0:C, 64:128], gB[0:64, 0:C], ident[0:64, 0:64]), reads=[d_const], writes=[b7d])
            hdv(lambda: nc.vector.tensor_tensor(out=vn[0:C, :], in0=X[0:C, 64:128], in1=b7[0:C, 0:64], op=ALU.subtract), reads=[b7d])
            hdv(lambda: nc.vector.tensor_scalar_mul(out=o1[0:C, :], in0=b7[0:C, 64:128], scalar1=sv4[0:C, 5, h:h + 1]), reads=[b7d])
            yield
            b8, b8d = pbank()
            hpe(lambda: nc.tensor.matmul(b8[0:C, 0:64], lhsT=AT[0:C, 0:C], rhs=vn[0:C, :], start=True, stop=True), writes=[b8d])
            hdv(lambda: nc.vector.tensor_tensor(out=oo[0:C, :], in0=o1[0:C, :], in1=b8[0:C, 0:64], op=ALU.add), reads=[b8d])
            if nbb == 1:
                b9, b9d = pbank()
                hpe(lambda: nc.tensor.matmul(b9[0:64, 0:64], lhsT=kd[0:C, :], rhs=vn[0:C, :], start=True, stop=True), writes=[b9d])
                op(DVE, lambda: nc.vector.scalar_tensor_tensor(out=Sv[0:64, 0, :], in0=Sv[0:64, 0, :], scalar=tT[0:64, 0:1], in1=b9[0:64, 0:64], op0=ALU.mult, op1=ALU.add),
                   reads=[b9d, hd, sdp], writes=[sdp])
            else:
                for half in range(2):
                    b9, b9d = pbank()
                    for bb in range(8):
                        b = half * 8 + bb
                        hdv(lambda b=b: nc.vector.tensor_scalar_mul(out=kdm[0:C, :], in0=kd[0:C, :], scalar1=Ecol[0:C, b:b + 1]))
                        hpe(lambda bb=bb, b9=b9: nc.tensor.matmul(b9[0:64, bb * 64:(bb + 1) * 64], lhsT=kdm[0:C, :], rhs=vn[0:C, :], start=True, stop=True), writes=[b9d])
                        op(DVE, lambda b=b, bb=bb, b9=b9: nc.vector.scalar_tensor_tensor(out=Sv[0:64, b, :], in0=Sv[0:64, b, :], scalar=tT[0:64, b * 8:b * 8 + 1], in1=b9[0:64, bb * 64:(bb + 1) * 64],
                                                                                        op0=ALU.mult, op1=ALU.add), reads=[b9d, hd, sdp], writes=[sdp])
                for b in range(16):
                    store(O["s_delta"][l, b, h], Ssm[0:64, b, :], reads=[dS])
            yield
            hav(lambda: nc.scalar.activation(out=vn[0:C, :], in_=oo[0:C, :], func=AF.Square))
            yield
            hdv(lambda: nc.vector.reduce_sum(out=kd[0:C, 0:1], in_=vn[0:C, :], axis=mybir.AxisListType.X))
            yield
            hav(lambda: nc.scalar.activation(out=kd[0:C, 0:1], in_=kd[0:C, 0:1], func=AF.Ln, scale=1.0 / 64, bias=eps_t[0:C, 0:1]), reads=[d_const])
            hav(lambda: nc.scalar.activation(out=kd[0:C, 0:1], in_=kd[0:C, 0:1], func=AF.Exp, scale=-0.5))
            yield
            hdv(lambda: nc.vector.scalar_tensor_tensor(out=oo[0:C, :], in0=oo[0:C, :], scalar=kd[0:C, 0:1], in1=ngB[0:C, :], op0=ALU.mult, op1=ALU.mult))
            op(DVE, lambda: nc.vector.tensor_tensor(out=oat[0:C, h * 64:(h + 1) * 64], in0=oo[0:C, :], in1=zsil[0:C, h * 64:(h + 1) * 64], op=ALU.mult), reads=[hd, dC], writes=[dC])

        def epilogue(c0, C, par):
            oat, dC = oat2[par], dC2[par]
            bo, bod = pbank()
            for c in range(2):
                op(PE, lambda c=c: nc.tensor.transpose(bo[:, c * 128:c * 128 + C], oat[0:C, c * 128:(c + 1) * 128], ident[0:C, 0:C]), reads=[dC, d_const], writes=[bod], inc=(c == 1))
            for c in range(2):
                op(ACT, lambda c=c: nc.scalar.copy(out=brT[:, c, c0:c0 + C], in_=bo[:, c * 128:c * 128 + C]), reads=[bod], writes=[br_dep[c][ti] for ti in range(len(tiles))])

        def drain(gens):
            gens = list(gens)
            while gens:
                for g_ in list(gens):
                    try:
                        next(g_)
                    except StopIteration:
                        gens.remove(g_)

        chunks = []
        for (kind, col0, nb, T) in pieces:
            if kind == "p":
                Cs = ([16] if name == "A" else []) + [CHUNK] * (1024 // CHUNK)
                c0 = col0
                for C in Cs:
                    chunks.append(("p", c0, C, 1, C))
                    c0 += C
            else:
                chunks.append(("s", col0, 128, nb, T))
        if name == "A":
            dv(lambda: nc.vector.memset(dnS[l][:], 0.0), writes=[st_dep])
        drain([prologue(chunks[0][0], chunks[0][1], chunks[0][2], 0)])
        for ci, (kind, c0, C, nbb, Tq) in enumerate(chunks):
            par = ci % 2
            nxt = [prologue(chunks[ci + 1][0], chunks[ci + 1][1], chunks[ci + 1][2], 1 - par)] if ci + 1 < len(chunks) else []
            hs = [head(kind, c0, C, par, h, nbb, Tq) for h in range(4)]
            if kind == "p":
                drain(hs + nxt)
            else:
                for hg in hs:
                    drain([hg])
                drain(nxt)
            epilogue(c0, C, par)
            if kind == "p" and name == "B" and (ci + 1 == len(chunks) or chunks[ci + 1][0] != "p"):
                for h in range(4):
                    store(O["p_delta"][l, h], dnS[l][:, h, 0, :], reads=[st_dep])

    def merge(l, name, TS, tiles):
        A = Arena()
        mg = A.take(KC * TSM // 2).bitcast(BF16).rearrange("p (c t) -> p c t", c=KC)
        dmg = [[Dep() for _ in tiles] for _ in range(KC)]
        wb_v2 = I["w_branch"][l].rearrange("n (k p) d -> p n k d", p=128)
        wo_v = I["w_out"][l].rearrange("(k p) d -> p k d", p=128)
        accs = [[A.take(512) for _ in tiles] for _ in range(2)]
        dac = [[Dep() for _ in tiles] for _ in range(2)]
        sgs = [A.take(512) for _ in range(3)]
        dsg = [Dep() for _ in range(3)]
        si = 0
        for mp in range(KC // 2):
            bslot, bdep, bsem = wslot()
            bv = bslot[:, 0:2048].rearrange("p (n k d) -> p n k d", n=4, k=2)
            for n in range(4):
                wload(bv[:, n], wb_v2[:, n, :, mp * 256:(mp + 1) * 256], bdep, bsem)
            for n in range(4):
                slot, sdep, ssem = wslot()
                sv = slot[:, 0:KC * 256].rearrange("p (k d) -> p k d", k=KC)
                c0 = OFF["zg"] + n * D + mp * 256
                wload(sv, w_in_v[l][:, :, c0:c0 + 256], sdep, ssem)
                for mm in range(2):
                    m = 2 * mp + mm
                    for ti, (s, w) in enumerate(tiles):
                        acc, da = accs[mm][ti], dac[mm][ti]
                        bz, bzd = pbank()
                        bp, bpd = pbank()
                        for k in range(KC):
                            op(PE, lambda k=k: nc.tensor.matmul(bz[:, 0:w], lhsT=sv[:, k, mm * 128:(mm + 1) * 128], rhs=uT[:, k, s:s + w], start=(k == 0), stop=(k == KC - 1)),
                               reads=[sdep, u_dep[k][ti]], writes=[bzd], inc=(k == KC - 1))
                        for k in range(2):
                            op(PE, lambda k=k: nc.tensor.matmul(bp[:, 0:w], lhsT=bv[:, n, k, mm * 128:(mm + 1) * 128], rhs=brT[:, 2 * n + k, s:s + w], start=(k == 0), stop=(k == 1)),
                               reads=[bdep, br_dep[2 * n + k][ti]], writes=[bpd], inc=(k == 1))
                        sgt, sgd = sgs[si % 3], dsg[si % 3]
                        si += 1
                        op(ACT, lambda: nc.scalar.activation(out=sgt[:, 0:w], in_=bz[:, 0:w], func=AF.Sigmoid), reads=[bzd], writes=[sgd])
                        if n == 0:
                            op(DVE, lambda: nc.vector.tensor_tensor(out=acc[:, 0:w], in0=sgt[:, 0:w], in1=bp[:, 0:w], op=ALU.mult), reads=[sgd, bpd], writes=[da])
                        else:
                            op(DVE, lambda: nc.vector.tensor_tensor(out=sgt[:, 0:w], in0=sgt[:, 0:w], in1=bp[:, 0:w], op=ALU.mult), reads=[sgd, bpd], writes=[sgd])
                            if n < 3:
                                op(DVE, lambda: nc.vector.tensor_tensor(out=acc[:, 0:w], in0=acc[:, 0:w], in1=sgt[:, 0:w], op=ALU.add), reads=[sgd, da], writes=[da])
                            else:
                                op(DVE, lambda: nc.vector.tensor_tensor(out=mg[:, m, s:s + w], in0=acc[:, 0:w], in1=sgt[:, 0:w], op=ALU.add), reads=[sgd, da], writes=[dmg[m][ti]])
        for m in range(KC):
            slot, sdep, ssem = wslot()
            sv = slot[:, 0:KC * 128].rearrange("p (k n) -> p k n", k=KC)
            wload(sv, wo_v[:, :, m * 128:(m + 1) * 128], sdep, ssem)
            for ti, (s, w) in enumerate(tiles):
                bank, bd = pbank()
                for k in range(KC):
                    op(PE, lambda k=k: nc.tensor.matmul(bank[:, 0:w], lhsT=sv[:, k, :], rhs=mg[:, k, s:s + w], start=(k == 0), stop=(k == KC - 1)),
                       reads=[sdep, dmg[k][ti]], writes=[bd], inc=(k == KC - 1))
                op(DVE, lambda: nc.vector.tensor_tensor(out=xT[:, m, s:s + w], in0=xT[:, m, s:s + w], in1=bank[:, 0:w], op=ALU.add),
                   reads=[bd, x_dep[m][ti]], writes=[x_dep[m][ti]])

    def zero_branch(i, tiles):
        for c in range(2):
            for ti, (s, w) in enumerate(tiles):
                op(DVE, lambda: nc.vector.memset(brT[:, 2 * i + c, s:s + w], 0.0), writes=[br_dep[2 * i + c][ti]])

    def mixer(g, l, name, TS, tiles):
        rmsnorm_to(tiles, GIDX["mix"] + l, lambda c, s, w: uT[:, c, s:s + w], lambda c, ti: [u_dep[c][ti]])
        barrier()
        skip = debug.get("skip", "")
        for i, (ch, fn) in enumerate((("a", branch_dn), ("b", branch_s5), ("c", branch_lru), ("d", branch_conv))):
            if fn is None or ch in skip:
                zero_branch(i, tiles)
            else:
                fn(l, name, TS, tiles)
                barrier()
            if f"{name}{l}_o{ch}" in debug:
                dump(f"{name}{l}_o{ch}", brT[:, 2 * i:2 * i + 2, 0:TS], [128, 2, TS], [br_dep[2 * i + c][ti] for c in range(2) for ti in range(len(tiles))])
        merge(l, name, TS, tiles)
        if f"{name}{l}_x2" in debug:
            dump(f"{name}{l}_x2", xT[:, :, 0:TS], [128, KC, TS], [x_dep[c][ti] for c in range(KC) for ti in range(len(tiles))])

    def load_tokens(src_rows, R, col0):
        load_T(lambda c: xT[:, c, col0:col0 + R], src_rows, R, D, [x_dep[c][ti] for c in range(KC) for ti in range(MAXT)])

    def run_st(name, TS, blocks, yblocks):
        tiles = split_tiles(TS)
        for (src, R, col0) in blocks:
            load_tokens(src, R, col0)
        if f"{name}0_x0" in debug:
            dump(f"{name}0_x0", xT[:, :, 0:TS], [128, KC, TS], [x_dep[c][ti] for c in range(KC) for ti in range(len(tiles))])
        for l in range(debug.get("nl", NL)):
            ffn(l, "ffn1", tiles)
            if f"{name}{l}_x1" in debug:
                dump(f"{name}{l}_x1", xT[:, :, 0:TS], [128, KC, TS], [x_dep[c][ti] for c in range(KC) for ti in range(len(tiles))])
            barrier()
            mixer(g, l, name, TS, tiles)
            barrier()
            ffn(l, "ffn2", tiles)
            barrier()
        if debug.get("nofinal"):
            return
        yT = arena[:, 0:KC * TSM].rearrange("p (c t) -> p c t", c=KC)
        y_dep = [[Dep() for _ in range(MAXT)] for _ in range(KC)]
        rmsnorm_to(tiles, 6, lambda c, s, w: yT[:, c, s:s + w], lambda c, ti: [y_dep[c][ti]])
        ally = [y_dep[c][ti] for c in range(KC) for ti in range(len(tiles))]
        for (dst, R, col0) in yblocks:
            store_T(dst, lambda c: yT[:, c, col0:col0 + R], R, D, ally)
        barrier()

    xp = I["x_prompt"]
    blocksA = [(I["meta_tokens"], 16, 0)] + [(xp[i * 128:(i + 1) * 128, :], 128, 16 + i * 128) for i in range(8)]
    yA = [(O["y_prompt"][i * 128:(i + 1) * 128, :], 128, 16 + i * 128) for i in range(8)]
    blocksB = [(xp[1024 + i * 128:1024 + (i + 1) * 128, :], 128, i * 128) for i in range(8)] + [(I["x_sample"], 128, 1024)]
    yB = [(O["y_prompt"][1024 + i * 128:1024 + (i + 1) * 128, :], 128, i * 128) for i in range(8)] + [(O["y_sample"], 128, 1024)]
    sts = debug.get("sts", "AB")
    if "A" in sts:
        run_st("A", TA, blocksA, yA)
    if "B" in sts:
        run_st("B", TB, blocksB, yB)

    for t in st_sems:
        if t.cnt:
            nc.sync.wait_ge(t.sem, t.cnt)


_CACHE = {}


def make_in_maps(inputs, cores=range(8)):
    maps = []
    for c in cores:
        m = {}
        m["x_prompt"] = np.ascontiguousarray(inputs["x_prompt"][c])
        m["x_sample"] = np.ascontiguousarray(inputs["x_sample"][16 * c:16 * c + 16]).reshape(128, D)
        for k in SNAMES:
            m[k] = np.ascontiguousarray(inputs[k][:, 16 * c:16 * c + 16])
        for k in WNAMES:
            m[k] = np.ascontiguousarray(inputs[k])
        maps.append(m)
    return maps


def kernel(**inputs):
    inputs = {k: np.asarray(v, dtype=np.float32) for k, v in inputs.items()}
    if "nc" not in _CACHE:
        _CACHE["nc"] = build()
    nc, g = _CACHE["nc"]
    maps = make_in_maps(inputs)
    for m in maps:
        for k, shp in g.shapes_in.items():
            m[k] = m[k].reshape(shp)
    res = run_bass_kernel_spmd(nc, maps, core_ids=list(range(8)))
    outs = []
    for nm in ONAMES:
        per = [r["o_" + nm] for r in res.results]
        if nm == "y_prompt":
            outs.append(np.stack(per, 0))
        elif nm == "y_sample":
            outs.append(np.concatenate([p.reshape(16, 8, D) for p in per], 0))
        elif nm.startswith("p_"):
            full = np.stack(per, 1)
            outs.append(full)
        else:
            full = np.concatenate(per, 1)
            outs.append(full)
    ref_shapes = dict(p_s5_re=(NL, 8, 16, 64), p_s5_im=(NL, 8, 16, 64), s_s5_re=(NL, 128, 16, 64), s_s5_im=(NL, 128, 16, 64))
    outs = [o.reshape(ref_shapes[nm]) if nm in ref_shapes else o for nm, o in zip(ONAMES, outs)]
    return tuple(np.ascontiguousarray(o, dtype=np.float32) for o in outs)
```

```python
import numpy as np
from contextlib import ExitStack
import concourse.bass as bass
import concourse.mybir as mybir
from concourse.bass_utils import run_bass_kernel_spmd

F32 = mybir.dt.float32
BF16 = mybir.dt.bfloat16
I32 = mybir.dt.int32
AF = mybir.ActivationFunctionType
ALU = mybir.AluOpType

D = 1024
KC = 8
FF = 2816
FJ = 22
NL = 2
EPS = 1e-6
TA = 1040
TB = 1152
TSM = 1152
N_IN = 6408
OFF = dict(dq=0, dk=256, dv=512, dz=768, db=1024, da=1028, su=1032, lx=1288, lg=1544, cval=1800, cgate=2056, zg=2312)

WNAMES = ['meta_tokens', 'ffn1_norm', 'ffn1_w_gu', 'ffn1_w_down', 'mix_norm', 'w_in', 'dn_conv_w', 'dn_a_log', 'dn_dt_bias',
          'dn_norm', 's5_lam_re', 's5_lam_im', 's5_log_step', 's5_b_re', 's5_b_im', 's5_c_re', 's5_c_im', 's5_d', 's5_w_glu',
          's5_b_glu', 'lru_conv_w', 'lru_conv_b', 'lru_w_a', 'lru_b_a', 'lru_w_x', 'lru_b_x', 'lru_lam', 'cv_conv_w',
          'cv_conv_b', 'cv_ln_g', 'cv_ln_b', 'w_branch', 'w_out', 'ffn2_norm', 'ffn2_w_gu', 'ffn2_w_down', 'final_norm']
SNAMES = ['state_delta', 'state_delta_conv', 'state_s5_re', 'state_s5_im', 'state_lru', 'state_lru_conv', 'state_conv']
ONAMES = ['y_prompt', 'y_sample', 'p_delta', 'p_delta_conv', 'p_s5_re', 'p_s5_im', 'p_lru', 'p_lru_conv', 'p_conv',
          's_delta', 's_delta_conv', 's_s5_re', 's_s5_im', 's_lru', 's_lru_conv', 's_conv']


class Trk:
    def __init__(self, name, obj, sem):
        self.name, self.obj, self.sem = name, obj, sem
        self.cnt = 0
        self.seen = {}


class Dep:
    __slots__ = ("w", "r")

    def __init__(self):
        self.w = None
        self.r = {}


def _wait(eng, reads, writes):
    need = {}

    def add(t, v):
        if t is eng and (eng.name == "pe" or v > eng.cnt):
            return
        if need.get(t, 0) < v:
            need[t] = v
    for d in reads:
        if d.w is not None:
            add(*d.w)
    for d in writes:
        if d.w is not None:
            add(*d.w)
        for t, v in d.r.items():
            add(t, v)
    for t, v in need.items():
        if eng.seen.get(t, 0) < v:
            eng.obj.wait_ge(t.sem, v)
            eng.seen[t] = v


def op(eng, fn, reads=(), writes=(), inc=True):
    _wait(eng, reads, writes)
    ins = fn()
    if inc:
        ins.then_inc(eng.sem, 1)
        eng.cnt += 1
        val = eng.cnt
    else:
        val = eng.cnt + 1
    for d in reads:
        if d.r.get(eng, 0) < val:
            d.r[eng] = val
    for d in writes:
        d.w = (eng, val)
        d.r = {}
    return ins


def dma(eng, dsem, out, in_, reads=(), writes=(), **kw):
    _wait(eng, reads, writes)
    ins = eng.obj.dma_start(out=out, in_=in_, **kw)
    ins.then_inc(dsem.sem, 16)
    dsem.cnt += 16
    val = dsem.cnt
    for d in reads:
        if d.r.get(dsem, 0) < val:
            d.r[dsem] = val
    for d in writes:
        d.w = (dsem, val)
        d.r = {}
    return ins


def split_tiles(n, mx=512):
    k = -(-n // mx)
    base = -(-n // k)
    out = []
    s = 0
    while s < n:
        w = min(base, n - s)
        out.append((s, w))
        s += w
    return out


class K:
    pass


def build(debug=None):
    debug = debug or {}
    nc = bass.Bass("TRN2", target_bir_lowering=False)
    g = K()
    g.nc = nc
    g.dbg_outs = {}
    shapes_in = dict(
        x_prompt=[2048, D], x_sample=[128, D],
        state_delta=[NL, 16, 4, 64, 64], state_delta_conv=[NL, 16, 3, 768], state_s5_re=[NL, 16, 1024],
        state_s5_im=[NL, 16, 1024], state_lru=[NL, 16, 256], state_lru_conv=[NL, 16, 3, 256], state_conv=[NL, 16, 30, 256],
        meta_tokens=[16, D], ffn1_norm=[NL, D], ffn1_w_gu=[NL, D, 2 * FF], ffn1_w_down=[NL, FF, D], mix_norm=[NL, D],
        w_in=[NL, D, N_IN], dn_conv_w=[NL, 4, 768], dn_a_log=[NL, 4], dn_dt_bias=[NL, 4], dn_norm=[NL, 64],
        s5_lam_re=[NL, 1024], s5_lam_im=[NL, 1024], s5_log_step=[NL, 16], s5_b_re=[NL, 1024, 16], s5_b_im=[NL, 1024, 16],
        s5_c_re=[NL, 256, 64], s5_c_im=[NL, 256, 64], s5_d=[NL, 256], s5_w_glu=[NL, 256, 512], s5_b_glu=[NL, 512],
        lru_conv_w=[NL, 4, 256], lru_conv_b=[NL, 256], lru_w_a=[NL, 256, 64], lru_b_a=[NL, 256], lru_w_x=[NL, 256, 64],
        lru_b_x=[NL, 256], lru_lam=[NL, 256], cv_conv_w=[NL, 31, 256], cv_conv_b=[NL, 256], cv_ln_g=[NL, 256],
        cv_ln_b=[NL, 256], w_branch=[NL, 4, 256, D], w_out=[NL, D, D], ffn2_norm=[NL, D], ffn2_w_gu=[NL, D, 2 * FF],
        ffn2_w_down=[NL, FF, D], final_norm=[1, D])
    shapes_out = dict(
        y_prompt=[2048, D], y_sample=[128, D], p_delta=[NL, 4, 64, 64], p_delta_conv=[NL, 3, 768], p_s5_re=[NL, 1024],
        p_s5_im=[NL, 1024], p_lru=[NL, 256], p_lru_conv=[NL, 3, 256], p_conv=[NL, 30, 256],
        s_delta=[NL, 16, 4, 64, 64], s_delta_conv=[NL, 16, 3, 768], s_s5_re=[NL, 16, 1024], s_s5_im=[NL, 16, 1024],
        s_lru=[NL, 16, 256], s_lru_conv=[NL, 16, 3, 256], s_conv=[NL, 16, 30, 256])
    g.I = {k: nc.dram_tensor(k, v, F32, kind="ExternalInput").ap() for k, v in shapes_in.items()}
    g.O = {k: nc.dram_tensor("o_" + k, v, F32, kind="ExternalOutput").ap() for k, v in shapes_out.items()}
    g.shapes_in = shapes_in
    g.shapes_out = shapes_out

    with ExitStack() as es:
        g.es = es
        _emit(g, debug)
    return nc, g


def _emit(g, debug):
    nc, es = g.nc, g.es
    I, O = g.I, g.O
    cnt = [0]

    def sb(shape, dt=F32, name=None):
        cnt[0] += 1
        return es.enter_context(nc.sbuf_tensor(name or f"t{cnt[0]}", shape, dt))

    def sem(name):
        return es.enter_context(nc.semaphore(name))

    PE = Trk("pe", nc.tensor, sem("s_pe"))
    ACT = Trk("act", nc.scalar, sem("s_act"))
    DVE = Trk("dve", nc.vector, sem("s_dve"))
    POOL = Trk("pool", nc.gpsimd, sem("s_pool"))
    SP = Trk("sp", nc.sync, sem("s_sp"))
    g.PE, g.ACT, g.DVE, g.POOL, g.SP = PE, ACT, DVE, POOL, SP
    engines = [PE, ACT, DVE, POOL, SP]
    dsems = []

    def dsem(name):
        t = Trk(name, None, sem(name))
        dsems.append(t)
        return t

    ld_sems = [dsem(f"ld{i}") for i in range(8)]
    st_sems = [dsem(f"st{i}") for i in range(4)]
    rr = dict(ld=0, st=0)

    def ldsem():
        rr['ld'] += 1
        return ld_sems[rr['ld'] % len(ld_sems)]

    def stsem():
        rr['st'] += 1
        return st_sems[rr['st'] % len(st_sems)]

    def load(out, in_, writes, reads=(), **kw):
        return dma(SP, ldsem(), out, in_, reads=reads, writes=writes, **kw)

    def store(out, in_, reads, **kw):
        return dma(SP, stsem(), out, in_, reads=reads, **kw)

    def barrier():
        for e in engines:
            for t in engines + dsems:
                if t is e:
                    continue
                if t.cnt > 0 and e.seen.get(t, 0) < t.cnt:
                    e.obj.wait_ge(t.sem, t.cnt)
                    e.seen[t] = t.cnt

    banks = [es.enter_context(nc.psum_tensor(f"bank{i}", [128, 512], F32)) for i in range(8)]
    bank_dep = [Dep() for _ in range(8)]
    bk = [0]

    def pbank():
        i = bk[0] % 8
        bk[0] += 1
        return banks[i], bank_dep[i]

    def dump(name, ap, shape, dep_list):
        if name not in debug:
            return
        t = nc.dram_tensor("dbg_" + name, shape, ap.dtype, kind="ExternalOutput").ap()
        g.dbg_outs[name] = t
        store(t, ap, reads=dep_list)

    it_i = sb([128, 128], I32)
    itf = sb([128, 128])
    ident = sb([128, 128])
    ones_bf = sb([128, 128], BF16)
    d_const = Dep()
    op(POOL, lambda: nc.gpsimd.iota(it_i[:], pattern=[[1, 128]], base=0, channel_multiplier=-1), writes=[d_const])
    op(DVE, lambda: nc.vector.tensor_copy(out=itf[:], in_=it_i[:]), reads=[d_const], writes=[d_const])
    op(DVE, lambda: nc.vector.tensor_single_scalar(out=ident[:], in_=itf[:], scalar=0.0, op=ALU.is_equal), reads=[d_const], writes=[d_const])
    op(DVE, lambda: nc.vector.memset(ones_bf[:], 1.0), writes=[d_const])
    g.ident, g.itf, g.d_const = ident, itf, d_const
    ones_f = sb([128, 128], F32)
    op(DVE, lambda: nc.vector.memset(ones_f[:], 1.0), writes=[d_const])

    xT = sb([128, KC, TSM], F32, "xT")
    uT = sb([128, KC, TSM], BF16, "uT")
    brT = sb([128, 8, TSM], BF16, "brT")
    ARENA_F32 = 19712
    arena = sb([128, ARENA_F32], F32, "arena")
    MAXT = 3
    x_dep = [[Dep() for _ in range(MAXT)] for _ in range(KC)]
    u_dep = [[Dep() for _ in range(MAXT)] for _ in range(KC)]
    br_dep = [[Dep() for _ in range(MAXT)] for _ in range(8)]

    gains = sb([128, KC, 8], F32, "gains")
    d_gain = Dep()
    GIDX = dict(ffn1=0, mix=2, ffn2=4)

    stg = [sb([128, D], F32, f"stg{i}") for i in range(2)]
    stg_dep = [Dep() for _ in range(2)]
    sg = [0]

    def staging():
        i = sg[0] % 2
        sg[0] += 1
        return stg[i], stg_dep[i]

    NSLOT = 6
    SLOT_ELEMS = 2048
    wslots = [sb([128, SLOT_ELEMS], BF16, f"wslot{i}") for i in range(NSLOT)]
    wslot_dep = [Dep() for _ in range(NSLOT)]
    wslot_sem = [dsem(f"ws{i}") for i in range(NSLOT)]
    ws = [0]

    def wslot():
        i = ws[0] % NSLOT
        ws[0] += 1
        return wslots[i], wslot_dep[i], wslot_sem[i]

    def wload(slot_ap, src_ap, sdep, ssem):
        return dma(POOL, ssem, slot_ap, src_ap, writes=[sdep])

    NTMP = 4
    tmps = [sb([128, 512], F32, f"tmp{i}") for i in range(NTMP)]
    tmp_dep = [Dep() for _ in range(NTMP)]
    tp = [0]

    def tmp():
        i = tp[0] % NTMP
        tp[0] += 1
        return tmps[i], tmp_dep[i]

    sqs = [sb([128, KC, 512], BF16, f"sq{i}") for i in range(1)] * 2
    sq_dep = [Dep()] * 2
    sqi = [0]

    def load_T(dst_fn, src, R, C, wdeps):
        st, sd = staging()
        if isinstance(src, list):
            r0 = 0
            for (ap, nr) in src:
                load(st[r0:r0 + nr, 0:C], ap, writes=[sd])
                r0 += nr
            assert r0 == R
        else:
            load(st[0:R, 0:C], src, writes=[sd])
        nchunk = -(-C // 128)
        per_bank = max(1, 512 // R)
        c = 0
        while c < nchunk:
            bank, bd = pbank()
            grp = list(range(c, min(nchunk, c + per_bank)))
            for i, cc in enumerate(grp):
                cw = min(128, C - cc * 128)
                op(PE, lambda cc=cc, cw=cw, i=i: nc.tensor.transpose(bank[0:cw, i * R:(i + 1) * R], st[0:R, cc * 128:cc * 128 + cw], ident[0:R, 0:R]),
                   reads=[sd, d_const], writes=[bd], inc=(i == len(grp) - 1))
            for i, cc in enumerate(grp):
                cw = min(128, C - cc * 128)
                op(ACT, lambda cc=cc, cw=cw, i=i: nc.scalar.copy(out=dst_fn(cc), in_=bank[0:cw, i * R:(i + 1) * R]), reads=[bd], writes=wdeps)
            c += per_bank

    def store_T(dst, src_fn, R, C, rdeps):
        st, sd = staging()
        nchunk = -(-C // 128)
        c = 0
        while c < nchunk:
            bank, bd = pbank()
            grp = list(range(c, min(nchunk, c + 4)))
            for i, cc in enumerate(grp):
                cw = min(128, C - cc * 128)
                src = src_fn(cc)
                if len(src.shape) > 2:
                    tt_, ttd = tmp()
                    o_ = tt_[0:cw, 0:R].rearrange("p (a b) -> p a b", a=src.shape[1])
                    op(DVE, lambda o_=o_, src=src: nc.vector.tensor_copy(out=o_, in_=src), reads=list(rdeps), writes=[ttd])
                    op(PE, lambda cw=cw, i=i, tt_=tt_: nc.tensor.transpose(bank[0:R, i * 128:i * 128 + cw], tt_[0:cw, 0:R], ident[0:cw, 0:cw]),
                       reads=[ttd, d_const], writes=[bd], inc=(i == len(grp) - 1))
                    continue
                op(PE, lambda cc=cc, cw=cw, i=i: nc.tensor.transpose(bank[0:R, i * 128:i * 128 + cw], src_fn(cc), ident[0:cw, 0:cw]),
                   reads=list(rdeps) + [d_const], writes=[bd], inc=(i == len(grp) - 1))
            w = min(C - c * 128, 512)
            op(ACT, lambda c=c, w=w: nc.scalar.copy(out=st[0:R, c * 128:c * 128 + w], in_=bank[0:R, 0:w]), reads=[bd], writes=[sd])
            c += 4
        store(dst, st[0:R, 0:C], reads=[sd])

    def rmsnorm_to(tiles, gidx, out_fn, out_deps_fn, final=False):
        n = len(tiles)
        SQ_OFF = 10000
        sqv = [arena[:, SQ_OFF + i * 2048:SQ_OFF + (i + 1) * 2048].bitcast(BF16).rearrange("p (c t) -> p c t", c=KC) for i in range(n)]
        sqd = [Dep() for _ in range(n)]
        bl, rl = [], []
        for ti, (s, w) in enumerate(tiles):
            op(ACT, lambda ti=ti, s=s, w=w: nc.scalar.activation(out=sqv[ti][:, :, 0:w], in_=xT[:, :, s:s + w], func=AF.Square),
               reads=[x_dep[c][ti] for c in range(KC)], writes=[sqd[ti]])
        for ti, (s, w) in enumerate(tiles):
            bank, bd = pbank()
            bl.append((bank, bd))
            for c in range(KC):
                op(PE, lambda c=c, ti=ti, w=w, bank=bank: nc.tensor.matmul(bank[:, 0:w], lhsT=ones_bf[:], rhs=sqv[ti][:, c, 0:w], start=(c == 0), stop=(c == KC - 1)),
                   reads=[sqd[ti], d_const], writes=[bd], inc=(c == KC - 1))
        for ti, (s, w) in enumerate(tiles):
            bank, bd = bl[ti]
            rs, rsd = tmp()
            rl.append((rs, rsd))
            op(ACT, lambda rs=rs, bank=bank, w=w: nc.scalar.activation(out=rs[:, 0:w], in_=bank[:, 0:w], func=AF.Ln, scale=1.0 / D, bias=eps_t[:, 0:1]), reads=[bd, d_const], writes=[rsd])
        for ti, (s, w) in enumerate(tiles):
            rs, rsd = rl[ti]
            op(ACT, lambda rs=rs, w=w: nc.scalar.activation(out=rs[:, 0:w], in_=rs[:, 0:w], func=AF.Exp, scale=-0.5), reads=[rsd], writes=[rsd])
        for ti, (s, w) in enumerate(tiles):
            rs, rsd = rl[ti]
            for c in range(KC):
                op(DVE, lambda c=c, rs=rs, s=s, w=w: nc.vector.scalar_tensor_tensor(out=out_fn(c, s, w), in0=xT[:, c, s:s + w], scalar=gains[:, c, gidx:gidx + 1],
                                                                                    in1=rs[:, 0:w], op0=ALU.mult, op1=ALU.mult),
                   reads=[x_dep[c][ti], rsd, d_gain], writes=out_deps_fn(c, ti))

    load_T(lambda c: gains[:, c, 0:7],
           [(I[nm][l:l + 1, :], 1) for (nm, l) in [("ffn1_norm", 0), ("ffn1_norm", 1), ("mix_norm", 0), ("mix_norm", 1),
                                                    ("ffn2_norm", 0), ("ffn2_norm", 1), ("final_norm", 0)]], 7, D, [d_gain])
    eps_t = sb([128, 1], F32, "eps")
    op(DVE, lambda: nc.vector.memset(eps_t[:], EPS), writes=[d_const])
    g.eps_t = eps_t

    def ffn(l, which, tiles):
        wgu = I[f"{which}_w_gu"][l]
        wdn = I[f"{which}_w_down"][l]
        gidx = GIDX[which] + l
        rmsnorm_to(tiles, gidx, lambda c, s, w: uT[:, c, s:s + w], lambda c, ti: [u_dep[c][ti]])
        hT = arena[:, 0:11 * TSM // 2].bitcast(BF16).rearrange("p (j t) -> p j t", j=11)
        h_dep = [[Dep() for _ in range(MAXT)] for _ in range(11)]
        wgu_v = wgu.rearrange("(k p) n -> p k n", p=128)
        wdn_v = wdn.rearrange("(j p) n -> p j n", p=128)
        for half in range(2):
            for jj in range(11):
                j = half * 11 + jj
                slot, sdep, ssem = wslot()
                sv = slot[:].rearrange("p (a k n) -> p a k n", a=2, k=KC)
                wload(sv[:, 0], wgu_v[:, :, j * 128:(j + 1) * 128], sdep, ssem)
                wload(sv[:, 1], wgu_v[:, :, FF + j * 128:FF + (j + 1) * 128], sdep, ssem)
                for ti, (s, w) in enumerate(tiles):
                    bg, bgd = pbank()
                    bu, bud = pbank()
                    for a, (bank, bd) in enumerate(((bg, bgd), (bu, bud))):
                        for k in range(KC):
                            op(PE, lambda a=a, k=k, bank=bank: nc.tensor.matmul(bank[:, 0:w], lhsT=sv[:, a, k, :], rhs=uT[:, k, s:s + w],
                                                                                   start=(k == 0), stop=(k == KC - 1)),
                               reads=[sdep, u_dep[k][ti]], writes=[bd], inc=(k == KC - 1))
                    t, td = tmp()
                    op(ACT, lambda: nc.scalar.activation(out=t[:, 0:w], in_=bg[:, 0:w], func=AF.Silu), reads=[bgd], writes=[td])
                    op(DVE, lambda: nc.vector.tensor_tensor(out=hT[:, jj, s:s + w], in0=t[:, 0:w], in1=bu[:, 0:w], op=ALU.mult),
                       reads=[td, bud], writes=[h_dep[jj][ti]])
            for m in range(KC):
                slot, sdep, ssem = wslot()
                sv = slot[:, 0:11 * 128].rearrange("p (j n) -> p j n", j=11)
                wload(sv, wdn_v[:, half * 11:(half + 1) * 11, m * 128:(m + 1) * 128], sdep, ssem)
                for ti, (s, w) in enumerate(tiles):
                    bank, bd = pbank()
                    for jj in range(11):
                        op(PE, lambda jj=jj: nc.tensor.matmul(bank[:, 0:w], lhsT=sv[:, jj, :], rhs=hT[:, jj, s:s + w], start=(jj == 0), stop=(jj == 10)),
                           reads=[sdep, h_dep[jj][ti]], writes=[bd], inc=(jj == 10))
                    op(DVE, lambda: nc.vector.scalar_tensor_tensor(out=xT[:, m, s:s + w], in0=bank[:, 0:w], scalar=0.5, in1=xT[:, m, s:s + w],
                                                                   op0=ALU.mult, op1=ALU.add),
                       reads=[bd, x_dep[m][ti]], writes=[x_dep[m][ti]])


    w_in_v = [I["w_in"][l].rearrange("(k p) n -> p k n", p=128) for l in range(NL)]

    class Arena:
        def __init__(self):
            self.off = 0

        def take(self, n):
            a = arena[:, self.off:self.off + n]
            self.off += n
            assert self.off <= ARENA_F32, self.off
            return a

    def proj_in(l, off, ncols, tiles, evac):
        slot, sdep, ssem = wslot()
        sv = slot[:, 0:KC * ncols].rearrange("p (k n) -> p k n", k=KC)
        wload(sv, w_in_v[l][:, :, off:off + ncols], sdep, ssem)
        for ti, (s, w) in enumerate(tiles):
            bank, bd = pbank()
            for k in range(KC):
                op(PE, lambda k=k: nc.tensor.matmul(bank[0:ncols, 0:w], lhsT=sv[:, k, :], rhs=uT[:, k, s:s + w], start=(k == 0), stop=(k == KC - 1)),
                   reads=[sdep, u_dep[k][ti]], writes=[bd], inc=(k == KC - 1))
            evac(bank, bd, ti, s, w)

    def pieces_of(name):
        if name == "A":
            return [("p", 0, 1, TA)]
        return [("p", 0, 1, 1024), ("s", 1024, 16, 8)]

    def scatter(pieces, s, w, fn):
        for pi, (kind, col0, nb, T) in enumerate(pieces):
            lo, hi = max(s, col0), min(s + w, col0 + nb * T)
            if lo >= hi:
                continue
            if kind == "s":
                assert lo == col0 and hi == col0 + nb * T
            fn(pi, pieces[pi], lo - col0, hi - lo, lo - s)

    def xf_view(xf, kind, nb, T, Kt, c, p_off, n):
        if kind == "p":
            return xf[:, c, 0, Kt + p_off:Kt + p_off + n]
        return xf[:, c, :, Kt:Kt + T]

    def src_view(ap2d, kind, nb, T):
        if kind == "p":
            return ap2d
        return ap2d.rearrange("p (b t) -> p b t", b=nb)

    cvtail = [sb([128, 2, 30], F32, f"cvtail{l}") for l in range(NL)]
    lrutail = [sb([128, 2, 3], F32, f"lrutail{l}") for l in range(NL)]
    lruh = [sb([128, 2], F32, f"lruh{l}") for l in range(NL)]
    st_dep = Dep()

    def load_small(dst_fn, srcs, R, C, deps):
        load_T(dst_fn, srcs, R, C, deps)

    def conv_taps(xf_p, acc_p, wv, bv, c, Kw, T, reads, writes):
        sl = lambda j: xf_p[..., j:j + T]
        op(DVE, lambda: nc.vector.tensor_scalar(out=acc_p, in0=sl(0), scalar1=wv[:, c, 0:1], scalar2=bv[:, c:c + 1], op0=ALU.mult, op1=ALU.add),
           reads=reads, writes=writes)
        for j in range(1, Kw):
            op(DVE, lambda j=j: nc.vector.scalar_tensor_tensor(out=acc_p, in0=sl(j), scalar=wv[:, c, j:j + 1], in1=acc_p, op0=ALU.mult, op1=ALU.add),
               reads=list(reads) + list(writes), writes=writes)

    def branch_conv(l, name, TS, tiles):
        pieces = pieces_of(name)
        A = Arena()
        dW = Dep()
        wv = A.take(2 * 32).rearrange("p (c j) -> p c j", c=2)
        prm = A.take(8).rearrange("p (c j) -> p c j", c=2)
        load_T(lambda c: wv[:, c, 0:31], I["cv_conv_w"][l], 31, 256, [dW])
        load_T(lambda c: prm[:, c, 0:3], [(I["cv_conv_b"][l:l + 1, :], 1), (I["cv_ln_g"][l:l + 1, :], 1), (I["cv_ln_b"][l:l + 1, :], 1)], 3, 256, [dW])
        xfs, accs, dxf = [], [], []
        for (kind, col0, nb, T) in pieces:
            xfs.append(A.take(2 * nb * (30 + T)).rearrange("p (c b t) -> p c b t", c=2, b=nb))
            dxf.append([Dep(), Dep()])
        acc = A.take(2 * TS).rearrange("p (c t) -> p c t", c=2)
        dacc = [[Dep() for _ in tiles] for _ in range(2)]
        for pi, (kind, col0, nb, T) in enumerate(pieces):
            for c in range(2):
                if kind == "p" and name == "A":
                    op(DVE, lambda c=c, pi=pi: nc.vector.memset(xfs[pi][:, c, :, 0:30], 0.0), writes=[dxf[pi][c]])
                elif kind == "p":
                    op(DVE, lambda c=c, pi=pi: nc.vector.tensor_copy(out=xfs[pi][:, c, 0, 0:30], in_=cvtail[l][:, c, :]), reads=[st_dep], writes=[dxf[pi][c]])
            if kind == "s":
                for b0 in range(0, 16, 4):
                    load_T(lambda c, b0=b0, pi=pi: xfs[pi][:, c, b0:b0 + 4, 0:30], I["state_conv"][l, b0:b0 + 4].rearrange("b j c -> (b j) c"), 120, 256,
                           [dxf[pi][0], dxf[pi][1]])
        sgt = A.take(TS)
        dsg = [Dep() for _ in tiles]
        for c in range(2):

            def ev_gate(bank, bd, ti, s, w):
                op(ACT, lambda: nc.scalar.activation(out=sgt[:, s:s + w], in_=bank[:, 0:w], func=AF.Sigmoid), reads=[bd], writes=[dsg[ti]])

            def ev_val(bank, bd, ti, s, w, c=c):
                def f(pi, piece, p_off, n, b_off):
                    kind, col0, nb, T = piece
                    op(DVE, lambda: nc.vector.tensor_tensor(out=xf_view(xfs[pi], kind, nb, T, 30, c, p_off, n),
                                                            in0=src_view(bank[:, b_off:b_off + n], kind, nb, T),
                                                            in1=src_view(sgt[:, s + b_off:s + b_off + n], kind, nb, T), op=ALU.mult),
                       reads=[bd, dsg[ti]], writes=[dxf[pi][c]])
                scatter(pieces, s, w, f)
            proj_in(l, OFF["cgate"] + c * 128, 128, tiles, ev_gate)
            proj_in(l, OFF["cval"] + c * 128, 128, tiles, ev_val)
        for pi, (kind, col0, nb, T) in enumerate(pieces):
            for c in range(2):
                xf_p = xfs[pi][:, c, 0, :] if kind == "p" else xfs[pi][:, c, :, :]
                acc_p = acc[:, c, col0:col0 + nb * T] if kind == "p" else acc[:, c, col0:col0 + nb * T].rearrange("p (b t) -> p b t", b=nb)
                conv_taps(xf_p, acc_p, wv, prm[:, :, 0], c, 31, T, [dW, dxf[pi][c]], [dacc[c][ti] for ti in range(len(tiles))])
        nt = len(tiles)
        sqs_ = [A.take(2 * 512).rearrange("p (c t) -> p c t", c=2) for _ in range(nt)]
        dsq = [Dep() for _ in range(nt)]
        means = [A.take(512) for _ in range(nt)]
        rstds = [A.take(512) for _ in range(nt)]
        dmean = [Dep() for _ in range(nt)]
        drstd = [Dep() for _ in range(nt)]
        bks = []
        for ti, (s, w) in enumerate(tiles):
            op(ACT, lambda ti=ti, s=s, w=w: nc.scalar.activation(out=sqs_[ti][:, :, 0:w], in_=acc[:, :, s:s + w], func=AF.Square), reads=[dacc[0][ti], dacc[1][ti]], writes=[dsq[ti]])
        for ti, (s, w) in enumerate(tiles):
            b1, b1d = pbank()
            b2, b2d = pbank()
            bks.append((b1, b1d, b2, b2d))
            for c in range(2):
                op(PE, lambda c=c, b1=b1, s=s, w=w: nc.tensor.matmul(b1[:, 0:w], lhsT=ones_f[:], rhs=acc[:, c, s:s + w], start=(c == 0), stop=(c == 1)),
                   reads=[dacc[c][ti], d_const], writes=[b1d], inc=(c == 1))
            for c in range(2):
                op(PE, lambda c=c, b2=b2, ti=ti, w=w: nc.tensor.matmul(b2[:, 0:w], lhsT=ones_f[:], rhs=sqs_[ti][:, c, 0:w], start=(c == 0), stop=(c == 1)),
                   reads=[dsq[ti], d_const], writes=[b2d], inc=(c == 1))
        for ti, (s, w) in enumerate(tiles):
            b1, b1d, b2, b2d = bks[ti]
            op(ACT, lambda ti=ti, b1=b1, w=w: nc.scalar.mul(out=means[ti][:, 0:w], in_=b1[:, 0:w], mul=1.0 / 256), reads=[b1d], writes=[dmean[ti]])
        for ti, (s, w) in enumerate(tiles):
            op(DVE, lambda ti=ti, w=w: nc.vector.tensor_tensor(out=rstds[ti][:, 0:w], in0=means[ti][:, 0:w], in1=means[ti][:, 0:w], op=ALU.mult), reads=[dmean[ti]], writes=[drstd[ti]])
        for ti, (s, w) in enumerate(tiles):
            b1, b1d, b2, b2d = bks[ti]
            op(DVE, lambda ti=ti, b2=b2, w=w: nc.vector.scalar_tensor_tensor(out=rstds[ti][:, 0:w], in0=b2[:, 0:w], scalar=1.0 / 256, in1=rstds[ti][:, 0:w], op0=ALU.mult, op1=ALU.subtract),
               reads=[b2d, drstd[ti]], writes=[drstd[ti]])
        for ti, (s, w) in enumerate(tiles):
            op(ACT, lambda ti=ti, w=w: nc.scalar.activation(out=rstds[ti][:, 0:w], in_=rstds[ti][:, 0:w], func=AF.Ln, bias=eps_t[:, 0:1]), reads=[drstd[ti], d_const], writes=[drstd[ti]])
        for ti, (s, w) in enumerate(tiles):
            op(ACT, lambda ti=ti, w=w: nc.scalar.activation(out=rstds[ti][:, 0:w], in_=rstds[ti][:, 0:w], func=AF.Exp, scale=-0.5), reads=[drstd[ti]], writes=[drstd[ti]])
        for ti, (s, w) in enumerate(tiles):
            for c in range(2):
                op(DVE, lambda c=c, ti=ti, s=s, w=w: nc.vector.tensor_tensor(out=acc[:, c, s:s + w], in0=acc[:, c, s:s + w], in1=means[ti][:, 0:w], op=ALU.subtract),
                   reads=[dacc[c][ti], dmean[ti]], writes=[dacc[c][ti]])
                op(DVE, lambda c=c, ti=ti, s=s, w=w: nc.vector.tensor_tensor(out=acc[:, c, s:s + w], in0=acc[:, c, s:s + w], in1=rstds[ti][:, 0:w], op=ALU.mult),
                   reads=[dacc[c][ti], drstd[ti]], writes=[dacc[c][ti]])
                op(ACT, lambda c=c, s=s, w=w: nc.scalar.activation(out=brT[:, 6 + c, s:s + w], in_=acc[:, c, s:s + w], func=AF.Silu, scale=prm[:, c, 1:2], bias=prm[:, c, 2:3]),
                   reads=[dacc[c][ti], dW], writes=[br_dep[6 + c][ti]])
        for pi, (kind, col0, nb, T) in enumerate(pieces):
            if kind == "p" and name == "A":
                for c in range(2):
                    op(DVE, lambda c=c, pi=pi: nc.vector.tensor_copy(out=cvtail[l][:, c, :], in_=xfs[pi][:, c, 0, T:T + 30]), reads=[dxf[pi][c]], writes=[st_dep])
            elif kind == "p":
                store_T(O["p_conv"][l], lambda c, pi=pi, T=T: xfs[pi][:, c, 0, T:T + 30], 30, 256, dxf[pi])
            else:
                for b0 in range(0, 16, 4):
                    store_T(O["s_conv"][l, b0:b0 + 4].rearrange("b j c -> (b j) c"), lambda c, pi=pi, b0=b0, T=T: xfs[pi][:, c, b0:b0 + 4, T:T + 30], 120, 256, dxf[pi])

    def branch_lru(l, name, TS, tiles):
        pieces = pieces_of(name)
        A = Arena()
        dW = Dep()
        wv = A.take(8).rearrange("p (c j) -> p c j", c=2)
        prm = A.take(12).rearrange("p (c j) -> p c j", c=2)
        load_T(lambda c: wv[:, c, 0:4], I["lru_conv_w"][l], 4, 256, [dW])
        load_T(lambda c: prm[:, c, 0:4], [(I[k][l:l + 1, :], 1) for k in ("lru_conv_b", "lru_b_a", "lru_b_x", "lru_lam")], 4, 256, [dW])
        op(ACT, lambda: nc.scalar.activation(out=prm[:, :, 4:5], in_=prm[:, :, 3:4], func=AF.Exp, scale=-1.0), reads=[dW], writes=[dW])
        op(ACT, lambda: nc.scalar.activation(out=prm[:, :, 4:5], in_=prm[:, :, 4:5], func=AF.Ln, bias=1.0), reads=[dW], writes=[dW])
        op(ACT, lambda: nc.scalar.mul(out=prm[:, :, 4:5], in_=prm[:, :, 4:5], mul=-8.0), reads=[dW], writes=[dW])
        wg = A.take(2 * 2 * 128).rearrange("p (a c n) -> p a c n", a=2, c=2)
        op(DVE, lambda: nc.vector.memset(wg, 0.0), writes=[dW])
        for a, nm in enumerate(("lru_w_a", "lru_w_x")):
            for c in range(2):
                for i in range(2):
                    load(wg[i * 64:(i + 1) * 64, a, c, i * 64:(i + 1) * 64], I[nm][l, (2 * c + i) * 64:(2 * c + i + 1) * 64, :], writes=[dW])
        xfs, dxf = [], []
        for (kind, col0, nb, T) in pieces:
            xfs.append(A.take(2 * nb * (3 + T)).rearrange("p (c b t) -> p c b t", c=2, b=nb))
            dxf.append([Dep(), Dep()])
        mk = lambda: A.take(2 * TS).rearrange("p (c t) -> p c t", c=2)
        xc, ra, ib, hh, gl = mk(), mk(), mk(), mk(), mk()
        dxc, dra, dib, dhh, dgl = [[Dep(), Dep()] for _ in range(5)]
        h0s = A.take(2 * 16).rearrange("p (c b) -> p c b", c=2)
        dh0 = Dep()
        for pi, (kind, col0, nb, T) in enumerate(pieces):
            for c in range(2):
                if kind == "p" and name == "A":
                    op(DVE, lambda c=c, pi=pi: nc.vector.memset(xfs[pi][:, c, :, 0:3], 0.0), writes=[dxf[pi][c]])
                elif kind == "p":
                    op(DVE, lambda c=c, pi=pi: nc.vector.tensor_copy(out=xfs[pi][:, c, 0, 0:3], in_=lrutail[l][:, c, :]), reads=[st_dep], writes=[dxf[pi][c]])
            if kind == "s":
                load_T(lambda c, pi=pi: xfs[pi][:, c, :, 0:3], I["state_lru_conv"][l].rearrange("b j c -> (b j) c"), 48, 256, [dxf[pi][0], dxf[pi][1]])
                load_T(lambda c: h0s[:, c, :], I["state_lru"][l], 16, 256, [dh0])
        for c in range(2):
            def ev_x(bank, bd, ti, s, w, c=c):
                def f(pi, piece, p_off, n, b_off):
                    kind, col0, nb, T = piece
                    op(ACT, lambda: nc.scalar.copy(out=xf_view(xfs[pi], kind, nb, T, 3, c, p_off, n), in_=src_view(bank[:, b_off:b_off + n], kind, nb, T)),
                       reads=[bd], writes=[dxf[pi][c]])
                scatter(pieces, s, w, f)

            def ev_g(bank, bd, ti, s, w, c=c):
                op(ACT, lambda: nc.scalar.activation(out=gl[:, c, s:s + w], in_=bank[:, 0:w], func=AF.Gelu), reads=[bd], writes=[dgl[c]])
            proj_in(l, OFF["lx"] + c * 128, 128, tiles, ev_x)
            proj_in(l, OFF["lg"] + c * 128, 128, tiles, ev_g)
        for pi, (kind, col0, nb, T) in enumerate(pieces):
            for c in range(2):
                xf_p = xfs[pi][:, c, 0, :] if kind == "p" else xfs[pi][:, c, :, :]
                v = lambda t: (t[:, c, col0:col0 + nb * T] if kind == "p" else t[:, c, col0:col0 + nb * T].rearrange("p (b t) -> p b t", b=nb))
                conv_taps(xf_p, v(xc), wv, prm[:, :, 0], c, 4, T, [dW, dxf[pi][c]], [dxc[c]])
        for c in range(2):
            gbk = []
            for ti, (s, w) in enumerate(tiles):
                for a, (dst, dd) in enumerate(((ra, dra), (ib, dib))):
                    bank, bd = pbank()
                    gbk.append((bank, bd, a, dst, dd, s, w))
                    op(PE, lambda a=a, bank=bank, s=s, w=w: nc.tensor.matmul(bank[:, 0:w], lhsT=wg[:, a, c, :], rhs=xc[:, c, s:s + w], start=True, stop=True), reads=[dW, dxc[c]], writes=[bd])
            for (bank, bd, a, dst, dd, s, w) in gbk:
                op(ACT, lambda a=a, dst=dst, bank=bank, s=s, w=w: nc.scalar.activation(out=dst[:, c, s:s + w], in_=bank[:, 0:w], func=AF.Sigmoid, bias=prm[:, c, 1 + a:2 + a]),
                   reads=[bd, dW], writes=[dd[c]])
            op(ACT, lambda: nc.scalar.activation(out=ra[:, c, 0:TS], in_=ra[:, c, 0:TS], func=AF.Exp, scale=prm[:, c, 4:5]), reads=[dra[c], dW], writes=[dra[c]])
            op(DVE, lambda: nc.vector.tensor_tensor(out=ib[:, c, 0:TS], in0=ib[:, c, 0:TS], in1=xc[:, c, 0:TS], op=ALU.mult), reads=[dib[c], dxc[c]], writes=[dib[c]])
            op(DVE, lambda: nc.vector.tensor_tensor(out=hh[:, c, 0:TS], in0=ra[:, c, 0:TS], in1=ra[:, c, 0:TS], op=ALU.mult), reads=[dra[c]], writes=[dhh[c]])
            op(ACT, lambda: nc.scalar.activation(out=hh[:, c, 0:TS], in_=hh[:, c, 0:TS], func=AF.Sqrt, scale=-1.0, bias=1.0), reads=[dhh[c]], writes=[dhh[c]])
            op(DVE, lambda: nc.vector.tensor_tensor(out=ib[:, c, 0:TS], in0=ib[:, c, 0:TS], in1=hh[:, c, 0:TS], op=ALU.mult), reads=[dib[c], dhh[c]], writes=[dib[c]])
            for pi, (kind, col0, nb, T) in enumerate(pieces):
                a3 = ra[:, c, col0:col0 + nb * T].rearrange("p (b t) -> p b t", b=nb)
                b3 = ib[:, c, col0:col0 + nb * T].rearrange("p (b t) -> p b t", b=nb)
                if not (kind == "p" and name == "A"):
                    h0v = lruh[l][:, c:c + 1] if kind == "p" else h0s[:, c, :]
                    hd = st_dep if kind == "p" else dh0
                    t0, t0d = tmp()
                    op(DVE, lambda: nc.vector.tensor_tensor(out=t0[:, 0:nb], in0=a3[:, :, 0], in1=h0v, op=ALU.mult), reads=[dra[c], hd], writes=[t0d])
                    op(DVE, lambda: nc.vector.tensor_tensor(out=b3[:, :, 0], in0=b3[:, :, 0], in1=t0[:, 0:nb], op=ALU.add), reads=[dib[c], t0d], writes=[dib[c]])
                if nb > 1:
                    op(DVE, lambda: nc.vector.memset(a3[:, :, 0:1], 0.0), reads=[dra[c]], writes=[dra[c]])
                op(DVE, lambda: nc.vector.tensor_tensor_scan(out=hh[:, c, col0:col0 + nb * T], data0=ra[:, c, col0:col0 + nb * T], data1=ib[:, c, col0:col0 + nb * T],
                                                             initial=0.0, op0=ALU.mult, op1=ALU.add), reads=[dra[c], dib[c], dhh[c]], writes=[dhh[c]])
            for ti, (s, w) in enumerate(tiles):
                op(DVE, lambda: nc.vector.tensor_tensor(out=brT[:, 4 + c, s:s + w], in0=hh[:, c, s:s + w], in1=gl[:, c, s:s + w], op=ALU.mult),
                   reads=[dhh[c], dgl[c]], writes=[br_dep[4 + c][ti]])
        for pi, (kind, col0, nb, T) in enumerate(pieces):
            if kind == "p" and name == "A":
                for c in range(2):
                    op(DVE, lambda c=c, pi=pi: nc.vector.tensor_copy(out=lrutail[l][:, c, :], in_=xfs[pi][:, c, 0, T:T + 3]), reads=[dxf[pi][c]], writes=[st_dep])
                    op(DVE, lambda c=c: nc.vector.tensor_copy(out=lruh[l][:, c:c + 1], in_=hh[:, c, col0 + T - 1:col0 + T]), reads=[dhh[c]], writes=[st_dep])
            elif kind == "p":
                store_T(O["p_lru_conv"][l], lambda c, pi=pi, T=T: xfs[pi][:, c, 0, T:T + 3], 3, 256, dxf[pi])
                store_T(O["p_lru"][l:l + 1, :], lambda c, col0=col0, T=T: hh[:, c, col0 + T - 1:col0 + T], 1, 256, dhh)
            else:
                store_T(O["s_lru_conv"][l].rearrange("b j c -> (b j) c"), lambda c, pi=pi, T=T: xfs[pi][:, c, :, T:T + 3], 48, 256, dxf[pi])
                store_T(O["s_lru"][l], lambda c, col0=col0, nb=nb, T=T: hh[:, c, col0:col0 + nb * T].rearrange("p (b t) -> p b t", b=nb)[:, :, T - 1], 16, 256, dhh)


    s5h = [sb([128, 8, 2, 1], F32, f"s5h{l}") for l in range(NL)]

    def branch_s5(l, name, TS, tiles):
        import math
        pieces = pieces_of(name)
        A = Arena()
        dW = Dep()
        LT = 130 if name == "A" else 128
        dv = lambda f, reads=(), writes=(): op(DVE, f, reads=list(reads) + [dW], writes=list(writes) + [dW])
        av = lambda f, reads=(), writes=(): op(ACT, f, reads=list(reads) + [dW], writes=list(writes) + [dW])
        lam = A.take(16).rearrange("p (m r) -> p m r", m=8)
        load_T(lambda m: lam[:, m, 0:2], [(I["s5_lam_re"][l:l + 1, :], 1), (I["s5_lam_im"][l:l + 1, :], 1)], 2, 1024, [dW])
        row = A.take(16)
        load(row[0:1, 0:16], I["s5_log_step"][l:l + 1, :], writes=[dW])
        bank, bd = pbank()
        op(PE, lambda: nc.tensor.matmul(bank[:, 0:16], lhsT=ones_f[0:1, :], rhs=row[0:1, 0:16], start=True, stop=True), reads=[dW, d_const], writes=[bd])
        sm = A.take(8 * 16).rearrange("p (m j) -> p m j", m=8)
        V = lambda j: sm[:, :, j]
        DT, TH, MAG, CC, SS, LBR, LBI, CFR, CFI, DEN, T1, T2, T3, NCFI = range(14)
        LR, LI = lam[:, :, 0], lam[:, :, 1]
        for h in range(2):
            av(lambda h=h: nc.scalar.activation(out=sm[h * 64:(h + 1) * 64, :, DT], in_=bank[h * 64:(h + 1) * 64, 0:16].rearrange("p (m x) -> p m x", x=2)[:, :, h], func=AF.Exp), reads=[bd])
        tt = lambda o, a, b, f: dv(lambda: nc.vector.tensor_tensor(out=o, in0=a, in1=b, op=f))
        tt(V(TH), LI, V(DT), ALU.mult)
        tt(V(T1), LR, V(DT), ALU.mult)
        av(lambda: nc.scalar.activation(out=V(MAG), in_=V(T1), func=AF.Exp))
        hp = A.take(1)
        dv(lambda: nc.vector.memset(hp, math.pi / 2))
        av(lambda: nc.scalar.activation(out=V(SS), in_=V(TH), func=AF.Sin, scale=1.0 / 16))
        av(lambda: nc.scalar.activation(out=V(CC), in_=V(TH), func=AF.Sin, scale=1.0 / 16, bias=hp[:, 0:1]))
        for _ in range(4):
            tt(V(T1), V(CC), V(CC), ALU.mult)
            tt(V(T2), V(SS), V(SS), ALU.mult)
            tt(V(T3), V(CC), V(SS), ALU.mult)
            tt(V(CC), V(T1), V(T2), ALU.subtract)
            tt(V(SS), V(T3), V(T3), ALU.add)
        tt(V(LBR), V(MAG), V(CC), ALU.mult)
        tt(V(LBI), V(MAG), V(SS), ALU.mult)
        tt(V(T1), LR, LR, ALU.mult)
        tt(V(T2), LI, LI, ALU.mult)
        tt(V(DEN), V(T1), V(T2), ALU.add)
        dv(lambda: nc.vector.reciprocal(out=V(DEN), in_=V(DEN)))
        dv(lambda: nc.vector.tensor_scalar_add(out=V(T3), in0=V(LBR), scalar1=-1.0))
        tt(V(T1), V(T3), LR, ALU.mult)
        tt(V(T2), V(LBI), LI, ALU.mult)
        tt(V(T1), V(T1), V(T2), ALU.add)
        tt(V(CFR), V(T1), V(DEN), ALU.mult)
        tt(V(T1), V(LBI), LR, ALU.mult)
        tt(V(T2), V(T3), LI, ALU.mult)
        tt(V(T1), V(T1), V(T2), ALU.subtract)
        tt(V(CFI), V(T1), V(DEN), ALU.mult)
        dv(lambda: nc.vector.tensor_scalar_mul(out=V(NCFI), in0=V(CFI), scalar1=-1.0))
        Bre = A.take(128).rearrange("p (m c) -> p m c", m=8)
        Bim = A.take(128).rearrange("p (m c) -> p m c", m=8)
        Bp = A.take(128).rearrange("p (m c) -> p m c", m=8)
        tb = A.take(16)
        for m in range(8):
            load(Bre[:, m, :], I["s5_b_re"][l, m * 128:(m + 1) * 128, :], writes=[dW])
            load(Bim[:, m, :], I["s5_b_im"][l, m * 128:(m + 1) * 128, :], writes=[dW])
        BT = A.take(8 * 2 * 128).rearrange("p (m r n) -> p m r n", m=8, r=2)
        CT = A.take(8 * 2 * 128).rearrange("p (m r n) -> p m r n", m=8, r=2)
        E = A.take(8 * 128).rearrange("p (m n) -> p m n", m=8)
        for ri in range(2):
            dv(lambda: nc.vector.memset(E, 0.0))
            for m in range(8):
                X1, X2, sc2 = (Bre, Bim, NCFI) if ri == 0 else (Bim, Bre, CFI)
                dv(lambda m=m, X2=X2, sc2=sc2: nc.vector.tensor_scalar_mul(out=tb, in0=X2[:, m, :], scalar1=sm[:, m, sc2:sc2 + 1]))
                dv(lambda m=m, X1=X1: nc.vector.scalar_tensor_tensor(out=Bp[:, m, :], in0=X1[:, m, :], scalar=sm[:, m, CFR:CFR + 1], in1=tb, op0=ALU.mult, op1=ALU.add))
                for h in range(2):
                    o0 = (m % 4) * 32 + h * 16
                    dv(lambda m=m, h=h, o0=o0: nc.vector.tensor_copy(out=E[h * 64:(h + 1) * 64, m, o0:o0 + 16], in_=Bp[h * 64:(h + 1) * 64, m, :]))
            for m in range(8):
                bk, bkd = pbank()
                op(PE, lambda m=m: nc.tensor.transpose(bk[:, 0:128], E[:, m, :], ident[:]), reads=[dW, d_const], writes=[bkd])
                av(lambda m=m, bk=bk: nc.scalar.copy(out=BT[:, m, ri, :], in_=bk[:, 0:128]), reads=[bkd])
        dv(lambda: nc.vector.memset(CT, 0.0))
        for ri, nm in enumerate(("s5_c_re", "s5_c_im")):
            for rb in range(2):
                st, sd = staging()
                load(st[:, 0:64], I[nm][l, rb * 128:(rb + 1) * 128, :], writes=[sd])
                load(st[:, 64:128], I[nm][l, rb * 128:(rb + 1) * 128, :], writes=[sd])
                bk, bkd = pbank()
                op(PE, lambda: nc.tensor.transpose(bk[:, 0:128], st[:, 0:128], ident[:]), reads=[sd, d_const], writes=[bkd])
                for m in range(rb * 4, rb * 4 + 4):
                    for h in range(2):
                        o0 = (m % 4) * 32 + h * 16
                        av(lambda m=m, h=h, o0=o0, bk=bk: nc.scalar.mul(out=CT[h * 64:(h + 1) * 64, m, ri, o0:o0 + 16], in_=bk[h * 64:(h + 1) * 64, o0:o0 + 16],
                                                                        mul=(1.0 if ri == 0 else -1.0)), reads=[bkd])
        cosT = A.take(8 * LT).rearrange("p (m t) -> p m t", m=8)
        sinT = A.take(8 * LT).rearrange("p (m t) -> p m t", m=8)
        rho = A.take(8 * LT).rearrange("p (m t) -> p m t", m=8)
        rhos = A.take(8 * 128).rearrange("p (m t) -> p m t", m=8)
        ta = A.take(8 * 65).rearrange("p (m t) -> p m t", m=8)
        tb2 = A.take(8 * 65).rearrange("p (m t) -> p m t", m=8)
        dv(lambda: nc.vector.tensor_copy(out=cosT[:, :, 0:1], in_=sm[:, :, CC:CC + 1]))
        dv(lambda: nc.vector.tensor_copy(out=sinT[:, :, 0:1], in_=sm[:, :, SS:SS + 1]))
        n = 1
        while n < LT:
            k = min(n, LT - n)
            cn = cosT[:, :, n - 1:n].to_broadcast([128, 8, k])
            sn = sinT[:, :, n - 1:n].to_broadcast([128, 8, k])
            tt(ta[:, :, 0:k], cosT[:, :, 0:k], cn, ALU.mult)
            tt(tb2[:, :, 0:k], sinT[:, :, 0:k], sn, ALU.mult)
            tt(cosT[:, :, n:n + k], ta[:, :, 0:k], tb2[:, :, 0:k], ALU.subtract)
            tt(ta[:, :, 0:k], sinT[:, :, 0:k], cn, ALU.mult)
            tt(tb2[:, :, 0:k], cosT[:, :, 0:k], sn, ALU.mult)
            tt(sinT[:, :, n:n + k], ta[:, :, 0:k], tb2[:, :, 0:k], ALU.add)
            n += k
        dv(lambda: nc.vector.tensor_copy(out=rho, in_=sm[:, :, MAG:MAG + 1].to_broadcast([128, 8, LT])))
        dv(lambda: nc.vector.tensor_copy(out=rhos, in_=sm[:, :, MAG:MAG + 1].to_broadcast([128, 8, 128])))
        dv(lambda: nc.vector.memset(rhos.rearrange("p m (b t) -> p m b t", t=8)[:, :, :, 0:1], 0.0))
        su = A.take(2 * TS).rearrange("p (c t) -> p c t", c=2)
        dsu = Dep()
        for c in range(2):
            proj_in(l, OFF["su"] + c * 128, 128, tiles,
                    lambda bank, bd, ti, s, w, c=c: op(ACT, lambda: nc.scalar.copy(out=su[:, c, s:s + w], in_=bank[:, 0:w]), reads=[bd], writes=[dsu]))
        dsk = A.take(2)
        load_T(lambda c: dsk[:, c:c + 1], I["s5_d"][l:l + 1, :], 1, 256, [dW])
        Hs = A.take(8 * 2 * LT).rearrange("p (m r t) -> p m r t", m=8, r=2)
        dH = Dep()
        xt = [A.take(8 * LT).rearrange("p (m t) -> p m t", m=8) for _ in range(4)]
        dx = Dep()
        hS = A.take(8 * 2 * 16).rearrange("p (m r b) -> p m r b", m=8, r=2)
        dHP = Dep()
        dv(lambda: nc.vector.memset(rho[:, :, 0:1], 0.0))
        for (kind, col0, nb, T) in pieces:
            if kind == "s":
                for ri, nm in enumerate(("state_s5_re", "state_s5_im")):
                    load_T(lambda m, ri=ri: hS[:, m, ri, 0:16], I[nm][l], 16, 1024, [dW])
                hprev = hS
                subs = [(col0, nb, T)]
            else:
                hprev = s5h[l]
                if name == "A":
                    dv(lambda: nc.vector.memset(s5h[l][:], 0.0), writes=[st_dep])
                subs = [(col0 + i * LT, 1, LT) for i in range(T // LT)]
            for (c0, nbb, Tl) in subs:
                nn = nbb * Tl
                assert nn == LT
                gsz = max(1, 512 // nn)
                groups = [list(range(i, min(8, i + gsz))) for i in range(0, 8, gsz)]
                v4 = lambda ap: ap.rearrange("p m (b t) -> p m b t", b=nbb)
                rd = [dx, st_dep, dW, dHP]
                f = lambda o, a, b, g_, extra=(): op(DVE, lambda: nc.vector.tensor_tensor(out=o, in0=a, in1=b, op=g_), reads=rd + list(extra), writes=[dx])
                for grp in groups:
                    g0, gl = grp[0], len(grp)
                    br_, brd = pbank()
                    bi_, bid = pbank()
                    for (bk_, bkd_, ri) in ((br_, brd, 0), (bi_, bid, 1)):
                        for j, m in enumerate(grp):
                            op(PE, lambda m=m, j=j, bk_=bk_, ri=ri: nc.tensor.matmul(bk_[:, j * nn:(j + 1) * nn], lhsT=BT[:, m, ri, :], rhs=su[:, m // 4, c0:c0 + nn], start=True, stop=True),
                               reads=[dW, dsu], writes=[bkd_], inc=(j == gl - 1))
                    pr = br_[:, 0:gl * nn].rearrange("p (m b t) -> p m b t", m=gl, b=nbb)
                    pi_ = bi_[:, 0:gl * nn].rearrange("p (m b t) -> p m b t", m=gl, b=nbb)
                    cv = cosT[:, g0:g0 + gl, 0:Tl].unsqueeze(2).to_broadcast([128, gl, nbb, Tl])
                    sv_ = sinT[:, g0:g0 + gl, 0:Tl].unsqueeze(2).to_broadcast([128, gl, nbb, Tl])
                    Xg = [v4(x[:, g0:g0 + gl, :]) for x in xt]
                    f(Xg[0], pr, cv, ALU.mult, [brd])
                    f(Xg[2], pi_, sv_, ALU.mult, [bid])
                    f(Xg[1], pi_, cv, ALU.mult, [bid])
                    f(Xg[3], pr, sv_, ALU.mult, [brd])
                f(xt[0], xt[0], xt[2], ALU.add)
                f(xt[1], xt[1], xt[3], ALU.subtract)
                X4 = [v4(x) for x in xt]
                for ri in range(2):
                    tsc = xt[2][:, :, 0:nbb]
                    f(tsc, hprev[:, :, ri, 0:nbb], sm[:, :, MAG:MAG + 1].to_broadcast([128, 8, nbb]), ALU.mult)
                    f(X4[ri][:, :, :, 0], X4[ri][:, :, :, 0], tsc, ALU.add)
                rv = rho if nbb == 1 else rhos
                for ri in range(2):
                    op(DVE, lambda ri=ri: nc.vector.tensor_tensor_scan(out=xt[ri].rearrange("p m t -> p (m t)"), data0=rv.rearrange("p m t -> p (m t)"),
                                                                       data1=xt[ri].rearrange("p m t -> p (m t)"), initial=0.0, op0=ALU.mult, op1=ALU.add),
                       reads=[dx, dW], writes=[dx])
                cvA = cosT[:, :, 0:Tl].unsqueeze(2).to_broadcast([128, 8, nbb, Tl])
                svA = sinT[:, :, 0:Tl].unsqueeze(2).to_broadcast([128, 8, nbb, Tl])
                H0, H1 = v4(Hs[:, :, 0, 0:nn]), v4(Hs[:, :, 1, 0:nn])
                fh = lambda o, a, b, g_, wr: op(DVE, lambda: nc.vector.tensor_tensor(out=o, in0=a, in1=b, op=g_), reads=[dx, dH, dW], writes=[wr])
                fh(X4[2], X4[0], cvA, ALU.mult, dx)
                fh(X4[3], X4[1], svA, ALU.mult, dx)
                fh(H0, X4[2], X4[3], ALU.subtract, dH)
                fh(X4[2], X4[0], svA, ALU.mult, dx)
                fh(X4[3], X4[1], cvA, ALU.mult, dx)
                fh(H1, X4[2], X4[3], ALU.add, dH)
                for ri in range(2):
                    op(DVE, lambda ri=ri: nc.vector.tensor_copy(out=hprev[:, :, ri, 0:nbb], in_=v4(Hs[:, :, ri, 0:nn])[:, :, :, Tl - 1]), reads=[dH, dx], writes=[st_dep, dHP])
                for cc in range(2):
                    by, byd = pbank()
                    i = 0
                    for m in range(4 * cc, 4 * cc + 4):
                        for ri in range(2):
                            op(PE, lambda m=m, ri=ri, i=i: nc.tensor.matmul(by[:, 0:nn], lhsT=CT[:, m, ri, :], rhs=Hs[:, m, ri, 0:nn], start=(i == 0), stop=(i == 7)),
                               reads=[dH, dW], writes=[byd], inc=(i == 7))
                            i += 1
                    t0, t0d = tmp()
                    op(DVE, lambda: nc.vector.scalar_tensor_tensor(out=t0[:, 0:nn], in0=su[:, cc, c0:c0 + nn], scalar=dsk[:, cc:cc + 1], in1=by[:, 0:nn], op0=ALU.mult, op1=ALU.add),
                       reads=[dsu, byd, dW], writes=[t0d])
                    op(ACT, lambda: nc.scalar.activation(out=su[:, cc, c0:c0 + nn], in_=t0[:, 0:nn], func=AF.Gelu), reads=[t0d], writes=[dsu])
            if kind == "p" and name == "B":
                for ri, nm in enumerate(("p_s5_re", "p_s5_im")):
                    store_T(O[nm][l:l + 1, :], lambda m, ri=ri: s5h[l][:, m, ri, 0:1], 1, 1024, [st_dep])
            elif kind == "s":
                for ri, nm in enumerate(("s_s5_re", "s_s5_im")):
                    store_T(O[nm][l], lambda m, ri=ri: hS[:, m, ri, 0:16], 16, 1024, [dW, dHP])
        wgl = E.rearrange("p m n -> p (m n)").rearrange("p (k n) -> p k n", k=2)
        load(wgl, I["s5_w_glu"][l].rearrange("(k p) n -> p k n", p=128), writes=[dW])
        bg = A.take(4)
        load_T(lambda c: bg[:, c:c + 1], I["s5_b_glu"][l:l + 1, :], 1, 512, [dW])
        for ti, (s, w) in enumerate(tiles):
            gl_ = []
            for oc in range(2):
                ba, bad = pbank()
                bb, bbd = pbank()
                gl_.append((ba, bad, bb, bbd))
                for (bk, bkd, o) in ((ba, bad, oc), (bb, bbd, 2 + oc)):
                    for kc in range(2):
                        op(PE, lambda bk=bk, o=o, kc=kc: nc.tensor.matmul(bk[:, 0:w], lhsT=wgl[:, kc, o * 128:(o + 1) * 128], rhs=su[:, kc, s:s + w], start=(kc == 0), stop=(kc == 1)),
                           reads=[dsu, dW], writes=[bkd], inc=(kc == 1))
            tt_ = []
            for oc in range(2):
                ba, bad, bb, bbd = gl_[oc]
                t1_, t1d = tmp()
                t2_, t2d = tmp()
                tt_.append((t1_, t1d, t2_, t2d))
                op(ACT, lambda t1_=t1_, ba=ba, oc=oc: nc.scalar.activation(out=t1_[:, 0:w], in_=ba[:, 0:w], func=AF.Identity, bias=bg[:, oc:oc + 1]), reads=[bad, dW], writes=[t1d])
                op(ACT, lambda t2_=t2_, bb=bb, oc=oc: nc.scalar.activation(out=t2_[:, 0:w], in_=bb[:, 0:w], func=AF.Sigmoid, bias=bg[:, 2 + oc:3 + oc]), reads=[bbd, dW], writes=[t2d])
            for oc in range(2):
                t1_, t1d, t2_, t2d = tt_[oc]
                op(DVE, lambda t1_=t1_, t2_=t2_, oc=oc: nc.vector.tensor_tensor(out=brT[:, 2 + oc, s:s + w], in0=t1_[:, 0:w], in1=t2_[:, 0:w], op=ALU.mult), reads=[t1d, t2d], writes=[br_dep[2 + oc][ti]])

    dnS = [sb([64, 4, 1, 64], F32, f"dnS{l}") for l in range(NL)]
    dntail = [sb([128, 6, 1, 3], F32, f"dntail{l}") for l in range(NL)]
    dn_wsem = dsem("dnw")

    def branch_dn(l, name, TS, tiles):
        pieces = pieces_of(name)
        A = Arena()
        dW = Dep()
        dv = lambda f, reads=(), writes=(): op(DVE, f, reads=list(reads) + [dW], writes=list(writes) + [dW])
        av = lambda f, reads=(), writes=(): op(ACT, f, reads=list(reads) + [dW], writes=list(writes) + [dW])
        wv = A.take(24).rearrange("p (c j) -> p c j", c=6)
        zb6 = A.take(6)
        load_T(lambda c: wv[:, c, 0:4], I["dn_conv_w"][l], 4, 768, [dW])
        dv(lambda: nc.vector.memset(zb6, 0.0))
        row = A.take(72)
        load(row[0:1, 0:4], I["dn_a_log"][l:l + 1, :], writes=[dW])
        load(row[0:1, 4:8], I["dn_dt_bias"][l:l + 1, :], writes=[dW])
        load(row[0:1, 8:72], I["dn_norm"][l:l + 1, :], writes=[dW])
        prmB = A.take(72)
        bk, bkd = pbank()
        op(PE, lambda: nc.tensor.matmul(bk[:, 0:72], lhsT=ones_f[0:1, :], rhs=row[0:1, 0:72], start=True, stop=True), reads=[dW, d_const], writes=[bkd])
        av(lambda: nc.scalar.copy(out=prmB, in_=bk[:, 0:72]), reads=[bkd])
        negA = A.take(4)
        av(lambda: nc.scalar.activation(out=negA, in_=prmB[:, 0:4], func=AF.Exp))
        dv(lambda: nc.vector.tensor_scalar_mul(out=negA, in0=negA, scalar1=-1.0))
        ngB = prmB[:, 8:72]
        triU, trilS, blk1, blkS, triUs, trilSs, blkones = [A.take(128) for _ in range(7)]
        dv(lambda: nc.vector.tensor_single_scalar(out=triU, in_=itf[:], scalar=0.0, op=ALU.is_ge), reads=[d_const])
        dv(lambda: nc.vector.tensor_single_scalar(out=trilS, in_=itf[:], scalar=0.0, op=ALU.is_lt), reads=[d_const])
        dv(lambda: nc.vector.memset(blk1, 1.0))
        dv(lambda: nc.vector.memset(blkones, 0.0))
        dv(lambda: nc.vector.memset(blkones[0:64, 0:64], 1.0))
        dv(lambda: nc.vector.memset(blkones[64:128, 64:128], 1.0))
        has_s = any(k == "s" for (k, _, _, _) in pieces)
        Ecol = A.take(16)
        if has_s:
            Ei = A.take(128)
            Ef = A.take(128)
            Ef2 = A.take(128)
            op(POOL, lambda: nc.gpsimd.iota(Ei[0:16, :].bitcast(I32), pattern=[[1, 128]], base=0, channel_multiplier=-8), writes=[dW])
            dv(lambda: nc.vector.tensor_copy(out=Ef[0:16, :], in_=Ei[0:16, :].bitcast(I32)))
            dv(lambda: nc.vector.tensor_single_scalar(out=Ef2[0:16, :], in_=Ef[0:16, :], scalar=0.0, op=ALU.is_ge))
            dv(lambda: nc.vector.tensor_single_scalar(out=Ef[0:16, :], in_=Ef[0:16, :], scalar=7.0, op=ALU.is_le))
            dv(lambda: nc.vector.tensor_tensor(out=Ef[0:16, :], in0=Ef[0:16, :], in1=Ef2[0:16, :], op=ALU.mult))
            bk, bkd = pbank()
            op(PE, lambda: nc.tensor.matmul(bk[:, 0:128], lhsT=Ef[0:16, :], rhs=Ef[0:16, :], start=True, stop=True), reads=[dW], writes=[bkd])
            av(lambda: nc.scalar.copy(out=blkS, in_=bk[:, 0:128]), reads=[bkd])
            bk2, bk2d = pbank()
            op(PE, lambda: nc.tensor.transpose(bk2[:, 0:16], Ef[0:16, :], ident[0:16, 0:16]), reads=[dW, d_const], writes=[bk2d])
            av(lambda: nc.scalar.copy(out=Ecol, in_=bk2[:, 0:16]), reads=[bk2d])
            dv(lambda: nc.vector.tensor_tensor(out=triUs, in0=triU, in1=blkS, op=ALU.mult))
            dv(lambda: nc.vector.tensor_tensor(out=trilSs, in0=trilS, in1=blkS, op=ALU.mult))
        wz = A.take(KC * 264 // 2).bitcast(BF16).rearrange("p (k n) -> p k n", k=KC)
        dma(POOL, dn_wsem, wz, w_in_v[l][:, :, OFF["dz"]:OFF["dz"] + 264], writes=[dW])
        qkv = A.take(6 * TS).rearrange("p (c t) -> p c t", c=6)
        dq = [Dep() for _ in range(6)]
        xf1 = [A.take(nb * (3 + T)).rearrange("p (b t) -> p b t", b=nb) for (kind, col0, nb, T) in pieces]
        dxf = [Dep() for _ in pieces]
        tailS = A.take(6 * 16 * 3).rearrange("p (c b j) -> p c b j", c=6, b=16)
        tailN = A.take(6 * 16 * 3).rearrange("p (c b j) -> p c b j", c=6, b=16)
        dtl = Dep()
        if has_s:
            load_T(lambda c: tailS[:, c, :, :], I["state_delta_conv"][l].rearrange("b j c -> (b j) c"), 48, 768, [dtl])
        for c6 in range(6):
            for pi, (kind, col0, nb, T) in enumerate(pieces):
                if kind == "p" and name == "A":
                    dv(lambda pi=pi: nc.vector.memset(xf1[pi][:, :, 0:3], 0.0), writes=[dxf[pi]])
                elif kind == "p":
                    dv(lambda pi=pi, c6=c6: nc.vector.tensor_copy(out=xf1[pi][:, :, 0:3], in_=dntail[l][:, c6, :, :]), reads=[st_dep], writes=[dxf[pi]])
                else:
                    dv(lambda pi=pi, c6=c6: nc.vector.tensor_copy(out=xf1[pi][:, :, 0:3], in_=tailS[:, c6, :, :]), reads=[dtl], writes=[dxf[pi]])

            def ev(bank, bd, ti, s, w):
                def f(pi, piece, p_off, n, b_off):
                    kind, col0, nb, T = piece
                    dst = xf1[pi][:, 0, 3 + p_off:3 + p_off + n] if kind == "p" else xf1[pi][:, :, 3:3 + T]
                    op(ACT, lambda: nc.scalar.copy(out=dst, in_=src_view(bank[:, b_off:b_off + n], kind, nb, T)), reads=[bd], writes=[dxf[pi]])
                scatter(pieces, s, w, f)
            proj_in(l, OFF["dq"] + c6 * 128, 128, tiles, ev)
            for pi, (kind, col0, nb, T) in enumerate(pieces):
                xf_p = xf1[pi][:, 0, :] if kind == "p" else xf1[pi][:, :, :]
                o_p = qkv[:, c6, col0:col0 + nb * T] if kind == "p" else qkv[:, c6, col0:col0 + nb * T].rearrange("p (b t) -> p b t", b=nb)
                conv_taps(xf_p, o_p, wv, zb6, c6, 4, T, [dW, dxf[pi]], [dq[c6]])
                if kind == "p" and name == "A":
                    dv(lambda pi=pi, c6=c6, T=T: nc.vector.tensor_copy(out=dntail[l][:, c6, :, :], in_=xf1[pi][:, :, T:T + 3]), reads=[dxf[pi]], writes=[st_dep])
                else:
                    dv(lambda pi=pi, c6=c6, T=T, nb=nb, kind=kind: nc.vector.tensor_copy(out=(tailN[:, c6, 0:1, :] if kind == "p" else tailS[:, c6, :, :]), in_=xf1[pi][:, :, T:T + 3]),
                       reads=[dxf[pi], dtl], writes=[dtl])
            op(ACT, lambda c6=c6: nc.scalar.activation(out=qkv[:, c6, 0:TS], in_=qkv[:, c6, 0:TS], func=AF.Silu), reads=[dq[c6]], writes=[dq[c6]])
        if name == "B":
            store_T(O["p_delta_conv"][l], lambda c: tailN[:, c, 0, :], 3, 768, [dtl])
            store_T(O["s_delta_conv"][l].rearrange("b j c -> (b j) c"), lambda c: tailS[:, c, :, :], 48, 768, [dtl])
        for c4 in range(4):
            tl = [tmp() for _ in tiles]
            bl = []
            for ti, (s, w) in enumerate(tiles):
                t0, t0d = tl[ti]
                op(ACT, lambda t0=t0, s=s, w=w: nc.scalar.activation(out=t0[:, 0:w], in_=qkv[:, c4, s:s + w], func=AF.Square), reads=[dq[c4]], writes=[t0d])
            for ti, (s, w) in enumerate(tiles):
                t0, t0d = tl[ti]
                bk, bkd = pbank()
                bl.append((bk, bkd))
                op(PE, lambda t0=t0, bk=bk, w=w: nc.tensor.matmul(bk[:, 0:w], lhsT=blkones, rhs=t0[:, 0:w], start=True, stop=True), reads=[t0d, dW], writes=[bkd])
            for ti, (s, w) in enumerate(tiles):
                t0, t0d = tl[ti]
                bk, bkd = bl[ti]
                op(ACT, lambda t0=t0, bk=bk, w=w: nc.scalar.activation(out=t0[:, 0:w], in_=bk[:, 0:w], func=AF.Ln, bias=eps_t[:, 0:1]), reads=[bkd, d_const], writes=[t0d])
            for ti, (s, w) in enumerate(tiles):
                t0, t0d = tl[ti]
                op(ACT, lambda t0=t0, w=w: nc.scalar.activation(out=t0[:, 0:w], in_=t0[:, 0:w], func=AF.Exp, scale=-0.5), reads=[t0d], writes=[t0d])
            for ti, (s, w) in enumerate(tiles):
                t0, t0d = tl[ti]
                op(DVE, lambda t0=t0, s=s, w=w: nc.vector.scalar_tensor_tensor(out=qkv[:, c4, s:s + w], in0=qkv[:, c4, s:s + w], scalar=(0.125 if c4 < 2 else 1.0), in1=t0[:, 0:w],
                                                                               op0=ALU.mult, op1=ALU.mult), reads=[dq[c4], t0d], writes=[dq[c4]])
        dqa = dq
        HB = 4
        mkh = lambda n: [A.take(n) for _ in range(HB)]
        Xb, Nb, NTb, ATb, gBb, wTb, tTb = mkh(128), mkh(128), mkh(128), mkh(128), mkh(128), mkh(128), mkh(128)
        kdb, vnb, ob, o1b = mkh(64), mkh(64), mkh(64), mkh(64)
        qsb = mkh(128)
        dh = [Dep() for _ in range(HB)]
        zsil2 = [A.take(256) for _ in range(2)]
        oat2 = [A.take(256) for _ in range(2)]
        sv42 = [A.take(64).rearrange("p (j h) -> p j h", h=4) for _ in range(2)]
        ktv2 = [A.take(512) for _ in range(2)]
        dC2 = [Dep(), Dep()]
        Ssm = A.take(16 * 64).rearrange("p (b v) -> p b v", b=16)
        kdm = A.take(64)
        dS = Dep()
        CHUNK = debug.get("chunk", 128)

        def prologue(kind, c0, C, par):
            zsil, sv4, ktv, dC = zsil2[par], sv42[par], ktv2[par], dC2[par]
            mU, mL, mB = (triUs, trilSs, blkS) if kind == "s" else (triU, trilS, blk1)
            cd = lambda f, reads=(), writes=(): op(DVE, f, reads=list(reads) + [dC, dW], writes=list(writes) + [dC])
            ca = lambda f, reads=(), writes=(): op(ACT, f, reads=list(reads) + [dC, dW], writes=list(writes) + [dC])
            bz, bzd = pbank()
            for k in range(KC):
                op(PE, lambda k=k: nc.tensor.matmul(bz[0:C, 0:264], lhsT=uT[:, k, c0:c0 + C], rhs=wz[:, k, :], start=(k == 0), stop=(k == KC - 1)),
                   reads=[dW] + [u_dep[k][ti] for ti in range(len(tiles))], writes=[bzd], inc=(k == KC - 1))
            ca(lambda: nc.scalar.activation(out=zsil[0:C, :], in_=bz[0:C, 0:256], func=AF.Silu), reads=[bzd])
            ca(lambda: nc.scalar.activation(out=sv4[0:C, 0, :], in_=bz[0:C, 256:260], func=AF.Sigmoid), reads=[bzd])
            cd(lambda: nc.vector.tensor_tensor(out=sv4[0:C, 8, :], in0=bz[0:C, 260:264], in1=prmB[0:C, 4:8], op=ALU.add), reads=[bzd])
            yield
            cd(lambda: nc.vector.tensor_scalar_mul(out=sv4[0:C, 1, :], in0=sv4[0:C, 0, :], scalar1=-1.0))
            ca(lambda: nc.scalar.activation(out=sv4[0:C, 8, :], in_=sv4[0:C, 8, :], func=AF.Exp))
            ca(lambda: nc.scalar.activation(out=sv4[0:C, 8, :], in_=sv4[0:C, 8, :], func=AF.Ln, bias=1.0))
            yield
            cd(lambda: nc.vector.tensor_tensor(out=sv4[0:C, 2, :], in0=sv4[0:C, 8, :], in1=negA[0:C, :], op=ALU.mult))
            yield
            bg_, bgd = pbank()
            op(PE, lambda: nc.tensor.matmul(bg_[0:C, 0:4], lhsT=mU[0:C, 0:C], rhs=sv4[0:C, 2, :], start=True, stop=True), reads=[dC, dW], writes=[bgd])
            op(PE, lambda: nc.tensor.matmul(bg_[0:C, 4:8], lhsT=mB[0:C, 0:C], rhs=sv4[0:C, 2, :], start=True, stop=True), reads=[dC, dW], writes=[bgd])
            ca(lambda: nc.scalar.copy(out=sv4[0:C, 3:5, :], in_=bg_[0:C, 0:8].rearrange("p (j h) -> p j h", h=4)), reads=[bgd])
            bkt, bktd = pbank()
            for j, c6 in enumerate((2, 3, 4, 5)):
                op(PE, lambda j=j, c6=c6: nc.tensor.transpose(bkt[0:C, j * 128:(j + 1) * 128], qkv[:, c6, c0:c0 + C], ident[:]), reads=[dqa[c6], d_const], writes=[bktd], inc=(j == 3))
            ca(lambda: nc.scalar.copy(out=ktv[0:C, :], in_=bkt[0:C, :]), reads=[bktd])
            yield
            ca(lambda: nc.scalar.activation(out=sv4[0:C, 5, :], in_=sv4[0:C, 3, :], func=AF.Exp))
            cd(lambda: nc.vector.tensor_tensor(out=sv4[0:C, 7, :], in0=sv4[0:C, 4, :], in1=sv4[0:C, 3, :], op=ALU.subtract))
            yield
            cd(lambda: nc.vector.tensor_tensor(out=sv4[0:C, 6, :], in0=sv4[0:C, 0, :], in1=sv4[0:C, 5, :], op=ALU.mult))
            ca(lambda: nc.scalar.activation(out=sv4[0:C, 7, :], in_=sv4[0:C, 7, :], func=AF.Exp))

        def head(kind, c0, C, par, h, nbb, Tq):
            zsil, sv4, ktv, dC, oat = zsil2[par], sv42[par], ktv2[par], dC2[par], oat2[par]
            mU, mL, mB = (triUs, trilSs, blkS) if kind == "s" else (triU, trilS, blk1)
            levels = max(1, (Tq - 1).bit_length())
            hc, hp = h // 2, (h % 2) * 64
            hd = dh[h]
            hdv = lambda f, reads=(), writes=(): op(DVE, f, reads=list(reads) + [hd, dC, dW], writes=list(writes) + [hd])
            hav = lambda f, reads=(), writes=(): op(ACT, f, reads=list(reads) + [hd, dC, dW], writes=list(writes) + [hd])
            hpe = lambda f, reads=(), writes=(), inc=True: op(PE, f, reads=list(reads) + [hd, dC, dW], writes=list(writes), inc=inc)
            X, N, NT, AT, gB, wT, tT, kd, vn, oo, o1 = Xb[h], Nb[h], NTb[h], ATb[h], gBb[h], wTb[h], tTb[h], kdb[h], vnb[h], ob[h], o1b[h]
            qT = qkv[hp:hp + 64, 0 + hc, c0:c0 + C]
            kT = qkv[hp:hp + 64, 2 + hc, c0:c0 + C]
            ktok = ktv[0:C, hc * 128 + hp:hc * 128 + hp + 64]
            vtok = ktv[0:C, (2 + hc) * 128 + hp:(2 + hc) * 128 + hp + 64]
            hdv(lambda: nc.vector.tensor_copy(out=gB[0:C, :], in_=sv4[0:C, 2, h:h + 1].to_broadcast([C, 128])))
            hdv(lambda: nc.vector.tensor_scalar_mul(out=X[0:C, 0:64], in0=ktok, scalar1=sv4[0:C, 6, h:h + 1]))
            hdv(lambda: nc.vector.tensor_scalar_mul(out=X[0:C, 64:128], in0=vtok, scalar1=sv4[0:C, 0, h:h + 1]))
            hdv(lambda: nc.vector.tensor_scalar_mul(out=kd[0:C, :], in0=ktok, scalar1=sv4[0:C, 7, h:h + 1]))
            yield
            b1, b1d = pbank()
            hpe(lambda: nc.tensor.matmul(b1[0:C, 0:C], lhsT=gB[0:C, 0:C], rhs=mU[0:C, 0:C], start=True, stop=True), writes=[b1d])
            hpe(lambda: nc.tensor.matmul(b1[0:64, 128:128 + C], lhsT=gB[0:C, 0:64], rhs=mB[0:C, 0:C], start=True, stop=True), writes=[b1d])
            hdv(lambda: nc.vector.tensor_scalar(out=N[0:C, 0:C], in0=b1[0:C, 0:C], scalar1=sv4[0:C, 3, h:h + 1], scalar2=0.0, op0=ALU.subtract, op1=ALU.max), reads=[b1d])
            hdv(lambda: nc.vector.tensor_scalar(out=AT[0:C, 0:C], in0=b1[0:C, 0:C], scalar1=sv4[0:C, 3, h:h + 1], scalar2=0.0, op0=ALU.subtract, op1=ALU.min), reads=[b1d])
            hav(lambda: nc.scalar.activation(out=tT[0:64, 0:C], in_=b1[0:64, 128:128 + C], func=AF.Exp), reads=[b1d])
            bq, bqd = pbank()
            hpe(lambda: nc.tensor.matmul(bq[0:64, 0:C], lhsT=ident[:, hp:hp + 64], rhs=qkv[:, hc, c0:c0 + C], start=True, stop=True), reads=[dqa[hc], d_const], writes=[bqd])
            qs = qsb[h]
            hav(lambda: nc.scalar.copy(out=qs[0:64, 0:C], in_=bq[0:64, 0:C]), reads=[bqd])
            yield
            hav(lambda: nc.scalar.activation(out=N[0:C, 0:C], in_=N[0:C, 0:C], func=AF.Exp, scale=-1.0))
            hav(lambda: nc.scalar.activation(out=AT[0:C, 0:C], in_=AT[0:C, 0:C], func=AF.Exp))
            yield
            hdv(lambda: nc.vector.tensor_tensor(out=N[0:C, 0:C], in0=N[0:C, 0:C], in1=mL[0:C, 0:C], op=ALU.mult))
            hdv(lambda: nc.vector.tensor_tensor(out=AT[0:C, 0:C], in0=AT[0:C, 0:C], in1=mU[0:C, 0:C], op=ALU.mult))
            b2, b2d = pbank()
            op(PE, lambda: nc.tensor.matmul(b2[0:C, 0:C], lhsT=kT, rhs=kT, start=True, stop=True), reads=[dqa[2 + hc]], writes=[b2d])
            op(PE, lambda: nc.tensor.matmul(b2[0:C, 128:128 + C], lhsT=kT, rhs=qT, start=True, stop=True), reads=[dqa[2 + hc], dqa[hc]], writes=[b2d])
            hdv(lambda: nc.vector.scalar_tensor_tensor(out=N[0:C, 0:C], in0=b2[0:C, 0:C], scalar=sv4[0:C, 1, h:h + 1], in1=N[0:C, 0:C], op0=ALU.mult, op1=ALU.mult), reads=[b2d])
            hdv(lambda: nc.vector.tensor_tensor(out=AT[0:C, 0:C], in0=b2[0:C, 128:128 + C], in1=AT[0:C, 0:C], op=ALU.mult), reads=[b2d])
            yield
            b3, b3d = pbank()
            hpe(lambda: nc.tensor.transpose(b3[0:C, 0:C], N[0:C, 0:C], ident[0:C, 0:C]), reads=[d_const], writes=[b3d])
            hav(lambda: nc.scalar.copy(out=NT[0:C, 0:C], in_=b3[0:C, 0:C]), reads=[b3d])
            yield
            for lev in range(levels):
                b4, b4d = pbank()
                hpe(lambda: nc.tensor.matmul(b4[0:C, 0:128], lhsT=NT[0:C, 0:C], rhs=X[0:C, :], start=True, stop=True), writes=[b4d])
                if lev < levels - 1:
                    hpe(lambda: nc.tensor.matmul(b4[0:C, 128:128 + C], lhsT=NT[0:C, 0:C], rhs=N[0:C, 0:C], start=True, stop=True), writes=[b4d])
                    hpe(lambda: nc.tensor.matmul(b4[0:C, 256:256 + C], lhsT=N[0:C, 0:C], rhs=NT[0:C, 0:C], start=True, stop=True), writes=[b4d])
                hdv(lambda: nc.vector.tensor_tensor(out=X[0:C, :], in0=X[0:C, :], in1=b4[0:C, 0:128], op=ALU.add), reads=[b4d])
                if lev < levels - 1:
                    hav(lambda: nc.scalar.copy(out=N[0:C, 0:C], in_=b4[0:C, 128:128 + C]), reads=[b4d])
                    hav(lambda: nc.scalar.copy(out=NT[0:C, 0:C], in_=b4[0:C, 256:256 + C]), reads=[b4d])
                yield
            b5, b5d = pbank()
            hpe(lambda: nc.tensor.transpose(b5[:, 0:C], X[0:C, :], ident[0:C, 0:C]), reads=[d_const], writes=[b5d])
            hav(lambda: nc.scalar.copy(out=wT[0:64, 0:C], in_=b5[0:64, 0:C]), reads=[b5d])
            if kind == "s":
                Sv = Ssm
                sdp = dS
                for b in range(16):
                    load(Ssm[0:64, b, :], I["state_delta"][l, b, h], writes=[dS])
            else:
                Sv = dnS[l][:, h, :, :]
                sdp = st_dep
            yield
            b6, b6d = pbank()
            for b in range(nbb):
                hpe(lambda b=b: nc.tensor.matmul(b6[0:64, b * Tq:(b + 1) * Tq], lhsT=Sv[0:64, b, :], rhs=wT[0:64, b * Tq:(b + 1) * Tq], start=True, stop=True), reads=[sdp], writes=[b6d], inc=(b == nbb - 1))
            for b in range(nbb):
                hpe(lambda b=b: nc.tensor.matmul(b6[0:64, 128 + b * Tq:128 + (b + 1) * Tq], lhsT=Sv[0:64, b, :], rhs=qs[0:64, b * Tq:(b + 1) * Tq], start=True, stop=True),
                    reads=[sdp, dqa[hc]], writes=[b6d], inc=(b == nbb - 1))
            hav(lambda: nc.scalar.copy(out=wT[0:64, 0:C], in_=b6[0:64, 0:C]), reads=[b6d])
            hav(lambda: nc.scalar.copy(out=gB[0:64, 0:C], in_=b6[0:64, 128:128 + C]), reads=[b6d])
            yield
            b7, b7d = pbank()
            hpe(lambda: nc.tensor.transpose(b7[0:C, 0:64], wT[0:64, 0:C], ident[0:64, 0:64]), reads=[d_const], writes=[b7d])
            hpe(lambda: nc.tensor.transpose(b7[0:C, 64:128], gB[0:64, 0:C], ident[0:64, 0:64]), reads=[d_const], writes=[b7d])
            hdv(lambda: nc.vector.tensor_tensor(out=vn[0:C, :], in0=X[0:C, 64:128], in1=b7[0:C, 0:64], op=ALU.subtract), reads=[b7d])
            hdv(lambda: nc.vector.tensor_scalar_mul(out=o1[0:C, :], in0=b7[0:C, 64:128], scalar1=sv4[0:C, 5, h:h + 1]), reads=[b7d])
            yield
            b8, b8d = pbank()
            hpe(lambda: nc.tensor.matmul(b8[0:C, 0:64], lhsT=AT[0:C, 0:C], rhs=vn[0:C, :], start=True, stop=True), writes=[b8d])
            hdv(lambda: nc.vector.tensor_tensor(out=oo[0:C, :], in0=o1[0:C, :], in1=b8[0:C, 0:64], op=ALU.add), reads=[b8d])
            if nbb == 1:
                b9, b9d = pbank()
                hpe(lambda: nc.tensor.matmul(b9[0:64, 0:64], lhsT=kd[0:C, :], rhs=vn[0:C, :], start=True, stop=True), writes=[b9d])
                op(DVE, lambda: nc.vector.scalar_tensor_tensor(out=Sv[0:64, 0, :], in0=Sv[0:64, 0, :], scalar=tT[0:64, 0:1], in1=b9[0:64, 0:64], op0=ALU.mult, op1=ALU.add),
                   reads=[b9d, hd, sdp], writes=[sdp])
            else:
                for half in range(2):
                    b9, b9d = pbank()
                    for bb in range(8):
                        b = half * 8 + bb
                        hdv(lambda b=b: nc.vector.tensor_scalar_mul(out=kdm[0:C, :], in0=kd[0:C, :], scalar1=Ecol[0:C, b:b + 1]))
                        hpe(lambda bb=bb, b9=b9: nc.tensor.matmul(b9[0:64, bb * 64:(bb + 1) * 64], lhsT=kdm[0:C, :], rhs=vn[0:C, :], start=True, stop=True), writes=[b9d])
                        op(DVE, lambda b=b, bb=bb, b9=b9: nc.vector.scalar_tensor_tensor(out=Sv[0:64, b, :], in0=Sv[0:64, b, :], scalar=tT[0:64, b * 8:b * 8 + 1], in1=b9[0:64, bb * 64:(bb + 1) * 64],
                                                                                        op0=ALU.mult, op1=ALU.add), reads=[b9d, hd, sdp], writes=[sdp])
                for b in range(16):
                    store(O["s_delta"][l, b, h], Ssm[0:64, b, :], reads=[dS])
            yield
            hav(lambda: nc.scalar.activation(out=vn[0:C, :], in_=oo[0:C, :], func=AF.Square))
            yield
            hdv(lambda: nc.vector.reduce_sum(out=kd[0:C, 0:1], in_=vn[0:C, :], axis=mybir.AxisListType.X))
            yield
            hav(lambda: nc.scalar.activation(out=kd[0:C, 0:1], in_=kd[0:C, 0:1], func=AF.Ln, scale=1.0 / 64, bias=eps_t[0:C, 0:1]), reads=[d_const])
            hav(lambda: nc.scalar.activation(out=kd[0:C, 0:1], in_=kd[0:C, 0:1], func=AF.Exp, scale=-0.5))
            yield
            hdv(lambda: nc.vector.scalar_tensor_tensor(out=oo[0:C, :], in0=oo[0:C, :], scalar=kd[0:C, 0:1], in1=ngB[0:C, :], op0=ALU.mult, op1=ALU.mult))
            op(DVE, lambda: nc.vector.tensor_tensor(out=oat[0:C, h * 64:(h + 1) * 64], in0=oo[0:C, :], in1=zsil[0:C, h * 64:(h + 1) * 64], op=ALU.mult), reads=[hd, dC], writes=[dC])

        def epilogue(c0, C, par):
            oat, dC = oat2[par], dC2[par]
            bo, bod = pbank()
            for c in range(2):
                op(PE, lambda c=c: nc.tensor.transpose(bo[:, c * 128:c * 128 + C], oat[0:C, c * 128:(c + 1) * 128], ident[0:C, 0:C]), reads=[dC, d_const], writes=[bod], inc=(c == 1))
            for c in range(2):
                op(ACT, lambda c=c: nc.scalar.copy(out=brT[:, c, c0:c0 + C], in_=bo[:, c * 128:c * 128 + C]), reads=[bod], writes=[br_dep[c][ti] for ti in range(len(tiles))])

        def drain(gens):
            gens = list(gens)
            while gens:
                for g_ in list(gens):
                    try:
                        next(g_)
                    except StopIteration:
                        gens.remove(g_)

        chunks = []
        for (kind, col0, nb, T) in pieces:
            if kind == "p":
                Cs = ([16] if name == "A" else []) + [CHUNK] * (1024 // CHUNK)
                c0 = col0
                for C in Cs:
                    chunks.append(("p", c0, C, 1, C))
                    c0 += C
            else:
                chunks.append(("s", col0, 128, nb, T))
        if name == "A":
            dv(lambda: nc.vector.memset(dnS[l][:], 0.0), writes=[st_dep])
        drain([prologue(chunks[0][0], chunks[0][1], chunks[0][2], 0)])
        for ci, (kind, c0, C, nbb, Tq) in enumerate(chunks):
            par = ci % 2
            nxt = [prologue(chunks[ci + 1][0], chunks[ci + 1][1], chunks[ci + 1][2], 1 - par)] if ci + 1 < len(chunks) else []
            hs = [head(kind, c0, C, par, h, nbb, Tq) for h in range(4)]
            if kind == "p":
                drain(hs + nxt)
            else:
                for hg in hs:
                    drain([hg])
                drain(nxt)
            epilogue(c0, C, par)
            if kind == "p" and name == "B" and (ci + 1 == len(chunks) or chunks[ci + 1][0] != "p"):
                for h in range(4):
                    store(O["p_delta"][l, h], dnS[l][:, h, 0, :], reads=[st_dep])

    def merge(l, name, TS, tiles):
        A = Arena()
        mg = A.take(KC * TSM // 2).bitcast(BF16).rearrange("p (c t) -> p c t", c=KC)
        dmg = [[Dep() for _ in tiles] for _ in range(KC)]
        wb_v2 = I["w_branch"][l].rearrange("n (k p) d -> p n k d", p=128)
        wo_v = I["w_out"][l].rearrange("(k p) d -> p k d", p=128)
        accs = [[A.take(512) for _ in tiles] for _ in range(2)]
        dac = [[Dep() for _ in tiles] for _ in range(2)]
        sgs = [A.take(512) for _ in range(3)]
        dsg = [Dep() for _ in range(3)]
        si = 0
        for mp in range(KC // 2):
            bslot, bdep, bsem = wslot()
            bv = bslot[:, 0:2048].rearrange("p (n k d) -> p n k d", n=4, k=2)
            for n in range(4):
                wload(bv[:, n], wb_v2[:, n, :, mp * 256:(mp + 1) * 256], bdep, bsem)
            for n in range(4):
                slot, sdep, ssem = wslot()
                sv = slot[:, 0:KC * 256].rearrange("p (k d) -> p k d", k=KC)
                c0 = OFF["zg"] + n * D + mp * 256
                wload(sv, w_in_v[l][:, :, c0:c0 + 256], sdep, ssem)
                for mm in range(2):
                    m = 2 * mp + mm
                    for ti, (s, w) in enumerate(tiles):
                        acc, da = accs[mm][ti], dac[mm][ti]
                        bz, bzd = pbank()
                        bp, bpd = pbank()
                        for k in range(KC):
                            op(PE, lambda k=k: nc.tensor.matmul(bz[:, 0:w], lhsT=sv[:, k, mm * 128:(mm + 1) * 128], rhs=uT[:, k, s:s + w], start=(k == 0), stop=(k == KC - 1)),
                               reads=[sdep, u_dep[k][ti]], writes=[bzd], inc=(k == KC - 1))
                        for k in range(2):
                            op(PE, lambda k=k: nc.tensor.matmul(bp[:, 0:w], lhsT=bv[:, n, k, mm * 128:(mm + 1) * 128], rhs=brT[:, 2 * n + k, s:s + w], start=(k == 0), stop=(k == 1)),
                               reads=[bdep, br_dep[2 * n + k][ti]], writes=[bpd], inc=(k == 1))
                        sgt, sgd = sgs[si % 3], dsg[si % 3]
                        si += 1
                        op(ACT, lambda: nc.scalar.activation(out=sgt[:, 0:w], in_=bz[:, 0:w], func=AF.Sigmoid), reads=[bzd], writes=[sgd])
                        if n == 0:
                            op(DVE, lambda: nc.vector.tensor_tensor(out=acc[:, 0:w], in0=sgt[:, 0:w], in1=bp[:, 0:w], op=ALU.mult), reads=[sgd, bpd], writes=[da])
                        else:
                            op(DVE, lambda: nc.vector.tensor_tensor(out=sgt[:, 0:w], in0=sgt[:, 0:w], in1=bp[:, 0:w], op=ALU.mult), reads=[sgd, bpd], writes=[sgd])
                            if n < 3:
                                op(DVE, lambda: nc.vector.tensor_tensor(out=acc[:, 0:w], in0=acc[:, 0:w], in1=sgt[:, 0:w], op=ALU.add), reads=[sgd, da], writes=[da])
                            else:
                                op(DVE, lambda: nc.vector.tensor_tensor(out=mg[:, m, s:s + w], in0=acc[:, 0:w], in1=sgt[:, 0:w], op=ALU.add), reads=[sgd, da], writes=[dmg[m][ti]])
        for m in range(KC):
            slot, sdep, ssem = wslot()
            sv = slot[:, 0:KC * 128].rearrange("p (k n) -> p k n", k=KC)
            wload(sv, wo_v[:, :, m * 128:(m + 1) * 128], sdep, ssem)
            for ti, (s, w) in enumerate(tiles):
                bank, bd = pbank()
                for k in range(KC):
                    op(PE, lambda k=k: nc.tensor.matmul(bank[:, 0:w], lhsT=sv[:, k, :], rhs=mg[:, k, s:s + w], start=(k == 0), stop=(k == KC - 1)),
                       reads=[sdep, dmg[k][ti]], writes=[bd], inc=(k == KC - 1))
                op(DVE, lambda: nc.vector.tensor_tensor(out=xT[:, m, s:s + w], in0=xT[:, m, s:s + w], in1=bank[:, 0:w], op=ALU.add),
                   reads=[bd, x_dep[m][ti]], writes=[x_dep[m][ti]])

    def zero_branch(i, tiles):
        for c in range(2):
            for ti, (s, w) in enumerate(tiles):
                op(DVE, lambda: nc.vector.memset(brT[:, 2 * i + c, s:s + w], 0.0), writes=[br_dep[2 * i + c][ti]])

    def mixer(g, l, name, TS, tiles):
        rmsnorm_to(tiles, GIDX["mix"] + l, lambda c, s, w: uT[:, c, s:s + w], lambda c, ti: [u_dep[c][ti]])
        barrier()
        skip = debug.get("skip", "")
        for i, (ch, fn) in enumerate((("a", branch_dn), ("b", branch_s5), ("c", branch_lru), ("d", branch_conv))):
            if fn is None or ch in skip:
                zero_branch(i, tiles)
            else:
                fn(l, name, TS, tiles)
                barrier()
            if f"{name}{l}_o{ch}" in debug:
                dump(f"{name}{l}_o{ch}", brT[:, 2 * i:2 * i + 2, 0:TS], [128, 2, TS], [br_dep[2 * i + c][ti] for c in range(2) for ti in range(len(tiles))])
        merge(l, name, TS, tiles)
        if f"{name}{l}_x2" in debug:
            dump(f"{name}{l}_x2", xT[:, :, 0:TS], [128, KC, TS], [x_dep[c][ti] for c in range(KC) for ti in range(len(tiles))])

    def load_tokens(src_rows, R, col0):
        load_T(lambda c: xT[:, c, col0:col0 + R], src_rows, R, D, [x_dep[c][ti] for c in range(KC) for ti in range(MAXT)])

    def run_st(name, TS, blocks, yblocks):
        tiles = split_tiles(TS)
        for (src, R, col0) in blocks:
            load_tokens(src, R, col0)
        if f"{name}0_x0" in debug:
            dump(f"{name}0_x0", xT[:, :, 0:TS], [128, KC, TS], [x_dep[c][ti] for c in range(KC) for ti in range(len(tiles))])
        for l in range(debug.get("nl", NL)):
            ffn(l, "ffn1", tiles)
            if f"{name}{l}_x1" in debug:
                dump(f"{name}{l}_x1", xT[:, :, 0:TS], [128, KC, TS], [x_dep[c][ti] for c in range(KC) for ti in range(len(tiles))])
            barrier()
            mixer(g, l, name, TS, tiles)
            barrier()
            ffn(l, "ffn2", tiles)
            barrier()
        if debug.get("nofinal"):
            return
        yT = arena[:, 0:KC * TSM].rearrange("p (c t) -> p c t", c=KC)
        y_dep = [[Dep() for _ in range(MAXT)] for _ in range(KC)]
        rmsnorm_to(tiles, 6, lambda c, s, w: yT[:, c, s:s + w], lambda c, ti: [y_dep[c][ti]])
        ally = [y_dep[c][ti] for c in range(KC) for ti in range(len(tiles))]
        for (dst, R, col0) in yblocks:
            store_T(dst, lambda c: yT[:, c, col0:col0 + R], R, D, ally)
        barrier()

    xp = I["x_prompt"]
    blocksA = [(I["meta_tokens"], 16, 0)] + [(xp[i * 128:(i + 1) * 128, :], 128, 16 + i * 128) for i in range(8)]
    yA = [(O["y_prompt"][i * 128:(i + 1) * 128, :], 128, 16 + i * 128) for i in range(8)]
    blocksB = [(xp[1024 + i * 128:1024 + (i + 1) * 128, :], 128, i * 128) for i in range(8)] + [(I["x_sample"], 128, 1024)]
    yB = [(O["y_prompt"][1024 + i * 128:1024 + (i + 1) * 128, :], 128, i * 128) for i in range(8)] + [(O["y_sample"], 128, 1024)]
    sts = debug.get("sts", "AB")
    if "A" in sts:
        run_st("A", TA, blocksA, yA)
    if "B" in sts:
        run_st("B", TB, blocksB, yB)

    for t in st_sems:
        if t.cnt:
            nc.sync.wait_ge(t.sem, t.cnt)


_CACHE = {}


def make_in_maps(inputs, cores=range(8)):
    maps = []
    for c in cores:
        m = {}
        m["x_prompt"] = np.ascontiguousarray(inputs["x_prompt"][c])
        m["x_sample"] = np.ascontiguousarray(inputs["x_sample"][16 * c:16 * c + 16]).reshape(128, D)
        for k in SNAMES:
            m[k] = np.ascontiguousarray(inputs[k][:, 16 * c:16 * c + 16])
        for k in WNAMES:
            m[k] = np.ascontiguousarray(inputs[k])
        maps.append(m)
    return maps


def kernel(**inputs):
    inputs = {k: np.asarray(v, dtype=np.float32) for k, v in inputs.items()}
    if "nc" not in _CACHE:
        _CACHE["nc"] = build()
    nc, g = _CACHE["nc"]
    maps = make_in_maps(inputs)
    for m in maps:
        for k, shp in g.shapes_in.items():
            m[k] = m[k].reshape(shp)
    res = run_bass_kernel_spmd(nc, maps, core_ids=list(range(8)))
    outs = []
    for nm in ONAMES:
        per = [r["o_" + nm] for r in res.results]
        if nm == "y_prompt":
            outs.append(np.stack(per, 0))
        elif nm == "y_sample":
            outs.append(np.concatenate([p.reshape(16, 8, D) for p in per], 0))
        elif nm.startswith("p_"):
            full = np.stack(per, 1)
            outs.append(full)
        else:
            full = np.concatenate(per, 1)
            outs.append(full)
    ref_shapes = dict(p_s5_re=(NL, 8, 16, 64), p_s5_im=(NL, 8, 16, 64), s_s5_re=(NL, 128, 16, 64), s_s5_im=(NL, 128, 16, 64))
    outs = [o.reshape(ref_shapes[nm]) if nm in ref_shapes else o for nm, o in zip(ONAMES, outs)]
    return tuple(np.ascontiguousarray(o, dtype=np.float32) for o in outs)
```

```python
import numpy as np
from contextlib import ExitStack
import concourse.bass as bass
import concourse.mybir as mybir
from concourse.bass_utils import run_bass_kernel_spmd

F32 = mybir.dt.float32
BF16 = mybir.dt.bfloat16
I32 = mybir.dt.int32
AF = mybir.ActivationFunctionType
ALU = mybir.AluOpType

D = 1024
KC = 8
FF = 2816
FJ = 22
NL = 2
EPS = 1e-6
TA = 1040
TB = 1152
TSM = 1152
N_IN = 6408
OFF = dict(dq=0, dk=256, dv=512, dz=768, db=1024, da=1028, su=1032, lx=1288, lg=1544, cval=1800, cgate=2056, zg=2312)

WNAMES = ['meta_tokens', 'ffn1_norm', 'ffn1_w_gu', 'ffn1_w_down', 'mix_norm', 'w_in', 'dn_conv_w', 'dn_a_log', 'dn_dt_bias',
          'dn_norm', 's5_lam_re', 's5_lam_im', 's5_log_step', 's5_b_re', 's5_b_im', 's5_c_re', 's5_c_im', 's5_d', 's5_w_glu',
          's5_b_glu', 'lru_conv_w', 'lru_conv_b', 'lru_w_a', 'lru_b_a', 'lru_w_x', 'lru_b_x', 'lru_lam', 'cv_conv_w',
          'cv_conv_b', 'cv_ln_g', 'cv_ln_b', 'w_branch', 'w_out', 'ffn2_norm', 'ffn2_w_gu', 'ffn2_w_down', 'final_norm']
SNAMES = ['state_delta', 'state_delta_conv', 'state_s5_re', 'state_s5_im', 'state_lru', 'state_lru_conv', 'state_conv']
ONAMES = ['y_prompt', 'y_sample', 'p_delta', 'p_delta_conv', 'p_s5_re', 'p_s5_im', 'p_lru', 'p_lru_conv', 'p_conv',
          's_delta', 's_delta_conv', 's_s5_re', 's_s5_im', 's_lru', 's_lru_conv', 's_conv']


class Trk:
    def __init__(self, name, obj, sem):
        self.name, self.obj, self.sem = name, obj, sem
        self.cnt = 0
        self.seen = {}


class Dep:
    __slots__ = ("w", "r")

    def __init__(self):
        self.w = None
        self.r = {}


def _wait(eng, reads, writes):
    need = {}

    def add(t, v):
        if t is eng and (eng.name == "pe" or v > eng.cnt):
            return
        if need.get(t, 0) < v:
            need[t] = v
    for d in reads:
        if d.w is not None:
            add(*d.w)
    for d in writes:
        if d.w is not None:
            add(*d.w)
        for t, v in d.r.items():
            add(t, v)
    for t, v in need.items():
        if eng.seen.get(t, 0) < v:
            eng.obj.wait_ge(t.sem, v)
            eng.seen[t] = v


def op(eng, fn, reads=(), writes=(), inc=True):
    _wait(eng, reads, writes)
    ins = fn()
    if inc:
        ins.then_inc(eng.sem, 1)
        eng.cnt += 1
        val = eng.cnt
    else:
        val = eng.cnt + 1
    for d in reads:
        if d.r.get(eng, 0) < val:
            d.r[eng] = val
    for d in writes:
        d.w = (eng, val)
        d.r = {}
    return ins


def dma(eng, dsem, out, in_, reads=(), writes=(), **kw):
    _wait(eng, reads, writes)
    ins = eng.obj.dma_start(out=out, in_=in_, **kw)
    ins.then_inc(dsem.sem, 16)
    dsem.cnt += 16
    val = dsem.cnt
    for d in reads:
        if d.r.get(dsem, 0) < val:
            d.r[dsem] = val
    for d in writes:
        d.w = (dsem, val)
        d.r = {}
    return ins


def split_tiles(n, mx=512):
    k = -(-n // mx)
    base = -(-n // k)
    out = []
    s = 0
    while s < n:
        w = min(base, n - s)
        out.append((s, w))
        s += w
    return out


class K:
    pass


def build(debug=None):
    debug = debug or {}
    nc = bass.Bass("TRN2", target_bir_lowering=False)
    g = K()
    g.nc = nc
    g.dbg_outs = {}
    shapes_in = dict(
        x_prompt=[2048, D], x_sample=[128, D],
        state_delta=[NL, 16, 4, 64, 64], state_delta_conv=[NL, 16, 3, 768], state_s5_re=[NL, 16, 1024],
        state_s5_im=[NL, 16, 1024], state_lru=[NL, 16, 256], state_lru_conv=[NL, 16, 3, 256], state_conv=[NL, 16, 30, 256],
        meta_tokens=[16, D], ffn1_norm=[NL, D], ffn1_w_gu=[NL, D, 2 * FF], ffn1_w_down=[NL, FF, D], mix_norm=[NL, D],
        w_in=[NL, D, N_IN], dn_conv_w=[NL, 4, 768], dn_a_log=[NL, 4], dn_dt_bias=[NL, 4], dn_norm=[NL, 64],
        s5_lam_re=[NL, 1024], s5_lam_im=[NL, 1024], s5_log_step=[NL, 16], s5_b_re=[NL, 1024, 16], s5_b_im=[NL, 1024, 16],
        s5_c_re=[NL, 256, 64], s5_c_im=[NL, 256, 64], s5_d=[NL, 256], s5_w_glu=[NL, 256, 512], s5_b_glu=[NL, 512],
        lru_conv_w=[NL, 4, 256], lru_conv_b=[NL, 256], lru_w_a=[NL, 256, 64], lru_b_a=[NL, 256], lru_w_x=[NL, 256, 64],
        lru_b_x=[NL, 256], lru_lam=[NL, 256], cv_conv_w=[NL, 31, 256], cv_conv_b=[NL, 256], cv_ln_g=[NL, 256],
        cv_ln_b=[NL, 256], w_branch=[NL, 4, 256, D], w_out=[NL, D, D], ffn2_norm=[NL, D], ffn2_w_gu=[NL, D, 2 * FF],
        ffn2_w_down=[NL, FF, D], final_norm=[1, D])
    shapes_out = dict(
        y_prompt=[2048, D], y_sample=[128, D], p_delta=[NL, 4, 64, 64], p_delta_conv=[NL, 3, 768], p_s5_re=[NL, 1024],
        p_s5_im=[NL, 1024], p_lru=[NL, 256], p_lru_conv=[NL, 3, 256], p_conv=[NL, 30, 256],
        s_delta=[NL, 16, 4, 64, 64], s_delta_conv=[NL, 16, 3, 768], s_s5_re=[NL, 16, 1024], s_s5_im=[NL, 16, 1024],
        s_lru=[NL, 16, 256], s_lru_conv=[NL, 16, 3, 256], s_conv=[NL, 16, 30, 256])
    g.I = {k: nc.dram_tensor(k, v, F32, kind="ExternalInput").ap() for k, v in shapes_in.items()}
    g.O = {k: nc.dram_tensor("o_" + k, v, F32, kind="ExternalOutput").ap() for k, v in shapes_out.items()}
    g.shapes_in = shapes_in
    g.shapes_out = shapes_out

    with ExitStack() as es:
        g.es = es
        _emit(g, debug)
    return nc, g


def _emit(g, debug):
    nc, es = g.nc, g.es
    I, O = g.I, g.O
    cnt = [0]

    def sb(shape, dt=F32, name=None):
        cnt[0] += 1
        return es.enter_context(nc.sbuf_tensor(name or f"t{cnt[0]}", shape, dt))

    def sem(name):
        return es.enter_context(nc.semaphore(name))

    PE = Trk("pe", nc.tensor, sem("s_pe"))
    ACT = Trk("act", nc.scalar, sem("s_act"))
    DVE = Trk("dve", nc.vector, sem("s_dve"))
    POOL = Trk("pool", nc.gpsimd, sem("s_pool"))
    SP = Trk("sp", nc.sync, sem("s_sp"))
    g.PE, g.ACT, g.DVE, g.POOL, g.SP = PE, ACT, DVE, POOL, SP
    engines = [PE, ACT, DVE, POOL, SP]
    dsems = []

    def dsem(name):
        t = Trk(name, None, sem(name))
        dsems.append(t)
        return t

    ld_sems = [dsem(f"ld{i}") for i in range(8)]
    st_sems = [dsem(f"st{i}") for i in range(4)]
    rr = dict(ld=0, st=0)

    def ldsem():
        rr['ld'] += 1
        return ld_sems[rr['ld'] % len(ld_sems)]

    def stsem():
        rr['st'] += 1
        return st_sems[rr['st'] % len(st_sems)]

    def load(out, in_, writes, reads=(), **kw):
        return dma(SP, ldsem(), out, in_, reads=reads, writes=writes, **kw)

    def store(out, in_, reads, **kw):
        return dma(SP, stsem(), out, in_, reads=reads, **kw)

    def barrier():
        for e in engines:
            for t in engines + dsems:
                if t is e:
                    continue
                if t.cnt > 0 and e.seen.get(t, 0) < t.cnt:
                    e.obj.wait_ge(t.sem, t.cnt)
                    e.seen[t] = t.cnt

    banks = [es.enter_context(nc.psum_tensor(f"bank{i}", [128, 512], F32)) for i in range(8)]
    bank_dep = [Dep() for _ in range(8)]
    bk = [0]

    def pbank():
        i = bk[0] % 8
        bk[0] += 1
        return banks[i], bank_dep[i]

    def dump(name, ap, shape, dep_list):
        if name not in debug:
            return
        t = nc.dram_tensor("dbg_" + name, shape, ap.dtype, kind="ExternalOutput").ap()
        g.dbg_outs[name] = t
        store(t, ap, reads=dep_list)

    it_i = sb([128, 128], I32)
    itf = sb([128, 128])
    ident = sb([128, 128])
    ones_bf = sb([128, 128], BF16)
    d_const = Dep()
    op(POOL, lambda: nc.gpsimd.iota(it_i[:], pattern=[[1, 128]], base=0, channel_multiplier=-1), writes=[d_const])
    op(DVE, lambda: nc.vector.tensor_copy(out=itf[:], in_=it_i[:]), reads=[d_const], writes=[d_const])
    op(DVE, lambda: nc.vector.tensor_single_scalar(out=ident[:], in_=itf[:], scalar=0.0, op=ALU.is_equal), reads=[d_const], writes=[d_const])
    op(DVE, lambda: nc.vector.memset(ones_bf[:], 1.0), writes=[d_const])
    g.ident, g.itf, g.d_const = ident, itf, d_const
    ones_f = sb([128, 128], F32)
    op(DVE, lambda: nc.vector.memset(ones_f[:], 1.0), writes=[d_const])

    xT = sb([128, KC, TSM], F32, "xT")
    uT = sb([128, KC, TSM], BF16, "uT")
    brT = sb([128, 8, TSM], BF16, "brT")
    ARENA_F32 = 19712
    arena = sb([128, ARENA_F32], F32, "arena")
    MAXT = 3
    x_dep = [[Dep() for _ in range(MAXT)] for _ in range(KC)]
    u_dep = [[Dep() for _ in range(MAXT)] for _ in range(KC)]
    br_dep = [[Dep() for _ in range(MAXT)] for _ in range(8)]

    gains = sb([128, KC, 8], F32, "gains")
    d_gain = Dep()
    GIDX = dict(ffn1=0, mix=2, ffn2=4)

    stg = [sb([128, D], F32, f"stg{i}") for i in range(2)]
    stg_dep = [Dep() for _ in range(2)]
    sg = [0]

    def staging():
        i = sg[0] % 2
        sg[0] += 1
        return stg[i], stg_dep[i]

    NSLOT = 6
    SLOT_ELEMS = 2048
    wslots = [sb([128, SLOT_ELEMS], BF16, f"wslot{i}") for i in range(NSLOT)]
    wslot_dep = [Dep() for _ in range(NSLOT)]
    wslot_sem = [dsem(f"ws{i}") for i in range(NSLOT)]
    ws = [0]

    def wslot():
        i = ws[0] % NSLOT
        ws[0] += 1
        return wslots[i], wslot_dep[i], wslot_sem[i]

    def wload(slot_ap, src_ap, sdep, ssem):
        return dma(POOL, ssem, slot_ap, src_ap, writes=[sdep])

    NTMP = 4
    tmps = [sb([128, 512], F32, f"tmp{i}") for i in range(NTMP)]
    tmp_dep = [Dep() for _ in range(NTMP)]
    tp = [0]

    def tmp():
        i = tp[0] % NTMP
        tp[0] += 1
        return tmps[i], tmp_dep[i]

    sqs = [sb([128, KC, 512], BF16, f"sq{i}") for i in range(1)] * 2
    sq_dep = [Dep()] * 2
    sqi = [0]

    def load_T(dst_fn, src, R, C, wdeps):
        st, sd = staging()
        if isinstance(src, list):
            r0 = 0
            for (ap, nr) in src:
                load(st[r0:r0 + nr, 0:C], ap, writes=[sd])
                r0 += nr
            assert r0 == R
        else:
            load(st[0:R, 0:C], src, writes=[sd])
        nchunk = -(-C // 128)
        per_bank = max(1, 512 // R)
        c = 0
        while c < nchunk:
            bank, bd = pbank()
            grp = list(range(c, min(nchunk, c + per_bank)))
            for i, cc in enumerate(grp):
                cw = min(128, C - cc * 128)
                op(PE, lambda cc=cc, cw=cw, i=i: nc.tensor.transpose(bank[0:cw, i * R:(i + 1) * R], st[0:R, cc * 128:cc * 128 + cw], ident[0:R, 0:R]),
                   reads=[sd, d_const], writes=[bd], inc=(i == len(grp) - 1))
            for i, cc in enumerate(grp):
                cw = min(128, C - cc * 128)
                op(ACT, lambda cc=cc, cw=cw, i=i: nc.scalar.copy(out=dst_fn(cc), in_=bank[0:cw, i * R:(i + 1) * R]), reads=[bd], writes=wdeps)
            c += per_bank

    def store_T(dst, src_fn, R, C, rdeps):
        st, sd = staging()
        nchunk = -(-C // 128)
        c = 0
        while c < nchunk:
            bank, bd = pbank()
            grp = list(range(c, min(nchunk, c + 4)))
            for i, cc in enumerate(grp):
                cw = min(128, C - cc * 128)
                src = src_fn(cc)
                if len(src.shape) > 2:
                    tt_, ttd = tmp()
                    o_ = tt_[0:cw, 0:R].rearrange("p (a b) -> p a b", a=src.shape[1])
                    op(DVE, lambda o_=o_, src=src: nc.vector.tensor_copy(out=o_, in_=src), reads=list(rdeps), writes=[ttd])
                    op(PE, lambda cw=cw, i=i, tt_=tt_: nc.tensor.transpose(bank[0:R, i * 128:i * 128 + cw], tt_[0:cw, 0:R], ident[0:cw, 0:cw]),
                       reads=[ttd, d_const], writes=[bd], inc=(i == len(grp) - 1))
                    continue
                op(PE, lambda cc=cc, cw=cw, i=i: nc.tensor.transpose(bank[0:R, i * 128:i * 128 + cw], src_fn(cc), ident[0:cw, 0:cw]),
                   reads=list(rdeps) + [d_const], writes=[bd], inc=(i == len(grp) - 1))
            w = min(C - c * 128, 512)
            op(ACT, lambda c=c, w=w: nc.scalar.copy(out=st[0:R, c * 128:c * 128 + w], in_=bank[0:R, 0:w]), reads=[bd], writes=[sd])
            c += 4
        store(dst, st[0:R, 0:C], reads=[sd])

    def rmsnorm_to(tiles, gidx, out_fn, out_deps_fn, final=False):
        n = len(tiles)
        SQ_OFF = 10000
        sqv = [arena[:, SQ_OFF + i * 2048:SQ_OFF + (i + 1) * 2048].bitcast(BF16).rearrange("p (c t) -> p c t", c=KC) for i in range(n)]
        sqd = [Dep() for _ in range(n)]
        bl, rl = [], []
        for ti, (s, w) in enumerate(tiles):
            op(ACT, lambda ti=ti, s=s, w=w: nc.scalar.activation(out=sqv[ti][:, :, 0:w], in_=xT[:, :, s:s + w], func=AF.Square),
               reads=[x_dep[c][ti] for c in range(KC)], writes=[sqd[ti]])
        for ti, (s, w) in enumerate(tiles):
            bank, bd = pbank()
            bl.append((bank, bd))
            for c in range(KC):
                op(PE, lambda c=c, ti=ti, w=w, bank=bank: nc.tensor.matmul(bank[:, 0:w], lhsT=ones_bf[:], rhs=sqv[ti][:, c, 0:w], start=(c == 0), stop=(c == KC - 1)),
                   reads=[sqd[ti], d_const], writes=[bd], inc=(c == KC - 1))
        for ti, (s, w) in enumerate(tiles):
            bank, bd = bl[ti]
            rs, rsd = tmp()
            rl.append((rs, rsd))
            op(ACT, lambda rs=rs, bank=bank, w=w: nc.scalar.activation(out=rs[:, 0:w], in_=bank[:, 0:w], func=AF.Ln, scale=1.0 / D, bias=eps_t[:, 0:1]), reads=[bd, d_const], writes=[rsd])
        for ti, (s, w) in enumerate(tiles):
            rs, rsd = rl[ti]
            op(ACT, lambda rs=rs, w=w: nc.scalar.activation(out=rs[:, 0:w], in_=rs[:, 0:w], func=AF.Exp, scale=-0.5), reads=[rsd], writes=[rsd])
        for ti, (s, w) in enumerate(tiles):
            rs, rsd = rl[ti]
            for c in range(KC):
                op(DVE, lambda c=c, rs=rs, s=s, w=w: nc.vector.scalar_tensor_tensor(out=out_fn(c, s, w), in0=xT[:, c, s:s + w], scalar=gains[:, c, gidx:gidx + 1],
                                                                                    in1=rs[:, 0:w], op0=ALU.mult, op1=ALU.mult),
                   reads=[x_dep[c][ti], rsd, d_gain], writes=out_deps_fn(c, ti))

    load_T(lambda c: gains[:, c, 0:7],
           [(I[nm][l:l + 1, :], 1) for (nm, l) in [("ffn1_norm", 0), ("ffn1_norm", 1), ("mix_norm", 0), ("mix_norm", 1),
                                                    ("ffn2_norm", 0), ("ffn2_norm", 1), ("final_norm", 0)]], 7, D, [d_gain])
    eps_t = sb([128, 1], F32, "eps")
    op(DVE, lambda: nc.vector.memset(eps_t[:], EPS), writes=[d_const])
    g.eps_t = eps_t

    def ffn(l, which, tiles):
        wgu = I[f"{which}_w_gu"][l]
        wdn = I[f"{which}_w_down"][l]
        gidx = GIDX[which] + l
        rmsnorm_to(tiles, gidx, lambda c, s, w: uT[:, c, s:s + w], lambda c, ti: [u_dep[c][ti]])
        hT = arena[:, 0:11 * TSM // 2].bitcast(BF16).rearrange("p (j t) -> p j t", j=11)
        h_dep = [[Dep() for _ in range(MAXT)] for _ in range(11)]
        wgu_v = wgu.rearrange("(k p) n -> p k n", p=128)
        wdn_v = wdn.rearrange("(j p) n -> p j n", p=128)
        for half in range(2):
            for jj in range(11):
                j = half * 11 + jj
                slot, sdep, ssem = wslot()
                sv = slot[:].rearrange("p (a k n) -> p a k n", a=2, k=KC)
                wload(sv[:, 0], wgu_v[:, :, j * 128:(j + 1) * 128], sdep, ssem)
                wload(sv[:, 1], wgu_v[:, :, FF + j * 128:FF + (j + 1) * 128], sdep, ssem)
                for ti, (s, w) in enumerate(tiles):
                    bg, bgd = pbank()
                    bu, bud = pbank()
                    for a, (bank, bd) in enumerate(((bg, bgd), (bu, bud))):
                        for k in range(KC):
                            op(PE, lambda a=a, k=k, bank=bank: nc.tensor.matmul(bank[:, 0:w], lhsT=sv[:, a, k, :], rhs=uT[:, k, s:s + w],
                                                                                   start=(k == 0), stop=(k == KC - 1)),
                               reads=[sdep, u_dep[k][ti]], writes=[bd], inc=(k == KC - 1))
                    t, td = tmp()
                    op(ACT, lambda: nc.scalar.activation(out=t[:, 0:w], in_=bg[:, 0:w], func=AF.Silu), reads=[bgd], writes=[td])
                    op(DVE, lambda: nc.vector.tensor_tensor(out=hT[:, jj, s:s + w], in0=t[:, 0:w], in1=bu[:, 0:w], op=ALU.mult),
                       reads=[td, bud], writes=[h_dep[jj][ti]])
            for m in range(KC):
                slot, sdep, ssem = wslot()
                sv = slot[:, 0:11 * 128].rearrange("p (j n) -> p j n", j=11)
                wload(sv, wdn_v[:, half * 11:(half + 1) * 11, m * 128:(m + 1) * 128], sdep, ssem)
                for ti, (s, w) in enumerate(tiles):
                    bank, bd = pbank()
                    for jj in range(11):
                        op(PE, lambda jj=jj: nc.tensor.matmul(bank[:, 0:w], lhsT=sv[:, jj, :], rhs=hT[:, jj, s:s + w], start=(jj == 0), stop=(jj == 10)),
                           reads=[sdep, h_dep[jj][ti]], writes=[bd], inc=(jj == 10))
                    op(DVE, lambda: nc.vector.scalar_tensor_tensor(out=xT[:, m, s:s + w], in0=bank[:, 0:w], scalar=0.5, in1=xT[:, m, s:s + w],
                                                                   op0=ALU.mult, op1=ALU.add),
                       reads=[bd, x_dep[m][ti]], writes=[x_dep[m][ti]])


    w_in_v = [I["w_in"][l].rearrange("(k p) n -> p k n", p=128) for l in range(NL)]

    class Arena:
        def __init__(self):
            self.off = 0

        def take(self, n):
            a = arena[:, self.off:self.off + n]
            self.off += n
            assert self.off <= ARENA_F32, self.off
            return a

    def proj_in(l, off, ncols, tiles, evac):
        slot, sdep, ssem = wslot()
        sv = slot[:, 0:KC * ncols].rearrange("p (k n) -> p k n", k=KC)
        wload(sv, w_in_v[l][:, :, off:off + ncols], sdep, ssem)
        for ti, (s, w) in enumerate(tiles):
            bank, bd = pbank()
            for k in range(KC):
                op(PE, lambda k=k: nc.tensor.matmul(bank[0:ncols, 0:w], lhsT=sv[:, k, :], rhs=uT[:, k, s:s + w], start=(k == 0), stop=(k == KC - 1)),
                   reads=[sdep, u_dep[k][ti]], writes=[bd], inc=(k == KC - 1))
            evac(bank, bd, ti, s, w)

    def pieces_of(name):
        if name == "A":
            return [("p", 0, 1, TA)]
        return [("p", 0, 1, 1024), ("s", 1024, 16, 8)]

    def scatter(pieces, s, w, fn):
        for pi, (kind, col0, nb, T) in enumerate(pieces):
            lo, hi = max(s, col0), min(s + w, col0 + nb * T)
            if lo >= hi:
                continue
            if kind == "s":
                assert lo == col0 and hi == col0 + nb * T
            fn(pi, pieces[pi], lo - col0, hi - lo, lo - s)

    def xf_view(xf, kind, nb, T, Kt, c, p_off, n):
        if kind == "p":
            return xf[:, c, 0, Kt + p_off:Kt + p_off + n]
        return xf[:, c, :, Kt:Kt + T]

    def src_view(ap2d, kind, nb, T):
        if kind == "p":
            return ap2d
        return ap2d.rearrange("p (b t) -> p b t", b=nb)

    cvtail = [sb([128, 2, 30], F32, f"cvtail{l}") for l in range(NL)]
    lrutail = [sb([128, 2, 3], F32, f"lrutail{l}") for l in range(NL)]
    lruh = [sb([128, 2], F32, f"lruh{l}") for l in range(NL)]
    st_dep = Dep()

    def load_small(dst_fn, srcs, R, C, deps):
        load_T(dst_fn, srcs, R, C, deps)

    def conv_taps(xf_p, acc_p, wv, bv, c, Kw, T, reads, writes):
        sl = lambda j: xf_p[..., j:j + T]
        op(DVE, lambda: nc.vector.tensor_scalar(out=acc_p, in0=sl(0), scalar1=wv[:, c, 0:1], scalar2=bv[:, c:c + 1], op0=ALU.mult, op1=ALU.add),
           reads=reads, writes=writes)
        for j in range(1, Kw):
            op(DVE, lambda j=j: nc.vector.scalar_tensor_tensor(out=acc_p, in0=sl(j), scalar=wv[:, c, j:j + 1], in1=acc_p, op0=ALU.mult, op1=ALU.add),
               reads=list(reads) + list(writes), writes=writes)

    def branch_conv(l, name, TS, tiles):
        pieces = pieces_of(name)
        A = Arena()
        dW = Dep()
        wv = A.take(2 * 32).rearrange("p (c j) -> p c j", c=2)
        prm = A.take(8).rearrange("p (c j) -> p c j", c=2)
        load_T(lambda c: wv[:, c, 0:31], I["cv_conv_w"][l], 31, 256, [dW])
        load_T(lambda c: prm[:, c, 0:3], [(I["cv_conv_b"][l:l + 1, :], 1), (I["cv_ln_g"][l:l + 1, :], 1), (I["cv_ln_b"][l:l + 1, :], 1)], 3, 256, [dW])
        xfs, accs, dxf = [], [], []
        for (kind, col0, nb, T) in pieces:
            xfs.append(A.take(2 * nb * (30 + T)).rearrange("p (c b t) -> p c b t", c=2, b=nb))
            dxf.append([Dep(), Dep()])
        acc = A.take(2 * TS).rearrange("p (c t) -> p c t", c=2)
        dacc = [[Dep() for _ in tiles] for _ in range(2)]
        for pi, (kind, col0, nb, T) in enumerate(pieces):
            for c in range(2):
                if kind == "p" and name == "A":
                    op(DVE, lambda c=c, pi=pi: nc.vector.memset(xfs[pi][:, c, :, 0:30], 0.0), writes=[dxf[pi][c]])
                elif kind == "p":
                    op(DVE, lambda c=c, pi=pi: nc.vector.tensor_copy(out=xfs[pi][:, c, 0, 0:30], in_=cvtail[l][:, c, :]), reads=[st_dep], writes=[dxf[pi][c]])
            if kind == "s":
                for b0 in range(0, 16, 4):
                    load_T(lambda c, b0=b0, pi=pi: xfs[pi][:, c, b0:b0 + 4, 0:30], I["state_conv"][l, b0:b0 + 4].rearrange("b j c -> (b j) c"), 120, 256,
                           [dxf[pi][0], dxf[pi][1]])
        sgt = A.take(TS)
        dsg = [Dep() for _ in tiles]
        for c in range(2):

            def ev_gate(bank, bd, ti, s, w):
                op(ACT, lambda: nc.scalar.activation(out=sgt[:, s:s + w], in_=bank[:, 0:w], func=AF.Sigmoid), reads=[bd], writes=[dsg[ti]])

            def ev_val(bank, bd, ti, s, w, c=c):
                def f(pi, piece, p_off, n, b_off):
                    kind, col0, nb, T = piece
                    op(DVE, lambda: nc.vector.tensor_tensor(out=xf_view(xfs[pi], kind, nb, T, 30, c, p_off, n),
                                                            in0=src_view(bank[:, b_off:b_off + n], kind, nb, T),
                                                            in1=src_view(sgt[:, s + b_off:s + b_off + n], kind, nb, T), op=ALU.mult),
                       reads=[bd, dsg[ti]], writes=[dxf[pi][c]])
                scatter(pieces, s, w, f)
            proj_in(l, OFF["cgate"] + c * 128, 128, tiles, ev_gate)
            proj_in(l, OFF["cval"] + c * 128, 128, tiles, ev_val)
        for pi, (kind, col0, nb, T) in enumerate(pieces):
            for c in range(2):
                xf_p = xfs[pi][:, c, 0, :] if kind == "p" else xfs[pi][:, c, :, :]
                acc_p = acc[:, c, col0:col0 + nb * T] if kind == "p" else acc[:, c, col0:col0 + nb * T].rearrange("p (b t) -> p b t", b=nb)
                conv_taps(xf_p, acc_p, wv, prm[:, :, 0], c, 31, T, [dW, dxf[pi][c]], [dacc[c][ti] for ti in range(len(tiles))])
        nt = len(tiles)
        sqs_ = [A.take(2 * 512).rearrange("p (c t) -> p c t", c=2) for _ in range(nt)]
        dsq = [Dep() for _ in range(nt)]
        means = [A.take(512) for _ in range(nt)]
        rstds = [A.take(512) for _ in range(nt)]
        dmean = [Dep() for _ in range(nt)]
        drstd = [Dep() for _ in range(nt)]
        bks = []
        for ti, (s, w) in enumerate(tiles):
            op(ACT, lambda ti=ti, s=s, w=w: nc.scalar.activation(out=sqs_[ti][:, :, 0:w], in_=acc[:, :, s:s + w], func=AF.Square), reads=[dacc[0][ti], dacc[1][ti]], writes=[dsq[ti]])
        for ti, (s, w) in enumerate(tiles):
            b1, b1d = pbank()
            b2, b2d = pbank()
            bks.append((b1, b1d, b2, b2d))
            for c in range(2):
                op(PE, lambda c=c, b1=b1, s=s, w=w: nc.tensor.matmul(b1[:, 0:w], lhsT=ones_f[:], rhs=acc[:, c, s:s + w], start=(c == 0), stop=(c == 1)),
                   reads=[dacc[c][ti], d_const], writes=[b1d], inc=(c == 1))
            for c in range(2):
                op(PE, lambda c=c, b2=b2, ti=ti, w=w: nc.tensor.matmul(b2[:, 0:w], lhsT=ones_f[:], rhs=sqs_[ti][:, c, 0:w], start=(c == 0), stop=(c == 1)),
                   reads=[dsq[ti], d_const], writes=[b2d], inc=(c == 1))
        for ti, (s, w) in enumerate(tiles):
            b1, b1d, b2, b2d = bks[ti]
            op(ACT, lambda ti=ti, b1=b1, w=w: nc.scalar.mul(out=means[ti][:, 0:w], in_=b1[:, 0:w], mul=1.0 / 256), reads=[b1d], writes=[dmean[ti]])
        for ti, (s, w) in enumerate(tiles):
            op(DVE, lambda ti=ti, w=w: nc.vector.tensor_tensor(out=rstds[ti][:, 0:w], in0=means[ti][:, 0:w], in1=means[ti][:, 0:w], op=ALU.mult), reads=[dmean[ti]], writes=[drstd[ti]])
        for ti, (s, w) in enumerate(tiles):
            b1, b1d, b2, b2d = bks[ti]
            op(DVE, lambda ti=ti, b2=b2, w=w: nc.vector.scalar_tensor_tensor(out=rstds[ti][:, 0:w], in0=b2[:, 0:w], scalar=1.0 / 256, in1=rstds[ti][:, 0:w], op0=ALU.mult, op1=ALU.subtract),
               reads=[b2d, drstd[ti]], writes=[drstd[ti]])
        for ti, (s, w) in enumerate(tiles):
            op(ACT, lambda ti=ti, w=w: nc.scalar.activation(out=rstds[ti][:, 0:w], in_=rstds[ti][:, 0:w], func=AF.Ln, bias=eps_t[:, 0:1]), reads=[drstd[ti], d_const], writes=[drstd[ti]])
        for ti, (s, w) in enumerate(tiles):
            op(ACT, lambda ti=ti, w=w: nc.scalar.activation(out=rstds[ti][:, 0:w], in_=rstds[ti][:, 0:w], func=AF.Exp, scale=-0.5), reads=[drstd[ti]], writes=[drstd[ti]])
        for ti, (s, w) in enumerate(tiles):
            for c in range(2):
                op(DVE, lambda c=c, ti=ti, s=s, w=w: nc.vector.tensor_tensor(out=acc[:, c, s:s + w], in0=acc[:, c, s:s + w], in1=means[ti][:, 0:w], op=ALU.subtract),
                   reads=[dacc[c][ti], dmean[ti]], writes=[dacc[c][ti]])
                op(DVE, lambda c=c, ti=ti, s=s, w=w: nc.vector.tensor_tensor(out=acc[:, c, s:s + w], in0=acc[:, c, s:s + w], in1=rstds[ti][:, 0:w], op=ALU.mult),
                   reads=[dacc[c][ti], drstd[ti]], writes=[dacc[c][ti]])
                op(ACT, lambda c=c, s=s, w=w: nc.scalar.activation(out=brT[:, 6 + c, s:s + w], in_=acc[:, c, s:s + w], func=AF.Silu, scale=prm[:, c, 1:2], bias=prm[:, c, 2:3]),
                   reads=[dacc[c][ti], dW], writes=[br_dep[6 + c][ti]])
        for pi, (kind, col0, nb, T) in enumerate(pieces):
            if kind == "p" and name == "A":
                for c in range(2):
                    op(DVE, lambda c=c, pi=pi: nc.vector.tensor_copy(out=cvtail[l][:, c, :], in_=xfs[pi][:, c, 0, T:T + 30]), reads=[dxf[pi][c]], writes=[st_dep])
            elif kind == "p":
                store_T(O["p_conv"][l], lambda c, pi=pi, T=T: xfs[pi][:, c, 0, T:T + 30], 30, 256, dxf[pi])
            else:
                for b0 in range(0, 16, 4):
                    store_T(O["s_conv"][l, b0:b0 + 4].rearrange("b j c -> (b j) c"), lambda c, pi=pi, b0=b0, T=T: xfs[pi][:, c, b0:b0 + 4, T:T + 30], 120, 256, dxf[pi])

    def branch_lru(l, name, TS, tiles):
        pieces = pieces_of(name)
        A = Arena()
        dW = Dep()
        wv = A.take(8).rearrange("p (c j) -> p c j", c=2)
        prm = A.take(12).rearrange("p (c j) -> p c j", c=2)
        load_T(lambda c: wv[:, c, 0:4], I["lru_conv_w"][l], 4, 256, [dW])
        load_T(lambda c: prm[:, c, 0:4], [(I[k][l:l + 1, :], 1) for k in ("lru_conv_b", "lru_b_a", "lru_b_x", "lru_lam")], 4, 256, [dW])
        op(ACT, lambda: nc.scalar.activation(out=prm[:, :, 4:5], in_=prm[:, :, 3:4], func=AF.Exp, scale=-1.0), reads=[dW], writes=[dW])
        op(ACT, lambda: nc.scalar.activation(out=prm[:, :, 4:5], in_=prm[:, :, 4:5], func=AF.Ln, bias=1.0), reads=[dW], writes=[dW])
        op(ACT, lambda: nc.scalar.mul(out=prm[:, :, 4:5], in_=prm[:, :, 4:5], mul=-8.0), reads=[dW], writes=[dW])
        wg = A.take(2 * 2 * 128).rearrange("p (a c n) -> p a c n", a=2, c=2)
        op(DVE, lambda: nc.vector.memset(wg, 0.0), writes=[dW])
        for a, nm in enumerate(("lru_w_a", "lru_w_x")):
            for c in range(2):
                for i in range(2):
                    load(wg[i * 64:(i + 1) * 64, a, c, i * 64:(i + 1) * 64], I[nm][l, (2 * c + i) * 64:(2 * c + i + 1) * 64, :], writes=[dW])
        xfs, dxf = [], []
        for (kind, col0, nb, T) in pieces:
            xfs.append(A.take(2 * nb * (3 + T)).rearrange("p (c b t) -> p c b t", c=2, b=nb))
            dxf.append([Dep(), Dep()])
        mk = lambda: A.take(2 * TS).rearrange("p (c t) -> p c t", c=2)
        xc, ra, ib, hh, gl = mk(), mk(), mk(), mk(), mk()
        dxc, dra, dib, dhh, dgl = [[Dep(), Dep()] for _ in range(5)]
        h0s = A.take(2 * 16).rearrange("p (c b) -> p c b", c=2)
        dh0 = Dep()
        for pi, (kind, col0, nb, T) in enumerate(pieces):
            for c in range(2):
                if kind == "p" and name == "A":
                    op(DVE, lambda c=c, pi=pi: nc.vector.memset(xfs[pi][:, c, :, 0:3], 0.0), writes=[dxf[pi][c]])
                elif kind == "p":
                    op(DVE, lambda c=c, pi=pi: nc.vector.tensor_copy(out=xfs[pi][:, c, 0, 0:3], in_=lrutail[l][:, c, :]), reads=[st_dep], writes=[dxf[pi][c]])
            if kind == "s":
                load_T(lambda c, pi=pi: xfs[pi][:, c, :, 0:3], I["state_lru_conv"][l].rearrange("b j c -> (b j) c"), 48, 256, [dxf[pi][0], dxf[pi][1]])
                load_T(lambda c: h0s[:, c, :], I["state_lru"][l], 16, 256, [dh0])
        for c in range(2):
            def ev_x(bank, bd, ti, s, w, c=c):
                def f(pi, piece, p_off, n, b_off):
                    kind, col0, nb, T = piece
                    op(ACT, lambda: nc.scalar.copy(out=xf_view(xfs[pi], kind, nb, T, 3, c, p_off, n), in_=src_view(bank[:, b_off:b_off + n], kind, nb, T)),
                       reads=[bd], writes=[dxf[pi][c]])
                scatter(pieces, s, w, f)

            def ev_g(bank, bd, ti, s, w, c=c):
                op(ACT, lambda: nc.scalar.activation(out=gl[:, c, s:s + w], in_=bank[:, 0:w], func=AF.Gelu), reads=[bd], writes=[dgl[c]])
            proj_in(l, OFF["lx"] + c * 128, 128, tiles, ev_x)
            proj_in(l, OFF["lg"] + c * 128, 128, tiles, ev_g)
        for pi, (kind, col0, nb, T) in enumerate(pieces):
            for c in range(2):
                xf_p = xfs[pi][:, c, 0, :] if kind == "p" else xfs[pi][:, c, :, :]
                v = lambda t: (t[:, c, col0:col0 + nb * T] if kind == "p" else t[:, c, col0:col0 + nb * T].rearrange("p (b t) -> p b t", b=nb))
                conv_taps(xf_p, v(xc), wv, prm[:, :, 0], c, 4, T, [dW, dxf[pi][c]], [dxc[c]])
        for c in range(2):
            gbk = []
            for ti, (s, w) in enumerate(tiles):
                for a, (dst, dd) in enumerate(((ra, dra), (ib, dib))):
                    bank, bd = pbank()
                    gbk.append((bank, bd, a, dst, dd, s, w))
                    op(PE, lambda a=a, bank=bank, s=s, w=w: nc.tensor.matmul(bank[:, 0:w], lhsT=wg[:, a, c, :], rhs=xc[:, c, s:s + w], start=True, stop=True), reads=[dW, dxc[c]], writes=[bd])
            for (bank, bd, a, dst, dd, s, w) in gbk:
                op(ACT, lambda a=a, dst=dst, bank=bank, s=s, w=w: nc.scalar.activation(out=dst[:, c, s:s + w], in_=bank[:, 0:w], func=AF.Sigmoid, bias=prm[:, c, 1 + a:2 + a]),
                   reads=[bd, dW], writes=[dd[c]])
            op(ACT, lambda: nc.scalar.activation(out=ra[:, c, 0:TS], in_=ra[:, c, 0:TS], func=AF.Exp, scale=prm[:, c, 4:5]), reads=[dra[c], dW], writes=[dra[c]])
            op(DVE, lambda: nc.vector.tensor_tensor(out=ib[:, c, 0:TS], in0=ib[:, c, 0:TS], in1=xc[:, c, 0:TS], op=ALU.mult), reads=[dib[c], dxc[c]], writes=[dib[c]])
            op(DVE, lambda: nc.vector.tensor_tensor(out=hh[:, c, 0:TS], in0=ra[:, c, 0:TS], in1=ra[:, c, 0:TS], op=ALU.mult), reads=[dra[c]], writes=[dhh[c]])
            op(ACT, lambda: nc.scalar.activation(out=hh[:, c, 0:TS], in_=hh[:, c, 0:TS], func=AF.Sqrt, scale=-1.0, bias=1.0), reads=[dhh[c]], writes=[dhh[c]])
            op(DVE, lambda: nc.vector.tensor_tensor(out=ib[:, c, 0:TS], in0=ib[:, c, 0:TS], in1=hh[:, c, 0:TS], op=ALU.mult), reads=[dib[c], dhh[c]], writes=[dib[c]])
            for pi, (kind, col0, nb, T) in enumerate(pieces):
                a3 = ra[:, c, col0:col0 + nb * T].rearrange("p (b t) -> p b t", b=nb)
                b3 = ib[:, c, col0:col0 + nb * T].rearrange("p (b t) -> p b t", b=nb)
                if not (kind == "p" and name == "A"):
                    h0v = lruh[l][:, c:c + 1] if kind == "p" else h0s[:, c, :]
                    hd = st_dep if kind == "p" else dh0
                    t0, t0d = tmp()
                    op(DVE, lambda: nc.vector.tensor_tensor(out=t0[:, 0:nb], in0=a3[:, :, 0], in1=h0v, op=ALU.mult), reads=[dra[c], hd], writes=[t0d])
                    op(DVE, lambda: nc.vector.tensor_tensor(out=b3[:, :, 0], in0=b3[:, :, 0], in1=t0[:, 0:nb], op=ALU.add), reads=[dib[c], t0d], writes=[dib[c]])
                if nb > 1:
                    op(DVE, lambda: nc.vector.memset(a3[:, :, 0:1], 0.0), reads=[dra[c]], writes=[dra[c]])
                op(DVE, lambda: nc.vector.tensor_tensor_scan(out=hh[:, c, col0:col0 + nb * T], data0=ra[:, c, col0:col0 + nb * T], data1=ib[:, c, col0:col0 + nb * T],
                                                             initial=0.0, op0=ALU.mult, op1=ALU.add), reads=[dra[c], dib[c], dhh[c]], writes=[dhh[c]])
            for ti, (s, w) in enumerate(tiles):
                op(DVE, lambda: nc.vector.tensor_tensor(out=brT[:, 4 + c, s:s + w], in0=hh[:, c, s:s + w], in1=gl[:, c, s:s + w], op=ALU.mult),
                   reads=[dhh[c], dgl[c]], writes=[br_dep[4 + c][ti]])
        for pi, (kind, col0, nb, T) in enumerate(pieces):
            if kind == "p" and name == "A":
                for c in range(2):
                    op(DVE, lambda c=c, pi=pi: nc.vector.tensor_copy(out=lrutail[l][:, c, :], in_=xfs[pi][:, c, 0, T:T + 3]), reads=[dxf[pi][c]], writes=[st_dep])
                    op(DVE, lambda c=c: nc.vector.tensor_copy(out=lruh[l][:, c:c + 1], in_=hh[:, c, col0 + T - 1:col0 + T]), reads=[dhh[c]], writes=[st_dep])
            elif kind == "p":
                store_T(O["p_lru_conv"][l], lambda c, pi=pi, T=T: xfs[pi][:, c, 0, T:T + 3], 3, 256, dxf[pi])
                store_T(O["p_lru"][l:l + 1, :], lambda c, col0=col0, T=T: hh[:, c, col0 + T - 1:col0 + T], 1, 256, dhh)
            else:
                store_T(O["s_lru_conv"][l].rearrange("b j c -> (b j) c"), lambda c, pi=pi, T=T: xfs[pi][:, c, :, T:T + 3], 48, 256, dxf[pi])
                store_T(O["s_lru"][l], lambda c, col0=col0, nb=nb, T=T: hh[:, c, col0:col0 + nb * T].rearrange("p (b t) -> p b t", b=nb)[:, :, T - 1], 16, 256, dhh)


    s5h = [sb([128, 8, 2, 1], F32, f"s5h{l}") for l in range(NL)]

    def branch_s5(l, name, TS, tiles):
        import math
        pieces = pieces_of(name)
        A = Arena()
        dW = Dep()
        LT = 130 if name == "A" else 128
        dv = lambda f, reads=(), writes=(): op(DVE, f, reads=list(reads) + [dW], writes=list(writes) + [dW])
        av = lambda f, reads=(), writes=(): op(ACT, f, reads=list(reads) + [dW], writes=list(writes) + [dW])
        lam = A.take(16).rearrange("p (m r) -> p m r", m=8)
        load_T(lambda m: lam[:, m, 0:2], [(I["s5_lam_re"][l:l + 1, :], 1), (I["s5_lam_im"][l:l + 1, :], 1)], 2, 1024, [dW])
        row = A.take(16)
        load(row[0:1, 0:16], I["s5_log_step"][l:l + 1, :], writes=[dW])
        bank, bd = pbank()
        op(PE, lambda: nc.tensor.matmul(bank[:, 0:16], lhsT=ones_f[0:1, :], rhs=row[0:1, 0:16], start=True, stop=True), reads=[dW, d_const], writes=[bd])
        sm = A.take(8 * 16).rearrange("p (m j) -> p m j", m=8)
        V = lambda j: sm[:, :, j]
        DT, TH, MAG, CC, SS, LBR, LBI, CFR, CFI, DEN, T1, T2, T3, NCFI = range(14)
        LR, LI = lam[:, :, 0], lam[:, :, 1]
        for h in range(2):
            av(lambda h=h: nc.scalar.activation(out=sm[h * 64:(h + 1) * 64, :, DT], in_=bank[h * 64:(h + 1) * 64, 0:16].rearrange("p (m x) -> p m x", x=2)[:, :, h], func=AF.Exp), reads=[bd])
        tt = lambda o, a, b, f: dv(lambda: nc.vector.tensor_tensor(out=o, in0=a, in1=b, op=f))
        tt(V(TH), LI, V(DT), ALU.mult)
        tt(V(T1), LR, V(DT), ALU.mult)
        av(lambda: nc.scalar.activation(out=V(MAG), in_=V(T1), func=AF.Exp))
        hp = A.take(1)
        dv(lambda: nc.vector.memset(hp, math.pi / 2))
        av(lambda: nc.scalar.activation(out=V(SS), in_=V(TH), func=AF.Sin, scale=1.0 / 16))
        av(lambda: nc.scalar.activation(out=V(CC), in_=V(TH), func=AF.Sin, scale=1.0 / 16, bias=hp[:, 0:1]))
        for _ in range(4):
            tt(V(T1), V(CC), V(CC), ALU.mult)
            tt(V(T2), V(SS), V(SS), ALU.mult)
            tt(V(T3), V(CC), V(SS), ALU.mult)
            tt(V(CC), V(T1), V(T2), ALU.subtract)
            tt(V(SS), V(T3), V(T3), ALU.add)
        tt(V(LBR), V(MAG), V(CC), ALU.mult)
        tt(V(LBI), V(MAG), V(SS), ALU.mult)
        tt(V(T1), LR, LR, ALU.mult)
        tt(V(T2), LI, LI, ALU.mult)
        tt(V(DEN), V(T1), V(T2), ALU.add)
        dv(lambda: nc.vector.reciprocal(out=V(DEN), in_=V(DEN)))
        dv(lambda: nc.vector.tensor_scalar_add(out=V(T3), in0=V(LBR), scalar1=-1.0))
        tt(V(T1), V(T3), LR, ALU.mult)
        tt(V(T2), V(LBI), LI, ALU.mult)
        tt(V(T1), V(T1), V(T2), ALU.add)
        tt(V(CFR), V(T1), V(DEN), ALU.mult)
        tt(V(T1), V(LBI), LR, ALU.mult)
        tt(V(T2), V(T3), LI, ALU.mult)
        tt(V(T1), V(T1), V(T2), ALU.subtract)
        tt(V(CFI), V(T1), V(DEN), ALU.mult)
        dv(lambda: nc.vector.tensor_scalar_mul(out=V(NCFI), in0=V(CFI), scalar1=-1.0))
        Bre = A.take(128).rearrange("p (m c) -> p m c", m=8)
        Bim = A.take(128).rearrange("p (m c) -> p m c", m=8)
        Bp = A.take(128).rearrange("p (m c) -> p m c", m=8)
        tb = A.take(16)
        for m in range(8):
            load(Bre[:, m, :], I["s5_b_re"][l, m * 128:(m + 1) * 128, :], writes=[dW])
            load(Bim[:, m, :], I["s5_b_im"][l, m * 128:(m + 1) * 128, :], writes=[dW])
        BT = A.take(8 * 2 * 128).rearrange("p (m r n) -> p m r n", m=8, r=2)
        CT = A.take(8 * 2 * 128).rearrange("p (m r n) -> p m r n", m=8, r=2)
        E = A.take(8 * 128).rearrange("p (m n) -> p m n", m=8)
        for ri in range(2):
            dv(lambda: nc.vector.memset(E, 0.0))
            for m in range(8):
                X1, X2, sc2 = (Bre, Bim, NCFI) if ri == 0 else (Bim, Bre, CFI)
                dv(lambda m=m, X2=X2, sc2=sc2: nc.vector.tensor_scalar_mul(out=tb, in0=X2[:, m, :], scalar1=sm[:, m, sc2:sc2 + 1]))
                dv(lambda m=m, X1=X1: nc.vector.scalar_tensor_tensor(out=Bp[:, m, :], in0=X1[:, m, :], scalar=sm[:, m, CFR:CFR + 1], in1=tb, op0=ALU.mult, op1=ALU.add))
                for h in range(2):
                    o0 = (m % 4) * 32 + h * 16
                    dv(lambda m=m, h=h, o0=o0: nc.vector.tensor_copy(out=E[h * 64:(h + 1) * 64, m, o0:o0 + 16], in_=Bp[h * 64:(h + 1) * 64, m, :]))
            for m in range(8):
                bk, bkd = pbank()
                op(PE, lambda m=m: nc.tensor.transpose(bk[:, 0:128], E[:, m, :], ident[:]), reads=[dW, d_const], writes=[bkd])
                av(lambda m=m, bk=bk: nc.scalar.copy(out=BT[:, m, ri, :], in_=bk[:, 0:128]), reads=[bkd])
        dv(lambda: nc.vector.memset(CT, 0.0))
        for ri, nm in enumerate(("s5_c_re", "s5_c_im")):
            for rb in range(2):
                st, sd = staging()
                load(st[:, 0:64], I[nm][l, rb * 128:(rb + 1) * 128, :], writes=[sd])
                load(st[:, 64:128], I[nm][l, rb * 128:(rb + 1) * 128, :], writes=[sd])
                bk, bkd = pbank()
                op(PE, lambda: nc.tensor.transpose(bk[:, 0:128], st[:, 0:128], ident[:]), reads=[sd, d_const], writes=[bkd])
                for m in range(rb * 4, rb * 4 + 4):
                    for h in range(2):
                        o0 = (m % 4) * 32 + h * 16
                        av(lambda m=m, h=h, o0=o0, bk=bk: nc.scalar.mul(out=CT[h * 64:(h + 1) * 64, m, ri, o0:o0 + 16], in_=bk[h * 64:(h + 1) * 64, o0:o0 + 16],
                                                                        mul=(1.0 if ri == 0 else -1.0)), reads=[bkd])
        cosT = A.take(8 * LT).rearrange("p (m t) -> p m t", m=8)
        sinT = A.take(8 * LT).rearrange("p (m t) -> p m t", m=8)
        rho = A.take(8 * LT).rearrange("p (m t) -> p m t", m=8)
        rhos = A.take(8 * 128).rearrange("p (m t) -> p m t", m=8)
        ta = A.take(8 * 65).rearrange("p (m t) -> p m t", m=8)
        tb2 = A.take(8 * 65).rearrange("p (m t) -> p m t", m=8)
        dv(lambda: nc.vector.tensor_copy(out=cosT[:, :, 0:1], in_=sm[:, :, CC:CC + 1]))
        dv(lambda: nc.vector.tensor_copy(out=sinT[:, :, 0:1], in_=sm[:, :, SS:SS + 1]))
        n = 1
        while n < LT:
            k = min(n, LT - n)
            cn = cosT[:, :, n - 1:n].to_broadcast([128, 8, k])
            sn = sinT[:, :, n - 1:n].to_broadcast([128, 8, k])
            tt(ta[:, :, 0:k], cosT[:, :, 0:k], cn, ALU.mult)
            tt(tb2[:, :, 0:k], sinT[:, :, 0:k], sn, ALU.mult)
            tt(cosT[:, :, n:n + k], ta[:, :, 0:k], tb2[:, :, 0:k], ALU.subtract)
            tt(ta[:, :, 0:k], sinT[:, :, 0:k], cn, ALU.mult)
            tt(tb2[:, :, 0:k], cosT[:, :, 0:k], sn, ALU.mult)
            tt(sinT[:, :, n:n + k], ta[:, :, 0:k], tb2[:, :, 0:k], ALU.add)
            n += k
        dv(lambda: nc.vector.tensor_copy(out=rho, in_=sm[:, :, MAG:MAG + 1].to_broadcast([128, 8, LT])))
        dv(lambda: nc.vector.tensor_copy(out=rhos, in_=sm[:, :, MAG:MAG + 1].to_broadcast([128, 8, 128])))
        dv(lambda: nc.vector.memset(rhos.rearrange("p m (b t) -> p m b t", t=8)[:, :, :, 0:1], 0.0))
        su = A.take(2 * TS).rearrange("p (c t) -> p c t", c=2)
        dsu = Dep()
        for c in range(2):
            proj_in(l, OFF["su"] + c * 128, 128, tiles,
                    lambda bank, bd, ti, s, w, c=c: op(ACT, lambda: nc.scalar.copy(out=su[:, c, s:s + w], in_=bank[:, 0:w]), reads=[bd], writes=[dsu]))
        dsk = A.take(2)
        load_T(lambda c: dsk[:, c:c + 1], I["s5_d"][l:l + 1, :], 1, 256, [dW])
        Hs = A.take(8 * 2 * LT).rearrange("p (m r t) -> p m r t", m=8, r=2)
        dH = Dep()
        xt = [A.take(8 * LT).rearrange("p (m t) -> p m t", m=8) for _ in range(4)]
        dx = Dep()
        hS = A.take(8 * 2 * 16).rearrange("p (m r b) -> p m r b", m=8, r=2)
        dHP = Dep()
        dv(lambda: nc.vector.memset(rho[:, :, 0:1], 0.0))
        for (kind, col0, nb, T) in pieces:
            if kind == "s":
                for ri, nm in enumerate(("state_s5_re", "state_s5_im")):
                    load_T(lambda m, ri=ri: hS[:, m, ri, 0:16], I[nm][l], 16, 1024, [dW])
                hprev = hS
                subs = [(col0, nb, T)]
            else:
                hprev = s5h[l]
                if name == "A":
                    dv(lambda: nc.vector.memset(s5h[l][:], 0.0), writes=[st_dep])
                subs = [(col0 + i * LT, 1, LT) for i in range(T // LT)]
            for (c0, nbb, Tl) in subs:
                nn = nbb * Tl
                assert nn == LT
                gsz = max(1, 512 // nn)
                groups = [list(range(i, min(8, i + gsz))) for i in range(0, 8, gsz)]
                v4 = lambda ap: ap.rearrange("p m (b t) -> p m b t", b=nbb)
                rd = [dx, st_dep, dW, dHP]
                f = lambda o, a, b, g_, extra=(): op(DVE, lambda: nc.vector.tensor_tensor(out=o, in0=a, in1=b, op=g_), reads=rd + list(extra), writes=[dx])
                for grp in groups:
                    g0, gl = grp[0], len(grp)
                    br_, brd = pbank()
                    bi_, bid = pbank()
                    for (bk_, bkd_, ri) in ((br_, brd, 0), (bi_, bid, 1)):
                        for j, m in enumerate(grp):
                            op(PE, lambda m=m, j=j, bk_=bk_, ri=ri: nc.tensor.matmul(bk_[:, j * nn:(j + 1) * nn], lhsT=BT[:, m, ri, :], rhs=su[:, m // 4, c0:c0 + nn], start=True, stop=True),
                               reads=[dW, dsu], writes=[bkd_], inc=(j == gl - 1))
                    pr = br_[:, 0:gl * nn].rearrange("p (m b t) -> p m b t", m=gl, b=nbb)
                    pi_ = bi_[:, 0:gl * nn].rearrange("p (m b t) -> p m b t", m=gl, b=nbb)
                    cv = cosT[:, g0:g0 + gl, 0:Tl].unsqueeze(2).to_broadcast([128, gl, nbb, Tl])
                    sv_ = sinT[:, g0:g0 + gl, 0:Tl].unsqueeze(2).to_broadcast([128, gl, nbb, Tl])
                    Xg = [v4(x[:, g0:g0 + gl, :]) for x in xt]
                    f(Xg[0], pr, cv, ALU.mult, [brd])
                    f(Xg[2], pi_, sv_, ALU.mult, [bid])
                    f(Xg[1], pi_, cv, ALU.mult, [bid])
                    f(Xg[3], pr, sv_, ALU.mult, [brd])
                f(xt[0], xt[0], xt[2], ALU.add)
                f(xt[1], xt[1], xt[3], ALU.subtract)
                X4 = [v4(x) for x in xt]
                for ri in range(2):
                    tsc = xt[2][:, :, 0:nbb]
                    f(tsc, hprev[:, :, ri, 0:nbb], sm[:, :, MAG:MAG + 1].to_broadcast([128, 8, nbb]), ALU.mult)
                    f(X4[ri][:, :, :, 0], X4[ri][:, :, :, 0], tsc, ALU.add)
                rv = rho if nbb == 1 else rhos
                for ri in range(2):
                    op(DVE, lambda ri=ri: nc.vector.tensor_tensor_scan(out=xt[ri].rearrange("p m t -> p (m t)"), data0=rv.rearrange("p m t -> p (m t)"),
                                                                       data1=xt[ri].rearrange("p m t -> p (m t)"), initial=0.0, op0=ALU.mult, op1=ALU.add),
                       reads=[dx, dW], writes=[dx])
                cvA = cosT[:, :, 0:Tl].unsqueeze(2).to_broadcast([128, 8, nbb, Tl])
                svA = sinT[:, :, 0:Tl].unsqueeze(2).to_broadcast([128, 8, nbb, Tl])
                H0, H1 = v4(Hs[:, :, 0, 0:nn]), v4(Hs[:, :, 1, 0:nn])
                fh = lambda o, a, b, g_, wr: op(DVE, lambda: nc.vector.tensor_tensor(out=o, in0=a, in1=b, op=g_), reads=[dx, dH, dW], writes=[wr])
                fh(X4[2], X4[0], cvA, ALU.mult, dx)
                fh(X4[3], X4[1], svA, ALU.mult, dx)
                fh(H0, X4[2], X4[3], ALU.subtract, dH)
                fh(X4[2], X4[0], svA, ALU.mult, dx)
                fh(X4[3], X4[1], cvA, ALU.mult, dx)
                fh(H1, X4[2], X4[3], ALU.add, dH)
                for ri in range(2):
                    op(DVE, lambda ri=ri: nc.vector.tensor_copy(out=hprev[:, :, ri, 0:nbb], in_=v4(Hs[:, :, ri, 0:nn])[:, :, :, Tl - 1]), reads=[dH, dx], writes=[st_dep, dHP])
                for cc in range(2):
                    by, byd = pbank()
                    i = 0
                    for m in range(4 * cc, 4 * cc + 4):
                        for ri in range(2):
                            op(PE, lambda m=m, ri=ri, i=i: nc.tensor.matmul(by[:, 0:nn], lhsT=CT[:, m, ri, :], rhs=Hs[:, m, ri, 0:nn], start=(i == 0), stop=(i == 7)),
                               reads=[dH, dW], writes=[byd], inc=(i == 7))
                            i += 1
                    t0, t0d = tmp()
                    op(DVE, lambda: nc.vector.scalar_tensor_tensor(out=t0[:, 0:nn], in0=su[:, cc, c0:c0 + nn], scalar=dsk[:, cc:cc + 1], in1=by[:, 0:nn], op0=ALU.mult, op1=ALU.add),
                       reads=[dsu, byd, dW], writes=[t0d])
                    op(ACT, lambda: nc.scalar.activation(out=su[:, cc, c0:c0 + nn], in_=t0[:, 0:nn], func=AF.Gelu), reads=[t0d], writes=[dsu])
            if kind == "p" and name == "B":
                for ri, nm in enumerate(("p_s5_re", "p_s5_im")):
                    store_T(O[nm][l:l + 1, :], lambda m, ri=ri: s5h[l][:, m, ri, 0:1], 1, 1024, [st_dep])
            elif kind == "s":
                for ri, nm in enumerate(("s_s5_re", "s_s5_im")):
                    store_T(O[nm][l], lambda m, ri=ri: hS[:, m, ri, 0:16], 16, 1024, [dW, dHP])
        wgl = E.rearrange("p m n -> p (m n)").rearrange("p (k n) -> p k n", k=2)
        load(wgl, I["s5_w_glu"][l].rearrange("(k p) n -> p k n", p=128), writes=[dW])
        bg = A.take(4)
        load_T(lambda c: bg[:, c:c + 1], I["s5_b_glu"][l:l + 1, :], 1, 512, [dW])
        for ti, (s, w) in enumerate(tiles):
            for oc in range(2):
                ba, bad = pbank()
                bb, bbd = pbank()
                for (bk, bkd, o) in ((ba, bad, oc), (bb, bbd, 2 + oc)):
                    for kc in range(2):
                        op(PE, lambda bk=bk, o=o, kc=kc: nc.tensor.matmul(bk[:, 0:w], lhsT=wgl[:, kc, o * 128:(o + 1) * 128], rhs=su[:, kc, s:s + w], start=(kc == 0), stop=(kc == 1)),
                           reads=[dsu, dW], writes=[bkd], inc=(kc == 1))
                t1_, t1d = tmp()
                t2_, t2d = tmp()
                op(ACT, lambda: nc.scalar.activation(out=t1_[:, 0:w], in_=ba[:, 0:w], func=AF.Identity, bias=bg[:, oc:oc + 1]), reads=[bad, dW], writes=[t1d])
                op(ACT, lambda: nc.scalar.activation(out=t2_[:, 0:w], in_=bb[:, 0:w], func=AF.Sigmoid, bias=bg[:, 2 + oc:3 + oc]), reads=[bbd, dW], writes=[t2d])
                op(DVE, lambda: nc.vector.tensor_tensor(out=brT[:, 2 + oc, s:s + w], in0=t1_[:, 0:w], in1=t2_[:, 0:w], op=ALU.mult), reads=[t1d, t2d], writes=[br_dep[2 + oc][ti]])


    dnS = [sb([64, 4, 1, 64], F32, f"dnS{l}") for l in range(NL)]
    dntail = [sb([128, 6, 1, 3], F32, f"dntail{l}") for l in range(NL)]
    dn_wsem = dsem("dnw")

    def branch_dn(l, name, TS, tiles):
        pieces = pieces_of(name)
        A = Arena()
        dW = Dep()
        dv = lambda f, reads=(), writes=(): op(DVE, f, reads=list(reads) + [dW], writes=list(writes) + [dW])
        av = lambda f, reads=(), writes=(): op(ACT, f, reads=list(reads) + [dW], writes=list(writes) + [dW])
        wv = A.take(24).rearrange("p (c j) -> p c j", c=6)
        zb6 = A.take(6)
        load_T(lambda c: wv[:, c, 0:4], I["dn_conv_w"][l], 4, 768, [dW])
        dv(lambda: nc.vector.memset(zb6, 0.0))
        row = A.take(72)
        load(row[0:1, 0:4], I["dn_a_log"][l:l + 1, :], writes=[dW])
        load(row[0:1, 4:8], I["dn_dt_bias"][l:l + 1, :], writes=[dW])
        load(row[0:1, 8:72], I["dn_norm"][l:l + 1, :], writes=[dW])
        prmB = A.take(72)
        bk, bkd = pbank()
        op(PE, lambda: nc.tensor.matmul(bk[:, 0:72], lhsT=ones_f[0:1, :], rhs=row[0:1, 0:72], start=True, stop=True), reads=[dW, d_const], writes=[bkd])
        av(lambda: nc.scalar.copy(out=prmB, in_=bk[:, 0:72]), reads=[bkd])
        negA = A.take(4)
        av(lambda: nc.scalar.activation(out=negA, in_=prmB[:, 0:4], func=AF.Exp))
        dv(lambda: nc.vector.tensor_scalar_mul(out=negA, in0=negA, scalar1=-1.0))
        ngB = prmB[:, 8:72]
        triU, trilS, blk1, blkS, triUs, trilSs, blkones = [A.take(128) for _ in range(7)]
        dv(lambda: nc.vector.tensor_single_scalar(out=triU, in_=itf[:], scalar=0.0, op=ALU.is_ge), reads=[d_const])
        dv(lambda: nc.vector.tensor_single_scalar(out=trilS, in_=itf[:], scalar=0.0, op=ALU.is_lt), reads=[d_const])
        dv(lambda: nc.vector.memset(blk1, 1.0))
        dv(lambda: nc.vector.memset(blkones, 0.0))
        dv(lambda: nc.vector.memset(blkones[0:64, 0:64], 1.0))
        dv(lambda: nc.vector.memset(blkones[64:128, 64:128], 1.0))
        has_s = any(k == "s" for (k, _, _, _) in pieces)
        Ecol = A.take(16)
        if has_s:
            Ei = A.take(128)
            Ef = A.take(128)
            Ef2 = A.take(128)
            op(POOL, lambda: nc.gpsimd.iota(Ei[0:16, :].bitcast(I32), pattern=[[1, 128]], base=0, channel_multiplier=-8), writes=[dW])
            dv(lambda: nc.vector.tensor_copy(out=Ef[0:16, :], in_=Ei[0:16, :].bitcast(I32)))
            dv(lambda: nc.vector.tensor_single_scalar(out=Ef2[0:16, :], in_=Ef[0:16, :], scalar=0.0, op=ALU.is_ge))
            dv(lambda: nc.vector.tensor_single_scalar(out=Ef[0:16, :], in_=Ef[0:16, :], scalar=7.0, op=ALU.is_le))
            dv(lambda: nc.vector.tensor_tensor(out=Ef[0:16, :], in0=Ef[0:16, :], in1=Ef2[0:16, :], op=ALU.mult))
            bk, bkd = pbank()
            op(PE, lambda: nc.tensor.matmul(bk[:, 0:128], lhsT=Ef[0:16, :], rhs=Ef[0:16, :], start=True, stop=True), reads=[dW], writes=[bkd])
            av(lambda: nc.scalar.copy(out=blkS, in_=bk[:, 0:128]), reads=[bkd])
            bk2, bk2d = pbank()
            op(PE, lambda: nc.tensor.transpose(bk2[:, 0:16], Ef[0:16, :], ident[0:16, 0:16]), reads=[dW, d_const], writes=[bk2d])
            av(lambda: nc.scalar.copy(out=Ecol, in_=bk2[:, 0:16]), reads=[bk2d])
            dv(lambda: nc.vector.tensor_tensor(out=triUs, in0=triU, in1=blkS, op=ALU.mult))
            dv(lambda: nc.vector.tensor_tensor(out=trilSs, in0=trilS, in1=blkS, op=ALU.mult))
        wz = A.take(KC * 264 // 2).bitcast(BF16).rearrange("p (k n) -> p k n", k=KC)
        dma(POOL, dn_wsem, wz, w_in_v[l][:, :, OFF["dz"]:OFF["dz"] + 264], writes=[dW])
        qkv = A.take(6 * TS).rearrange("p (c t) -> p c t", c=6)
        dq = [Dep() for _ in range(6)]
        xf1 = [A.take(nb * (3 + T)).rearrange("p (b t) -> p b t", b=nb) for (kind, col0, nb, T) in pieces]
        dxf = [Dep() for _ in pieces]
        tailS = A.take(6 * 16 * 3).rearrange("p (c b j) -> p c b j", c=6, b=16)
        tailN = A.take(6 * 16 * 3).rearrange("p (c b j) -> p c b j", c=6, b=16)
        dtl = Dep()
        if has_s:
            load_T(lambda c: tailS[:, c, :, :], I["state_delta_conv"][l].rearrange("b j c -> (b j) c"), 48, 768, [dtl])
        for c6 in range(6):
            for pi, (kind, col0, nb, T) in enumerate(pieces):
                if kind == "p" and name == "A":
                    dv(lambda pi=pi: nc.vector.memset(xf1[pi][:, :, 0:3], 0.0), writes=[dxf[pi]])
                elif kind == "p":
                    dv(lambda pi=pi, c6=c6: nc.vector.tensor_copy(out=xf1[pi][:, :, 0:3], in_=dntail[l][:, c6, :, :]), reads=[st_dep], writes=[dxf[pi]])
                else:
                    dv(lambda pi=pi, c6=c6: nc.vector.tensor_copy(out=xf1[pi][:, :, 0:3], in_=tailS[:, c6, :, :]), reads=[dtl], writes=[dxf[pi]])

            def ev(bank, bd, ti, s, w):
                def f(pi, piece, p_off, n, b_off):
                    kind, col0, nb, T = piece
                    dst = xf1[pi][:, 0, 3 + p_off:3 + p_off + n] if kind == "p" else xf1[pi][:, :, 3:3 + T]
                    op(ACT, lambda: nc.scalar.copy(out=dst, in_=src_view(bank[:, b_off:b_off + n], kind, nb, T)), reads=[bd], writes=[dxf[pi]])
                scatter(pieces, s, w, f)
            proj_in(l, OFF["dq"] + c6 * 128, 128, tiles, ev)
            for pi, (kind, col0, nb, T) in enumerate(pieces):
                xf_p = xf1[pi][:, 0, :] if kind == "p" else xf1[pi][:, :, :]
                o_p = qkv[:, c6, col0:col0 + nb * T] if kind == "p" else qkv[:, c6, col0:col0 + nb * T].rearrange("p (b t) -> p b t", b=nb)
                conv_taps(xf_p, o_p, wv, zb6, c6, 4, T, [dW, dxf[pi]], [dq[c6]])
                if kind == "p" and name == "A":
                    dv(lambda pi=pi, c6=c6, T=T: nc.vector.tensor_copy(out=dntail[l][:, c6, :, :], in_=xf1[pi][:, :, T:T + 3]), reads=[dxf[pi]], writes=[st_dep])
                else:
                    dv(lambda pi=pi, c6=c6, T=T, nb=nb, kind=kind: nc.vector.tensor_copy(out=(tailN[:, c6, 0:1, :] if kind == "p" else tailS[:, c6, :, :]), in_=xf1[pi][:, :, T:T + 3]),
                       reads=[dxf[pi], dtl], writes=[dtl])
            op(ACT, lambda c6=c6: nc.scalar.activation(out=qkv[:, c6, 0:TS], in_=qkv[:, c6, 0:TS], func=AF.Silu), reads=[dq[c6]], writes=[dq[c6]])
        if name == "B":
            store_T(O["p_delta_conv"][l], lambda c: tailN[:, c, 0, :], 3, 768, [dtl])
            store_T(O["s_delta_conv"][l].rearrange("b j c -> (b j) c"), lambda c: tailS[:, c, :, :], 48, 768, [dtl])
        for c4 in range(4):
            tl = [tmp() for _ in tiles]
            bl = []
            for ti, (s, w) in enumerate(tiles):
                t0, t0d = tl[ti]
                op(ACT, lambda t0=t0, s=s, w=w: nc.scalar.activation(out=t0[:, 0:w], in_=qkv[:, c4, s:s + w], func=AF.Square), reads=[dq[c4]], writes=[t0d])
            for ti, (s, w) in enumerate(tiles):
                t0, t0d = tl[ti]
                bk, bkd = pbank()
                bl.append((bk, bkd))
                op(PE, lambda t0=t0, bk=bk, w=w: nc.tensor.matmul(bk[:, 0:w], lhsT=blkones, rhs=t0[:, 0:w], start=True, stop=True), reads=[t0d, dW], writes=[bkd])
            for ti, (s, w) in enumerate(tiles):
                t0, t0d = tl[ti]
                bk, bkd = bl[ti]
                op(ACT, lambda t0=t0, bk=bk, w=w: nc.scalar.activation(out=t0[:, 0:w], in_=bk[:, 0:w], func=AF.Ln, bias=eps_t[:, 0:1]), reads=[bkd, d_const], writes=[t0d])
            for ti, (s, w) in enumerate(tiles):
                t0, t0d = tl[ti]
                op(ACT, lambda t0=t0, w=w: nc.scalar.activation(out=t0[:, 0:w], in_=t0[:, 0:w], func=AF.Exp, scale=-0.5), reads=[t0d], writes=[t0d])
            for ti, (s, w) in enumerate(tiles):
                t0, t0d = tl[ti]
                op(DVE, lambda t0=t0, s=s, w=w: nc.vector.scalar_tensor_tensor(out=qkv[:, c4, s:s + w], in0=qkv[:, c4, s:s + w], scalar=(0.125 if c4 < 2 else 1.0), in1=t0[:, 0:w],
                                                                               op0=ALU.mult, op1=ALU.mult), reads=[dq[c4], t0d], writes=[dq[c4]])
        dqa = dq
        HB = 4
        mkh = lambda n: [A.take(n) for _ in range(HB)]
        Xb, Nb, NTb, ATb, gBb, wTb, tTb = mkh(128), mkh(128), mkh(128), mkh(128), mkh(128), mkh(128), mkh(128)
        kdb, vnb, ob, o1b = mkh(64), mkh(64), mkh(64), mkh(64)
        qsb = mkh(128)
        dh = [Dep() for _ in range(HB)]
        zsil2 = [A.take(256) for _ in range(2)]
        oat2 = [A.take(256) for _ in range(2)]
        sv42 = [A.take(64).rearrange("p (j h) -> p j h", h=4) for _ in range(2)]
        ktv2 = [A.take(512) for _ in range(2)]
        dC2 = [Dep(), Dep()]
        Ssm = A.take(16 * 64).rearrange("p (b v) -> p b v", b=16)
        kdm = A.take(64)
        dS = Dep()
        CHUNK = debug.get("chunk", 128)

        def prologue(kind, c0, C, par):
            zsil, sv4, ktv, dC = zsil2[par], sv42[par], ktv2[par], dC2[par]
            mU, mL, mB = (triUs, trilSs, blkS) if kind == "s" else (triU, trilS, blk1)
            cd = lambda f, reads=(), writes=(): op(DVE, f, reads=list(reads) + [dC, dW], writes=list(writes) + [dC])
            ca = lambda f, reads=(), writes=(): op(ACT, f, reads=list(reads) + [dC, dW], writes=list(writes) + [dC])
            bz, bzd = pbank()
            for k in range(KC):
                op(PE, lambda k=k: nc.tensor.matmul(bz[0:C, 0:264], lhsT=uT[:, k, c0:c0 + C], rhs=wz[:, k, :], start=(k == 0), stop=(k == KC - 1)),
                   reads=[dW] + [u_dep[k][ti] for ti in range(len(tiles))], writes=[bzd], inc=(k == KC - 1))
            ca(lambda: nc.scalar.activation(out=zsil[0:C, :], in_=bz[0:C, 0:256], func=AF.Silu), reads=[bzd])
            ca(lambda: nc.scalar.activation(out=sv4[0:C, 0, :], in_=bz[0:C, 256:260], func=AF.Sigmoid), reads=[bzd])
            cd(lambda: nc.vector.tensor_tensor(out=sv4[0:C, 8, :], in0=bz[0:C, 260:264], in1=prmB[0:C, 4:8], op=ALU.add), reads=[bzd])
            yield
            cd(lambda: nc.vector.tensor_scalar_mul(out=sv4[0:C, 1, :], in0=sv4[0:C, 0, :], scalar1=-1.0))
            ca(lambda: nc.scalar.activation(out=sv4[0:C, 8, :], in_=sv4[0:C, 8, :], func=AF.Exp))
            ca(lambda: nc.scalar.activation(out=sv4[0:C, 8, :], in_=sv4[0:C, 8, :], func=AF.Ln, bias=1.0))
            yield
            cd(lambda: nc.vector.tensor_tensor(out=sv4[0:C, 2, :], in0=sv4[0:C, 8, :], in1=negA[0:C, :], op=ALU.mult))
            yield
            bg_, bgd = pbank()
            op(PE, lambda: nc.tensor.matmul(bg_[0:C, 0:4], lhsT=mU[0:C, 0:C], rhs=sv4[0:C, 2, :], start=True, stop=True), reads=[dC, dW], writes=[bgd])
            op(PE, lambda: nc.tensor.matmul(bg_[0:C, 4:8], lhsT=mB[0:C, 0:C], rhs=sv4[0:C, 2, :], start=True, stop=True), reads=[dC, dW], writes=[bgd])
            ca(lambda: nc.scalar.copy(out=sv4[0:C, 3:5, :], in_=bg_[0:C, 0:8].rearrange("p (j h) -> p j h", h=4)), reads=[bgd])
            bkt, bktd = pbank()
            for j, c6 in enumerate((2, 3, 4, 5)):
                op(PE, lambda j=j, c6=c6: nc.tensor.transpose(bkt[0:C, j * 128:(j + 1) * 128], qkv[:, c6, c0:c0 + C], ident[:]), reads=[dqa[c6], d_const], writes=[bktd], inc=(j == 3))
            ca(lambda: nc.scalar.copy(out=ktv[0:C, :], in_=bkt[0:C, :]), reads=[bktd])
            yield
            ca(lambda: nc.scalar.activation(out=sv4[0:C, 5, :], in_=sv4[0:C, 3, :], func=AF.Exp))
            cd(lambda: nc.vector.tensor_tensor(out=sv4[0:C, 7, :], in0=sv4[0:C, 4, :], in1=sv4[0:C, 3, :], op=ALU.subtract))
            yield
            cd(lambda: nc.vector.tensor_tensor(out=sv4[0:C, 6, :], in0=sv4[0:C, 0, :], in1=sv4[0:C, 5, :], op=ALU.mult))
            ca(lambda: nc.scalar.activation(out=sv4[0:C, 7, :], in_=sv4[0:C, 7, :], func=AF.Exp))

        def head(kind, c0, C, par, h, nbb, Tq):
            zsil, sv4, ktv, dC, oat = zsil2[par], sv42[par], ktv2[par], dC2[par], oat2[par]
            mU, mL, mB = (triUs, trilSs, blkS) if kind == "s" else (triU, trilS, blk1)
            levels = max(1, (Tq - 1).bit_length())
            hc, hp = h // 2, (h % 2) * 64
            hd = dh[h]
            hdv = lambda f, reads=(), writes=(): op(DVE, f, reads=list(reads) + [hd, dC, dW], writes=list(writes) + [hd])
            hav = lambda f, reads=(), writes=(): op(ACT, f, reads=list(reads) + [hd, dC, dW], writes=list(writes) + [hd])
            hpe = lambda f, reads=(), writes=(), inc=True: op(PE, f, reads=list(reads) + [hd, dC, dW], writes=list(writes), inc=inc)
            X, N, NT, AT, gB, wT, tT, kd, vn, oo, o1 = Xb[h], Nb[h], NTb[h], ATb[h], gBb[h], wTb[h], tTb[h], kdb[h], vnb[h], ob[h], o1b[h]
            qT = qkv[hp:hp + 64, 0 + hc, c0:c0 + C]
            kT = qkv[hp:hp + 64, 2 + hc, c0:c0 + C]
            ktok = ktv[0:C, hc * 128 + hp:hc * 128 + hp + 64]
            vtok = ktv[0:C, (2 + hc) * 128 + hp:(2 + hc) * 128 + hp + 64]
            hdv(lambda: nc.vector.tensor_copy(out=gB[0:C, :], in_=sv4[0:C, 2, h:h + 1].to_broadcast([C, 128])))
            hdv(lambda: nc.vector.tensor_scalar_mul(out=X[0:C, 0:64], in0=ktok, scalar1=sv4[0:C, 6, h:h + 1]))
            hdv(lambda: nc.vector.tensor_scalar_mul(out=X[0:C, 64:128], in0=vtok, scalar1=sv4[0:C, 0, h:h + 1]))
            hdv(lambda: nc.vector.tensor_scalar_mul(out=kd[0:C, :], in0=ktok, scalar1=sv4[0:C, 7, h:h + 1]))
            yield
            b1, b1d = pbank()
            hpe(lambda: nc.tensor.matmul(b1[0:C, 0:C], lhsT=gB[0:C, 0:C], rhs=mU[0:C, 0:C], start=True, stop=True), writes=[b1d])
            hpe(lambda: nc.tensor.matmul(b1[0:64, 128:128 + C], lhsT=gB[0:C, 0:64], rhs=mB[0:C, 0:C], start=True, stop=True), writes=[b1d])
            hdv(lambda: nc.vector.tensor_scalar(out=N[0:C, 0:C], in0=b1[0:C, 0:C], scalar1=sv4[0:C, 3, h:h + 1], scalar2=0.0, op0=ALU.subtract, op1=ALU.max), reads=[b1d])
            hdv(lambda: nc.vector.tensor_scalar(out=AT[0:C, 0:C], in0=b1[0:C, 0:C], scalar1=sv4[0:C, 3, h:h + 1], scalar2=0.0, op0=ALU.subtract, op1=ALU.min), reads=[b1d])
            hav(lambda: nc.scalar.activation(out=tT[0:64, 0:C], in_=b1[0:64, 128:128 + C], func=AF.Exp), reads=[b1d])
            bq, bqd = pbank()
            hpe(lambda: nc.tensor.matmul(bq[0:64, 0:C], lhsT=ident[:, hp:hp + 64], rhs=qkv[:, hc, c0:c0 + C], start=True, stop=True), reads=[dqa[hc], d_const], writes=[bqd])
            qs = qsb[h]
            hav(lambda: nc.scalar.copy(out=qs[0:64, 0:C], in_=bq[0:64, 0:C]), reads=[bqd])
            yield
            hav(lambda: nc.scalar.activation(out=N[0:C, 0:C], in_=N[0:C, 0:C], func=AF.Exp, scale=-1.0))
            hav(lambda: nc.scalar.activation(out=AT[0:C, 0:C], in_=AT[0:C, 0:C], func=AF.Exp))
            yield
            hdv(lambda: nc.vector.tensor_tensor(out=N[0:C, 0:C], in0=N[0:C, 0:C], in1=mL[0:C, 0:C], op=ALU.mult))
            hdv(lambda: nc.vector.tensor_tensor(out=AT[0:C, 0:C], in0=AT[0:C, 0:C], in1=mU[0:C, 0:C], op=ALU.mult))
            b2, b2d = pbank()
            op(PE, lambda: nc.tensor.matmul(b2[0:C, 0:C], lhsT=kT, rhs=kT, start=True, stop=True), reads=[dqa[2 + hc]], writes=[b2d])
            op(PE, lambda: nc.tensor.matmul(b2[0:C, 128:128 + C], lhsT=kT, rhs=qT, start=True, stop=True), reads=[dqa[2 + hc], dqa[hc]], writes=[b2d])
            hdv(lambda: nc.vector.scalar_tensor_tensor(out=N[0:C, 0:C], in0=b2[0:C, 0:C], scalar=sv4[0:C, 1, h:h + 1], in1=N[0:C, 0:C], op0=ALU.mult, op1=ALU.mult), reads=[b2d])
            hdv(lambda: nc.vector.tensor_tensor(out=AT[0:C, 0:C], in0=b2[0:C, 128:128 + C], in1=AT[0:C, 0:C], op=ALU.mult), reads=[b2d])
            yield
            b3, b3d = pbank()
            hpe(lambda: nc.tensor.transpose(b3[0:C, 0:C], N[0:C, 0:C], ident[0:C, 0:C]), reads=[d_const], writes=[b3d])
            hav(lambda: nc.scalar.copy(out=NT[0:C, 0:C], in_=b3[0:C, 0:C]), reads=[b3d])
            yield
            for lev in range(levels):
                b4, b4d = pbank()
                hpe(lambda: nc.tensor.matmul(b4[0:C, 0:128], lhsT=NT[0:C, 0:C], rhs=X[0:C, :], start=True, stop=True), writes=[b4d])
                if lev < levels - 1:
                    hpe(lambda: nc.tensor.matmul(b4[0:C, 128:128 + C], lhsT=NT[0:C, 0:C], rhs=N[0:C, 0:C], start=True, stop=True), writes=[b4d])
                    hpe(lambda: nc.tensor.matmul(b4[0:C, 256:256 + C], lhsT=N[0:C, 0:C], rhs=NT[0:C, 0:C], start=True, stop=True), writes=[b4d])
                hdv(lambda: nc.vector.tensor_tensor(out=X[0:C, :], in0=X[0:C, :], in1=b4[0:C, 0:128], op=ALU.add), reads=[b4d])
                if lev < levels - 1:
                    hav(lambda: nc.scalar.copy(out=N[0:C, 0:C], in_=b4[0:C, 128:128 + C]), reads=[b4d])
                    hav(lambda: nc.scalar.copy(out=NT[0:C, 0:C], in_=b4[0:C, 256:256 + C]), reads=[b4d])
                yield
            b5, b5d = pbank()
            hpe(lambda: nc.tensor.transpose(b5[:, 0:C], X[0:C, :], ident[0:C, 0:C]), reads=[d_const], writes=[b5d])
            hav(lambda: nc.scalar.copy(out=wT[0:64, 0:C], in_=b5[0:64, 0:C]), reads=[b5d])
            if kind == "s":
                Sv = Ssm
                sdp = dS
                for b in range(16):
                    load(Ssm[0:64, b, :], I["state_delta"][l, b, h], writes=[dS])
            else:
                Sv = dnS[l][:, h, :, :]
                sdp = st_dep
            yield
            b6, b6d = pbank()
            if nbb == 1:
                hpe(lambda: nc.tensor.matmul(b6[0:C, 0:64], lhsT=wT[0:64, 0:C], rhs=Sv[0:64, 0, :], start=True, stop=True), reads=[sdp], writes=[b6d])
                hpe(lambda: nc.tensor.matmul(b6[0:C, 64:128], lhsT=qs[0:64, 0:C], rhs=Sv[0:64, 0, :], start=True, stop=True), reads=[sdp, dqa[hc]], writes=[b6d])
                hdv(lambda: nc.vector.tensor_tensor(out=vn[0:C, :], in0=X[0:C, 64:128], in1=b6[0:C, 0:64], op=ALU.subtract), reads=[b6d])
                hdv(lambda: nc.vector.tensor_scalar_mul(out=o1[0:C, :], in0=b6[0:C, 64:128], scalar1=sv4[0:C, 5, h:h + 1]), reads=[b6d])
                yield
            else:
                for b in range(nbb):
                    hpe(lambda b=b: nc.tensor.matmul(b6[0:64, b * Tq:(b + 1) * Tq], lhsT=Sv[0:64, b, :], rhs=wT[0:64, b * Tq:(b + 1) * Tq], start=True, stop=True), reads=[sdp], writes=[b6d], inc=(b == nbb - 1))
                for b in range(nbb):
                    hpe(lambda b=b: nc.tensor.matmul(b6[0:64, 128 + b * Tq:128 + (b + 1) * Tq], lhsT=Sv[0:64, b, :], rhs=qs[0:64, b * Tq:(b + 1) * Tq], start=True, stop=True),
                        reads=[sdp, dqa[hc]], writes=[b6d], inc=(b == nbb - 1))
                hav(lambda: nc.scalar.copy(out=wT[0:64, 0:C], in_=b6[0:64, 0:C]), reads=[b6d])
                hav(lambda: nc.scalar.copy(out=gB[0:64, 0:C], in_=b6[0:64, 128:128 + C]), reads=[b6d])
                yield
                b7, b7d = pbank()
                hpe(lambda: nc.tensor.transpose(b7[0:C, 0:64], wT[0:64, 0:C], ident[0:64, 0:64]), reads=[d_const], writes=[b7d])
                hpe(lambda: nc.tensor.transpose(b7[0:C, 64:128], gB[0:64, 0:C], ident[0:64, 0:64]), reads=[d_const], writes=[b7d])
                hdv(lambda: nc.vector.tensor_tensor(out=vn[0:C, :], in0=X[0:C, 64:128], in1=b7[0:C, 0:64], op=ALU.subtract), reads=[b7d])
                hdv(lambda: nc.vector.tensor_scalar_mul(out=o1[0:C, :], in0=b7[0:C, 64:128], scalar1=sv4[0:C, 5, h:h + 1]), reads=[b7d])
                yield
            b8, b8d = pbank()
            hpe(lambda: nc.tensor.matmul(b8[0:C, 0:64], lhsT=AT[0:C, 0:C], rhs=vn[0:C, :], start=True, stop=True), writes=[b8d])
            hdv(lambda: nc.vector.tensor_tensor(out=oo[0:C, :], in0=o1[0:C, :], in1=b8[0:C, 0:64], op=ALU.add), reads=[b8d])
            if nbb == 1:
                b9, b9d = pbank()
                hpe(lambda: nc.tensor.matmul(b9[0:64, 0:64], lhsT=kd[0:C, :], rhs=vn[0:C, :], start=True, stop=True), writes=[b9d])
                op(DVE, lambda: nc.vector.scalar_tensor_tensor(out=Sv[0:64, 0, :], in0=Sv[0:64, 0, :], scalar=tT[0:64, 0:1], in1=b9[0:64, 0:64], op0=ALU.mult, op1=ALU.add),
                   reads=[b9d, hd, sdp], writes=[sdp])
            else:
                for half in range(2):
                    b9, b9d = pbank()
                    for bb in range(8):
                        b = half * 8 + bb
                        hdv(lambda b=b: nc.vector.tensor_scalar_mul(out=kdm[0:C, :], in0=kd[0:C, :], scalar1=Ecol[0:C, b:b + 1]))
                        hpe(lambda bb=bb, b9=b9: nc.tensor.matmul(b9[0:64, bb * 64:(bb + 1) * 64], lhsT=kdm[0:C, :], rhs=vn[0:C, :], start=True, stop=True), writes=[b9d])
                        op(DVE, lambda b=b, bb=bb, b9=b9: nc.vector.scalar_tensor_tensor(out=Sv[0:64, b, :], in0=Sv[0:64, b, :], scalar=tT[0:64, b * 8:b * 8 + 1], in1=b9[0:64, bb * 64:(bb + 1) * 64],
                                                                                        op0=ALU.mult, op1=ALU.add), reads=[b9d, hd, sdp], writes=[sdp])
                for b in range(16):
                    store(O["s_delta"][l, b, h], Ssm[0:64, b, :], reads=[dS])
            yield
            hav(lambda: nc.scalar.activation(out=vn[0:C, :], in_=oo[0:C, :], func=AF.Square))
            yield
            hdv(lambda: nc.vector.reduce_sum(out=kd[0:C, 0:1], in_=vn[0:C, :], axis=mybir.AxisListType.X))
            yield
            hav(lambda: nc.scalar.activation(out=kd[0:C, 0:1], in_=kd[0:C, 0:1], func=AF.Ln, scale=1.0 / 64, bias=eps_t[0:C, 0:1]), reads=[d_const])
            hav(lambda: nc.scalar.activation(out=kd[0:C, 0:1], in_=kd[0:C, 0:1], func=AF.Exp, scale=-0.5))
            yield
            hdv(lambda: nc.vector.scalar_tensor_tensor(out=oo[0:C, :], in0=oo[0:C, :], scalar=kd[0:C, 0:1], in1=ngB[0:C, :], op0=ALU.mult, op1=ALU.mult))
            op(DVE, lambda: nc.vector.tensor_tensor(out=oat[0:C, h * 64:(h + 1) * 64], in0=oo[0:C, :], in1=zsil[0:C, h * 64:(h + 1) * 64], op=ALU.mult), reads=[hd, dC], writes=[dC])

        def epilogue(c0, C, par):
            oat, dC = oat2[par], dC2[par]
            bo, bod = pbank()
            for c in range(2):
                op(PE, lambda c=c: nc.tensor.transpose(bo[:, c * 128:c * 128 + C], oat[0:C, c * 128:(c + 1) * 128], ident[0:C, 0:C]), reads=[dC, d_const], writes=[bod], inc=(c == 1))
            for c in range(2):
                op(ACT, lambda c=c: nc.scalar.copy(out=brT[:, c, c0:c0 + C], in_=bo[:, c * 128:c * 128 + C]), reads=[bod], writes=[br_dep[c][ti] for ti in range(len(tiles))])

        def drain(gens):
            gens = list(gens)
            while gens:
                for g_ in list(gens):
                    try:
                        next(g_)
                    except StopIteration:
                        gens.remove(g_)

        chunks = []
        for (kind, col0, nb, T) in pieces:
            if kind == "p":
                Cs = ([16] if name == "A" else []) + [CHUNK] * (1024 // CHUNK)
                c0 = col0
                for C in Cs:
                    chunks.append(("p", c0, C, 1, C))
                    c0 += C
            else:
                chunks.append(("s", col0, 128, nb, T))
        if name == "A":
            dv(lambda: nc.vector.memset(dnS[l][:], 0.0), writes=[st_dep])
        drain([prologue(chunks[0][0], chunks[0][1], chunks[0][2], 0)])
        for ci, (kind, c0, C, nbb, Tq) in enumerate(chunks):
            par = ci % 2
            nxt = [prologue(chunks[ci + 1][0], chunks[ci + 1][1], chunks[ci + 1][2], 1 - par)] if ci + 1 < len(chunks) else []
            hs = [head(kind, c0, C, par, h, nbb, Tq) for h in range(4)]
            if kind == "p":
                drain(hs + nxt)
            else:
                for hg in hs:
                    drain([hg])
                drain(nxt)
            epilogue(c0, C, par)
            if kind == "p" and name == "B" and (ci + 1 == len(chunks) or chunks[ci + 1][0] != "p"):
                for h in range(4):
                    store(O["p_delta"][l, h], dnS[l][:, h, 0, :], reads=[st_dep])

    def merge(l, name, TS, tiles):
        A = Arena()
        mg = A.take(KC * TSM // 2).bitcast(BF16).rearrange("p (c t) -> p c t", c=KC)
        dmg = [[Dep() for _ in tiles] for _ in range(KC)]
        wb_v2 = I["w_branch"][l].rearrange("n (k p) d -> p n k d", p=128)
        wo_v = I["w_out"][l].rearrange("(k p) d -> p k d", p=128)
        accs = [[A.take(512) for _ in tiles] for _ in range(2)]
        dac = [[Dep() for _ in tiles] for _ in range(2)]
        sgs = [A.take(512) for _ in range(3)]
        dsg = [Dep() for _ in range(3)]
        si = 0
        for mp in range(KC // 2):
            bslot, bdep, bsem = wslot()
            bv = bslot[:, 0:2048].rearrange("p (n k d) -> p n k d", n=4, k=2)
            for n in range(4):
                wload(bv[:, n], wb_v2[:, n, :, mp * 256:(mp + 1) * 256], bdep, bsem)
            for n in range(4):
                slot, sdep, ssem = wslot()
                sv = slot[:, 0:KC * 256].rearrange("p (k d) -> p k d", k=KC)
                c0 = OFF["zg"] + n * D + mp * 256
                wload(sv, w_in_v[l][:, :, c0:c0 + 256], sdep, ssem)
                for mm in range(2):
                    m = 2 * mp + mm
                    for ti, (s, w) in enumerate(tiles):
                        acc, da = accs[mm][ti], dac[mm][ti]
                        bz, bzd = pbank()
                        bp, bpd = pbank()
                        for k in range(KC):
                            op(PE, lambda k=k: nc.tensor.matmul(bz[:, 0:w], lhsT=sv[:, k, mm * 128:(mm + 1) * 128], rhs=uT[:, k, s:s + w], start=(k == 0), stop=(k == KC - 1)),
                               reads=[sdep, u_dep[k][ti]], writes=[bzd], inc=(k == KC - 1))
                        for k in range(2):
                            op(PE, lambda k=k: nc.tensor.matmul(bp[:, 0:w], lhsT=bv[:, n, k, mm * 128:(mm + 1) * 128], rhs=brT[:, 2 * n + k, s:s + w], start=(k == 0), stop=(k == 1)),
                               reads=[bdep, br_dep[2 * n + k][ti]], writes=[bpd], inc=(k == 1))
                        sgt, sgd = sgs[si % 3], dsg[si % 3]
                        si += 1
                        op(ACT, lambda: nc.scalar.activation(out=sgt[:, 0:w], in_=bz[:, 0:w], func=AF.Sigmoid), reads=[bzd], writes=[sgd])
                        if n == 0:
                            op(DVE, lambda: nc.vector.tensor_tensor(out=acc[:, 0:w], in0=sgt[:, 0:w], in1=bp[:, 0:w], op=ALU.mult), reads=[sgd, bpd], writes=[da])
                        else:
                            op(DVE, lambda: nc.vector.tensor_tensor(out=sgt[:, 0:w], in0=sgt[:, 0:w], in1=bp[:, 0:w], op=ALU.mult), reads=[sgd, bpd], writes=[sgd])
                            if n < 3:
                                op(DVE, lambda: nc.vector.tensor_tensor(out=acc[:, 0:w], in0=acc[:, 0:w], in1=sgt[:, 0:w], op=ALU.add), reads=[sgd, da], writes=[da])
                            else:
                                op(DVE, lambda: nc.vector.tensor_tensor(out=mg[:, m, s:s + w], in0=acc[:, 0:w], in1=sgt[:, 0:w], op=ALU.add), reads=[sgd, da], writes=[dmg[m][ti]])
        for m in range(KC):
            slot, sdep, ssem = wslot()
            sv = slot[:, 0:KC * 128].rearrange("p (k n) -> p k n", k=KC)
            wload(sv, wo_v[:, :, m * 128:(m + 1) * 128], sdep, ssem)
            for ti, (s, w) in enumerate(tiles):
                bank, bd = pbank()
                for k in range(KC):
                    op(PE, lambda k=k: nc.tensor.matmul(bank[:, 0:w], lhsT=sv[:, k, :], rhs=mg[:, k, s:s + w], start=(k == 0), stop=(k == KC - 1)),
                       reads=[sdep, dmg[k][ti]], writes=[bd], inc=(k == KC - 1))
                op(DVE, lambda: nc.vector.tensor_tensor(out=xT[:, m, s:s + w], in0=xT[:, m, s:s + w], in1=bank[:, 0:w], op=ALU.add),
                   reads=[bd, x_dep[m][ti]], writes=[x_dep[m][ti]])

    def zero_branch(i, tiles):
        for c in range(2):
            for ti, (s, w) in enumerate(tiles):
                op(DVE, lambda: nc.vector.memset(brT[:, 2 * i + c, s:s + w], 0.0), writes=[br_dep[2 * i + c][ti]])

    def mixer(g, l, name, TS, tiles):
        rmsnorm_to(tiles, GIDX["mix"] + l, lambda c, s, w: uT[:, c, s:s + w], lambda c, ti: [u_dep[c][ti]])
        barrier()
        skip = debug.get("skip", "")
        for i, (ch, fn) in enumerate((("a", branch_dn), ("b", branch_s5), ("c", branch_lru), ("d", branch_conv))):
            if fn is None or ch in skip:
                zero_branch(i, tiles)
            else:
                fn(l, name, TS, tiles)
                barrier()
            if f"{name}{l}_o{ch}" in debug:
                dump(f"{name}{l}_o{ch}", brT[:, 2 * i:2 * i + 2, 0:TS], [128, 2, TS], [br_dep[2 * i + c][ti] for c in range(2) for ti in range(len(tiles))])
        merge(l, name, TS, tiles)
        if f"{name}{l}_x2" in debug:
            dump(f"{name}{l}_x2", xT[:, :, 0:TS], [128, KC, TS], [x_dep[c][ti] for c in range(KC) for ti in range(len(tiles))])

    def load_tokens(src_rows, R, col0):
        load_T(lambda c: xT[:, c, col0:col0 + R], src_rows, R, D, [x_dep[c][ti] for c in range(KC) for ti in range(MAXT)])

    def run_st(name, TS, blocks, yblocks):
        tiles = split_tiles(TS)
        for (src, R, col0) in blocks:
            load_tokens(src, R, col0)
        if f"{name}0_x0" in debug:
            dump(f"{name}0_x0", xT[:, :, 0:TS], [128, KC, TS], [x_dep[c][ti] for c in range(KC) for ti in range(len(tiles))])
        for l in range(debug.get("nl", NL)):
            ffn(l, "ffn1", tiles)
            if f"{name}{l}_x1" in debug:
                dump(f"{name}{l}_x1", xT[:, :, 0:TS], [128, KC, TS], [x_dep[c][ti] for c in range(KC) for ti in range(len(tiles))])
            barrier()
            mixer(g, l, name, TS, tiles)
            barrier()
            ffn(l, "ffn2", tiles)
            barrier()
        if debug.get("nofinal"):
            return
        yT = arena[:, 0:KC * TSM].rearrange("p (c t) -> p c t", c=KC)
        y_dep = [[Dep() for _ in range(MAXT)] for _ in range(KC)]
        rmsnorm_to(tiles, 6, lambda c, s, w: yT[:, c, s:s + w], lambda c, ti: [y_dep[c][ti]])
        ally = [y_dep[c][ti] for c in range(KC) for ti in range(len(tiles))]
        for (dst, R, col0) in yblocks:
            store_T(dst, lambda c: yT[:, c, col0:col0 + R], R, D, ally)
        barrier()

    xp = I["x_prompt"]
    blocksA = [(I["meta_tokens"], 16, 0)] + [(xp[i * 128:(i + 1) * 128, :], 128, 16 + i * 128) for i in range(8)]
    yA = [(O["y_prompt"][i * 128:(i + 1) * 128, :], 128, 16 + i * 128) for i in range(8)]
    blocksB = [(xp[1024 + i * 128:1024 + (i + 1) * 128, :], 128, i * 128) for i in range(8)] + [(I["x_sample"], 128, 1024)]
    yB = [(O["y_prompt"][1024 + i * 128:1024 + (i + 1) * 128, :], 128, i * 128) for i in range(8)] + [(O["y_sample"], 128, 1024)]
    sts = debug.get("sts", "AB")
    if "A" in sts:
        run_st("A", TA, blocksA, yA)
    if "B" in sts:
        run_st("B", TB, blocksB, yB)

    for t in st_sems:
        if t.cnt:
            nc.sync.wait_ge(t.sem, t.cnt)


_CACHE = {}


def make_in_maps(inputs, cores=range(8)):
    maps = []
    for c in cores:
        m = {}
        m["x_prompt"] = np.ascontiguousarray(inputs["x_prompt"][c])
        m["x_sample"] = np.ascontiguousarray(inputs["x_sample"][16 * c:16 * c + 16]).reshape(128, D)
        for k in SNAMES:
            m[k] = np.ascontiguousarray(inputs[k][:, 16 * c:16 * c + 16])
        for k in WNAMES:
            m[k] = np.ascontiguousarray(inputs[k])
        maps.append(m)
    return maps


def kernel(**inputs):
    inputs = {k: np.asarray(v, dtype=np.float32) for k, v in inputs.items()}
    if "nc" not in _CACHE:
        _CACHE["nc"] = build()
    nc, g = _CACHE["nc"]
    maps = make_in_maps(inputs)
    for m in maps:
        for k, shp in g.shapes_in.items():
            m[k] = m[k].reshape(shp)
    res = run_bass_kernel_spmd(nc, maps, core_ids=list(range(8)))
    outs = []
    for nm in ONAMES:
        per = [r["o_" + nm] for r in res.results]
        if nm == "y_prompt":
            outs.append(np.stack(per, 0))
        elif nm == "y_sample":
            outs.append(np.concatenate([p.reshape(16, 8, D) for p in per], 0))
        elif nm.startswith("p_"):
            full = np.stack(per, 1)
            outs.append(full)
        else:
            full = np.concatenate(per, 1)
            outs.append(full)
    ref_shapes = dict(p_s5_re=(NL, 8, 16, 64), p_s5_im=(NL, 8, 16, 64), s_s5_re=(NL, 128, 16, 64), s_s5_im=(NL, 128, 16, 64))
    outs = [o.reshape(ref_shapes[nm]) if nm in ref_shapes else o for nm, o in zip(ONAMES, outs)]
    return tuple(np.ascontiguousarray(o, dtype=np.float32) for o in outs)
```
